# Optimizing a Trainium2 kernel written in Bass

```python
import math
import jax
import jax.numpy as jnp
from jax import lax
import numpy as np

D_MODEL = 4096
BATCH = 1
SEQ = 8192
DEPTH = 2

GRID_W = 64
CTX_LEN = 256
SSD_HEAD_DIM = 64
SSD_WIDTH = D_MODEL
SSD_HEADS = SSD_WIDTH // SSD_HEAD_DIM
SSD_GROUPS = 8
SSD_STATE = 128
SSD_CHUNK = 128
SSD_CONV = 5
SSD_GN = SSD_GROUPS * SSD_STATE
SSD_CONV_CH = SSD_WIDTH + 2 * SSD_GN
CF_WIDTH = D_MODEL // 2
CF_KERNEL = 31
FN_WIDTH = D_MODEL // 2
FN_GROUPS = 8
FN_GROUP_DIM = FN_WIDTH // FN_GROUPS
N_BRANCHES = 3
W_MIX = SSD_WIDTH + CF_WIDTH + FN_WIDTH
IN_SIZES = (SSD_CONV_CH, SSD_WIDTH, 2 * SSD_HEADS, 2 * CF_WIDTH, CF_WIDTH, FN_WIDTH, FN_WIDTH, N_BRANCHES * D_MODEL)
IN_COLS = sum(IN_SIZES)
IN_SPLIT_POINTS = tuple(int(s) for s in np.cumsum(IN_SIZES)[:-1])
EPS = 1e-6
DT_MIN = 1e-3
DT_MAX = 1e-1
A_INIT_MAX = 16.0

kernel_name = 'hybrid_ssd_conformer_fourier_dit_block'


def rms_norm(x, g):
    xf = x.astype(jnp.float32)
    y = xf * lax.rsqrt(jnp.mean(xf * xf, axis=-1, keepdims=True) + EPS)
    return (y * g.astype(jnp.float32)).astype(x.dtype)


def layer_norm(x, g, b):
    xf = x.astype(jnp.float32)
    mu = jnp.mean(xf, axis=-1, keepdims=True)
    xc = xf - mu
    y = xc * lax.rsqrt(jnp.mean(xc * xc, axis=-1, keepdims=True) + EPS)
    return (y * g.astype(jnp.float32) + b.astype(jnp.float32)).astype(x.dtype)


def depthwise_conv(x, w, b):
    k, ch = w.shape
    out = lax.conv_general_dilated(x, w[:, None, :].astype(x.dtype), window_strides=(1,),
                                   padding=[(k // 2, k // 2)],
                                   dimension_numbers=('NWC', 'WIO', 'NWC'),
                                   feature_group_count=ch)
    return out + b.astype(x.dtype)


def ssd_chunked(xs, dt, a_log, bm, cm, h0):
    bsz, seq = xs.shape[:2]
    q, g, n, p = SSD_CHUNK, SSD_GROUPS, SSD_STATE, SSD_HEAD_DIM
    r = SSD_HEADS // g
    nc = seq // q
    a_neg = -jnp.exp(a_log.astype(jnp.float32))
    x = xs.astype(jnp.float32).reshape(bsz, nc, q, g, r, p)
    dtc = dt.reshape(bsz, nc, q, g, r)
    a = dtc * a_neg.reshape(g, r)
    xdt = x * dtc[..., None]
    bq = bm.astype(jnp.float32).reshape(bsz, nc, q, g, n)
    cq = cm.astype(jnp.float32).reshape(bsz, nc, q, g, n)
    a_cum = jnp.cumsum(a, axis=2)
    seg = a_cum[:, :, :, None] - a_cum[:, :, None, :]
    lower = jnp.tril(jnp.ones((q, q), dtype=bool))[:, :, None, None]
    decay = jnp.exp(jnp.where(lower, seg, -jnp.inf))
    scores = jnp.einsum('bcign,bcjgn->bcijg', cq, bq)
    y_diag = jnp.einsum('bcijg,bcijgr,bcjgrp->bcigrp', scores, decay, xdt)
    decay_to_end = jnp.exp(a_cum[:, :, -1:] - a_cum)
    states = jnp.einsum('bcjgn,bcjgr,bcjgrp->bcgrpn', bq, decay_to_end, xdt)
    chunk_decay = jnp.exp(a_cum[:, :, -1])

    def step(h, inp):
        s, d = inp
        return h * d[..., None, None] + s, h

    h_final, h_in = lax.scan(step, h0.reshape(bsz, g, r, p, n),
                             (jnp.moveaxis(states, 1, 0), jnp.moveaxis(chunk_decay, 1, 0)))
    h_in = jnp.moveaxis(h_in, 0, 1)
    y_off = jnp.einsum('bcign,bcgrpn,bcigr->bcigrp', cq, h_in, jnp.exp(a_cum))
    y = (y_diag + y_off).reshape(bsz, seq, SSD_HEADS, p)
    return y, h_final.reshape(bsz, SSD_HEADS, p, n)


def ssd_stream(xbc, dt_raw, h0_f, h0_b, conv_w, conv_b, dt_bias, a_log, d_skip):
    u = jax.nn.silu(depthwise_conv(xbc, conv_w, conv_b))
    bsz, seq, _ = u.shape
    xs, bm, cm = jnp.split(u, [SSD_WIDTH, SSD_WIDTH + SSD_GN], axis=-1)
    xs = xs.reshape(bsz, seq, SSD_HEADS, SSD_HEAD_DIM)
    bm = bm.reshape(bsz, seq, SSD_GROUPS, SSD_STATE)
    cm = cm.reshape(bsz, seq, SSD_GROUPS, SSD_STATE)
    dt = jax.nn.softplus(dt_raw.astype(jnp.float32).reshape(bsz, seq, 2, SSD_HEADS)
                         + dt_bias.astype(jnp.float32))
    flip = lambda t: jnp.flip(t, axis=1)
    y_f, h_f = ssd_chunked(xs, dt[:, :, 0], a_log[0], bm, cm, h0_f)
    y_b, h_b = ssd_chunked(flip(xs), flip(dt[:, :, 1]), a_log[1], flip(bm), flip(cm), h0_b)
    y = y_f + flip(y_b) + d_skip.astype(jnp.float32)[:, None] * xs.astype(jnp.float32)
    return y.reshape(bsz, seq, SSD_WIDTH).astype(xbc.dtype), h_f, h_b


def conformer_conv_branch(glu_in, z, conv_w, conv_b, ln_g, ln_b, on_grid):
    a, b = jnp.split(glu_in, 2, axis=-1)
    u = a * jax.nn.sigmoid(b)
    bsz, seq, ch = u.shape
    if on_grid:
        rows = seq // GRID_W
        v = depthwise_conv(u.reshape(bsz * rows, GRID_W, ch), conv_w, conv_b).reshape(bsz, seq, ch)
    else:
        v = depthwise_conv(u, conv_w, conv_b)
    v = layer_norm(v, ln_g, ln_b)
    return jax.nn.silu(v) * jax.nn.silu(z)


def fourier_branch(v, z):
    bsz, seq, _ = v.shape
    vg = v.astype(jnp.float32).reshape(bsz, seq, FN_GROUPS, FN_GROUP_DIM)
    f = jnp.fft.fft2(vg, axes=(1, 3), norm='ortho').real.reshape(bsz, seq, FN_WIDTH)
    return f.astype(v.dtype) * jax.nn.silu(z)


def merge_branches(y_a, y_b, y_c, gate_pre, w_branch, w_out):
    p_a = y_a @ w_branch[:SSD_WIDTH]
    p_b = y_b @ w_branch[SSD_WIDTH:SSD_WIDTH + CF_WIDTH]
    p_c = y_c @ w_branch[SSD_WIDTH + CF_WIDTH:]
    g_a, g_b, g_c = jnp.split(jax.nn.sigmoid(gate_pre), 3, axis=-1)
    return (g_a * p_a + g_b * p_b + g_c * p_c) @ w_out


def hybrid_layer(x, ctx, c, c_ctx, w_mod, b_mod, norm_g, w_in, ssd_conv_w, ssd_conv_b,
                 ssd_dt_bias, ssd_a_log, ssd_d, ssd_norm_g, cf_conv_w, cf_conv_b, cf_ln_g,
                 cf_ln_b, w_branch, w_out, update_ctx):
    shift_x, scale_x, gate_x = jnp.split((jax.nn.silu(c) @ w_mod + b_mod)[:, None, :], 3, axis=-1)
    shift_c, scale_c, gate_c = jnp.split(jax.nn.silu(c_ctx) @ w_mod + b_mod, 3, axis=-1)
    h_x = rms_norm(x, norm_g) * (1 + scale_x) + shift_x
    h_c = rms_norm(ctx, norm_g) * (1 + scale_c) + shift_c
    xbc_x, za_x, dt_x, glu_x, zb_x, v_x, zc_x, gates_x = jnp.split(h_x @ w_in, IN_SPLIT_POINTS, axis=-1)
    xbc_c, za_c, dt_c, glu_c, zb_c, v_c, zc_c, gates_c = jnp.split(h_c @ w_in, IN_SPLIT_POINTS, axis=-1)
    ssd_params = (ssd_conv_w, ssd_conv_b, ssd_dt_bias, ssd_a_log, ssd_d)
    h_zero = jnp.zeros((x.shape[0], SSD_HEADS, SSD_HEAD_DIM, SSD_STATE), jnp.float32)
    ya_c, h_f, h_b = ssd_stream(xbc_c, dt_c, h_zero, h_zero, *ssd_params)
    ya_x, _, _ = ssd_stream(xbc_x, dt_x, h_f, h_b, *ssd_params)
    ya_x = rms_norm(ya_x * jax.nn.silu(za_x), ssd_norm_g)
    yb_x = conformer_conv_branch(glu_x, zb_x, cf_conv_w, cf_conv_b, cf_ln_g, cf_ln_b, True)
    yc_x = fourier_branch(v_x, zc_x)
    x_new = x + gate_x * merge_branches(ya_x, yb_x, yc_x, gates_x, w_branch, w_out)
    if update_ctx:
        ya_c = rms_norm(ya_c * jax.nn.silu(za_c), ssd_norm_g)
        yb_c = conformer_conv_branch(glu_c, zb_c, cf_conv_w, cf_conv_b, cf_ln_g, cf_ln_b, False)
        yc_c = fourier_branch(v_c, zc_c)
        ctx_new = ctx + gate_c * merge_branches(ya_c, yb_c, yc_c, gates_c, w_branch, w_out)
    else:
        ctx_new = ctx
    return x_new, ctx_new


def setup_inputs(seed: int = 0) -> dict:
    key = jax.random.key(seed)
    ks = jax.random.split(key, 24)
    nrm = jax.random.normal
    d = D_MODEL
    x = nrm(ks[0], (BATCH, SEQ, d), jnp.float32)
    c = nrm(ks[1], (BATCH, d), jnp.float32)
    ctx = nrm(ks[2], (BATCH, CTX_LEN, d), jnp.float32)
    c_ctx = nrm(ks[3], (d,), jnp.float32)
    w_mod = nrm(ks[4], (DEPTH, d, 3 * d), jnp.float32) * (0.5 * d ** -0.5)
    b_mod = 0.01 * nrm(ks[5], (DEPTH, 3 * d), jnp.float32)
    norm_g = 1.0 + 0.02 * nrm(ks[6], (DEPTH, d), jnp.float32)
    w_in = nrm(ks[7], (DEPTH, d, IN_COLS), jnp.float32) * d ** -0.5
    ssd_conv_w = nrm(ks[8], (DEPTH, SSD_CONV, SSD_CONV_CH), jnp.float32) * SSD_CONV ** -0.5
    ssd_conv_b = 0.02 * nrm(ks[9], (DEPTH, SSD_CONV_CH), jnp.float32)
    u = jax.random.uniform(ks[10], (DEPTH, 2, SSD_HEADS), jnp.float32)
    dt0 = jnp.exp(u * (math.log(DT_MAX) - math.log(DT_MIN)) + math.log(DT_MIN))
    ssd_dt_bias = dt0 + jnp.log(-jnp.expm1(-dt0))
    ssd_a_log = jnp.log(jax.random.uniform(ks[11], (DEPTH, 2, SSD_HEADS), jnp.float32, 1.0, A_INIT_MAX))
    ssd_d = 1.0 + 0.1 * nrm(ks[12], (DEPTH, SSD_HEADS), jnp.float32)
    ssd_norm_g = 1.0 + 0.02 * nrm(ks[13], (DEPTH, SSD_WIDTH), jnp.float32)
    cf_conv_w = nrm(ks[14], (DEPTH, CF_KERNEL, CF_WIDTH), jnp.float32) * CF_KERNEL ** -0.5
    cf_conv_b = 0.02 * nrm(ks[15], (DEPTH, CF_WIDTH), jnp.float32)
    cf_ln_g = 1.0 + 0.02 * nrm(ks[16], (DEPTH, CF_WIDTH), jnp.float32)
    cf_ln_b = 0.02 * nrm(ks[17], (DEPTH, CF_WIDTH), jnp.float32)
    w_branch = jnp.concatenate([
        nrm(ks[18], (DEPTH, SSD_WIDTH, d), jnp.float32) * SSD_WIDTH ** -0.5,
        nrm(ks[19], (DEPTH, CF_WIDTH, d), jnp.float32) * CF_WIDTH ** -0.5,
        nrm(ks[20], (DEPTH, FN_WIDTH, d), jnp.float32) * FN_WIDTH ** -0.5], axis=1)
    w_out = nrm(ks[21], (DEPTH, d, d), jnp.float32) * d ** -0.5
    final_g = 1.0 + 0.02 * nrm(ks[22], (d,), jnp.float32)
    return {'x': x, 'c': c, 'ctx': ctx, 'c_ctx': c_ctx, 'w_mod': w_mod, 'b_mod': b_mod,
            'norm_g': norm_g, 'w_in': w_in, 'ssd_conv_w': ssd_conv_w, 'ssd_conv_b': ssd_conv_b,
            'ssd_dt_bias': ssd_dt_bias, 'ssd_a_log': ssd_a_log, 'ssd_d': ssd_d,
            'ssd_norm_g': ssd_norm_g, 'cf_conv_w': cf_conv_w, 'cf_conv_b': cf_conv_b,
            'cf_ln_g': cf_ln_g, 'cf_ln_b': cf_ln_b, 'w_branch': w_branch, 'w_out': w_out,
            'final_g': final_g}


def reference(x, c, ctx, c_ctx, w_mod, b_mod, norm_g, w_in, ssd_conv_w, ssd_conv_b, ssd_dt_bias,
              ssd_a_log, ssd_d, ssd_norm_g, cf_conv_w, cf_conv_b, cf_ln_g, cf_ln_b, w_branch,
              w_out, final_g):
    for i in range(DEPTH):
        x, ctx = hybrid_layer(x, ctx, c, c_ctx, w_mod[i], b_mod[i], norm_g[i], w_in[i],
                              ssd_conv_w[i], ssd_conv_b[i], ssd_dt_bias[i], ssd_a_log[i], ssd_d[i],
                              ssd_norm_g[i], cf_conv_w[i], cf_conv_b[i], cf_ln_g[i], cf_ln_b[i],
                              w_branch[i], w_out[i], update_ctx=(i < DEPTH - 1))
    return rms_norm(x, final_g)
```

```python
import numpy as np
import ml_dtypes
import concourse.bass as bass
import concourse.mybir as mybir
from concourse.bass_utils import run_bass_kernel_spmd

F32 = mybir.dt.float32
BF16 = mybir.dt.bfloat16
I32 = mybir.dt.int32
AF = mybir.ActivationFunctionType
ALU = mybir.AluOpType
AX = mybir.AxisListType

NCORES = 8
D = 4096
SEQ = 8192
CTX = 256
TOK_X = SEQ // NCORES
TOK_C = CTX // NCORES
TOK = TOK_X + TOK_C
NTOK = SEQ + CTX
KC = D // 128
EPS = 1e-6

NDMASEM = 40
CC_INC = 16


class Prog:
    def __init__(self):
        self.nc = bass.Bass("TRN2", target_bir_lowering=False)
        nc = self.nc
        self.eng_names = ["pe", "act", "dve", "pool", "sp"]
        self.lists = {e: [] for e in self.eng_names}
        self.count = {e: 0 for e in self.eng_names}
        self.sem = {e: nc.alloc_semaphore("s_" + e) for e in ["pe", "act", "dve", "pool"]}
        self.dsem = [nc.alloc_semaphore("d%d" % i) for i in range(NDMASEM)]
        self.dval = [0] * NDMASEM
        self.dnext = 0
        self.waited = {e: {} for e in self.eng_names}
        self.last_w = {}
        self.readers = {}
        self.n_alloc = 0
        self.sb_ptr = 16512
        self.sb_end = 229344
        self.banks = None

    def dram(self, name, shape, dt, kind):
        return self.nc.dram_tensor(name, list(shape), dt, kind=kind).ap()

    def sb(self, name, shape, dt):
        esz = 4 if dt in (F32, I32) else 2
        n = 1
        for d_ in shape[1:]:
            n *= d_
        nbytes = (n * esz + 63) // 64 * 64
        off = self.sb_ptr
        assert off + nbytes <= self.sb_end, ("SBUF overflow", name, off, nbytes)
        self.sb_ptr += nbytes
        self.n_alloc += 1
        return self.nc.alloc_sbuf_tensor_at("%s_%d" % (name, self.n_alloc), list(shape), dt, offset=off)

    def mark(self):
        return self.sb_ptr

    def release(self, mark):
        self.barrier()
        self.sb_ptr = mark

    def barrier(self):
        deps = [(e, self.count[e]) for e in ["pe", "act", "dve", "pool"] if self.count[e] > 0]
        deps += [(("d", i), self.dval[i]) for i in range(NDMASEM) if self.dval[i] > 0]
        for e in self.eng_names:
            self._emit_waits(e, [d_ for d_ in deps if d_[0] != e])

    def get_banks(self):
        if self.banks is None:
            self.banks = [self.nc.alloc_psum_tensor("bank%d" % i, [128, 512], F32) for i in range(8)]
        return self.banks

    def ps(self, name, shape, dt=F32):
        return self.nc.alloc_psum_tensor(name, list(shape), dt)

    def _deps(self, eng, r, w):
        deps = []
        for k in r:
            t = self.last_w.get(k)
            if t is not None:
                deps.append(t)
        for k in w:
            t = self.last_w.get(k)
            if t is not None and t[0] != eng:
                deps.append(t)
            for t in self.readers.get(k, ()):
                if t[0] != eng:
                    deps.append(t)
        if eng == "pe":
            deps = [t for t in deps if t[0] != "pe"]
        return deps

    def _emit_waits(self, eng, deps):
        wd = self.waited[eng]
        for (s, v) in deps:
            if wd.get(s, 0) < v:
                wd[s] = v
                self.lists[eng].append(("wait", s, v))

    def _record(self, tok, r, w):
        for k in r:
            self.readers.setdefault(k, []).append(tok)
        for k in w:
            self.last_w[k] = tok
            self.readers[k] = []

    def op(self, eng, fn, r=(), w=()):
        self._emit_waits(eng, self._deps(eng, r, w))
        self.count[eng] += 1
        tok = (eng, self.count[eng])
        self.lists[eng].append(("op", fn))
        self._record(tok, r, w)
        return tok

    def dma(self, q, out, in_, r=(), w=()):
        i = self.dnext
        self.dnext = (self.dnext + 1) % NDMASEM
        deps = self._deps(("d", i), r, w)
        if self.dval[i] > 0:
            deps.append((("d", i), self.dval[i]))
        self._emit_waits(q, deps)
        self.dval[i] += 16
        tok = (("d", i), self.dval[i])
        self.lists[q].append(("dma", out, in_, i))
        self._record(tok, r, w)
        return tok

    def _semh(self, s):
        if isinstance(s, tuple):
            return self.dsem[s[1]]
        return self.sem[s]

    def finish(self):
        nc = self.nc
        for i in range(NDMASEM):
            if self.dval[i] > 0:
                self._emit_waits("sp", [(("d", i), self.dval[i])])
        for e in ["pe", "act", "dve", "pool"]:
            if self.count[e] > 0:
                self._emit_waits("sp", [(e, self.count[e])])

        def replay(ename):
            def f(eng):
                for it in self.lists[ename]:
                    if it[0] == "wait":
                        eng.wait_ge(self._semh(it[1]), it[2])
                    elif it[0] == "op":
                        it[1](eng).then_inc(self.sem[ename], 1)
                    elif it[0] == "cc":
                        eng.collective_compute(it[1], ALU.bypass, replica_groups=[list(range(NCORES))],
                                               ins=[it[3]], outs=[it[2]]).then_inc(self.dsem[it[4]], CC_INC)
                    else:
                        eng.dma_start(out=it[1], in_=it[2]).then_inc(self.dsem[it[3]], 16)
            return f

        with nc.Block() as block:
            block.tensor(replay("pe"))
            block.scalar(replay("act"))
            block.vector(replay("dve"))
            block.gpsimd(replay("pool"))
            block.sync(replay("sp"))
        return nc


MODC = 3 * D // NCORES


def build_p0():
    P = Prog()
    c_in = P.dram("c2", [2, D], F32, "ExternalInput")
    wmod = P.dram("wmod", [2, D, MODC], F32, "ExternalInput")
    bmod = P.dram("bmod", [2, MODC], F32, "ExternalInput")
    out = P.dram("mod", [2, 2, MODC], F32, "ExternalOutput")

    craw = P.sb("craw", [128, 2, KC], F32)
    cs = P.sb("cs", [128, KC, 2], F32)
    bsb = P.sb("bsb", [2, 2, MODC], F32)
    res = P.sb("res", [2, 2, MODC], F32)
    wbuf = [P.sb("wbuf%d" % i, [128, 8, MODC], F32) for i in range(2)]
    pst = [P.ps("pst%d" % i, [2, 512], F32) for i in range(3)]

    P.dma("sp", craw[:, :, :], c_in.rearrange("j (p k) -> p j k", k=KC), w=["craw"])
    for l in range(2):
        for j in range(2):
            P.dma("sp", bsb[j:j + 1, l, :], bmod[l:l + 1, :], w=["bsb"])
    for j in range(2):
        P.op("act", lambda e, j=j: e.activation(out=cs[:, :, j], in_=craw[:, j, :], func=AF.Silu),
             r=["craw"], w=["cs"])
    it = 0
    for l in range(2):
        for kg in range(4):
            b = it % 2
            it += 1
            P.dma("sp", wbuf[b][:, :, :],
                  wmod[l].rearrange("(p k) n -> p k n", k=KC)[:, kg * 8:(kg + 1) * 8, :],
                  w=["wbuf%d" % b])
            for kk in range(8):
                k = kg * 8 + kk
                for n in range(3):
                    P.op("pe", lambda e, b=b, kk=kk, n=n, k=k: e.matmul(
                        pst[n][:, :], lhsT=cs[:, k, :], rhs=wbuf[b][:, kk, n * 512:(n + 1) * 512],
                        start=(k == 0), stop=(k == KC - 1)),
                        r=["cs", "wbuf%d" % b], w=["pst%d" % n])
        for n in range(3):
            P.op("dve", lambda e, n=n, l=l: e.tensor_tensor(
                out=res[:, l, n * 512:(n + 1) * 512], in0=pst[n][:, :], in1=bsb[:, l, n * 512:(n + 1) * 512],
                op=ALU.add), r=["pst%d" % n, "bsb"], w=["res"])
    P.dma("sp", out.rearrange("l j n -> j l n"), res[:, :, :], r=["res"])
    return P.finish()


def run_p0(inp):
    nc = build_p0()
    c2 = np.stack([inp["c"][0], inp["c_ctx"]]).astype(np.float32)
    in_maps = []
    for r in range(NCORES):
        sl = slice(r * MODC, (r + 1) * MODC)
        in_maps.append({"c2": c2,
                        "wmod": np.ascontiguousarray(inp["w_mod"][:, :, sl]),
                        "bmod": np.ascontiguousarray(inp["b_mod"][:, sl])})
    res = run_bass_kernel_spmd(nc, in_maps, core_ids=list(range(NCORES)))
    return np.concatenate([r["mod"] for r in res.results], axis=2)


def _prog_collective(self, kind, out, in_, r=(), w=()):
    i = self.dnext
    self.dnext = (self.dnext + 1) % NDMASEM
    deps = self._deps(("d", i), r, w)
    if self.dval[i] > 0:
        deps.append((("d", i), self.dval[i]))
    self._emit_waits("pool", deps)
    self.dval[i] += 16
    tok = (("d", i), self.dval[i])
    self.lists["pool"].append(("cc", kind, out, in_, i))
    self._record(tok, r, w)
    return tok


Prog.collective = _prog_collective


TTILES = [(0, 512, 0), (512, 1024, 0), (1024, 1056, 1)]


def vec_layout(v):
    return np.ascontiguousarray(np.asarray(v, np.float32).reshape(KC, 128).T)


def emit_norm_mod(P, xT, hT, gvec, modv, out_dt, pfx="n"):
    ones = P.sb(pfx + "ones", [128, 128], F32)
    g_sb = P.sb(pfx + "g", [128, KC], F32)
    m_sb = P.sb(pfx + "m", [128, 2, 3, KC], F32)
    G = P.sb(pfx + "G", [128, 2, KC], F32)
    xb = [P.sb(pfx + "xb%d" % i, [128, KC, 512], F32) for i in range(2)]
    sq = [P.sb(pfx + "sq%d" % i, [128, 512], F32) for i in range(3)]
    tmp = [P.sb(pfx + "tmp%d" % i, [128, 512], F32) for i in range(3)]
    ho = [P.sb(pfx + "ho%d" % i, [128, 512], out_dt) for i in range(3)]
    rt = P.sb(pfx + "rt", [128, 512], F32)
    rstd = P.sb(pfx + "rstd", [128, 512], F32)
    pss = P.ps(pfx + "pss", [128, 512], F32)

    P.op("pool", lambda e: e.memset(ones[:, :], 1.0), w=[pfx + "ones"])
    P.dma("sp", g_sb[:, :], gvec, w=[pfx + "g"])
    P.dma("sp", m_sb[:, :, :, :], modv, w=[pfx + "m"])
    for j in range(2):
        P.op("dve", lambda e, j=j: e.scalar_tensor_tensor(
            out=G[:, j, :], in0=m_sb[:, j, 1, :], scalar=1.0, in1=g_sb[:, :], op0=ALU.add, op1=ALU.mult),
            r=[pfx + "m", pfx + "g"], w=[pfx + "G"])
    xv = xT.rearrange("(k p) t -> p k t", p=128)
    n = 0
    for ti, (t0, t1, j) in enumerate(TTILES):
        w_ = t1 - t0
        b = ti % 2
        xk = pfx + "xb%d" % b
        P.dma("sp", xb[b][:, :, 0:w_], xv[:, :, t0:t1], w=[xk])
        for kc in range(KC):
            s = kc % 3
            P.op("act", lambda e, b=b, kc=kc, s=s, w_=w_: e.activation(
                out=sq[s][:, 0:w_], in_=xb[b][:, kc, 0:w_], func=AF.Square), r=[xk], w=[pfx + "sq%d" % s])
            P.op("pe", lambda e, kc=kc, s=s, w_=w_: e.matmul(
                pss[:, 0:w_], lhsT=ones[:, :], rhs=sq[s][:, 0:w_], start=(kc == 0), stop=(kc == KC - 1)),
                r=[pfx + "ones", pfx + "sq%d" % s], w=[pfx + "pss"])
        P.op("act", lambda e, w_=w_: e.activation(out=rt[:, 0:w_], in_=pss[:, 0:w_], func=AF.Sqrt,
                                                   bias=EPS, scale=1.0 / D), r=[pfx + "pss"], w=[pfx + "rt"])
        P.op("dve", lambda e, w_=w_: e.reciprocal(out=rstd[:, 0:w_], in_=rt[:, 0:w_]), r=[pfx + "rt"], w=[pfx + "rstd"])
        for kc in range(KC):
            s = n % 3
            n += 1
            P.op("dve", lambda e, b=b, kc=kc, s=s, w_=w_, j=j: e.scalar_tensor_tensor(
                out=tmp[s][:, 0:w_], in0=xb[b][:, kc, 0:w_], scalar=G[:, j, kc:kc + 1], in1=rstd[:, 0:w_],
                op0=ALU.mult, op1=ALU.mult), r=[xk, pfx + "G", pfx + "rstd"], w=[pfx + "tmp%d" % s])
            P.op("act", lambda e, kc=kc, s=s, w_=w_, j=j: e.activation(
                out=ho[s][:, 0:w_], in_=tmp[s][:, 0:w_], func=AF.Identity, bias=m_sb[:, j, 0, kc:kc + 1], scale=1.0),
                r=[pfx + "tmp%d" % s, pfx + "m"], w=[pfx + "ho%d" % s])
            P.dma("sp", hT[kc * 128:(kc + 1) * 128, t0:t1], ho[s][:, 0:w_], r=[pfx + "ho%d" % s])


def build_p1(out_dt):
    P = Prog()
    xT = P.dram("xT", [D, TOK], F32, "ExternalInput")
    gvec = P.dram("gvec", [128, KC], F32, "ExternalInput")
    modv = P.dram("modv", [128, 2, 3, KC], F32, "ExternalInput")
    hT = P.dram("hT", [D, TOK], out_dt, "ExternalOutput")
    emit_norm_mod(P, xT, hT, gvec, modv, out_dt)
    return P.finish()


def mod_layout(mod_l):
    m = np.asarray(mod_l, np.float32).reshape(2, 3, KC, 128)
    return np.ascontiguousarray(m.transpose(3, 0, 1, 2))


def shard_tokens_T(x2d, c2d):
    outs = []
    for r in range(NCORES):
        t = np.concatenate([x2d[r * TOK_X:(r + 1) * TOK_X], c2d[r * TOK_C:(r + 1) * TOK_C]], axis=0)
        outs.append(np.ascontiguousarray(t.T))
    return outs


def unshard_tokens_T(per_core):
    xs = np.concatenate([p[:, :TOK_X].T for p in per_core], axis=0)
    cs = np.concatenate([p[:, TOK_X:].T for p in per_core], axis=0)
    return xs, cs


_NC_CACHE = {}


def get_nc(key, builder):
    if key not in _NC_CACHE:
        _NC_CACHE[key] = builder()
    return _NC_CACHE[key]


def run_p1(xT_list, gvec, mod_l, final=False):
    nc = get_nc(("p1", final), lambda: build_p1(F32 if final else BF16))
    g = vec_layout(gvec)
    m = mod_layout(mod_l)
    in_maps = [{"xT": xT_list[r], "gvec": g, "modv": m} for r in range(NCORES)]
    res = run_bass_kernel_spmd(nc, in_maps, core_ids=list(range(NCORES)))
    return [r["hT"] for r in res.results]


NFM = 28
NTM = 528
FM_GROUPS = [(0, 7), (7, 14), (14, 21), (21, 28)]
A_TILES = [(0, 256)] + [(256 + 512 * i, 256 + 512 * (i + 1)) for i in range(16)]


def emit_gemm(P, hT_all, w_fm, w_tm, fm_dst, tm_vz, tm_dt):
    banks = P.get_banks()
    m0 = P.mark()
    hv = hT_all.rearrange("(k p) t -> p k t", p=128)
    wfv = w_fm.rearrange("(k p) n -> p k n", p=128)
    wtv = w_tm.rearrange("(k p) n -> p k n", p=128)
    wres = P.sb("wres", [128, KC, 896], BF16)
    wtm = P.sb("wtm", [128, KC, NTM], BF16)
    stg = [P.sb("stg%d" % i, [128, 4, 896], F32) for i in range(2)]
    hb = [P.sb("hb%d" % i, [128, KC, 512], BF16) for i in range(2)]
    ob = [P.sb("ob%d" % i, [128, 512], BF16) for i in range(4)]
    otv = [P.sb("otv%d" % i, [128, 512], BF16) for i in range(2)]
    otd = [P.sb("otd%d" % i, [128, 16], F32) for i in range(2)]
    nstg = 0
    for q in range(8):
        b = nstg % 2
        nstg += 1
        P.dma("sp", stg[b][:, :, 0:NTM], wtv[:, q * 4:(q + 1) * 4, :], w=["stg%d" % b])
        P.op("pool" if q % 2 else "dve", lambda e, b=b, q=q: e.tensor_copy(
            out=wtm[:, q * 4:(q + 1) * 4, :], in_=stg[b][:, :, 0:NTM]), r=["stg%d" % b], w=["wtm"])
    nob = 0
    nhb = 0
    nbank = 0
    ntm = 0
    for gi, (c0, c1) in enumerate(FM_GROUPS):
        ncg = c1 - c0
        for q in range(8):
            b = nstg % 2
            nstg += 1
            P.dma("sp", stg[b][:, :, 0:ncg * 128], wfv[:, q * 4:(q + 1) * 4, c0 * 128:c1 * 128], w=["stg%d" % b])
            P.op("pool" if q % 2 else "dve", lambda e, b=b, q=q, ncg=ncg: e.tensor_copy(
                out=wres[:, q * 4:(q + 1) * 4, 0:ncg * 128], in_=stg[b][:, :, 0:ncg * 128]),
                r=["stg%d" % b], w=["wres"])
        for (t0, t1) in A_TILES:
            tw = t1 - t0
            hbi = nhb % 2
            nhb += 1
            hk = "hb%d" % hbi
            P.dma("sp", hb[hbi][:, :, 0:tw], hv[:, :, t0:t1], w=[hk])
            for cj in range(ncg):
                bk = nbank % 4
                nbank += 1
                for kc in range(KC):
                    P.op("pe", lambda e, bk=bk, kc=kc, cj=cj, hbi=hbi, tw=tw: e.matmul(
                        banks[bk][:, 0:tw], lhsT=wres[:, kc, cj * 128:(cj + 1) * 128], rhs=hb[hbi][:, kc, 0:tw],
                        start=(kc == 0), stop=(kc == KC - 1)), r=["wres", hk], w=["bank%d" % bk])
                o = nob % 4
                nob += 1
                if nob % 2:
                    P.op("act", lambda e, o=o, bk=bk, tw=tw: e.copy(out=ob[o][:, 0:tw], in_=banks[bk][:, 0:tw]),
                         r=["bank%d" % bk], w=["ob%d" % o])
                else:
                    P.op("dve", lambda e, o=o, bk=bk, tw=tw: e.tensor_copy(out=ob[o][:, 0:tw], in_=banks[bk][:, 0:tw]),
                         r=["bank%d" % bk], w=["ob%d" % o])
                P.dma("pool", fm_dst(c0 + cj)[:, t0:t1], ob[o][:, 0:tw], r=["ob%d" % o])
            if gi == 0:
                for s0 in range(0, tw, 128):
                    for kc in range(KC):
                        P.op("pe", lambda e, kc=kc, hbi=hbi, s0=s0: e.matmul(
                            banks[4][:, :], lhsT=hb[hbi][:, kc, s0:s0 + 128], rhs=wtm[:, kc, 0:512],
                            start=(kc == 0), stop=(kc == KC - 1)), r=["wtm", hk], w=["bank4"])
                        P.op("pe", lambda e, kc=kc, hbi=hbi, s0=s0: e.matmul(
                            banks[5][:, 0:16], lhsT=hb[hbi][:, kc, s0:s0 + 128], rhs=wtm[:, kc, 512:528],
                            start=(kc == 0), stop=(kc == KC - 1)), r=["wtm", hk], w=["bank5"])
                    o = ntm % 2
                    ntm += 1
                    P.op("act", lambda e, o=o: e.copy(out=otv[o][:, :], in_=banks[4][:, :]), r=["bank4"], w=["otv%d" % o])
                    P.op("dve", lambda e, o=o: e.tensor_copy(out=otd[o][:, :], in_=banks[5][:, 0:16]),
                         r=["bank5"], w=["otd%d" % o])
                    P.dma("pool", tm_vz[t0 + s0:t0 + s0 + 128, :], otv[o][:, :], r=["otv%d" % o])
                    P.dma("pool", tm_dt[t0 + s0:t0 + s0 + 128, :], otd[o][:, :], r=["otd%d" % o])
    P.release(m0)


def wcols_for_core(w_in_l, g):
    cols = []
    cols += list(range(512 * g, 512 * g + 512))
    cols += list(range(4096 + 128 * g, 4096 + 128 * g + 128))
    cols += list(range(5120 + 128 * g, 5120 + 128 * g + 128))
    cols += list(range(6144 + 512 * g, 6144 + 512 * g + 512))
    cols += list(range(10368 + 256 * g, 10368 + 256 * g + 256))
    cols += list(range(12416 + 256 * g, 12416 + 256 * g + 256))
    cols += list(range(14464 + 256 * g, 14464 + 256 * g + 256))
    for j in range(3):
        cols += list(range(20608 + 4096 * j + 512 * g, 20608 + 4096 * j + 512 * g + 512))
    tcols = []
    tcols += list(range(16512 + 256 * g, 16512 + 256 * g + 256))
    tcols += list(range(18560 + 256 * g, 18560 + 256 * g + 256))
    tcols += list(range(10240 + 8 * g, 10240 + 8 * g + 8))
    tcols += list(range(10304 + 8 * g, 10304 + 8 * g + 8))
    return np.ascontiguousarray(w_in_l[:, cols]), np.ascontiguousarray(w_in_l[:, tcols])


HT = 528
P3_TILES = [(0, 512, 0), (512, 528, 1)]
CFW = 2048


def emit_p3(P, yaT, vbT, zbT, ycT, gT, xT, wb, wo, sng, lng, lnb, modv, xnT):
    banks = P.get_banks()
    m0 = P.mark()
    ones_f = P.sb("ones_f", [128, 128], F32)
    ones_b = P.sb("ones_b", [128, 128], BF16)
    sng_sb = P.sb("sng", [128, 32], F32)
    lng_sb = P.sb("lng", [128, 16], F32)
    lnb_sb = P.sb("lnb", [128, 16], F32)
    m_sb = P.sb("m3", [128, 2, 3, KC], F32)
    P.op("pool", lambda e: e.memset(ones_f[:, :], 1.0), w=["ones_f"])
    P.op("pool", lambda e: e.memset(ones_b[:, :], 1.0), w=["ones_b"])
    P.dma("sp", sng_sb[:, :], sng, w=["sng"])
    P.dma("sp", lng_sb[:, :], lng, w=["lng"])
    P.dma("sp", lnb_sb[:, :], lnb, w=["lnb"])
    P.dma("sp", m_sb[:, :, :, :], modv, w=["m3"])
    ya = P.sb("ya", [128, 32, HT], BF16)
    vb = P.sb("vb", [128, 16, HT], BF16)
    zb = P.sb("zb", [128, 16, HT], BF16)
    yc = P.sb("yc", [128, 16, HT], BF16)
    mg = P.sb("mg", [128, 32, HT], BF16)
    wst = [P.sb("wst%d" % i, [128, 16, 128], F32) for i in range(2)]
    wbf = [P.sb("wbf%d" % i, [128, 64, 128], BF16) for i in range(2)]
    gsb = [P.sb("gsb%d" % i, [128, 3, HT], BF16) for i in range(2)]
    sg = [P.sb("sg%d" % i, [128, 3, HT], BF16) for i in range(2)]
    t1 = [P.sb("t1_%d" % i, [128, HT], F32) for i in range(3)]
    t2 = [P.sb("t2_%d" % i, [128, HT], F32) for i in range(3)]
    stat = P.sb("stat", [128, HT], F32)
    rstd = P.sb("rstd3", [128, HT], F32)
    mean = P.sb("mean3", [128, HT], F32)
    xin = [P.sb("xin%d" % i, [128, HT], F32) for i in range(2)]
    xo = [P.sb("xo%d" % i, [128, HT], F32) for i in range(2)]
    cnt = {"w": 0, "t": 0, "x": 0, "cast": 0}

    def stat_reduce(src_fn, nchunks, use_f32, key_r):
        for kc in range(nchunks):
            rhs_ap, rkeys = src_fn(kc)
            for (a, b_, j) in P3_TILES:
                bk = 0 if j == 0 else 1
                P.op("pe", lambda e, kc=kc, a=a, b_=b_, bk=bk, rhs_ap=rhs_ap: e.matmul(
                    banks[bk][:, 0:b_ - a], lhsT=(ones_f if use_f32 else ones_b)[:, :], rhs=rhs_ap[:, a:b_],
                    start=(kc == 0), stop=(kc == nchunks - 1)),
                    r=["ones_f", "ones_b"] + rkeys, w=["bank%d" % bk])

    def bank_to(dst, scale, bias, func):
        for (a, b_, j) in P3_TILES:
            bk = 0 if j == 0 else 1
            P.op("act", lambda e, a=a, b_=b_, bk=bk: e.activation(
                out=dst[:, a:b_], in_=banks[bk][:, 0:b_ - a], func=func, bias=bias, scale=scale),
                r=["bank%d" % bk], w=[dst.name])

    def load_weights(src4, nch, blk):
        wi = cnt["w"] % 2
        cnt["w"] += 1
        for q in range(nch // 16):
            si = cnt["cast"] % 2
            cnt["cast"] += 1
            P.dma("sp", wst[si][:, :, :], src4[blk, :, q * 16:(q + 1) * 16, :], w=["wst%d" % si])
            P.op("pool", lambda e, si=si, wi=wi, q=q: e.tensor_copy(
                out=wbf[wi][:, q * 16:(q + 1) * 16, :], in_=wst[si][:, :, :]), r=["wst%d" % si], w=["wbf%d" % wi])
        return wi

    for hf in range(2):
        P.dma("sp", ya[:, :, :], yaT[hf].rearrange("(k p) t -> p k t", p=128), w=["ya"])
        P.dma("sp", vb[:, :, :], vbT[hf].rearrange("(k p) t -> p k t", p=128), w=["vb"])
        P.dma("sp", zb[:, :, :], zbT[hf].rearrange("(k p) t -> p k t", p=128), w=["zb"])
        P.dma("sp", yc[:, :, :], ycT[hf].rearrange("(k p) t -> p k t", p=128), w=["yc"])
        def sq_a(kc):
            i = cnt["t"] % 3
            cnt["t"] += 1
            P.op("act", lambda e, kc=kc, i=i: e.activation(out=t1[i][:, :], in_=ya[:, kc, :], func=AF.Square),
                 r=["ya"], w=[t1[i].name])
            return t1[i], [t1[i].name]
        stat_reduce(sq_a, 32, True, None)
        bank_to(stat, 1.0 / D, EPS, AF.Sqrt)
        P.op("dve", lambda e: e.reciprocal(out=rstd[:, :], in_=stat[:, :]), r=[stat.name], w=[rstd.name])
        for kc in range(32):
            P.op("dve", lambda e, kc=kc: e.scalar_tensor_tensor(
                out=ya[:, kc, :], in0=ya[:, kc, :], scalar=sng_sb[:, kc:kc + 1], in1=rstd[:, :],
                op0=ALU.mult, op1=ALU.mult), r=["ya", "sng", rstd.name], w=["ya"])
        stat_reduce(lambda kc: (vb[:, kc, :], ["vb"]), 16, False, None)
        bank_to(mean, 1.0 / CFW, 0.0, AF.Identity)

        def sq_b(kc):
            i = cnt["t"] % 3
            cnt["t"] += 1
            P.op("dve", lambda e, kc=kc, i=i: e.tensor_tensor(out=t2[i][:, :], in0=vb[:, kc, :], in1=mean[:, :],
                                                               op=ALU.subtract), r=["vb", mean.name], w=[t2[i].name])
            P.op("act", lambda e, i=i: e.activation(out=t1[i][:, :], in_=t2[i][:, :], func=AF.Square),
                 r=[t2[i].name], w=[t1[i].name])
            return t1[i], [t1[i].name]
        stat_reduce(sq_b, 16, True, None)
        bank_to(stat, 1.0 / CFW, EPS, AF.Sqrt)
        P.op("dve", lambda e: e.reciprocal(out=rstd[:, :], in_=stat[:, :]), r=[stat.name], w=[rstd.name])
        for kc in range(16):
            i = cnt["t"] % 3
            cnt["t"] += 1
            P.op("dve", lambda e, kc=kc, i=i: e.tensor_tensor(out=t2[i][:, :], in0=vb[:, kc, :], in1=mean[:, :],
                                                               op=ALU.subtract), r=["vb", mean.name], w=[t2[i].name])
            P.op("dve", lambda e, i=i: e.tensor_tensor(out=t2[i][:, :], in0=t2[i][:, :], in1=rstd[:, :], op=ALU.mult),
                 r=[t2[i].name, rstd.name], w=[t2[i].name])
            P.op("act", lambda e, kc=kc, i=i: e.activation(out=t2[i][:, :], in_=t2[i][:, :], func=AF.Silu,
                                                            bias=lnb_sb[:, kc:kc + 1], scale=lng_sb[:, kc:kc + 1]),
                 r=[t2[i].name, "lng", "lnb"], w=[t2[i].name])
            P.op("act", lambda e, kc=kc, i=i: e.activation(out=t1[i][:, :], in_=zb[:, kc, :], func=AF.Silu),
                 r=["zb"], w=[t1[i].name])
            P.op("dve", lambda e, kc=kc, i=i: e.tensor_tensor(out=vb[:, kc, :], in0=t2[i][:, :], in1=t1[i][:, :],
                                                               op=ALU.mult), r=[t1[i].name, t2[i].name], w=["vb"])
        gv = gT[hf].rearrange("(j n p) t -> n p j t", j=3, p=128)
        for n in range(32):
            wi = load_weights(wb, 64, n)
            gi = n % 2
            P.dma("sp", gsb[gi][:, :, :], gv[n], w=[gsb[gi].name])
            P.op("act", lambda e, gi=gi: e.activation(out=sg[gi][:, :, :], in_=gsb[gi][:, :, :], func=AF.Sigmoid),
                 r=[gsb[gi].name], w=[sg[gi].name])
            bo = 4 * (n % 2)
            srcs = [(ya, "ya", 0, 32), (vb, "vb", 32, 16), (yc, "yc", 48, 16)]
            for bi, (src, sk, c0, ncc) in enumerate(srcs):
                for kc in range(ncc):
                    for (a, b_, j) in P3_TILES:
                        if j == 0:
                            dst = banks[bo + bi][:, 0:512]
                            bk = bo + bi
                        else:
                            dst = banks[bo + 3][:, 16 * bi:16 * bi + 16]
                            bk = bo + 3
                        P.op("pe", lambda e, dst=dst, wi=wi, c0=c0, kc=kc, src=src, a=a, b_=b_, ncc=ncc: e.matmul(
                            dst, lhsT=wbf[wi][:, c0 + kc, :], rhs=src[:, kc, a:b_],
                            start=(kc == 0), stop=(kc == ncc - 1)),
                            r=["wbf%d" % wi, sk], w=["bank%d" % bk])
            i = cnt["t"] % 3
            cnt["t"] += 1
            for (a, b_, j) in P3_TILES:
                def pv(bi):
                    if j == 0:
                        return banks[bo + bi][:, 0:512], "bank%d" % (bo + bi)
                    return banks[bo + 3][:, 16 * bi:16 * bi + 16], "bank%d" % (bo + 3)
                pa, ka = pv(0)
                pb, kb = pv(1)
                pc, kc_ = pv(2)
                P.op("dve", lambda e, pa=pa, gi=gi, i=i, a=a, b_=b_: e.tensor_tensor(
                    out=t1[i][:, a:b_], in0=pa, in1=sg[gi][:, 0, a:b_], op=ALU.mult),
                    r=[ka, sg[gi].name], w=[t1[i].name])
                P.op("dve", lambda e, pb=pb, gi=gi, i=i, a=a, b_=b_: e.tensor_tensor(
                    out=t2[i][:, a:b_], in0=pb, in1=sg[gi][:, 1, a:b_], op=ALU.mult),
                    r=[kb, sg[gi].name], w=[t2[i].name])
                P.op("dve", lambda e, i=i, a=a, b_=b_: e.tensor_tensor(
                    out=t1[i][:, a:b_], in0=t1[i][:, a:b_], in1=t2[i][:, a:b_], op=ALU.add),
                    r=[t1[i].name, t2[i].name], w=[t1[i].name])
                P.op("dve", lambda e, pc=pc, gi=gi, i=i, a=a, b_=b_: e.tensor_tensor(
                    out=t2[i][:, a:b_], in0=pc, in1=sg[gi][:, 2, a:b_], op=ALU.mult),
                    r=[kc_, sg[gi].name], w=[t2[i].name])
                P.op("dve", lambda e, i=i, a=a, b_=b_, n=n: e.tensor_tensor(
                    out=mg[:, n, a:b_], in0=t1[i][:, a:b_], in1=t2[i][:, a:b_], op=ALU.add),
                    r=[t1[i].name, t2[i].name], w=["mg"])
        xv = xT[hf].rearrange("(k p) t -> k p t", p=128)
        ov = xnT[hf].rearrange("(k p) t -> k p t", p=128)
        for m in range(32):
            wi = load_weights(wo, 32, m)
            xi = m % 2
            P.dma("sp", xin[xi][:, :], xv[m], w=[xin[xi].name])
            bo = 4 * (m % 2)
            for kc in range(32):
                for (a, b_, j) in P3_TILES:
                    bk = bo + (0 if j == 0 else 1)
                    P.op("pe", lambda e, bk=bk, wi=wi, kc=kc, a=a, b_=b_: e.matmul(
                        banks[bk][:, 0:b_ - a], lhsT=wbf[wi][:, kc, :], rhs=mg[:, kc, a:b_],
                        start=(kc == 0), stop=(kc == 31)), r=["wbf%d" % wi, "mg"], w=["bank%d" % bk])
            for (a, b_, j) in P3_TILES:
                bk = bo + (0 if j == 0 else 1)
                P.op("dve", lambda e, bk=bk, xi=xi, a=a, b_=b_, j=j, m=m: e.scalar_tensor_tensor(
                    out=xo[xi][:, a:b_], in0=banks[bk][:, 0:b_ - a], scalar=m_sb[:, j, 2, m:m + 1],
                    in1=xin[xi][:, a:b_], op0=ALU.mult, op1=ALU.add),
                    r=["bank%d" % bk, xin[xi].name, "m3"], w=[xo[xi].name])
            P.dma("pool", ov[m], xo[xi][:, :], r=[xo[xi].name])
    P.release(m0)


def build_p3():
    P = Prog()
    yaT = P.dram("yaT", [2, D, HT], BF16, "ExternalInput")
    vbT = P.dram("vbT", [2, CFW, HT], BF16, "ExternalInput")
    zbT = P.dram("zbT", [2, CFW, HT], BF16, "ExternalInput")
    ycT = P.dram("ycT", [2, CFW, HT], BF16, "ExternalInput")
    gT = P.dram("gT", [2, 3 * D, HT], BF16, "ExternalInput")
    xT = P.dram("xT", [2, D, HT], F32, "ExternalInput")
    wb = P.dram("wb", [32, 128, 64, 128], F32, "ExternalInput")
    wo = P.dram("wo", [32, 128, 32, 128], F32, "ExternalInput")
    sng = P.dram("sng", [128, 32], F32, "ExternalInput")
    lng = P.dram("lng", [128, 16], F32, "ExternalInput")
    lnb = P.dram("lnb", [128, 16], F32, "ExternalInput")
    modv = P.dram("modv", [128, 2, 3, KC], F32, "ExternalInput")
    xnT = P.dram("xnT", [2, D, HT], F32, "ExternalOutput")
    emit_p3(P, yaT, vbT, zbT, ycT, gT, xT, wb, wo, sng, lng, lnb, modv, xnT)
    return P.finish()


def w_blocks(w):
    K_, N_ = w.shape
    return np.ascontiguousarray(w.reshape(K_ // 128, 128, N_ // 128, 128).transpose(2, 1, 0, 3))


def halves_T(x2d, c2d, r, dt=None):
    out = []
    for h in range(2):
        t = np.concatenate([x2d[r * TOK_X + 512 * h: r * TOK_X + 512 * (h + 1)],
                            c2d[r * TOK_C + 16 * h: r * TOK_C + 16 * (h + 1)]], axis=0)
        out.append(t.T)
    a = np.ascontiguousarray(np.stack(out))
    return a if dt is None else a.astype(dt)


NROW = SEQ // 64


def emit_conformer(P, fm_src, cfw, cfb, vb_out):
    m0 = P.mark()
    w_sb = P.sb("cfw", [128, 2, 31], F32)
    b_sb = P.sb("cfb", [128, 2], F32)
    a_sb = P.sb("cfa", [128, NTOK], BF16)
    g_sb = P.sb("cfg", [128, NTOK], BF16)
    sgm = P.sb("cfs", [128, NTOK], F32)
    upx = P.sb("upx", [128, NROW, 94], F32)
    upc = P.sb("upc", [128, CTX + 30], F32)
    acc = P.sb("cfacc", [128, NTOK], F32)
    ob = P.sb("cfob", [128, NTOK], BF16)
    P.dma("sp", w_sb[:, :, :], cfw, w=["cfw"])
    P.dma("sp", b_sb[:, :], cfb, w=["cfb"])
    P.op("pool", lambda e: e.memset(upx[:, :, :], 0.0), w=["upx"])
    P.op("pool", lambda e: e.memset(upc[:, :], 0.0), w=["upc"])
    for cc in range(2):
        P.dma("sp", a_sb[:, :], fm_src(10 + cc), w=["cfa"])
        P.dma("sp", g_sb[:, :], fm_src(12 + cc), w=["cfg"])
        P.op("act", lambda e: e.activation(out=sgm[:, :], in_=g_sb[:, :], func=AF.Sigmoid), r=["cfg"], w=["cfs"])
        P.op("dve", lambda e: e.tensor_tensor(out=upc[:, 15:15 + CTX], in0=a_sb[:, 0:CTX], in1=sgm[:, 0:CTX], op=ALU.mult),
             r=["cfa", "cfs"], w=["upc"])
        P.op("dve", lambda e: e.tensor_tensor(
            out=upx[:, :, 15:79], in0=a_sb[:, CTX:].rearrange("p (r t) -> p r t", t=64),
            in1=sgm[:, CTX:].rearrange("p (r t) -> p r t", t=64), op=ALU.mult), r=["cfa", "cfs"], w=["upx"])
        accx = acc[:, CTX:].rearrange("p (r t) -> p r t", t=64)
        for k in range(31):
            if k == 0:
                P.op("dve", lambda e, cc=cc: e.tensor_scalar(
                    out=acc[:, 0:CTX], in0=upc[:, 0:CTX], scalar1=w_sb[:, cc, 0:1], scalar2=b_sb[:, cc:cc + 1],
                    op0=ALU.mult, op1=ALU.add), r=["upc", "cfw", "cfb"], w=["cfacc"])
                P.op("dve", lambda e, cc=cc: e.tensor_scalar(
                    out=accx, in0=upx[:, :, 0:64], scalar1=w_sb[:, cc, 0:1], scalar2=b_sb[:, cc:cc + 1],
                    op0=ALU.mult, op1=ALU.add), r=["upx", "cfw", "cfb"], w=["cfacc"])
            else:
                P.op("dve", lambda e, cc=cc, k=k: e.scalar_tensor_tensor(
                    out=acc[:, 0:CTX], in0=upc[:, k:k + CTX], scalar=w_sb[:, cc, k:k + 1], in1=acc[:, 0:CTX],
                    op0=ALU.mult, op1=ALU.add), r=["upc", "cfw", "cfacc"], w=["cfacc"])
                P.op("dve", lambda e, cc=cc, k=k: e.scalar_tensor_tensor(
                    out=accx, in0=upx[:, :, k:k + 64], scalar=w_sb[:, cc, k:k + 1], in1=accx,
                    op0=ALU.mult, op1=ALU.add), r=["upx", "cfw", "cfacc"], w=["cfacc"])
        P.op("act", lambda e: e.copy(out=ob[:, :], in_=acc[:, :]), r=["cfacc"], w=["cfob"])
        P.dma("pool", vb_out[cc * 128:(cc + 1) * 128, :], ob[:, :], r=["cfob"])
    P.release(m0)


def emit_trig_table(P, dst, nrows, ncols, N, row_base, kind, name, scale=1.0):
    ip = P.sb(name + "_ip", [128, ncols], I32)
    ij = P.sb(name + "_ij", [128, ncols], I32)
    fp = P.sb(name + "_fp", [128, ncols], F32)
    fj = P.sb(name + "_fj", [128, ncols], F32)
    P.op("pool", lambda e: e.iota(ip[:, :], pattern=[[0, ncols]], base=row_base, channel_multiplier=1), w=[ip.name])
    P.op("pool", lambda e: e.iota(ij[:, :], pattern=[[1, ncols]], base=0, channel_multiplier=0), w=[ij.name])
    P.op("dve", lambda e: e.tensor_tensor(out=ip[:, :], in0=ip[:, :], in1=ij[:, :], op=ALU.mult),
         r=[ip.name, ij.name], w=[ip.name])
    off = N // 2 if kind == "sin" else 3 * N // 4
    P.op("dve", lambda e: e.tensor_scalar(out=ip[:, :], in0=ip[:, :], scalar1=float(off), scalar2=None,
                                          op0=ALU.add), r=[ip.name], w=[ip.name])
    P.op("dve", lambda e: e.tensor_scalar(out=ip[:, :], in0=ip[:, :], scalar1=int(N - 1), scalar2=None,
                                          op0=ALU.bitwise_and), r=[ip.name], w=[ip.name])
    P.op("dve", lambda e: e.tensor_scalar(out=fp[:, :], in0=ip[:, :], scalar1=float(-N / 2), scalar2=None,
                                          op0=ALU.add), r=[ip.name], w=[fp.name])
    P.op("act", lambda e: e.activation(out=fj[:, :], in_=fp[:, :], func=AF.Sin, scale=float(2 * np.pi / N)),
         r=[fp.name], w=[fj.name])
    P.op("dve", lambda e: e.tensor_scalar(out=dst, in0=fj[0:nrows, :], scalar1=float(scale), scalar2=None,
                                          op0=ALU.mult), r=[fj.name], w=[name])


def emit_fourier(P, tm_vz, t1s, yc_out):
    banks = P.get_banks()
    m0 = P.mark()
    c128 = P.sb("c128", [128, 128], BF16)
    s128 = P.sb("s128", [128, 128], BF16)
    twc = P.sb("twc", [128, 64], F32)
    tws = P.sb("tws", [128, 64], F32)
    ra = P.sb("ra", [64, 128], BF16)
    rm = P.sb("rm", [64, 128], BF16)
    c256 = P.sb("c256", [128, 2, 256], BF16)
    s256 = P.sb("s256", [128, 2, 256], BF16)
    ns256 = P.sb("ns256", [128, 2, 256], BF16)
    mt = P.mark()
    emit_trig_table(P, c128[:, :], 128, 128, 128, 0, "cos", "c128")
    emit_trig_table(P, s128[:, :], 128, 128, 128, 0, "sin", "s128")
    emit_trig_table(P, twc[:, :], 128, 64, 8192, 0, "cos", "twc")
    emit_trig_table(P, tws[:, :], 128, 64, 8192, 0, "sin", "tws")
    emit_trig_table(P, ra[:, 0:64], 64, 64, 64, 0, "cos", "ra")
    emit_trig_table(P, ra[:, 64:128], 64, 64, 64, 0, "sin", "ra")
    emit_trig_table(P, rm[:, 0:64], 64, 64, 64, 0, "sin", "rm", scale=-1.0)
    emit_trig_table(P, rm[:, 64:128], 64, 64, 64, 0, "cos", "rm")
    for cc in range(2):
        emit_trig_table(P, c256[:, cc, :], 128, 256, 256, cc * 128, "cos", "c256")
        emit_trig_table(P, s256[:, cc, :], 128, 256, 256, cc * 128, "sin", "s256")
        emit_trig_table(P, ns256[:, cc, :], 128, 256, 256, cc * 128, "sin", "ns256", scale=-1.0)
    P.release(mt)

    usb = P.sb("usb", [128, 2, 2, 64, 128], BF16)
    zt = [P.sb("zt%d" % i, [128, 256], BF16) for i in range(2)]
    zs = [P.sb("zs%d" % i, [128, 256], F32) for i in range(2)]
    yo = [P.sb("yo%d" % i, [128, 256], BF16) for i in range(2)]
    cnt = {"e": 0, "z": 0}

    def stage3(ntiles, tok_base, norm):
        for tl in range(ntiles):
            bk = 6 + (tl % 2)
            k = 0
            for cc in range(2):
                for ri in range(2):
                    rhs = (c256 if ri == 0 else ns256)[:, cc, :]
                    P.op("pe", lambda e, bk=bk, cc=cc, ri=ri, tl=tl, rhs=rhs, k=k: e.matmul(
                        banks[bk][:, 0:256], lhsT=usb[:, cc, ri, tl, :], rhs=rhs, start=(k == 0), stop=(k == 3)),
                        r=["usb", "c256", "ns256"], w=["bank%d" % bk])
                    k += 1
            zi = cnt["z"] % 2
            cnt["z"] += 1
            r0 = tok_base + tl * 128
            P.dma("sp", zt[zi][:, :], tm_vz[r0:r0 + 128, 256:512], w=[zt[zi].name])
            P.op("act", lambda e, zi=zi: e.activation(out=zs[zi][:, :], in_=zt[zi][:, :], func=AF.Silu),
                 r=[zt[zi].name], w=[zs[zi].name])
            P.op("dve", lambda e, zi=zi, bk=bk: e.scalar_tensor_tensor(
                out=yo[zi][:, :], in0=banks[bk][:, 0:256], scalar=float(norm), in1=zs[zi][:, :],
                op0=ALU.mult, op1=ALU.mult), r=["bank%d" % bk, zs[zi].name], w=[yo[zi].name])
            P.dma("pool", yc_out[r0:r0 + 128, :], yo[zi][:, :], r=[yo[zi].name])

    mc = P.mark()
    vc = P.sb("vc", [128, 2, 256], BF16)
    P.dma("sp", vc[:, :, :], tm_vz[0:CTX, 0:256].rearrange("(t p) c -> p t c", p=128), w=["vc"])
    for cc in range(2):
        for half, tab in enumerate([c256, s256]):
            bk = 2 * cc + half
            for tl in range(2):
                P.op("pe", lambda e, bk=bk, cc=cc, tl=tl, tab=tab: e.matmul(
                    banks[bk][:, 0:256], lhsT=vc[:, tl, cc * 128:(cc + 1) * 128], rhs=tab[:, tl, :],
                    start=(tl == 0), stop=(tl == 1)), r=["vc", "c256", "s256"], w=["bank%d" % bk])
            P.op("act", lambda e, bk=bk, cc=cc, half=half: e.copy(
                out=usb[:, cc, half, 0:2, :], in_=banks[bk][:, 0:256].rearrange("p (t s) -> p t s", s=128)),
                r=["bank%d" % bk], w=["usb"])
    stage3(2, 0, 1.0 / 256.0)
    P.release(mc)

    ms = P.mark()
    xr = P.sb("xr", [128, 64, 256], BF16)
    t1sb = P.sb("t1sb", [128, 64, 2, 256], BF16)
    tmpa = [P.sb("tmpa%d" % i, [128, 256], F32) for i in range(2)]
    P.dma("sp", xr[:, :, :], tm_vz[CTX:, 0:256].rearrange("(a b) c -> a b c", b=64), w=["xr"])
    for ct in range(32):
        bo = 2 * (ct % 2)
        P.op("pe", lambda e, ct=ct, bo=bo: e.matmul(
            banks[bo][:, :], lhsT=c128[:, :], rhs=xr[:, 2 * ct:2 * ct + 2, :], start=True, stop=True),
            r=["c128", "xr"], w=["bank%d" % bo])
        P.op("pe", lambda e, ct=ct, bo=bo: e.matmul(
            banks[bo + 1][:, :], lhsT=s128[:, :], rhs=xr[:, 2 * ct:2 * ct + 2, :], start=True, stop=True),
            r=["s128", "xr"], w=["bank%d" % (bo + 1)])
        for q in range(2):
            s2 = 2 * ct + q
            tr = banks[bo][:, q * 256:(q + 1) * 256]
            tn = banks[bo + 1][:, q * 256:(q + 1) * 256]
            ti = cnt["e"] % 2
            cnt["e"] += 1
            P.op("dve", lambda e, tn=tn, ti=ti, s2=s2: e.tensor_scalar(
                out=tmpa[ti][:, :], in0=tn, scalar1=tws[:, s2:s2 + 1], scalar2=-1.0, op0=ALU.mult, op1=ALU.mult),
                r=["bank%d" % (bo + 1), "tws"], w=[tmpa[ti].name])
            P.op("dve", lambda e, tr=tr, ti=ti, s2=s2: e.scalar_tensor_tensor(
                out=t1sb[:, s2, 0, :], in0=tr, scalar=twc[:, s2:s2 + 1], in1=tmpa[ti][:, :],
                op0=ALU.mult, op1=ALU.add), r=["bank%d" % bo, "twc", tmpa[ti].name], w=["t1sb"])
            ti = cnt["e"] % 2
            cnt["e"] += 1
            P.op("dve", lambda e, tr=tr, ti=ti, s2=s2: e.tensor_scalar(
                out=tmpa[ti][:, :], in0=tr, scalar1=tws[:, s2:s2 + 1], scalar2=None, op0=ALU.mult),
                r=["bank%d" % bo, "tws"], w=[tmpa[ti].name])
            P.op("dve", lambda e, tn=tn, ti=ti, s2=s2: e.scalar_tensor_tensor(
                out=t1sb[:, s2, 1, :], in0=tn, scalar=twc[:, s2:s2 + 1], in1=tmpa[ti][:, :],
                op0=ALU.mult, op1=ALU.add), r=["bank%d" % (bo + 1), "twc", tmpa[ti].name], w=["t1sb"])
    P.dma("sp", t1s, t1sb[:, :, :, :], r=["t1sb"], w=["t1s"])
    P.release(ms)
    m2 = P.mark()
    t2 = [P.sb("t2in%d" % i, [64, 32, 2, 256], BF16) for i in range(2)]
    t1v = t1s.rearrange("a b r c -> b a r c")
    ne = 0
    for gq in range(4):
        bi = gq % 2
        P.dma("sp", t2[bi][:, :, :, :], t1v[:, gq * 32:(gq + 1) * 32, :, :], r=["t1s"], w=[t2[bi].name])
        for sl in range(32):
            s1p = gq * 32 + sl
            for cc in range(2):
                bk = ne % 4
                P.op("pe", lambda e, bk=bk, bi=bi, sl=sl, cc=cc: e.matmul(
                    banks[bk][:, 0:128], lhsT=t2[bi][:, sl, 0, cc * 128:(cc + 1) * 128], rhs=ra[:, :],
                    start=True, stop=False), r=[t2[bi].name, "ra"], w=["bank%d" % bk])
                P.op("pe", lambda e, bk=bk, bi=bi, sl=sl, cc=cc: e.matmul(
                    banks[bk][:, 0:128], lhsT=t2[bi][:, sl, 1, cc * 128:(cc + 1) * 128], rhs=rm[:, :],
                    start=False, stop=True), r=[t2[bi].name, "rm"], w=["bank%d" % bk])
                eng = "act" if ne % 2 else "dve"
                ne += 1
                src = banks[bk][:, 0:128].rearrange("p (r s) -> p r s", s=64)
                if eng == "act":
                    P.op("act", lambda e, cc=cc, s1p=s1p, src=src: e.copy(out=usb[:, cc, :, :, s1p], in_=src),
                         r=["bank%d" % bk], w=["usb"])
                else:
                    P.op("dve", lambda e, cc=cc, s1p=s1p, src=src: e.tensor_copy(out=usb[:, cc, :, :, s1p], in_=src),
                         r=["bank%d" % bk], w=["usb"])
    stage3(64, CTX, 1.0 / np.sqrt(8192.0 * 256.0))
    P.release(m2)
    P.release(m0)


NCHUNK = NTOK // 128
SSD_DEBUG = None
SSD_STEP = 99


def emit_ssd(P, fm_src, za_src, tm_dt, ssdu, yfs, cw, cb, dtb, alog, dsk, ya_out):
    banks = P.get_banks()
    m0 = P.mark()
    cw_sb = P.sb("cw", [128, 6, 5], F32)
    cb_sb = P.sb("cb", [128, 6], F32)
    P.dma("sp", cw_sb[:, :, :], cw, w=["cw"])
    P.dma("sp", cb_sb[:, :], cb, w=["cb"])
    m1 = P.mark()
    pc = [P.sb("pc%d" % i, [128, CTX + 4], BF16) for i in range(2)]
    px = [P.sb("px%d" % i, [128, SEQ + 4], BF16) for i in range(2)]
    acc = [P.sb("cacc%d" % i, [128, NTOK], F32) for i in range(2)]
    ub = [P.sb("cub%d" % i, [128, NTOK], BF16) for i in range(2)]
    for i in range(2):
        P.op("pool", lambda e, i=i: e.memset(pc[i][:, :], 0.0), w=[pc[i].name])
        P.op("pool", lambda e, i=i: e.memset(px[i][:, :], 0.0), w=[px[i].name])
    for c in range(6):
        b = c % 2
        src = fm_src(c)
        P.dma("sp", pc[b][:, 2:2 + CTX], src[:, 0:CTX], w=[pc[b].name])
        P.dma("sp", px[b][:, 2:2 + SEQ], src[:, CTX:], w=[px[b].name])
        for k in range(5):
            for (pb, lo, n_) in ((pc[b], 0, CTX), (px[b], CTX, SEQ)):
                if k == 0:
                    P.op("dve", lambda e, pb=pb, lo=lo, n_=n_, c=c, b=b: e.tensor_scalar(
                        out=acc[b][:, lo:lo + n_], in0=pb[:, 0:n_], scalar1=cw_sb[:, c, 0:1], scalar2=cb_sb[:, c:c + 1],
                        op0=ALU.mult, op1=ALU.add), r=[pb.name, "cw", "cb"], w=[acc[b].name])
                else:
                    P.op("dve", lambda e, pb=pb, lo=lo, n_=n_, c=c, b=b, k=k: e.scalar_tensor_tensor(
                        out=acc[b][:, lo:lo + n_], in0=pb[:, k:k + n_], scalar=cw_sb[:, c, k:k + 1],
                        in1=acc[b][:, lo:lo + n_], op0=ALU.mult, op1=ALU.add),
                        r=[pb.name, "cw", acc[b].name], w=[acc[b].name])
        P.op("act", lambda e, b=b: e.activation(out=ub[b][:, :], in_=acc[b][:, :], func=AF.Silu),
             r=[acc[b].name], w=[ub[b].name])
        P.dma("pool", ssdu[c], ub[b][:, :], r=[ub[b].name], w=["ssdu"])
    P.release(m1)
    if SSD_DEBUG == "conv":
        P.release(m0)
        return
    dt_all = P.sb("dt_all", [128, NCHUNK, 16], F32)
    a_all = P.sb("a_all", [128, NCHUNK, 16], F32)
    dtb_sb = P.sb("dtb", [128, 16], F32)
    nex = P.sb("nex", [128, 16], F32)
    dsk_sb = P.sb("dsk", [128, 4], F32)
    P.dma("sp", dtb_sb[:, :], dtb, w=["dtb"])
    P.dma("sp", nex[:, :], alog, w=["nex"])
    P.dma("sp", dsk_sb[:, :], dsk, w=["dsk"])
    m2 = P.mark()
    xr_ = P.sb("dtx", [128, NCHUNK, 16], F32)
    ax_ = P.sb("dtax", [128, NCHUNK, 16], F32)
    P.dma("sp", xr_[:, :, :], tm_dt.rearrange("(n p) c -> p n c", p=128), w=["dtx"])
    P.op("dve", lambda e: e.tensor_tensor(out=xr_[:, :, :], in0=xr_[:, :, :],
                                          in1=dtb_sb[:, :].unsqueeze(1).broadcast_to([128, NCHUNK, 16]), op=ALU.add),
         r=["dtx", "dtb"], w=["dtx"])
    P.op("act", lambda e: e.activation(out=ax_[:, :, :], in_=xr_[:, :, :], func=AF.Abs), r=["dtx"], w=["dtax"])
    P.op("act", lambda e: e.activation(out=ax_[:, :, :], in_=ax_[:, :, :], func=AF.Exp, scale=-1.0),
         r=["dtax"], w=["dtax"])
    P.op("act", lambda e: e.activation(out=ax_[:, :, :], in_=ax_[:, :, :], func=AF.Ln, bias=1.0, scale=1.0),
         r=["dtax"], w=["dtax"])
    P.op("dve", lambda e: e.tensor_scalar_max(out=xr_[:, :, :], in0=xr_[:, :, :], scalar1=0.0), r=["dtx"], w=["dtx"])
    P.op("dve", lambda e: e.tensor_tensor(out=dt_all[:, :, :], in0=xr_[:, :, :], in1=ax_[:, :, :], op=ALU.add),
         r=["dtx", "dtax"], w=["dt_all"])
    P.op("act", lambda e: e.activation(out=nex[:, :], in_=nex[:, :], func=AF.Exp), r=["nex"], w=["nex"])
    P.op("dve", lambda e: e.scalar_tensor_tensor(
        out=a_all[:, :, :], in0=dt_all[:, :, :], scalar=-1.0,
        in1=nex[:, :].unsqueeze(1).broadcast_to([128, NCHUNK, 16]), op0=ALU.mult, op1=ALU.mult),
        r=["dt_all", "nex"], w=["a_all"])
    P.release(m2)
    onesf = P.sb("s_ones", [128, 128], F32)
    oneb = P.sb("s_oneb", [128, 128], BF16)
    idf = P.sb("s_idf", [128, 128], F32)
    idb = P.sb("s_idb", [128, 128], BF16)
    Lf = P.sb("s_Lf", [128, 128], F32)
    Uf = P.sb("s_Uf", [128, 128], F32)
    Lb = P.sb("s_Lb", [128, 128], F32)
    Ub = P.sb("s_Ub", [128, 128], F32)
    P.op("pool", lambda e: e.memset(onesf[:, :], 1.0), w=["s_ones"])
    P.op("pool", lambda e: e.memset(oneb[:, :], 1.0), w=["s_oneb"])

    def sel(dst, src, pat, cm, op):
        P.op("pool", lambda e: e.affine_select(out=dst[:, :], in_=src[:, :], pattern=[[pat, 128]], compare_op=op,
                                               fill=0.0, base=0, channel_multiplier=cm),
             r=[src.name], w=[dst.name])
    sel(idf, onesf, -1, 1, ALU.is_equal)
    sel(idb, oneb, -1, 1, ALU.is_equal)
    sel(Lf, onesf, -1, 1, ALU.is_gt)
    sel(Uf, onesf, 1, -1, ALU.is_ge)
    sel(Lb, onesf, 1, -1, ALU.is_gt)
    sel(Ub, onesf, -1, 1, ALU.is_ge)
    if SSD_DEBUG == "const":
        P.release(m0)
        return
    hst = P.sb("hst", [128, 512], F32)
    hbf = P.sb("hbf", [128, 512], BF16)
    ut = [P.sb("ut%d" % i, [128, 6, 128], BF16) for i in range(2)]
    xdt = P.sb("xdt", [128, 8, 64], BF16)
    xdtd = P.sb("xdtd", [128, 8, 64], BF16)
    xsb = P.sb("xsb", [128, 640], BF16)
    sme = P.sb("sme", [128, 24], F32)
    la = P.sb("la", [128, 8, 128], F32)
    dm = [P.sb("dm%d" % i, [128, 4, 128], BF16) for i in range(2)]
    wm = [P.sb("wm%d" % i, [128, 4, 128], BF16) for i in range(2)]
    mm = P.sb("mm", [128, 128], BF16)
    yt = [P.sb("yt%d" % i, [128, 512], F32) for i in range(2)]
    yf = [P.sb("yf%d" % i, [128, 512], F32) for i in range(2)]
    zat = [P.sb("zat%d" % i, [128, 4, 128], BF16) for i in range(2)]
    sz = P.sb("sz", [128, 4, 128], F32)
    y2 = P.sb("y2", [128, 4, 128], F32)
    yo = [P.sb("yob%d" % i, [128, 4, 128], BF16) for i in range(2)]
    htmp = P.sb("htmp", [128, 512], F32)
    b0 = banks[0][:, :].bitcast(BF16)
    uv = ssdu.rearrange("c p t -> p c t")
    zav = za_src.rearrange("c p t -> p c t")
    yov = ya_out.rearrange("(c p) t -> p c t", p=128)
    it = 0
    for d in range(2):
        order = list(range(NCHUNK)) if d == 0 else [1, 0] + list(range(NCHUNK - 1, 1, -1))
        Lm, Um = (Lf, Uf) if d == 0 else (Lb, Ub)
        P.op("pool", lambda e: e.memset(hst[:, :], 0.0), w=["hst"])
        P.op("pool", lambda e: e.memset(hbf[:, :], 0.0), w=["hbf"])
        for n in order:
            if isinstance(SSD_DEBUG, int) and it >= SSD_DEBUG:
                break
            t0 = n * 128
            ui = it % 2
            it += 1
            u = ut[ui]
            P.dma("sp", u[:, :, :], uv[:, :, t0:t0 + 128], r=["ssdu"], w=[u.name])
            a_c = a_all[:, n, d * 8:(d + 1) * 8]
            dt_c = dt_all[:, n, d * 8:(d + 1) * 8]
            for c in range(4):
                P.op("pe", lambda e, c=c, u=u: e.transpose(out=b0[:, c * 128:(c + 1) * 128], in_=u[:, c, :],
                                                           identity=idb[:, :]), r=[u.name, "s_idb"], w=["bank0"])
            P.op("pe", lambda e, u=u: e.transpose(out=b0[:, 512:640], in_=u[:, 4, :], identity=idb[:, :]),
                 r=[u.name, "s_idb"], w=["bank0"])
            if SSD_STEP < 1:
                continue
            P.op("act", lambda e: e.copy(out=xsb[:, :], in_=b0[:, 0:640]), r=["bank0"], w=["xsb"])
            for h_ in range(8):
                P.op("dve", lambda e, h_=h_, dt_c=dt_c: e.tensor_scalar(
                    out=xdt[:, h_, :], in0=xsb[:, h_ * 64:(h_ + 1) * 64], scalar1=dt_c[:, h_:h_ + 1], scalar2=None,
                    op0=ALU.mult), r=["xsb", "dt_all"], w=["xdt"])
            if SSD_STEP < 2:
                continue
            for q, lm in enumerate((Um, Lm, onesf)):
                P.op("pe", lambda e, q=q, lm=lm, a_c=a_c: e.matmul(
                    banks[3][:, 128 + 8 * q:136 + 8 * q], lhsT=lm[:, :], rhs=a_c, start=True, stop=True),
                    r=[lm.name, "a_all"], w=["bank3s"])
            P.op("act", lambda e: e.activation(out=sme[:, :], in_=banks[3][:, 128:152], func=AF.Exp),
                 r=["bank3s"], w=["sme"])
            if SSD_STEP < 3:
                continue
            for h_ in range(8):
                P.op("pool", lambda e, h_=h_, Lm=Lm, a_c=a_c: e.tensor_scalar(
                    out=la[:, h_, :], in0=Lm[:, :], scalar1=a_c[:, h_:h_ + 1], scalar2=None, op0=ALU.mult),
                    r=[Lm.name, "a_all"], w=["la"])
            for hg in range(2):
                for hh in range(4):
                    h_ = hg * 4 + hh
                    P.op("pe", lambda e, hg=hg, hh=hh, h_=h_, Um=Um: e.matmul(
                        banks[1 + hg][:, hh * 128:(hh + 1) * 128], lhsT=la[:, h_, :], rhs=Um[:, :],
                        start=True, stop=True), r=["la", Um.name], w=["bank%d" % (1 + hg)])
                P.op("act", lambda e, hg=hg: e.activation(
                    out=dm[hg][:, :, :], in_=banks[1 + hg][:, :].rearrange("p (h i) -> p h i", i=128), func=AF.Exp),
                    r=["bank%d" % (1 + hg)], w=[dm[hg].name])
            if SSD_STEP < 4:
                continue
            P.op("pe", lambda e, u=u: e.matmul(banks[3][:, 0:128], lhsT=u[:, 4, :], rhs=u[:, 5, :], start=True, stop=True),
                 r=[u.name], w=["bank3"])
            P.op("dve", lambda e, Um=Um: e.tensor_tensor(out=mm[:, :], in0=banks[3][:, 0:128], in1=Um[:, :], op=ALU.mult),
                 r=["bank3", Um.name], w=["mm"])
            for hg in range(2):
                P.op("dve", lambda e, hg=hg: e.tensor_tensor(
                    out=wm[hg][:, :, :], in0=dm[hg][:, :, :], in1=mm[:, :].unsqueeze(1).broadcast_to([128, 4, 128]),
                    op=ALU.mult), r=[dm[hg].name, "mm"], w=[wm[hg].name])
            if SSD_STEP < 5:
                continue
            for h_ in range(8):
                P.op("pe", lambda e, h_=h_: e.matmul(
                    banks[4][:, h_ * 64:(h_ + 1) * 64], lhsT=wm[h_ // 4][:, h_ % 4, :], rhs=xdt[:, h_, :],
                    start=True, stop=True), r=[wm[h_ // 4].name, "xdt"], w=["bank4"])
            P.op("pe", lambda e, u=u: e.matmul(banks[5][:, :], lhsT=u[:, 5, :], rhs=hbf[:, :], start=True, stop=True),
                 r=[u.name, "hbf"], w=["bank5"])
            for h_ in range(8):
                P.op("pool", lambda e, h_=h_: e.tensor_scalar(
                    out=xdtd[:, h_, :], in0=xdt[:, h_, :], scalar1=sme[:, 8 + h_:9 + h_], scalar2=None, op0=ALU.mult),
                    r=["xdt", "sme"], w=["xdtd"])
            P.op("pe", lambda e: e.matmul(banks[6][:, :], lhsT=xsb[:, 512:640], rhs=xdtd[:, :, :], start=True, stop=True),
                 r=["xsb", "xdtd"], w=["bank6"])
            if SSD_STEP < 6:
                continue
            yi = it % 2
            y_ = yt[yi]
            P.op("act", lambda e, y_=y_: e.copy(out=y_[:, :], in_=banks[4][:, :]), r=["bank4"], w=[y_.name])
            for h_ in range(8):
                P.op("dve", lambda e, y_=y_, h_=h_: e.scalar_tensor_tensor(
                    out=y_[:, h_ * 64:(h_ + 1) * 64], in0=banks[5][:, h_ * 64:(h_ + 1) * 64], scalar=sme[:, h_:h_ + 1],
                    in1=y_[:, h_ * 64:(h_ + 1) * 64], op0=ALU.mult, op1=ALU.add),
                    r=["bank5", "sme", y_.name], w=[y_.name])
            if d == 0:
                P.dma("pool", yfs[t0:t0 + 128, :], y_[:, :], r=[y_.name], w=["yfs"])
            else:
                f_ = yf[yi]
                z_ = zat[yi]
                o_ = yo[yi]
                P.dma("sp", f_[:, :], yfs[t0:t0 + 128, :], r=["yfs"], w=[f_.name])
                P.dma("sp", z_[:, :, :], zav[:, :, t0:t0 + 128], w=[z_.name])
                P.op("pool", lambda e, y_=y_, f_=f_: e.tensor_tensor(out=y_[:, :], in0=y_[:, :], in1=f_[:, :], op=ALU.add),
                     r=[y_.name, f_.name], w=[y_.name])
                for c in range(4):
                    P.op("pe", lambda e, c=c, y_=y_: e.transpose(
                        out=banks[7][:, c * 128:(c + 1) * 128], in_=y_[:, c * 128:(c + 1) * 128], identity=idf[:, :]),
                        r=[y_.name, "s_idf"], w=["bank7"])
                P.op("act", lambda e, z_=z_: e.activation(out=sz[:, :, :], in_=z_[:, :, :], func=AF.Silu),
                     r=[z_.name], w=["sz"])
                for c in range(4):
                    P.op("dve", lambda e, c=c, u=u: e.scalar_tensor_tensor(
                        out=y2[:, c, :], in0=u[:, c, :], scalar=dsk_sb[:, c:c + 1],
                        in1=banks[7][:, c * 128:(c + 1) * 128], op0=ALU.mult, op1=ALU.add),
                        r=[u.name, "dsk", "bank7"], w=["y2"])
                P.op("pool", lambda e, o_=o_: e.tensor_tensor(out=o_[:, :, :], in0=y2[:, :, :], in1=sz[:, :, :], op=ALU.mult),
                     r=["y2", "sz"], w=[o_.name])
                P.dma("pool", yov[:, :, t0:t0 + 128], o_[:, :, :], r=[o_.name])
            for h_ in range(8):
                P.op("dve", lambda e, h_=h_: e.scalar_tensor_tensor(
                    out=hst[:, h_ * 64:(h_ + 1) * 64], in0=hst[:, h_ * 64:(h_ + 1) * 64], scalar=sme[:, 16 + h_:17 + h_],
                    in1=banks[6][:, h_ * 64:(h_ + 1) * 64], op0=ALU.mult, op1=ALU.add),
                    r=["hst", "sme", "bank6"], w=["hst"])
            P.op("act", lambda e: e.copy(out=hbf[:, :], in_=hst[:, :]), r=["hst"], w=["hbf"])
    P.release(m0)


def ssd_params_for_core(inp, l, g):
    cwf = inp["ssd_conv_w"][l]
    cbf = inp["ssd_conv_b"][l]
    chans = list(range(512 * g, 512 * g + 512)) + list(range(4096 + 128 * g, 4096 + 128 * g + 128)) + \
        list(range(5120 + 128 * g, 5120 + 128 * g + 128))
    cw = np.ascontiguousarray(cwf[:, chans].reshape(5, 6, 128).transpose(2, 1, 0))
    cb = np.ascontiguousarray(cbf[chans].reshape(6, 128).T)
    hs = slice(8 * g, 8 * g + 8)
    dtb = np.concatenate([inp["ssd_dt_bias"][l][0, hs], inp["ssd_dt_bias"][l][1, hs]])
    alog = np.concatenate([inp["ssd_a_log"][l][0, hs], inp["ssd_a_log"][l][1, hs]])
    dtb = np.ascontiguousarray(np.broadcast_to(dtb[None, :], (128, 16))).astype(np.float32)
    alog = np.ascontiguousarray(np.broadcast_to(alog[None, :], (128, 16))).astype(np.float32)
    dvec = np.repeat(inp["ssd_d"][l][hs], 64)
    dsk = np.ascontiguousarray(dvec.reshape(4, 128).T).astype(np.float32)
    return cw, cb, dtb, alog, dsk


def build_p2():
    P = Prog()
    hT = P.dram("hT", [D, NTOK], BF16, "ExternalInput")
    wf = P.dram("wf", [D, NFM * 128], F32, "ExternalInput")
    wt = P.dram("wt", [D, NTM], F32, "ExternalInput")
    cw = P.dram("cw", [128, 6, 5], F32, "ExternalInput")
    cb = P.dram("cb", [128, 6], F32, "ExternalInput")
    dtb = P.dram("dtb", [128, 16], F32, "ExternalInput")
    alog = P.dram("alog", [128, 16], F32, "ExternalInput")
    dsk = P.dram("dsk", [128, 4], F32, "ExternalInput")
    cfw = P.dram("cfw", [128, 2, 31], F32, "ExternalInput")
    cfb = P.dram("cfb", [128, 2], F32, "ExternalInput")
    ofm = P.dram("ofm", [14, 128, NTOK], BF16, "ExternalOutput")
    ya = P.dram("ya", [512, NTOK], BF16, "ExternalOutput")
    vb = P.dram("vb", [256, NTOK], BF16, "ExternalOutput")
    yc = P.dram("yc", [NTOK, 256], BF16, "ExternalOutput")
    fms = P.dram("fms", [14, 128, NTOK], BF16, "Internal")
    vz = P.dram("vzs", [NTOK, 512], BF16, "Internal")
    dts = P.dram("dts", [NTOK, 16], F32, "Internal")
    ssdu = P.dram("ssdu", [6, 128, NTOK], BF16, "Internal")
    yfs = P.dram("yfs", [NTOK, 512], F32, "Internal")
    t1s = P.dram("t1s", [128, 64, 2, 256], BF16, "Internal")

    def fm_dst(c):
        return fms[c] if c < 14 else ofm[c - 14]
    emit_gemm(P, hT, wf, wt, fm_dst, vz, dts)
    P.barrier()
    emit_ssd(P, lambda c: fms[c], fms[6:10], dts, ssdu, yfs, cw, cb, dtb, alog, dsk, ya)
    emit_conformer(P, lambda c: fms[c], cfw, cfb, vb)
    emit_fourier(P, vz, t1s, yc)
    return P.finish()


def kernel(x, c, ctx, c_ctx, w_mod, b_mod, norm_g, w_in, ssd_conv_w, ssd_conv_b, ssd_dt_bias, ssd_a_log, ssd_d,
           ssd_norm_g, cf_conv_w, cf_conv_b, cf_ln_g, cf_ln_b, w_branch, w_out, final_g):
    bf = ml_dtypes.bfloat16
    inp = dict(x=x, c=c, ctx=ctx, c_ctx=c_ctx, w_mod=w_mod, b_mod=b_mod, norm_g=norm_g, w_in=w_in,
               ssd_conv_w=ssd_conv_w, ssd_conv_b=ssd_conv_b, ssd_dt_bias=ssd_dt_bias, ssd_a_log=ssd_a_log,
               ssd_d=ssd_d, ssd_norm_g=ssd_norm_g, cf_conv_w=cf_conv_w, cf_conv_b=cf_conv_b, cf_ln_g=cf_ln_g,
               cf_ln_b=cf_ln_b, w_branch=w_branch, w_out=w_out, final_g=final_g)
    inp = {k: np.asarray(v, dtype=np.float32) for k, v in inp.items()}
    cores = list(range(NCORES))
    mod = run_p0(inp)
    xc = inp["x"][0]
    cc = inp["ctx"][0]
    depth = 2
    for l in range(depth):
        hT = run_p1(shard_tokens_T(xc, cc), inp["norm_g"][l], mod[l])
        hx, hc = unshard_tokens_T(hT)
        hT_all = np.ascontiguousarray(np.concatenate([hc, hx], axis=0).T)
        del hx, hc, hT
        nc2 = get_nc("p2", build_p2)
        in_maps = []
        for g in cores:
            w_fm, w_tm = wcols_for_core(inp["w_in"][l], g)
            cw, cb, dtb, alog, dsk = ssd_params_for_core(inp, l, g)
            cfw = np.ascontiguousarray(inp["cf_conv_w"][l][:, g * 256:(g + 1) * 256].reshape(31, 2, 128).transpose(2, 1, 0))
            cfb = np.ascontiguousarray(inp["cf_conv_b"][l][g * 256:(g + 1) * 256].reshape(2, 128).T)
            in_maps.append({"hT": hT_all, "wf": w_fm, "wt": w_tm, "cw": cw, "cb": cb, "dtb": dtb, "alog": alog,
                            "dsk": dsk, "cfw": cfw, "cfb": cfb})
        res = run_bass_kernel_spmd(nc2, in_maps, core_ids=cores).results
        del in_maps, hT_all
        ya_all = np.concatenate([np.asarray(r["ya"]) for r in res], axis=0)
        vb_all = np.concatenate([np.asarray(r["vb"]) for r in res], axis=0)
        zb_all = np.concatenate([np.asarray(r["ofm"])[0:2].reshape(256, NTOK) for r in res], axis=0)
        yc_all = np.concatenate([np.asarray(r["yc"]) for r in res], axis=1)
        g_all = np.concatenate(
            [np.concatenate([np.asarray(r["ofm"])[2 + 4 * j:6 + 4 * j].reshape(512, NTOK) for r in res], axis=0)
             for j in range(3)], axis=0)
        del res

        def fm_halves(a, r):
            out = []
            for h in range(2):
                xs_ = a[:, CTX + r * TOK_X + 512 * h: CTX + r * TOK_X + 512 * (h + 1)]
                cs_ = a[:, r * TOK_C + 16 * h: r * TOK_C + 16 * (h + 1)]
                out.append(np.concatenate([xs_, cs_], axis=1))
            return np.ascontiguousarray(np.stack(out))
        nc3 = get_nc("p3", build_p3)
        wbk = w_blocks(inp["w_branch"][l])
        wok = w_blocks(inp["w_out"][l])
        sng = vec_layout(inp["ssd_norm_g"][l])
        lng = np.ascontiguousarray(inp["cf_ln_g"][l].reshape(16, 128).T)
        lnb = np.ascontiguousarray(inp["cf_ln_b"][l].reshape(16, 128).T)
        modv = mod_layout(mod[l])
        ycT_all = np.ascontiguousarray(yc_all.T)
        in_maps = []
        for r in cores:
            in_maps.append({"yaT": fm_halves(ya_all, r), "vbT": fm_halves(vb_all, r), "zbT": fm_halves(zb_all, r),
                            "ycT": fm_halves(ycT_all, r), "gT": fm_halves(g_all, r), "xT": halves_T(xc, cc, r),
                            "wb": wbk, "wo": wok, "sng": sng, "lng": lng, "lnb": lnb, "modv": modv})
        del ya_all, vb_all, zb_all, yc_all, g_all, ycT_all
        res = run_bass_kernel_spmd(nc3, in_maps, core_ids=cores).results
        del in_maps
        xn = np.empty_like(xc)
        cn = np.empty_like(cc)
        for r in cores:
            o = np.asarray(res[r]["xnT"])
            for h in range(2):
                xn[r * TOK_X + 512 * h: r * TOK_X + 512 * (h + 1)] = o[h][:, :512].T
                cn[r * TOK_C + 16 * h: r * TOK_C + 16 * (h + 1)] = o[h][:, 512:].T
        xc, cc = xn, cn
    zmod = np.zeros((2, 3 * D), np.float32)
    oT = run_p1(shard_tokens_T(xc, cc), inp["final_g"], zmod, final=True)
    ox, _ = unshard_tokens_T([np.asarray(o) for o in oT])
    return np.ascontiguousarray(ox[None].astype(np.float32))
```

```python
import numpy as np
import ml_dtypes
import concourse.bass as bass
import concourse.mybir as mybir
from concourse.bass_utils import run_bass_kernel_spmd

F32 = mybir.dt.float32
BF16 = mybir.dt.bfloat16
I32 = mybir.dt.int32
AF = mybir.ActivationFunctionType
ALU = mybir.AluOpType
AX = mybir.AxisListType

NCORES = 8
D = 4096
SEQ = 8192
CTX = 256
TOK_X = SEQ // NCORES
TOK_C = CTX // NCORES
TOK = TOK_X + TOK_C
NTOK = SEQ + CTX
KC = D // 128
EPS = 1e-6

NDMASEM = 40
CC_INC = 16


class Prog:
    def __init__(self):
        self.nc = bass.Bass("TRN2", target_bir_lowering=False)
        nc = self.nc
        self.eng_names = ["pe", "act", "dve", "pool", "sp"]
        self.lists = {e: [] for e in self.eng_names}
        self.count = {e: 0 for e in self.eng_names}
        self.sem = {e: nc.alloc_semaphore("s_" + e) for e in ["pe", "act", "dve", "pool"]}
        self.dsem = [nc.alloc_semaphore("d%d" % i) for i in range(NDMASEM)]
        self.dval = [0] * NDMASEM
        self.dnext = 0
        self.waited = {e: {} for e in self.eng_names}
        self.last_w = {}
        self.readers = {}
        self.n_alloc = 0
        self.sb_ptr = 16512
        self.sb_end = 229344
        self.banks = None

    def dram(self, name, shape, dt, kind):
        return self.nc.dram_tensor(name, list(shape), dt, kind=kind).ap()

    def sb(self, name, shape, dt):
        esz = 4 if dt in (F32, I32) else 2
        n = 1
        for d_ in shape[1:]:
            n *= d_
        nbytes = (n * esz + 63) // 64 * 64
        off = self.sb_ptr
        assert off + nbytes <= self.sb_end, ("SBUF overflow", name, off, nbytes)
        self.sb_ptr += nbytes
        self.n_alloc += 1
        return self.nc.alloc_sbuf_tensor_at("%s_%d" % (name, self.n_alloc), list(shape), dt, offset=off)

    def mark(self):
        return self.sb_ptr

    def release(self, mark):
        self.barrier()
        self.sb_ptr = mark

    def barrier(self):
        deps = [(e, self.count[e]) for e in ["pe", "act", "dve", "pool"] if self.count[e] > 0]
        deps += [(("d", i), self.dval[i]) for i in range(NDMASEM) if self.dval[i] > 0]
        for e in self.eng_names:
            self._emit_waits(e, [d_ for d_ in deps if d_[0] != e])

    def get_banks(self):
        if self.banks is None:
            self.banks = [self.nc.alloc_psum_tensor("bank%d" % i, [128, 512], F32) for i in range(8)]
        return self.banks

    def ps(self, name, shape, dt=F32):
        return self.nc.alloc_psum_tensor(name, list(shape), dt)

    def _deps(self, eng, r, w):
        deps = []
        for k in r:
            t = self.last_w.get(k)
            if t is not None:
                deps.append(t)
        for k in w:
            t = self.last_w.get(k)
            if t is not None and t[0] != eng:
                deps.append(t)
            for t in self.readers.get(k, ()):
                if t[0] != eng:
                    deps.append(t)
        if eng == "pe":
            deps = [t for t in deps if t[0] != "pe"]
        return deps

    def _emit_waits(self, eng, deps):
        wd = self.waited[eng]
        for (s, v) in deps:
            if wd.get(s, 0) < v:
                wd[s] = v
                self.lists[eng].append(("wait", s, v))

    def _record(self, tok, r, w):
        for k in r:
            self.readers.setdefault(k, []).append(tok)
        for k in w:
            self.last_w[k] = tok
            self.readers[k] = []

    def op(self, eng, fn, r=(), w=()):
        self._emit_waits(eng, self._deps(eng, r, w))
        self.count[eng] += 1
        tok = (eng, self.count[eng])
        self.lists[eng].append(("op", fn))
        self._record(tok, r, w)
        return tok

    def dma(self, q, out, in_, r=(), w=()):
        i = self.dnext
        self.dnext = (self.dnext + 1) % NDMASEM
        deps = self._deps(("d", i), r, w)
        if self.dval[i] > 0:
            deps.append((("d", i), self.dval[i]))
        self._emit_waits(q, deps)
        self.dval[i] += 16
        tok = (("d", i), self.dval[i])
        self.lists[q].append(("dma", out, in_, i))
        self._record(tok, r, w)
        return tok

    def _semh(self, s):
        if isinstance(s, tuple):
            return self.dsem[s[1]]
        return self.sem[s]

    def finish(self):
        nc = self.nc
        for i in range(NDMASEM):
            if self.dval[i] > 0:
                self._emit_waits("sp", [(("d", i), self.dval[i])])
        for e in ["pe", "act", "dve", "pool"]:
            if self.count[e] > 0:
                self._emit_waits("sp", [(e, self.count[e])])

        def replay(ename):
            def f(eng):
                for it in self.lists[ename]:
                    if it[0] == "wait":
                        eng.wait_ge(self._semh(it[1]), it[2])
                    elif it[0] == "op":
                        it[1](eng).then_inc(self.sem[ename], 1)
                    elif it[0] == "cc":
                        eng.collective_compute(it[1], ALU.bypass, replica_groups=[list(range(NCORES))],
                                               ins=[it[3]], outs=[it[2]]).then_inc(self.dsem[it[4]], CC_INC)
                    else:
                        eng.dma_start(out=it[1], in_=it[2]).then_inc(self.dsem[it[3]], 16)
            return f

        with nc.Block() as block:
            block.tensor(replay("pe"))
            block.scalar(replay("act"))
            block.vector(replay("dve"))
            block.gpsimd(replay("pool"))
            block.sync(replay("sp"))
        return nc


MODC = 3 * D // NCORES


def build_p0():
    P = Prog()
    c_in = P.dram("c2", [2, D], F32, "ExternalInput")
    wmod = P.dram("wmod", [2, D, MODC], F32, "ExternalInput")
    bmod = P.dram("bmod", [2, MODC], F32, "ExternalInput")
    out = P.dram("mod", [2, 2, MODC], F32, "ExternalOutput")

    craw = P.sb("craw", [128, 2, KC], F32)
    cs = P.sb("cs", [128, KC, 2], F32)
    bsb = P.sb("bsb", [2, 2, MODC], F32)
    res = P.sb("res", [2, 2, MODC], F32)
    wbuf = [P.sb("wbuf%d" % i, [128, 8, MODC], F32) for i in range(2)]
    pst = [P.ps("pst%d" % i, [2, 512], F32) for i in range(3)]

    P.dma("sp", craw[:, :, :], c_in.rearrange("j (p k) -> p j k", k=KC), w=["craw"])
    for l in range(2):
        for j in range(2):
            P.dma("sp", bsb[j:j + 1, l, :], bmod[l:l + 1, :], w=["bsb"])
    for j in range(2):
        P.op("act", lambda e, j=j: e.activation(out=cs[:, :, j], in_=craw[:, j, :], func=AF.Silu),
             r=["craw"], w=["cs"])
    it = 0
    for l in range(2):
        for kg in range(4):
            b = it % 2
            it += 1
            P.dma("sp", wbuf[b][:, :, :],
                  wmod[l].rearrange("(p k) n -> p k n", k=KC)[:, kg * 8:(kg + 1) * 8, :],
                  w=["wbuf%d" % b])
            for kk in range(8):
                k = kg * 8 + kk
                for n in range(3):
                    P.op("pe", lambda e, b=b, kk=kk, n=n, k=k: e.matmul(
                        pst[n][:, :], lhsT=cs[:, k, :], rhs=wbuf[b][:, kk, n * 512:(n + 1) * 512],
                        start=(k == 0), stop=(k == KC - 1)),
                        r=["cs", "wbuf%d" % b], w=["pst%d" % n])
        for n in range(3):
            P.op("dve", lambda e, n=n, l=l: e.tensor_tensor(
                out=res[:, l, n * 512:(n + 1) * 512], in0=pst[n][:, :], in1=bsb[:, l, n * 512:(n + 1) * 512],
                op=ALU.add), r=["pst%d" % n, "bsb"], w=["res"])
    P.dma("sp", out.rearrange("l j n -> j l n"), res[:, :, :], r=["res"])
    return P.finish()


def run_p0(inp):
    nc = build_p0()
    c2 = np.stack([inp["c"][0], inp["c_ctx"]]).astype(np.float32)
    in_maps = []
    for r in range(NCORES):
        sl = slice(r * MODC, (r + 1) * MODC)
        in_maps.append({"c2": c2,
                        "wmod": np.ascontiguousarray(inp["w_mod"][:, :, sl]),
                        "bmod": np.ascontiguousarray(inp["b_mod"][:, sl])})
    res = run_bass_kernel_spmd(nc, in_maps, core_ids=list(range(NCORES)))
    return np.concatenate([r["mod"] for r in res.results], axis=2)


def _prog_collective(self, kind, out, in_, r=(), w=()):
    i = self.dnext
    self.dnext = (self.dnext + 1) % NDMASEM
    deps = self._deps(("d", i), r, w)
    if self.dval[i] > 0:
        deps.append((("d", i), self.dval[i]))
    self._emit_waits("pool", deps)
    self.dval[i] += 16
    tok = (("d", i), self.dval[i])
    self.lists["pool"].append(("cc", kind, out, in_, i))
    self._record(tok, r, w)
    return tok


Prog.collective = _prog_collective


TTILES = [(0, 512, 0), (512, 1024, 0), (1024, 1056, 1)]


def vec_layout(v):
    return np.ascontiguousarray(np.asarray(v, np.float32).reshape(KC, 128).T)


def emit_norm_mod(P, xT, hT, gvec, modv, out_dt, pfx="n"):
    ones = P.sb(pfx + "ones", [128, 128], F32)
    g_sb = P.sb(pfx + "g", [128, KC], F32)
    m_sb = P.sb(pfx + "m", [128, 2, 3, KC], F32)
    G = P.sb(pfx + "G", [128, 2, KC], F32)
    xb = [P.sb(pfx + "xb%d" % i, [128, KC, 512], F32) for i in range(2)]
    sq = [P.sb(pfx + "sq%d" % i, [128, 512], F32) for i in range(3)]
    tmp = [P.sb(pfx + "tmp%d" % i, [128, 512], F32) for i in range(3)]
    ho = [P.sb(pfx + "ho%d" % i, [128, 512], out_dt) for i in range(3)]
    rt = P.sb(pfx + "rt", [128, 512], F32)
    rstd = P.sb(pfx + "rstd", [128, 512], F32)
    pss = P.ps(pfx + "pss", [128, 512], F32)

    P.op("pool", lambda e: e.memset(ones[:, :], 1.0), w=[pfx + "ones"])
    P.dma("sp", g_sb[:, :], gvec, w=[pfx + "g"])
    P.dma("sp", m_sb[:, :, :, :], modv, w=[pfx + "m"])
    for j in range(2):
        P.op("dve", lambda e, j=j: e.scalar_tensor_tensor(
            out=G[:, j, :], in0=m_sb[:, j, 1, :], scalar=1.0, in1=g_sb[:, :], op0=ALU.add, op1=ALU.mult),
            r=[pfx + "m", pfx + "g"], w=[pfx + "G"])
    xv = xT.rearrange("(k p) t -> p k t", p=128)
    n = 0
    for ti, (t0, t1, j) in enumerate(TTILES):
        w_ = t1 - t0
        b = ti % 2
        xk = pfx + "xb%d" % b
        P.dma("sp", xb[b][:, :, 0:w_], xv[:, :, t0:t1], w=[xk])
        for kc in range(KC):
            s = kc % 3
            P.op("act", lambda e, b=b, kc=kc, s=s, w_=w_: e.activation(
                out=sq[s][:, 0:w_], in_=xb[b][:, kc, 0:w_], func=AF.Square), r=[xk], w=[pfx + "sq%d" % s])
            P.op("pe", lambda e, kc=kc, s=s, w_=w_: e.matmul(
                pss[:, 0:w_], lhsT=ones[:, :], rhs=sq[s][:, 0:w_], start=(kc == 0), stop=(kc == KC - 1)),
                r=[pfx + "ones", pfx + "sq%d" % s], w=[pfx + "pss"])
        P.op("act", lambda e, w_=w_: e.activation(out=rt[:, 0:w_], in_=pss[:, 0:w_], func=AF.Sqrt,
                                                   bias=EPS, scale=1.0 / D), r=[pfx + "pss"], w=[pfx + "rt"])
        P.op("dve", lambda e, w_=w_: e.reciprocal(out=rstd[:, 0:w_], in_=rt[:, 0:w_]), r=[pfx + "rt"], w=[pfx + "rstd"])
        for kc in range(KC):
            s = n % 3
            n += 1
            P.op("dve", lambda e, b=b, kc=kc, s=s, w_=w_, j=j: e.scalar_tensor_tensor(
                out=tmp[s][:, 0:w_], in0=xb[b][:, kc, 0:w_], scalar=G[:, j, kc:kc + 1], in1=rstd[:, 0:w_],
                op0=ALU.mult, op1=ALU.mult), r=[xk, pfx + "G", pfx + "rstd"], w=[pfx + "tmp%d" % s])
            P.op("act", lambda e, kc=kc, s=s, w_=w_, j=j: e.activation(
                out=ho[s][:, 0:w_], in_=tmp[s][:, 0:w_], func=AF.Identity, bias=m_sb[:, j, 0, kc:kc + 1], scale=1.0),
                r=[pfx + "tmp%d" % s, pfx + "m"], w=[pfx + "ho%d" % s])
            P.dma("sp", hT[kc * 128:(kc + 1) * 128, t0:t1], ho[s][:, 0:w_], r=[pfx + "ho%d" % s])


def build_p1(out_dt):
    P = Prog()
    xT = P.dram("xT", [D, TOK], F32, "ExternalInput")
    gvec = P.dram("gvec", [128, KC], F32, "ExternalInput")
    modv = P.dram("modv", [128, 2, 3, KC], F32, "ExternalInput")
    hT = P.dram("hT", [D, TOK], out_dt, "ExternalOutput")
    emit_norm_mod(P, xT, hT, gvec, modv, out_dt)
    return P.finish()


def mod_layout(mod_l):
    m = np.asarray(mod_l, np.float32).reshape(2, 3, KC, 128)
    return np.ascontiguousarray(m.transpose(3, 0, 1, 2))


def shard_tokens_T(x2d, c2d):
    outs = []
    for r in range(NCORES):
        t = np.concatenate([x2d[r * TOK_X:(r + 1) * TOK_X], c2d[r * TOK_C:(r + 1) * TOK_C]], axis=0)
        outs.append(np.ascontiguousarray(t.T))
    return outs


def unshard_tokens_T(per_core):
    xs = np.concatenate([p[:, :TOK_X].T for p in per_core], axis=0)
    cs = np.concatenate([p[:, TOK_X:].T for p in per_core], axis=0)
    return xs, cs


_NC_CACHE = {}


def get_nc(key, builder):
    if key not in _NC_CACHE:
        _NC_CACHE[key] = builder()
    return _NC_CACHE[key]


def run_p1(xT_list, gvec, mod_l, final=False):
    nc = get_nc(("p1", final), lambda: build_p1(F32 if final else BF16))
    g = vec_layout(gvec)
    m = mod_layout(mod_l)
    in_maps = [{"xT": xT_list[r], "gvec": g, "modv": m} for r in range(NCORES)]
    res = run_bass_kernel_spmd(nc, in_maps, core_ids=list(range(NCORES)))
    return [r["hT"] for r in res.results]


NFM = 28
NTM = 528
FM_GROUPS = [(0, 7), (7, 14), (14, 21), (21, 28)]
A_TILES = [(0, 256)] + [(256 + 512 * i, 256 + 512 * (i + 1)) for i in range(16)]


def emit_gemm(P, hT_all, w_fm, w_tm, fm_dst, tm_vz, tm_dt):
    banks = P.get_banks()
    m0 = P.mark()
    hv = hT_all.rearrange("(k p) t -> p k t", p=128)
    wfv = w_fm.rearrange("(k p) n -> p k n", p=128)
    wtv = w_tm.rearrange("(k p) n -> p k n", p=128)
    wres = P.sb("wres", [128, KC, 896], BF16)
    wtm = P.sb("wtm", [128, KC, NTM], BF16)
    stg = [P.sb("stg%d" % i, [128, 4, 896], F32) for i in range(2)]
    hb = [P.sb("hb%d" % i, [128, KC, 512], BF16) for i in range(2)]
    ob = [P.sb("ob%d" % i, [128, 512], BF16) for i in range(4)]
    otv = [P.sb("otv%d" % i, [128, 512], BF16) for i in range(2)]
    otd = [P.sb("otd%d" % i, [128, 16], F32) for i in range(2)]
    nstg = 0
    for q in range(8):
        b = nstg % 2
        nstg += 1
        P.dma("sp", stg[b][:, :, 0:NTM], wtv[:, q * 4:(q + 1) * 4, :], w=["stg%d" % b])
        P.op("pool" if q % 2 else "dve", lambda e, b=b, q=q: e.tensor_copy(
            out=wtm[:, q * 4:(q + 1) * 4, :], in_=stg[b][:, :, 0:NTM]), r=["stg%d" % b], w=["wtm"])
    nob = 0
    nhb = 0
    nbank = 0
    ntm = 0
    for gi, (c0, c1) in enumerate(FM_GROUPS):
        ncg = c1 - c0
        for q in range(8):
            b = nstg % 2
            nstg += 1
            P.dma("sp", stg[b][:, :, 0:ncg * 128], wfv[:, q * 4:(q + 1) * 4, c0 * 128:c1 * 128], w=["stg%d" % b])
            P.op("pool" if q % 2 else "dve", lambda e, b=b, q=q, ncg=ncg: e.tensor_copy(
                out=wres[:, q * 4:(q + 1) * 4, 0:ncg * 128], in_=stg[b][:, :, 0:ncg * 128]),
                r=["stg%d" % b], w=["wres"])
        for (t0, t1) in A_TILES:
            tw = t1 - t0
            hbi = nhb % 2
            nhb += 1
            hk = "hb%d" % hbi
            P.dma("sp", hb[hbi][:, :, 0:tw], hv[:, :, t0:t1], w=[hk])
            for cj in range(ncg):
                bk = nbank % 4
                nbank += 1
                for kc in range(KC):
                    P.op("pe", lambda e, bk=bk, kc=kc, cj=cj, hbi=hbi, tw=tw: e.matmul(
                        banks[bk][:, 0:tw], lhsT=wres[:, kc, cj * 128:(cj + 1) * 128], rhs=hb[hbi][:, kc, 0:tw],
                        start=(kc == 0), stop=(kc == KC - 1)), r=["wres", hk], w=["bank%d" % bk])
                o = nob % 4
                nob += 1
                if nob % 2:
                    P.op("act", lambda e, o=o, bk=bk, tw=tw: e.copy(out=ob[o][:, 0:tw], in_=banks[bk][:, 0:tw]),
                         r=["bank%d" % bk], w=["ob%d" % o])
                else:
                    P.op("dve", lambda e, o=o, bk=bk, tw=tw: e.tensor_copy(out=ob[o][:, 0:tw], in_=banks[bk][:, 0:tw]),
                         r=["bank%d" % bk], w=["ob%d" % o])
                P.dma("pool", fm_dst(c0 + cj)[:, t0:t1], ob[o][:, 0:tw], r=["ob%d" % o])
            if gi == 0:
                for s0 in range(0, tw, 128):
                    for kc in range(KC):
                        P.op("pe", lambda e, kc=kc, hbi=hbi, s0=s0: e.matmul(
                            banks[4][:, :], lhsT=hb[hbi][:, kc, s0:s0 + 128], rhs=wtm[:, kc, 0:512],
                            start=(kc == 0), stop=(kc == KC - 1)), r=["wtm", hk], w=["bank4"])
                        P.op("pe", lambda e, kc=kc, hbi=hbi, s0=s0: e.matmul(
                            banks[5][:, 0:16], lhsT=hb[hbi][:, kc, s0:s0 + 128], rhs=wtm[:, kc, 512:528],
                            start=(kc == 0), stop=(kc == KC - 1)), r=["wtm", hk], w=["bank5"])
                    o = ntm % 2
                    ntm += 1
                    P.op("act", lambda e, o=o: e.copy(out=otv[o][:, :], in_=banks[4][:, :]), r=["bank4"], w=["otv%d" % o])
                    P.op("dve", lambda e, o=o: e.tensor_copy(out=otd[o][:, :], in_=banks[5][:, 0:16]),
                         r=["bank5"], w=["otd%d" % o])
                    P.dma("pool", tm_vz[t0 + s0:t0 + s0 + 128, :], otv[o][:, :], r=["otv%d" % o])
                    P.dma("pool", tm_dt[t0 + s0:t0 + s0 + 128, :], otd[o][:, :], r=["otd%d" % o])
    P.release(m0)


def wcols_for_core(w_in_l, g):
    cols = []
    cols += list(range(512 * g, 512 * g + 512))
    cols += list(range(4096 + 128 * g, 4096 + 128 * g + 128))
    cols += list(range(5120 + 128 * g, 5120 + 128 * g + 128))
    cols += list(range(6144 + 512 * g, 6144 + 512 * g + 512))
    cols += list(range(10368 + 256 * g, 10368 + 256 * g + 256))
    cols += list(range(12416 + 256 * g, 12416 + 256 * g + 256))
    cols += list(range(14464 + 256 * g, 14464 + 256 * g + 256))
    for j in range(3):
        cols += list(range(20608 + 4096 * j + 512 * g, 20608 + 4096 * j + 512 * g + 512))
    tcols = []
    tcols += list(range(16512 + 256 * g, 16512 + 256 * g + 256))
    tcols += list(range(18560 + 256 * g, 18560 + 256 * g + 256))
    tcols += list(range(10240 + 8 * g, 10240 + 8 * g + 8))
    tcols += list(range(10304 + 8 * g, 10304 + 8 * g + 8))
    return np.ascontiguousarray(w_in_l[:, cols]), np.ascontiguousarray(w_in_l[:, tcols])


HT = 528
P3_TILES = [(0, 512, 0), (512, 528, 1)]
CFW = 2048


def emit_p3(P, yaT, vbT, zbT, ycT, gT, xT, wb, wo, sng, lng, lnb, modv, xnT):
    banks = P.get_banks()
    m0 = P.mark()
    ones_f = P.sb("ones_f", [128, 128], F32)
    ones_b = P.sb("ones_b", [128, 128], BF16)
    sng_sb = P.sb("sng", [128, 32], F32)
    lng_sb = P.sb("lng", [128, 16], F32)
    lnb_sb = P.sb("lnb", [128, 16], F32)
    m_sb = P.sb("m3", [128, 2, 3, KC], F32)
    P.op("pool", lambda e: e.memset(ones_f[:, :], 1.0), w=["ones_f"])
    P.op("pool", lambda e: e.memset(ones_b[:, :], 1.0), w=["ones_b"])
    P.dma("sp", sng_sb[:, :], sng, w=["sng"])
    P.dma("sp", lng_sb[:, :], lng, w=["lng"])
    P.dma("sp", lnb_sb[:, :], lnb, w=["lnb"])
    P.dma("sp", m_sb[:, :, :, :], modv, w=["m3"])
    ya = P.sb("ya", [128, 32, HT], BF16)
    vb = P.sb("vb", [128, 16, HT], BF16)
    zb = P.sb("zb", [128, 16, HT], BF16)
    yc = P.sb("yc", [128, 16, HT], BF16)
    mg = P.sb("mg", [128, 32, HT], BF16)
    wst = [P.sb("wst%d" % i, [128, 16, 128], F32) for i in range(2)]
    wbf = [P.sb("wbf%d" % i, [128, 64, 128], BF16) for i in range(2)]
    gsb = [P.sb("gsb%d" % i, [128, 3, HT], BF16) for i in range(2)]
    sg = [P.sb("sg%d" % i, [128, 3, HT], BF16) for i in range(2)]
    t1 = [P.sb("t1_%d" % i, [128, HT], F32) for i in range(3)]
    t2 = [P.sb("t2_%d" % i, [128, HT], F32) for i in range(3)]
    stat = P.sb("stat", [128, HT], F32)
    rstd = P.sb("rstd3", [128, HT], F32)
    mean = P.sb("mean3", [128, HT], F32)
    xin = [P.sb("xin%d" % i, [128, HT], F32) for i in range(2)]
    xo = [P.sb("xo%d" % i, [128, HT], F32) for i in range(2)]
    cnt = {"w": 0, "t": 0, "x": 0, "cast": 0}

    def stat_reduce(src_fn, nchunks, use_f32, key_r):
        for kc in range(nchunks):
            rhs_ap, rkeys = src_fn(kc)
            for (a, b_, j) in P3_TILES:
                bk = 0 if j == 0 else 1
                P.op("pe", lambda e, kc=kc, a=a, b_=b_, bk=bk, rhs_ap=rhs_ap: e.matmul(
                    banks[bk][:, 0:b_ - a], lhsT=(ones_f if use_f32 else ones_b)[:, :], rhs=rhs_ap[:, a:b_],
                    start=(kc == 0), stop=(kc == nchunks - 1)),
                    r=["ones_f", "ones_b"] + rkeys, w=["bank%d" % bk])

    def bank_to(dst, scale, bias, func):
        for (a, b_, j) in P3_TILES:
            bk = 0 if j == 0 else 1
            P.op("act", lambda e, a=a, b_=b_, bk=bk: e.activation(
                out=dst[:, a:b_], in_=banks[bk][:, 0:b_ - a], func=func, bias=bias, scale=scale),
                r=["bank%d" % bk], w=[dst.name])

    def load_weights(src4, nch, blk):
        wi = cnt["w"] % 2
        cnt["w"] += 1
        for q in range(nch // 16):
            si = cnt["cast"] % 2
            cnt["cast"] += 1
            P.dma("sp", wst[si][:, :, :], src4[blk, :, q * 16:(q + 1) * 16, :], w=["wst%d" % si])
            P.op("act", lambda e, si=si, wi=wi, q=q: e.copy(
                out=wbf[wi][:, q * 16:(q + 1) * 16, :], in_=wst[si][:, :, :]), r=["wst%d" % si], w=["wbf%d" % wi])
        return wi

    for hf in range(2):
        P.dma("sp", ya[:, :, :], yaT[hf].rearrange("(k p) t -> p k t", p=128), w=["ya"])
        P.dma("sp", vb[:, :, :], vbT[hf].rearrange("(k p) t -> p k t", p=128), w=["vb"])
        P.dma("sp", zb[:, :, :], zbT[hf].rearrange("(k p) t -> p k t", p=128), w=["zb"])
        P.dma("sp", yc[:, :, :], ycT[hf].rearrange("(k p) t -> p k t", p=128), w=["yc"])
        def sq_a(kc):
            i = cnt["t"] % 3
            cnt["t"] += 1
            P.op("act", lambda e, kc=kc, i=i: e.activation(out=t1[i][:, :], in_=ya[:, kc, :], func=AF.Square),
                 r=["ya"], w=[t1[i].name])
            return t1[i], [t1[i].name]
        stat_reduce(sq_a, 32, True, None)
        bank_to(stat, 1.0 / D, EPS, AF.Sqrt)
        P.op("dve", lambda e: e.reciprocal(out=rstd[:, :], in_=stat[:, :]), r=[stat.name], w=[rstd.name])
        for kc in range(32):
            P.op("dve", lambda e, kc=kc: e.scalar_tensor_tensor(
                out=ya[:, kc, :], in0=ya[:, kc, :], scalar=sng_sb[:, kc:kc + 1], in1=rstd[:, :],
                op0=ALU.mult, op1=ALU.mult), r=["ya", "sng", rstd.name], w=["ya"])
        stat_reduce(lambda kc: (vb[:, kc, :], ["vb"]), 16, False, None)
        bank_to(mean, 1.0 / CFW, 0.0, AF.Identity)

        def sq_b(kc):
            i = cnt["t"] % 3
            cnt["t"] += 1
            P.op("dve", lambda e, kc=kc, i=i: e.tensor_tensor(out=t2[i][:, :], in0=vb[:, kc, :], in1=mean[:, :],
                                                               op=ALU.subtract), r=["vb", mean.name], w=[t2[i].name])
            P.op("act", lambda e, i=i: e.activation(out=t1[i][:, :], in_=t2[i][:, :], func=AF.Square),
                 r=[t2[i].name], w=[t1[i].name])
            return t1[i], [t1[i].name]
        stat_reduce(sq_b, 16, True, None)
        bank_to(stat, 1.0 / CFW, EPS, AF.Sqrt)
        P.op("dve", lambda e: e.reciprocal(out=rstd[:, :], in_=stat[:, :]), r=[stat.name], w=[rstd.name])
        for kc in range(16):
            i = cnt["t"] % 3
            cnt["t"] += 1
            P.op("dve", lambda e, kc=kc, i=i: e.tensor_tensor(out=t2[i][:, :], in0=vb[:, kc, :], in1=mean[:, :],
                                                               op=ALU.subtract), r=["vb", mean.name], w=[t2[i].name])
            P.op("dve", lambda e, i=i: e.tensor_tensor(out=t2[i][:, :], in0=t2[i][:, :], in1=rstd[:, :], op=ALU.mult),
                 r=[t2[i].name, rstd.name], w=[t2[i].name])
            P.op("act", lambda e, kc=kc, i=i: e.activation(out=t2[i][:, :], in_=t2[i][:, :], func=AF.Silu,
                                                            bias=lnb_sb[:, kc:kc + 1], scale=lng_sb[:, kc:kc + 1]),
                 r=[t2[i].name, "lng", "lnb"], w=[t2[i].name])
            P.op("act", lambda e, kc=kc, i=i: e.activation(out=t1[i][:, :], in_=zb[:, kc, :], func=AF.Silu),
                 r=["zb"], w=[t1[i].name])
            P.op("dve", lambda e, kc=kc, i=i: e.tensor_tensor(out=vb[:, kc, :], in0=t2[i][:, :], in1=t1[i][:, :],
                                                               op=ALU.mult), r=[t1[i].name, t2[i].name], w=["vb"])
        gv = gT[hf].rearrange("(j n p) t -> n p j t", j=3, p=128)
        for n in range(32):
            wi = load_weights(wb, 64, n)
            gi = n % 2
            P.dma("sp", gsb[gi][:, :, :], gv[n], w=[gsb[gi].name])
            P.op("act", lambda e, gi=gi: e.activation(out=sg[gi][:, :, :], in_=gsb[gi][:, :, :], func=AF.Sigmoid),
                 r=[gsb[gi].name], w=[sg[gi].name])
            bo = 4 * (n % 2)
            srcs = [(ya, "ya", 0, 32), (vb, "vb", 32, 16), (yc, "yc", 48, 16)]
            for bi, (src, sk, c0, ncc) in enumerate(srcs):
                for kc in range(ncc):
                    for (a, b_, j) in P3_TILES:
                        if j == 0:
                            dst = banks[bo + bi][:, 0:512]
                            bk = bo + bi
                        else:
                            dst = banks[bo + 3][:, 16 * bi:16 * bi + 16]
                            bk = bo + 3
                        P.op("pe", lambda e, dst=dst, wi=wi, c0=c0, kc=kc, src=src, a=a, b_=b_, ncc=ncc: e.matmul(
                            dst, lhsT=wbf[wi][:, c0 + kc, :], rhs=src[:, kc, a:b_],
                            start=(kc == 0), stop=(kc == ncc - 1)),
                            r=["wbf%d" % wi, sk], w=["bank%d" % bk])
            i = cnt["t"] % 3
            cnt["t"] += 1
            for (a, b_, j) in P3_TILES:
                def pv(bi):
                    if j == 0:
                        return banks[bo + bi][:, 0:512], "bank%d" % (bo + bi)
                    return banks[bo + 3][:, 16 * bi:16 * bi + 16], "bank%d" % (bo + 3)
                pa, ka = pv(0)
                pb, kb = pv(1)
                pc, kc_ = pv(2)
                P.op("dve", lambda e, pa=pa, gi=gi, i=i, a=a, b_=b_: e.tensor_tensor(
                    out=t1[i][:, a:b_], in0=pa, in1=sg[gi][:, 0, a:b_], op=ALU.mult),
                    r=[ka, sg[gi].name], w=[t1[i].name])
                P.op("dve", lambda e, pb=pb, gi=gi, i=i, a=a, b_=b_: e.tensor_tensor(
                    out=t2[i][:, a:b_], in0=pb, in1=sg[gi][:, 1, a:b_], op=ALU.mult),
                    r=[kb, sg[gi].name], w=[t2[i].name])
                P.op("dve", lambda e, i=i, a=a, b_=b_: e.tensor_tensor(
                    out=t1[i][:, a:b_], in0=t1[i][:, a:b_], in1=t2[i][:, a:b_], op=ALU.add),
                    r=[t1[i].name, t2[i].name], w=[t1[i].name])
                P.op("dve", lambda e, pc=pc, gi=gi, i=i, a=a, b_=b_: e.tensor_tensor(
                    out=t2[i][:, a:b_], in0=pc, in1=sg[gi][:, 2, a:b_], op=ALU.mult),
                    r=[kc_, sg[gi].name], w=[t2[i].name])
                P.op("dve", lambda e, i=i, a=a, b_=b_, n=n: e.tensor_tensor(
                    out=mg[:, n, a:b_], in0=t1[i][:, a:b_], in1=t2[i][:, a:b_], op=ALU.add),
                    r=[t1[i].name, t2[i].name], w=["mg"])
        xv = xT[hf].rearrange("(k p) t -> k p t", p=128)
        ov = xnT[hf].rearrange("(k p) t -> k p t", p=128)
        for m in range(32):
            wi = load_weights(wo, 32, m)
            xi = m % 2
            P.dma("sp", xin[xi][:, :], xv[m], w=[xin[xi].name])
            bo = 4 * (m % 2)
            for kc in range(32):
                for (a, b_, j) in P3_TILES:
                    bk = bo + (0 if j == 0 else 1)
                    P.op("pe", lambda e, bk=bk, wi=wi, kc=kc, a=a, b_=b_: e.matmul(
                        banks[bk][:, 0:b_ - a], lhsT=wbf[wi][:, kc, :], rhs=mg[:, kc, a:b_],
                        start=(kc == 0), stop=(kc == 31)), r=["wbf%d" % wi, "mg"], w=["bank%d" % bk])
            for (a, b_, j) in P3_TILES:
                bk = bo + (0 if j == 0 else 1)
                P.op("dve", lambda e, bk=bk, xi=xi, a=a, b_=b_, j=j, m=m: e.scalar_tensor_tensor(
                    out=xo[xi][:, a:b_], in0=banks[bk][:, 0:b_ - a], scalar=m_sb[:, j, 2, m:m + 1],
                    in1=xin[xi][:, a:b_], op0=ALU.mult, op1=ALU.add),
                    r=["bank%d" % bk, xin[xi].name, "m3"], w=[xo[xi].name])
            P.dma("pool", ov[m], xo[xi][:, :], r=[xo[xi].name])
    P.release(m0)


def build_p3():
    P = Prog()
    yaT = P.dram("yaT", [2, D, HT], BF16, "ExternalInput")
    vbT = P.dram("vbT", [2, CFW, HT], BF16, "ExternalInput")
    zbT = P.dram("zbT", [2, CFW, HT], BF16, "ExternalInput")
    ycT = P.dram("ycT", [2, CFW, HT], BF16, "ExternalInput")
    gT = P.dram("gT", [2, 3 * D, HT], BF16, "ExternalInput")
    xT = P.dram("xT", [2, D, HT], F32, "ExternalInput")
    wb = P.dram("wb", [32, 128, 64, 128], F32, "ExternalInput")
    wo = P.dram("wo", [32, 128, 32, 128], F32, "ExternalInput")
    sng = P.dram("sng", [128, 32], F32, "ExternalInput")
    lng = P.dram("lng", [128, 16], F32, "ExternalInput")
    lnb = P.dram("lnb", [128, 16], F32, "ExternalInput")
    modv = P.dram("modv", [128, 2, 3, KC], F32, "ExternalInput")
    xnT = P.dram("xnT", [2, D, HT], F32, "ExternalOutput")
    emit_p3(P, yaT, vbT, zbT, ycT, gT, xT, wb, wo, sng, lng, lnb, modv, xnT)
    return P.finish()


def w_blocks(w):
    K_, N_ = w.shape
    return np.ascontiguousarray(w.reshape(K_ // 128, 128, N_ // 128, 128).transpose(2, 1, 0, 3))


def halves_T(x2d, c2d, r, dt=None):
    out = []
    for h in range(2):
        t = np.concatenate([x2d[r * TOK_X + 512 * h: r * TOK_X + 512 * (h + 1)],
                            c2d[r * TOK_C + 16 * h: r * TOK_C + 16 * (h + 1)]], axis=0)
        out.append(t.T)
    a = np.ascontiguousarray(np.stack(out))
    return a if dt is None else a.astype(dt)


NROW = SEQ // 64


def emit_conformer(P, fm_src, cfw, cfb, vb_out):
    m0 = P.mark()
    w_sb = P.sb("cfw", [128, 2, 31], F32)
    b_sb = P.sb("cfb", [128, 2], F32)
    a_sb = P.sb("cfa", [128, NTOK], BF16)
    g_sb = P.sb("cfg", [128, NTOK], BF16)
    sgm = P.sb("cfs", [128, NTOK], F32)
    upx = P.sb("upx", [128, NROW, 94], F32)
    upc = P.sb("upc", [128, CTX + 30], F32)
    acc = P.sb("cfacc", [128, NTOK], F32)
    ob = P.sb("cfob", [128, NTOK], BF16)
    P.dma("sp", w_sb[:, :, :], cfw, w=["cfw"])
    P.dma("sp", b_sb[:, :], cfb, w=["cfb"])
    P.op("pool", lambda e: e.memset(upx[:, :, :], 0.0), w=["upx"])
    P.op("pool", lambda e: e.memset(upc[:, :], 0.0), w=["upc"])
    for cc in range(2):
        P.dma("sp", a_sb[:, :], fm_src(10 + cc), w=["cfa"])
        P.dma("sp", g_sb[:, :], fm_src(12 + cc), w=["cfg"])
        P.op("act", lambda e: e.activation(out=sgm[:, :], in_=g_sb[:, :], func=AF.Sigmoid), r=["cfg"], w=["cfs"])
        P.op("dve", lambda e: e.tensor_tensor(out=upc[:, 15:15 + CTX], in0=a_sb[:, 0:CTX], in1=sgm[:, 0:CTX], op=ALU.mult),
             r=["cfa", "cfs"], w=["upc"])
        P.op("dve", lambda e: e.tensor_tensor(
            out=upx[:, :, 15:79], in0=a_sb[:, CTX:].rearrange("p (r t) -> p r t", t=64),
            in1=sgm[:, CTX:].rearrange("p (r t) -> p r t", t=64), op=ALU.mult), r=["cfa", "cfs"], w=["upx"])
        accx = acc[:, CTX:].rearrange("p (r t) -> p r t", t=64)
        for k in range(31):
            if k == 0:
                P.op("dve", lambda e, cc=cc: e.tensor_scalar(
                    out=acc[:, 0:CTX], in0=upc[:, 0:CTX], scalar1=w_sb[:, cc, 0:1], scalar2=b_sb[:, cc:cc + 1],
                    op0=ALU.mult, op1=ALU.add), r=["upc", "cfw", "cfb"], w=["cfacc"])
                P.op("dve", lambda e, cc=cc: e.tensor_scalar(
                    out=accx, in0=upx[:, :, 0:64], scalar1=w_sb[:, cc, 0:1], scalar2=b_sb[:, cc:cc + 1],
                    op0=ALU.mult, op1=ALU.add), r=["upx", "cfw", "cfb"], w=["cfacc"])
            else:
                P.op("dve", lambda e, cc=cc, k=k: e.scalar_tensor_tensor(
                    out=acc[:, 0:CTX], in0=upc[:, k:k + CTX], scalar=w_sb[:, cc, k:k + 1], in1=acc[:, 0:CTX],
                    op0=ALU.mult, op1=ALU.add), r=["upc", "cfw", "cfacc"], w=["cfacc"])
                P.op("dve", lambda e, cc=cc, k=k: e.scalar_tensor_tensor(
                    out=accx, in0=upx[:, :, k:k + 64], scalar=w_sb[:, cc, k:k + 1], in1=accx,
                    op0=ALU.mult, op1=ALU.add), r=["upx", "cfw", "cfacc"], w=["cfacc"])
        P.op("act", lambda e: e.copy(out=ob[:, :], in_=acc[:, :]), r=["cfacc"], w=["cfob"])
        P.dma("pool", vb_out[cc * 128:(cc + 1) * 128, :], ob[:, :], r=["cfob"])
    P.release(m0)


def emit_trig_table(P, dst, nrows, ncols, N, row_base, kind, name, scale=1.0):
    ip = P.sb(name + "_ip", [128, ncols], I32)
    ij = P.sb(name + "_ij", [128, ncols], I32)
    fp = P.sb(name + "_fp", [128, ncols], F32)
    fj = P.sb(name + "_fj", [128, ncols], F32)
    P.op("pool", lambda e: e.iota(ip[:, :], pattern=[[0, ncols]], base=row_base, channel_multiplier=1), w=[ip.name])
    P.op("pool", lambda e: e.iota(ij[:, :], pattern=[[1, ncols]], base=0, channel_multiplier=0), w=[ij.name])
    P.op("dve", lambda e: e.tensor_tensor(out=ip[:, :], in0=ip[:, :], in1=ij[:, :], op=ALU.mult),
         r=[ip.name, ij.name], w=[ip.name])
    off = N // 2 if kind == "sin" else 3 * N // 4
    P.op("dve", lambda e: e.tensor_scalar(out=ip[:, :], in0=ip[:, :], scalar1=float(off), scalar2=None,
                                          op0=ALU.add), r=[ip.name], w=[ip.name])
    P.op("dve", lambda e: e.tensor_scalar(out=ip[:, :], in0=ip[:, :], scalar1=int(N - 1), scalar2=None,
                                          op0=ALU.bitwise_and), r=[ip.name], w=[ip.name])
    P.op("dve", lambda e: e.tensor_scalar(out=fp[:, :], in0=ip[:, :], scalar1=float(-N / 2), scalar2=None,
                                          op0=ALU.add), r=[ip.name], w=[fp.name])
    P.op("act", lambda e: e.activation(out=fj[:, :], in_=fp[:, :], func=AF.Sin, scale=float(2 * np.pi / N)),
         r=[fp.name], w=[fj.name])
    P.op("dve", lambda e: e.tensor_scalar(out=dst, in0=fj[0:nrows, :], scalar1=float(scale), scalar2=None,
                                          op0=ALU.mult), r=[fj.name], w=[name])


def emit_fourier(P, tm_vz, t1s, yc_out):
    banks = P.get_banks()
    m0 = P.mark()
    c128 = P.sb("c128", [128, 128], BF16)
    s128 = P.sb("s128", [128, 128], BF16)
    twc = P.sb("twc", [128, 64], F32)
    tws = P.sb("tws", [128, 64], F32)
    ra = P.sb("ra", [64, 128], BF16)
    rm = P.sb("rm", [64, 128], BF16)
    c256 = P.sb("c256", [128, 2, 256], BF16)
    s256 = P.sb("s256", [128, 2, 256], BF16)
    ns256 = P.sb("ns256", [128, 2, 256], BF16)
    mt = P.mark()
    emit_trig_table(P, c128[:, :], 128, 128, 128, 0, "cos", "c128")
    emit_trig_table(P, s128[:, :], 128, 128, 128, 0, "sin", "s128")
    emit_trig_table(P, twc[:, :], 128, 64, 8192, 0, "cos", "twc")
    emit_trig_table(P, tws[:, :], 128, 64, 8192, 0, "sin", "tws")
    emit_trig_table(P, ra[:, 0:64], 64, 64, 64, 0, "cos", "ra")
    emit_trig_table(P, ra[:, 64:128], 64, 64, 64, 0, "sin", "ra")
    emit_trig_table(P, rm[:, 0:64], 64, 64, 64, 0, "sin", "rm", scale=-1.0)
    emit_trig_table(P, rm[:, 64:128], 64, 64, 64, 0, "cos", "rm")
    for cc in range(2):
        emit_trig_table(P, c256[:, cc, :], 128, 256, 256, cc * 128, "cos", "c256")
        emit_trig_table(P, s256[:, cc, :], 128, 256, 256, cc * 128, "sin", "s256")
        emit_trig_table(P, ns256[:, cc, :], 128, 256, 256, cc * 128, "sin", "ns256", scale=-1.0)
    P.release(mt)

    usb = P.sb("usb", [128, 2, 2, 64, 128], BF16)
    zt = [P.sb("zt%d" % i, [128, 256], BF16) for i in range(2)]
    zs = [P.sb("zs%d" % i, [128, 256], F32) for i in range(2)]
    yo = [P.sb("yo%d" % i, [128, 256], BF16) for i in range(2)]
    cnt = {"e": 0, "z": 0}

    def stage3(ntiles, tok_base, norm):
        for tl in range(ntiles):
            bk = 6 + (tl % 2)
            k = 0
            for cc in range(2):
                for ri in range(2):
                    rhs = (c256 if ri == 0 else ns256)[:, cc, :]
                    P.op("pe", lambda e, bk=bk, cc=cc, ri=ri, tl=tl, rhs=rhs, k=k: e.matmul(
                        banks[bk][:, 0:256], lhsT=usb[:, cc, ri, tl, :], rhs=rhs, start=(k == 0), stop=(k == 3)),
                        r=["usb", "c256", "ns256"], w=["bank%d" % bk])
                    k += 1
            zi = cnt["z"] % 2
            cnt["z"] += 1
            r0 = tok_base + tl * 128
            P.dma("sp", zt[zi][:, :], tm_vz[r0:r0 + 128, 256:512], w=[zt[zi].name])
            P.op("act", lambda e, zi=zi: e.activation(out=zs[zi][:, :], in_=zt[zi][:, :], func=AF.Silu),
                 r=[zt[zi].name], w=[zs[zi].name])
            P.op("dve", lambda e, zi=zi, bk=bk: e.scalar_tensor_tensor(
                out=yo[zi][:, :], in0=banks[bk][:, 0:256], scalar=float(norm), in1=zs[zi][:, :],
                op0=ALU.mult, op1=ALU.mult), r=["bank%d" % bk, zs[zi].name], w=[yo[zi].name])
            P.dma("pool", yc_out[r0:r0 + 128, :], yo[zi][:, :], r=[yo[zi].name])

    mc = P.mark()
    vc = P.sb("vc", [128, 2, 256], BF16)
    P.dma("sp", vc[:, :, :], tm_vz[0:CTX, 0:256].rearrange("(t p) c -> p t c", p=128), w=["vc"])
    for cc in range(2):
        for half, tab in enumerate([c256, s256]):
            bk = 2 * cc + half
            for tl in range(2):
                P.op("pe", lambda e, bk=bk, cc=cc, tl=tl, tab=tab: e.matmul(
                    banks[bk][:, 0:256], lhsT=vc[:, tl, cc * 128:(cc + 1) * 128], rhs=tab[:, tl, :],
                    start=(tl == 0), stop=(tl == 1)), r=["vc", "c256", "s256"], w=["bank%d" % bk])
            P.op("act", lambda e, bk=bk, cc=cc, half=half: e.copy(
                out=usb[:, cc, half, 0:2, :], in_=banks[bk][:, 0:256].rearrange("p (t s) -> p t s", s=128)),
                r=["bank%d" % bk], w=["usb"])
    stage3(2, 0, 1.0 / 256.0)
    P.release(mc)

    ms = P.mark()
    xr = P.sb("xr", [128, 64, 256], BF16)
    t1sb = P.sb("t1sb", [128, 64, 2, 256], BF16)
    tmpa = [P.sb("tmpa%d" % i, [128, 256], F32) for i in range(2)]
    P.dma("sp", xr[:, :, :], tm_vz[CTX:, 0:256].rearrange("(a b) c -> a b c", b=64), w=["xr"])
    for ct in range(32):
        bo = 2 * (ct % 2)
        P.op("pe", lambda e, ct=ct, bo=bo: e.matmul(
            banks[bo][:, :], lhsT=c128[:, :], rhs=xr[:, 2 * ct:2 * ct + 2, :], start=True, stop=True),
            r=["c128", "xr"], w=["bank%d" % bo])
        P.op("pe", lambda e, ct=ct, bo=bo: e.matmul(
            banks[bo + 1][:, :], lhsT=s128[:, :], rhs=xr[:, 2 * ct:2 * ct + 2, :], start=True, stop=True),
            r=["s128", "xr"], w=["bank%d" % (bo + 1)])
        for q in range(2):
            s2 = 2 * ct + q
            tr = banks[bo][:, q * 256:(q + 1) * 256]
            tn = banks[bo + 1][:, q * 256:(q + 1) * 256]
            ti = cnt["e"] % 2
            cnt["e"] += 1
            P.op("dve", lambda e, tn=tn, ti=ti, s2=s2: e.tensor_scalar(
                out=tmpa[ti][:, :], in0=tn, scalar1=tws[:, s2:s2 + 1], scalar2=-1.0, op0=ALU.mult, op1=ALU.mult),
                r=["bank%d" % (bo + 1), "tws"], w=[tmpa[ti].name])
            P.op("dve", lambda e, tr=tr, ti=ti, s2=s2: e.scalar_tensor_tensor(
                out=t1sb[:, s2, 0, :], in0=tr, scalar=twc[:, s2:s2 + 1], in1=tmpa[ti][:, :],
                op0=ALU.mult, op1=ALU.add), r=["bank%d" % bo, "twc", tmpa[ti].name], w=["t1sb"])
            ti = cnt["e"] % 2
            cnt["e"] += 1
            P.op("dve", lambda e, tr=tr, ti=ti, s2=s2: e.tensor_scalar(
                out=tmpa[ti][:, :], in0=tr, scalar1=tws[:, s2:s2 + 1], scalar2=None, op0=ALU.mult),
                r=["bank%d" % bo, "tws"], w=[tmpa[ti].name])
            P.op("dve", lambda e, tn=tn, ti=ti, s2=s2: e.scalar_tensor_tensor(
                out=t1sb[:, s2, 1, :], in0=tn, scalar=twc[:, s2:s2 + 1], in1=tmpa[ti][:, :],
                op0=ALU.mult, op1=ALU.add), r=["bank%d" % (bo + 1), "twc", tmpa[ti].name], w=["t1sb"])
    P.dma("sp", t1s, t1sb[:, :, :, :], r=["t1sb"], w=["t1s"])
    P.release(ms)
    m2 = P.mark()
    t2 = [P.sb("t2in%d" % i, [64, 32, 2, 256], BF16) for i in range(2)]
    t1v = t1s.rearrange("a b r c -> b a r c")
    ne = 0
    for gq in range(4):
        bi = gq % 2
        P.dma("sp", t2[bi][:, :, :, :], t1v[:, gq * 32:(gq + 1) * 32, :, :], r=["t1s"], w=[t2[bi].name])
        for sl in range(32):
            s1p = gq * 32 + sl
            for cc in range(2):
                bk = ne % 4
                P.op("pe", lambda e, bk=bk, bi=bi, sl=sl, cc=cc: e.matmul(
                    banks[bk][:, 0:128], lhsT=t2[bi][:, sl, 0, cc * 128:(cc + 1) * 128], rhs=ra[:, :],
                    start=True, stop=False), r=[t2[bi].name, "ra"], w=["bank%d" % bk])
                P.op("pe", lambda e, bk=bk, bi=bi, sl=sl, cc=cc: e.matmul(
                    banks[bk][:, 0:128], lhsT=t2[bi][:, sl, 1, cc * 128:(cc + 1) * 128], rhs=rm[:, :],
                    start=False, stop=True), r=[t2[bi].name, "rm"], w=["bank%d" % bk])
                eng = "act" if ne % 2 else "dve"
                ne += 1
                src = banks[bk][:, 0:128].rearrange("p (r s) -> p r s", s=64)
                if eng == "act":
                    P.op("act", lambda e, cc=cc, s1p=s1p, src=src: e.copy(out=usb[:, cc, :, :, s1p], in_=src),
                         r=["bank%d" % bk], w=["usb"])
                else:
                    P.op("dve", lambda e, cc=cc, s1p=s1p, src=src: e.tensor_copy(out=usb[:, cc, :, :, s1p], in_=src),
                         r=["bank%d" % bk], w=["usb"])
    stage3(64, CTX, 1.0 / np.sqrt(8192.0 * 256.0))
    P.release(m2)
    P.release(m0)


NCHUNK = NTOK // 128
SSD_DEBUG = None
SSD_STEP = 99


def emit_ssd(P, fm_src, za_src, tm_dt, ssdu, yfs, cw, cb, dtb, alog, dsk, ya_out):
    banks = P.get_banks()
    m0 = P.mark()
    cw_sb = P.sb("cw", [128, 6, 5], F32)
    cb_sb = P.sb("cb", [128, 6], F32)
    P.dma("sp", cw_sb[:, :, :], cw, w=["cw"])
    P.dma("sp", cb_sb[:, :], cb, w=["cb"])
    m1 = P.mark()
    pc = [P.sb("pc%d" % i, [128, CTX + 4], BF16) for i in range(2)]
    px = [P.sb("px%d" % i, [128, SEQ + 4], BF16) for i in range(2)]
    acc = [P.sb("cacc%d" % i, [128, NTOK], F32) for i in range(2)]
    ub = [P.sb("cub%d" % i, [128, NTOK], BF16) for i in range(2)]
    for i in range(2):
        P.op("pool", lambda e, i=i: e.memset(pc[i][:, :], 0.0), w=[pc[i].name])
        P.op("pool", lambda e, i=i: e.memset(px[i][:, :], 0.0), w=[px[i].name])
    for c in range(6):
        b = c % 2
        src = fm_src(c)
        P.dma("sp", pc[b][:, 2:2 + CTX], src[:, 0:CTX], w=[pc[b].name])
        P.dma("sp", px[b][:, 2:2 + SEQ], src[:, CTX:], w=[px[b].name])
        for k in range(5):
            for (pb, lo, n_) in ((pc[b], 0, CTX), (px[b], CTX, SEQ)):
                if k == 0:
                    P.op("dve", lambda e, pb=pb, lo=lo, n_=n_, c=c, b=b: e.tensor_scalar(
                        out=acc[b][:, lo:lo + n_], in0=pb[:, 0:n_], scalar1=cw_sb[:, c, 0:1], scalar2=cb_sb[:, c:c + 1],
                        op0=ALU.mult, op1=ALU.add), r=[pb.name, "cw", "cb"], w=[acc[b].name])
                else:
                    P.op("dve", lambda e, pb=pb, lo=lo, n_=n_, c=c, b=b, k=k: e.scalar_tensor_tensor(
                        out=acc[b][:, lo:lo + n_], in0=pb[:, k:k + n_], scalar=cw_sb[:, c, k:k + 1],
                        in1=acc[b][:, lo:lo + n_], op0=ALU.mult, op1=ALU.add),
                        r=[pb.name, "cw", acc[b].name], w=[acc[b].name])
        P.op("act", lambda e, b=b: e.activation(out=ub[b][:, :], in_=acc[b][:, :], func=AF.Silu),
             r=[acc[b].name], w=[ub[b].name])
        P.dma("pool", ssdu[c], ub[b][:, :], r=[ub[b].name], w=["ssdu"])
    P.release(m1)
    if SSD_DEBUG == "conv":
        P.release(m0)
        return
    dt_all = P.sb("dt_all", [128, NCHUNK, 16], F32)
    a_all = P.sb("a_all", [128, NCHUNK, 16], F32)
    dtb_sb = P.sb("dtb", [128, 16], F32)
    nex = P.sb("nex", [128, 16], F32)
    dsk_sb = P.sb("dsk", [128, 4], F32)
    P.dma("sp", dtb_sb[:, :], dtb, w=["dtb"])
    P.dma("sp", nex[:, :], alog, w=["nex"])
    P.dma("sp", dsk_sb[:, :], dsk, w=["dsk"])
    m2 = P.mark()
    xr_ = P.sb("dtx", [128, NCHUNK, 16], F32)
    ax_ = P.sb("dtax", [128, NCHUNK, 16], F32)
    P.dma("sp", xr_[:, :, :], tm_dt.rearrange("(n p) c -> p n c", p=128), w=["dtx"])
    P.op("dve", lambda e: e.tensor_tensor(out=xr_[:, :, :], in0=xr_[:, :, :],
                                          in1=dtb_sb[:, :].unsqueeze(1).broadcast_to([128, NCHUNK, 16]), op=ALU.add),
         r=["dtx", "dtb"], w=["dtx"])
    P.op("act", lambda e: e.activation(out=ax_[:, :, :], in_=xr_[:, :, :], func=AF.Abs), r=["dtx"], w=["dtax"])
    P.op("act", lambda e: e.activation(out=ax_[:, :, :], in_=ax_[:, :, :], func=AF.Exp, scale=-1.0),
         r=["dtax"], w=["dtax"])
    P.op("act", lambda e: e.activation(out=ax_[:, :, :], in_=ax_[:, :, :], func=AF.Ln, bias=1.0, scale=1.0),
         r=["dtax"], w=["dtax"])
    P.op("dve", lambda e: e.tensor_scalar_max(out=xr_[:, :, :], in0=xr_[:, :, :], scalar1=0.0), r=["dtx"], w=["dtx"])
    P.op("dve", lambda e: e.tensor_tensor(out=dt_all[:, :, :], in0=xr_[:, :, :], in1=ax_[:, :, :], op=ALU.add),
         r=["dtx", "dtax"], w=["dt_all"])
    P.op("act", lambda e: e.activation(out=nex[:, :], in_=nex[:, :], func=AF.Exp), r=["nex"], w=["nex"])
    P.op("dve", lambda e: e.scalar_tensor_tensor(
        out=a_all[:, :, :], in0=dt_all[:, :, :], scalar=-1.0,
        in1=nex[:, :].unsqueeze(1).broadcast_to([128, NCHUNK, 16]), op0=ALU.mult, op1=ALU.mult),
        r=["dt_all", "nex"], w=["a_all"])
    P.release(m2)
    onesf = P.sb("s_ones", [128, 128], F32)
    oneb = P.sb("s_oneb", [128, 128], BF16)
    idf = P.sb("s_idf", [128, 128], F32)
    idb = P.sb("s_idb", [128, 128], BF16)
    Lf = P.sb("s_Lf", [128, 128], F32)
    Uf = P.sb("s_Uf", [128, 128], F32)
    Lb = P.sb("s_Lb", [128, 128], F32)
    Ub = P.sb("s_Ub", [128, 128], F32)
    P.op("pool", lambda e: e.memset(onesf[:, :], 1.0), w=["s_ones"])
    P.op("pool", lambda e: e.memset(oneb[:, :], 1.0), w=["s_oneb"])

    def sel(dst, src, pat, cm, op):
        P.op("pool", lambda e: e.affine_select(out=dst[:, :], in_=src[:, :], pattern=[[pat, 128]], compare_op=op,
                                               fill=0.0, base=0, channel_multiplier=cm),
             r=[src.name], w=[dst.name])
    sel(idf, onesf, -1, 1, ALU.is_equal)
    sel(idb, oneb, -1, 1, ALU.is_equal)
    sel(Lf, onesf, -1, 1, ALU.is_gt)
    sel(Uf, onesf, 1, -1, ALU.is_ge)
    sel(Lb, onesf, 1, -1, ALU.is_gt)
    sel(Ub, onesf, -1, 1, ALU.is_ge)
    if SSD_DEBUG == "const":
        P.release(m0)
        return
    hst = P.sb("hst", [128, 512], F32)
    hbf = P.sb("hbf", [128, 512], BF16)
    ut = [P.sb("ut%d" % i, [128, 6, 128], BF16) for i in range(2)]
    xdt = P.sb("xdt", [128, 8, 64], BF16)
    xdtd = P.sb("xdtd", [128, 8, 64], BF16)
    xsb = P.sb("xsb", [128, 640], BF16)
    sme = P.sb("sme", [128, 24], F32)
    la = P.sb("la", [128, 8, 128], F32)
    dm = [P.sb("dm%d" % i, [128, 4, 128], BF16) for i in range(2)]
    wm = [P.sb("wm%d" % i, [128, 4, 128], BF16) for i in range(2)]
    mm = P.sb("mm", [128, 128], BF16)
    yt = [P.sb("yt%d" % i, [128, 512], F32) for i in range(2)]
    yf = [P.sb("yf%d" % i, [128, 512], F32) for i in range(2)]
    zat = [P.sb("zat%d" % i, [128, 4, 128], BF16) for i in range(2)]
    sz = P.sb("sz", [128, 4, 128], F32)
    y2 = P.sb("y2", [128, 4, 128], F32)
    yo = [P.sb("yob%d" % i, [128, 4, 128], BF16) for i in range(2)]
    htmp = P.sb("htmp", [128, 512], F32)
    b0 = banks[0][:, :].bitcast(BF16)
    uv = ssdu.rearrange("c p t -> p c t")
    zav = za_src.rearrange("c p t -> p c t")
    yov = ya_out.rearrange("(c p) t -> p c t", p=128)
    it = 0
    for d in range(2):
        order = list(range(NCHUNK)) if d == 0 else [1, 0] + list(range(NCHUNK - 1, 1, -1))
        Lm, Um = (Lf, Uf) if d == 0 else (Lb, Ub)
        P.op("pool", lambda e: e.memset(hst[:, :], 0.0), w=["hst"])
        P.op("pool", lambda e: e.memset(hbf[:, :], 0.0), w=["hbf"])
        for n in order:
            if isinstance(SSD_DEBUG, int) and it >= SSD_DEBUG:
                break
            t0 = n * 128
            ui = it % 2
            it += 1
            u = ut[ui]
            P.dma("sp", u[:, :, :], uv[:, :, t0:t0 + 128], r=["ssdu"], w=[u.name])
            a_c = a_all[:, n, d * 8:(d + 1) * 8]
            dt_c = dt_all[:, n, d * 8:(d + 1) * 8]
            for c in range(4):
                P.op("pe", lambda e, c=c, u=u: e.transpose(out=b0[:, c * 128:(c + 1) * 128], in_=u[:, c, :],
                                                           identity=idb[:, :]), r=[u.name, "s_idb"], w=["bank0"])
            P.op("pe", lambda e, u=u: e.transpose(out=b0[:, 512:640], in_=u[:, 4, :], identity=idb[:, :]),
                 r=[u.name, "s_idb"], w=["bank0"])
            if SSD_STEP < 1:
                continue
            P.op("act", lambda e: e.copy(out=xsb[:, :], in_=b0[:, 0:640]), r=["bank0"], w=["xsb"])
            P.op("dve", lambda e, dt_c=dt_c: e.tensor_tensor(
                out=xdt[:, :, :], in0=xsb[:, 0:512].rearrange("p (h q) -> p h q", q=64),
                in1=dt_c.unsqueeze(2).broadcast_to([128, 8, 64]), op=ALU.mult), r=["xsb", "dt_all"], w=["xdt"])
            if SSD_STEP < 2:
                continue
            for q, lm in enumerate((Um, Lm, onesf)):
                P.op("pe", lambda e, q=q, lm=lm, a_c=a_c: e.matmul(
                    banks[3][:, 128 + 8 * q:136 + 8 * q], lhsT=lm[:, :], rhs=a_c, start=True, stop=True),
                    r=[lm.name, "a_all"], w=["bank3s"])
            P.op("act", lambda e: e.activation(out=sme[:, :], in_=banks[3][:, 128:152], func=AF.Exp),
                 r=["bank3s"], w=["sme"])
            if SSD_STEP < 3:
                continue
            P.op("dve", lambda e, Lm=Lm, a_c=a_c: e.tensor_tensor(
                out=la[:, :, :], in0=Lm[:, :].unsqueeze(1).broadcast_to([128, 8, 128]),
                in1=a_c.unsqueeze(2).broadcast_to([128, 8, 128]), op=ALU.mult), r=[Lm.name, "a_all"], w=["la"])
            for hg in range(2):
                for hh in range(4):
                    h_ = hg * 4 + hh
                    P.op("pe", lambda e, hg=hg, hh=hh, h_=h_, Um=Um: e.matmul(
                        banks[1 + hg][:, hh * 128:(hh + 1) * 128], lhsT=la[:, h_, :], rhs=Um[:, :],
                        start=True, stop=True), r=["la", Um.name], w=["bank%d" % (1 + hg)])
                P.op("act", lambda e, hg=hg: e.activation(
                    out=dm[hg][:, :, :], in_=banks[1 + hg][:, :].rearrange("p (h i) -> p h i", i=128), func=AF.Exp),
                    r=["bank%d" % (1 + hg)], w=[dm[hg].name])
            if SSD_STEP < 4:
                continue
            P.op("pe", lambda e, u=u: e.matmul(banks[3][:, 0:128], lhsT=u[:, 4, :], rhs=u[:, 5, :], start=True, stop=True),
                 r=[u.name], w=["bank3"])
            P.op("dve", lambda e, Um=Um: e.tensor_tensor(out=mm[:, :], in0=banks[3][:, 0:128], in1=Um[:, :], op=ALU.mult),
                 r=["bank3", Um.name], w=["mm"])
            for hg in range(2):
                P.op("dve", lambda e, hg=hg: e.tensor_tensor(
                    out=wm[hg][:, :, :], in0=dm[hg][:, :, :], in1=mm[:, :].unsqueeze(1).broadcast_to([128, 4, 128]),
                    op=ALU.mult), r=[dm[hg].name, "mm"], w=[wm[hg].name])
            if SSD_STEP < 5:
                continue
            for h_ in range(8):
                P.op("pe", lambda e, h_=h_: e.matmul(
                    banks[4][:, h_ * 64:(h_ + 1) * 64], lhsT=wm[h_ // 4][:, h_ % 4, :], rhs=xdt[:, h_, :],
                    start=True, stop=True), r=[wm[h_ // 4].name, "xdt"], w=["bank4"])
            P.op("pe", lambda e, u=u: e.matmul(banks[5][:, :], lhsT=u[:, 5, :], rhs=hbf[:, :], start=True, stop=True),
                 r=[u.name, "hbf"], w=["bank5"])
            P.op("dve", lambda e: e.tensor_tensor(
                out=xdtd[:, :, :], in0=xdt[:, :, :], in1=sme[:, 8:16].unsqueeze(2).broadcast_to([128, 8, 64]),
                op=ALU.mult), r=["xdt", "sme"], w=["xdtd"])
            P.op("pe", lambda e: e.matmul(banks[6][:, :], lhsT=xsb[:, 512:640], rhs=xdtd[:, :, :], start=True, stop=True),
                 r=["xsb", "xdtd"], w=["bank6"])
            if SSD_STEP < 6:
                continue
            yi = it % 2
            y_ = yt[yi]
            P.op("dve", lambda e, y_=y_: e.tensor_tensor(
                out=y_[:, :].rearrange("p (h q) -> p h q", q=64),
                in0=banks[5][:, :].rearrange("p (h q) -> p h q", q=64),
                in1=sme[:, 0:8].unsqueeze(2).broadcast_to([128, 8, 64]), op=ALU.mult),
                r=["bank5", "sme"], w=[y_.name])
            P.op("dve", lambda e, y_=y_: e.tensor_tensor(out=y_[:, :], in0=banks[4][:, :], in1=y_[:, :], op=ALU.add),
                 r=["bank4", y_.name], w=[y_.name])
            if d == 0:
                P.dma("pool", yfs[t0:t0 + 128, :], y_[:, :], r=[y_.name], w=["yfs"])
            else:
                f_ = yf[yi]
                z_ = zat[yi]
                o_ = yo[yi]
                P.dma("sp", f_[:, :], yfs[t0:t0 + 128, :], r=["yfs"], w=[f_.name])
                P.dma("sp", z_[:, :, :], zav[:, :, t0:t0 + 128], w=[z_.name])
                P.op("pool", lambda e, y_=y_, f_=f_: e.tensor_tensor(out=y_[:, :], in0=y_[:, :], in1=f_[:, :], op=ALU.add),
                     r=[y_.name, f_.name], w=[y_.name])
                for c in range(4):
                    P.op("pe", lambda e, c=c, y_=y_: e.transpose(
                        out=banks[7][:, c * 128:(c + 1) * 128], in_=y_[:, c * 128:(c + 1) * 128], identity=idf[:, :]),
                        r=[y_.name, "s_idf"], w=["bank7"])
                P.op("act", lambda e, z_=z_: e.activation(out=sz[:, :, :], in_=z_[:, :, :], func=AF.Silu),
                     r=[z_.name], w=["sz"])
                for c in range(4):
                    P.op("dve", lambda e, c=c, u=u: e.scalar_tensor_tensor(
                        out=y2[:, c, :], in0=u[:, c, :], scalar=dsk_sb[:, c:c + 1],
                        in1=banks[7][:, c * 128:(c + 1) * 128], op0=ALU.mult, op1=ALU.add),
                        r=[u.name, "dsk", "bank7"], w=["y2"])
                P.op("pool", lambda e, o_=o_: e.tensor_tensor(out=o_[:, :, :], in0=y2[:, :, :], in1=sz[:, :, :], op=ALU.mult),
                     r=["y2", "sz"], w=[o_.name])
                P.dma("pool", yov[:, :, t0:t0 + 128], o_[:, :, :], r=[o_.name])
            P.op("dve", lambda e: e.tensor_tensor(
                out=htmp[:, :].rearrange("p (h q) -> p h q", q=64), in0=hst[:, :].rearrange("p (h q) -> p h q", q=64),
                in1=sme[:, 16:24].unsqueeze(2).broadcast_to([128, 8, 64]), op=ALU.mult),
                r=["hst", "sme"], w=["htmp"])
            P.op("dve", lambda e: e.tensor_tensor(out=hst[:, :], in0=banks[6][:, :], in1=htmp[:, :], op=ALU.add),
                 r=["bank6", "htmp"], w=["hst"])
            P.op("act", lambda e: e.copy(out=hbf[:, :], in_=hst[:, :]), r=["hst"], w=["hbf"])
    P.release(m0)


def ssd_params_for_core(inp, l, g):
    cwf = inp["ssd_conv_w"][l]
    cbf = inp["ssd_conv_b"][l]
    chans = list(range(512 * g, 512 * g + 512)) + list(range(4096 + 128 * g, 4096 + 128 * g + 128)) + \
        list(range(5120 + 128 * g, 5120 + 128 * g + 128))
    cw = np.ascontiguousarray(cwf[:, chans].reshape(5, 6, 128).transpose(2, 1, 0))
    cb = np.ascontiguousarray(cbf[chans].reshape(6, 128).T)
    hs = slice(8 * g, 8 * g + 8)
    dtb = np.concatenate([inp["ssd_dt_bias"][l][0, hs], inp["ssd_dt_bias"][l][1, hs]])
    alog = np.concatenate([inp["ssd_a_log"][l][0, hs], inp["ssd_a_log"][l][1, hs]])
    dtb = np.ascontiguousarray(np.broadcast_to(dtb[None, :], (128, 16))).astype(np.float32)
    alog = np.ascontiguousarray(np.broadcast_to(alog[None, :], (128, 16))).astype(np.float32)
    dvec = np.repeat(inp["ssd_d"][l][hs], 64)
    dsk = np.ascontiguousarray(dvec.reshape(4, 128).T).astype(np.float32)
    return cw, cb, dtb, alog, dsk


def build_p2():
    P = Prog()
    hT = P.dram("hT", [D, NTOK], BF16, "ExternalInput")
    wf = P.dram("wf", [D, NFM * 128], F32, "ExternalInput")
    wt = P.dram("wt", [D, NTM], F32, "ExternalInput")
    cw = P.dram("cw", [128, 6, 5], F32, "ExternalInput")
    cb = P.dram("cb", [128, 6], F32, "ExternalInput")
    dtb = P.dram("dtb", [128, 16], F32, "ExternalInput")
    alog = P.dram("alog", [128, 16], F32, "ExternalInput")
    dsk = P.dram("dsk", [128, 4], F32, "ExternalInput")
    cfw = P.dram("cfw", [128, 2, 31], F32, "ExternalInput")
    cfb = P.dram("cfb", [128, 2], F32, "ExternalInput")
    ofm = P.dram("ofm", [14, 128, NTOK], BF16, "ExternalOutput")
    ya = P.dram("ya", [512, NTOK], BF16, "ExternalOutput")
    vb = P.dram("vb", [256, NTOK], BF16, "ExternalOutput")
    yc = P.dram("yc", [NTOK, 256], BF16, "ExternalOutput")
    fms = P.dram("fms", [14, 128, NTOK], BF16, "Internal")
    vz = P.dram("vzs", [NTOK, 512], BF16, "Internal")
    dts = P.dram("dts", [NTOK, 16], F32, "Internal")
    ssdu = P.dram("ssdu", [6, 128, NTOK], BF16, "Internal")
    yfs = P.dram("yfs", [NTOK, 512], F32, "Internal")
    t1s = P.dram("t1s", [128, 64, 2, 256], BF16, "Internal")

    def fm_dst(c):
        return fms[c] if c < 14 else ofm[c - 14]
    emit_gemm(P, hT, wf, wt, fm_dst, vz, dts)
    P.barrier()
    emit_ssd(P, lambda c: fms[c], fms[6:10], dts, ssdu, yfs, cw, cb, dtb, alog, dsk, ya)
    emit_conformer(P, lambda c: fms[c], cfw, cfb, vb)
    emit_fourier(P, vz, t1s, yc)
    return P.finish()


def kernel(x, c, ctx, c_ctx, w_mod, b_mod, norm_g, w_in, ssd_conv_w, ssd_conv_b, ssd_dt_bias, ssd_a_log, ssd_d,
           ssd_norm_g, cf_conv_w, cf_conv_b, cf_ln_g, cf_ln_b, w_branch, w_out, final_g):
    bf = ml_dtypes.bfloat16
    inp = dict(x=x, c=c, ctx=ctx, c_ctx=c_ctx, w_mod=w_mod, b_mod=b_mod, norm_g=norm_g, w_in=w_in,
               ssd_conv_w=ssd_conv_w, ssd_conv_b=ssd_conv_b, ssd_dt_bias=ssd_dt_bias, ssd_a_log=ssd_a_log,
               ssd_d=ssd_d, ssd_norm_g=ssd_norm_g, cf_conv_w=cf_conv_w, cf_conv_b=cf_conv_b, cf_ln_g=cf_ln_g,
               cf_ln_b=cf_ln_b, w_branch=w_branch, w_out=w_out, final_g=final_g)
    inp = {k: np.asarray(v, dtype=np.float32) for k, v in inp.items()}
    cores = list(range(NCORES))
    mod = run_p0(inp)
    xc = inp["x"][0]
    cc = inp["ctx"][0]
    depth = 2
    for l in range(depth):
        hT = run_p1(shard_tokens_T(xc, cc), inp["norm_g"][l], mod[l])
        hx, hc = unshard_tokens_T(hT)
        hT_all = np.ascontiguousarray(np.concatenate([hc, hx], axis=0).T)
        del hx, hc, hT
        nc2 = get_nc("p2", build_p2)
        in_maps = []
        for g in cores:
            w_fm, w_tm = wcols_for_core(inp["w_in"][l], g)
            cw, cb, dtb, alog, dsk = ssd_params_for_core(inp, l, g)
            cfw = np.ascontiguousarray(inp["cf_conv_w"][l][:, g * 256:(g + 1) * 256].reshape(31, 2, 128).transpose(2, 1, 0))
            cfb = np.ascontiguousarray(inp["cf_conv_b"][l][g * 256:(g + 1) * 256].reshape(2, 128).T)
            in_maps.append({"hT": hT_all, "wf": w_fm, "wt": w_tm, "cw": cw, "cb": cb, "dtb": dtb, "alog": alog,
                            "dsk": dsk, "cfw": cfw, "cfb": cfb})
        res = run_bass_kernel_spmd(nc2, in_maps, core_ids=cores).results
        del in_maps, hT_all
        ya_all = np.concatenate([np.asarray(r["ya"]) for r in res], axis=0)
        vb_all = np.concatenate([np.asarray(r["vb"]) for r in res], axis=0)
        zb_all = np.concatenate([np.asarray(r["ofm"])[0:2].reshape(256, NTOK) for r in res], axis=0)
        yc_all = np.concatenate([np.asarray(r["yc"]) for r in res], axis=1)
        g_all = np.concatenate(
            [np.concatenate([np.asarray(r["ofm"])[2 + 4 * j:6 + 4 * j].reshape(512, NTOK) for r in res], axis=0)
             for j in range(3)], axis=0)
        del res

        def fm_halves(a, r):
            out = []
            for h in range(2):
                xs_ = a[:, CTX + r * TOK_X + 512 * h: CTX + r * TOK_X + 512 * (h + 1)]
                cs_ = a[:, r * TOK_C + 16 * h: r * TOK_C + 16 * (h + 1)]
                out.append(np.concatenate([xs_, cs_], axis=1))
            return np.ascontiguousarray(np.stack(out))
        nc3 = get_nc("p3", build_p3)
        wbk = w_blocks(inp["w_branch"][l])
        wok = w_blocks(inp["w_out"][l])
        sng = vec_layout(inp["ssd_norm_g"][l])
        lng = np.ascontiguousarray(inp["cf_ln_g"][l].reshape(16, 128).T)
        lnb = np.ascontiguousarray(inp["cf_ln_b"][l].reshape(16, 128).T)
        modv = mod_layout(mod[l])
        ycT_all = np.ascontiguousarray(yc_all.T)
        in_maps = []
        for r in cores:
            in_maps.append({"yaT": fm_halves(ya_all, r), "vbT": fm_halves(vb_all, r), "zbT": fm_halves(zb_all, r),
                            "ycT": fm_halves(ycT_all, r), "gT": fm_halves(g_all, r), "xT": halves_T(xc, cc, r),
                            "wb": wbk, "wo": wok, "sng": sng, "lng": lng, "lnb": lnb, "modv": modv})
        del ya_all, vb_all, zb_all, yc_all, g_all, ycT_all
        res = run_bass_kernel_spmd(nc3, in_maps, core_ids=cores).results
        del in_maps
        xn = np.empty_like(xc)
        cn = np.empty_like(cc)
        for r in cores:
            o = np.asarray(res[r]["xnT"])
            for h in range(2):
                xn[r * TOK_X + 512 * h: r * TOK_X + 512 * (h + 1)] = o[h][:, :512].T
                cn[r * TOK_C + 16 * h: r * TOK_C + 16 * (h + 1)] = o[h][:, 512:].T
        xc, cc = xn, cn
    zmod = np.zeros((2, 3 * D), np.float32)
    oT = run_p1(shard_tokens_T(xc, cc), inp["final_g"], zmod, final=True)
    ox, _ = unshard_tokens_T([np.asarray(o) for o in oT])
    return np.ascontiguousarray(ox[None].astype(np.float32))
```

```python
import numpy as np
import ml_dtypes
import concourse.bass as bass
import concourse.mybir as mybir
from concourse.bass_utils import run_bass_kernel_spmd

F32 = mybir.dt.float32
BF16 = mybir.dt.bfloat16
I32 = mybir.dt.int32
AF = mybir.ActivationFunctionType
ALU = mybir.AluOpType
AX = mybir.AxisListType

NCORES = 8
D = 4096
SEQ = 8192
CTX = 256
TOK_X = SEQ // NCORES
TOK_C = CTX // NCORES
TOK = TOK_X + TOK_C
NTOK = SEQ + CTX
KC = D // 128
EPS = 1e-6

NDMASEM = 40
CC_INC = 16


class Prog:
    def __init__(self):
        self.nc = bass.Bass("TRN2", target_bir_lowering=False)
        nc = self.nc
        self.eng_names = ["pe", "act", "dve", "pool", "sp"]
        self.lists = {e: [] for e in self.eng_names}
        self.count = {e: 0 for e in self.eng_names}
        self.sem = {e: nc.alloc_semaphore("s_" + e) for e in ["pe", "act", "dve", "pool"]}
        self.dsem = [nc.alloc_semaphore("d%d" % i) for i in range(NDMASEM)]
        self.dval = [0] * NDMASEM
        self.dnext = 0
        self.waited = {e: {} for e in self.eng_names}
        self.last_w = {}
        self.readers = {}
        self.n_alloc = 0
        self.sb_ptr = 16512
        self.sb_end = 229344
        self.banks = None

    def dram(self, name, shape, dt, kind):
        return self.nc.dram_tensor(name, list(shape), dt, kind=kind).ap()

    def sb(self, name, shape, dt):
        esz = 4 if dt in (F32, I32) else 2
        n = 1
        for d_ in shape[1:]:
            n *= d_
        nbytes = (n * esz + 63) // 64 * 64
        off = self.sb_ptr
        assert off + nbytes <= self.sb_end, ("SBUF overflow", name, off, nbytes)
        self.sb_ptr += nbytes
        self.n_alloc += 1
        return self.nc.alloc_sbuf_tensor_at("%s_%d" % (name, self.n_alloc), list(shape), dt, offset=off)

    def mark(self):
        return self.sb_ptr

    def release(self, mark):
        self.barrier()
        self.sb_ptr = mark

    def barrier(self):
        deps = [(e, self.count[e]) for e in ["pe", "act", "dve", "pool"] if self.count[e] > 0]
        deps += [(("d", i), self.dval[i]) for i in range(NDMASEM) if self.dval[i] > 0]
        for e in self.eng_names:
            self._emit_waits(e, [d_ for d_ in deps if d_[0] != e])

    def get_banks(self):
        if self.banks is None:
            self.banks = [self.nc.alloc_psum_tensor("bank%d" % i, [128, 512], F32) for i in range(8)]
        return self.banks

    def ps(self, name, shape, dt=F32):
        return self.nc.alloc_psum_tensor(name, list(shape), dt)

    def _deps(self, eng, r, w):
        deps = []
        for k in r:
            t = self.last_w.get(k)
            if t is not None:
                deps.append(t)
        for k in w:
            t = self.last_w.get(k)
            if t is not None and t[0] != eng:
                deps.append(t)
            for t in self.readers.get(k, ()):
                if t[0] != eng:
                    deps.append(t)
        if eng == "pe":
            deps = [t for t in deps if t[0] != "pe"]
        return deps

    def _emit_waits(self, eng, deps):
        wd = self.waited[eng]
        for (s, v) in deps:
            if wd.get(s, 0) < v:
                wd[s] = v
                self.lists[eng].append(("wait", s, v))

    def _record(self, tok, r, w):
        for k in r:
            self.readers.setdefault(k, []).append(tok)
        for k in w:
            self.last_w[k] = tok
            self.readers[k] = []

    def op(self, eng, fn, r=(), w=()):
        self._emit_waits(eng, self._deps(eng, r, w))
        self.count[eng] += 1
        tok = (eng, self.count[eng])
        self.lists[eng].append(("op", fn))
        self._record(tok, r, w)
        return tok

    def dma(self, q, out, in_, r=(), w=()):
        i = self.dnext
        self.dnext = (self.dnext + 1) % NDMASEM
        deps = self._deps(("d", i), r, w)
        if self.dval[i] > 0:
            deps.append((("d", i), self.dval[i]))
        self._emit_waits(q, deps)
        self.dval[i] += 16
        tok = (("d", i), self.dval[i])
        self.lists[q].append(("dma", out, in_, i))
        self._record(tok, r, w)
        return tok

    def _semh(self, s):
        if isinstance(s, tuple):
            return self.dsem[s[1]]
        return self.sem[s]

    def finish(self):
        nc = self.nc
        for i in range(NDMASEM):
            if self.dval[i] > 0:
                self._emit_waits("sp", [(("d", i), self.dval[i])])
        for e in ["pe", "act", "dve", "pool"]:
            if self.count[e] > 0:
                self._emit_waits("sp", [(e, self.count[e])])

        def replay(ename):
            def f(eng):
                for it in self.lists[ename]:
                    if it[0] == "wait":
                        eng.wait_ge(self._semh(it[1]), it[2])
                    elif it[0] == "op":
                        it[1](eng).then_inc(self.sem[ename], 1)
                    elif it[0] == "cc":
                        eng.collective_compute(it[1], ALU.bypass, replica_groups=[list(range(NCORES))],
                                               ins=[it[3]], outs=[it[2]]).then_inc(self.dsem[it[4]], CC_INC)
                    else:
                        eng.dma_start(out=it[1], in_=it[2]).then_inc(self.dsem[it[3]], 16)
            return f

        with nc.Block() as block:
            block.tensor(replay("pe"))
            block.scalar(replay("act"))
            block.vector(replay("dve"))
            block.gpsimd(replay("pool"))
            block.sync(replay("sp"))
        return nc


MODC = 3 * D // NCORES


def build_p0():
    P = Prog()
    c_in = P.dram("c2", [2, D], F32, "ExternalInput")
    wmod = P.dram("wmod", [2, D, MODC], F32, "ExternalInput")
    bmod = P.dram("bmod", [2, MODC], F32, "ExternalInput")
    out = P.dram("mod", [2, 2, MODC], F32, "ExternalOutput")

    craw = P.sb("craw", [128, 2, KC], F32)
    cs = P.sb("cs", [128, KC, 2], F32)
    bsb = P.sb("bsb", [2, 2, MODC], F32)
    res = P.sb("res", [2, 2, MODC], F32)
    wbuf = [P.sb("wbuf%d" % i, [128, 8, MODC], F32) for i in range(2)]
    pst = [P.ps("pst%d" % i, [2, 512], F32) for i in range(3)]

    P.dma("sp", craw[:, :, :], c_in.rearrange("j (p k) -> p j k", k=KC), w=["craw"])
    for l in range(2):
        for j in range(2):
            P.dma("sp", bsb[j:j + 1, l, :], bmod[l:l + 1, :], w=["bsb"])
    for j in range(2):
        P.op("act", lambda e, j=j: e.activation(out=cs[:, :, j], in_=craw[:, j, :], func=AF.Silu),
             r=["craw"], w=["cs"])
    it = 0
    for l in range(2):
        for kg in range(4):
            b = it % 2
            it += 1
            P.dma("sp", wbuf[b][:, :, :],
                  wmod[l].rearrange("(p k) n -> p k n", k=KC)[:, kg * 8:(kg + 1) * 8, :],
                  w=["wbuf%d" % b])
            for kk in range(8):
                k = kg * 8 + kk
                for n in range(3):
                    P.op("pe", lambda e, b=b, kk=kk, n=n, k=k: e.matmul(
                        pst[n][:, :], lhsT=cs[:, k, :], rhs=wbuf[b][:, kk, n * 512:(n + 1) * 512],
                        start=(k == 0), stop=(k == KC - 1)),
                        r=["cs", "wbuf%d" % b], w=["pst%d" % n])
        for n in range(3):
            P.op("dve", lambda e, n=n, l=l: e.tensor_tensor(
                out=res[:, l, n * 512:(n + 1) * 512], in0=pst[n][:, :], in1=bsb[:, l, n * 512:(n + 1) * 512],
                op=ALU.add), r=["pst%d" % n, "bsb"], w=["res"])
    P.dma("sp", out.rearrange("l j n -> j l n"), res[:, :, :], r=["res"])
    return P.finish()


def run_p0(inp):
    nc = build_p0()
    c2 = np.stack([inp["c"][0], inp["c_ctx"]]).astype(np.float32)
    in_maps = []
    for r in range(NCORES):
        sl = slice(r * MODC, (r + 1) * MODC)
        in_maps.append({"c2": c2,
                        "wmod": np.ascontiguousarray(inp["w_mod"][:, :, sl]),
                        "bmod": np.ascontiguousarray(inp["b_mod"][:, sl])})
    res = run_bass_kernel_spmd(nc, in_maps, core_ids=list(range(NCORES)))
    return np.concatenate([r["mod"] for r in res.results], axis=2)


def _prog_collective(self, kind, out, in_, r=(), w=()):
    i = self.dnext
    self.dnext = (self.dnext + 1) % NDMASEM
    deps = self._deps(("d", i), r, w)
    if self.dval[i] > 0:
        deps.append((("d", i), self.dval[i]))
    self._emit_waits("pool", deps)
    self.dval[i] += 16
    tok = (("d", i), self.dval[i])
    self.lists["pool"].append(("cc", kind, out, in_, i))
    self._record(tok, r, w)
    return tok


Prog.collective = _prog_collective


TTILES = [(0, 512, 0), (512, 1024, 0), (1024, 1056, 1)]


def vec_layout(v):
    return np.ascontiguousarray(np.asarray(v, np.float32).reshape(KC, 128).T)


def emit_norm_mod(P, xT, hT, gvec, modv, out_dt, pfx="n"):
    ones = P.sb(pfx + "ones", [128, 128], F32)
    g_sb = P.sb(pfx + "g", [128, KC], F32)
    m_sb = P.sb(pfx + "m", [128, 2, 3, KC], F32)
    G = P.sb(pfx + "G", [128, 2, KC], F32)
    xb = [P.sb(pfx + "xb%d" % i, [128, KC, 512], F32) for i in range(2)]
    sq = [P.sb(pfx + "sq%d" % i, [128, 512], F32) for i in range(3)]
    tmp = [P.sb(pfx + "tmp%d" % i, [128, 512], F32) for i in range(3)]
    ho = [P.sb(pfx + "ho%d" % i, [128, 512], out_dt) for i in range(3)]
    rt = P.sb(pfx + "rt", [128, 512], F32)
    rstd = P.sb(pfx + "rstd", [128, 512], F32)
    pss = P.ps(pfx + "pss", [128, 512], F32)

    P.op("pool", lambda e: e.memset(ones[:, :], 1.0), w=[pfx + "ones"])
    P.dma("sp", g_sb[:, :], gvec, w=[pfx + "g"])
    P.dma("sp", m_sb[:, :, :, :], modv, w=[pfx + "m"])
    for j in range(2):
        P.op("dve", lambda e, j=j: e.scalar_tensor_tensor(
            out=G[:, j, :], in0=m_sb[:, j, 1, :], scalar=1.0, in1=g_sb[:, :], op0=ALU.add, op1=ALU.mult),
            r=[pfx + "m", pfx + "g"], w=[pfx + "G"])
    xv = xT.rearrange("(k p) t -> p k t", p=128)
    n = 0
    for ti, (t0, t1, j) in enumerate(TTILES):
        w_ = t1 - t0
        b = ti % 2
        xk = pfx + "xb%d" % b
        P.dma("sp", xb[b][:, :, 0:w_], xv[:, :, t0:t1], w=[xk])
        for kc in range(KC):
            s = kc % 3
            P.op("act", lambda e, b=b, kc=kc, s=s, w_=w_: e.activation(
                out=sq[s][:, 0:w_], in_=xb[b][:, kc, 0:w_], func=AF.Square), r=[xk], w=[pfx + "sq%d" % s])
            P.op("pe", lambda e, kc=kc, s=s, w_=w_: e.matmul(
                pss[:, 0:w_], lhsT=ones[:, :], rhs=sq[s][:, 0:w_], start=(kc == 0), stop=(kc == KC - 1)),
                r=[pfx + "ones", pfx + "sq%d" % s], w=[pfx + "pss"])
        P.op("act", lambda e, w_=w_: e.activation(out=rt[:, 0:w_], in_=pss[:, 0:w_], func=AF.Sqrt,
                                                   bias=EPS, scale=1.0 / D), r=[pfx + "pss"], w=[pfx + "rt"])
        P.op("dve", lambda e, w_=w_: e.reciprocal(out=rstd[:, 0:w_], in_=rt[:, 0:w_]), r=[pfx + "rt"], w=[pfx + "rstd"])
        for kc in range(KC):
            s = n % 3
            n += 1
            P.op("dve", lambda e, b=b, kc=kc, s=s, w_=w_, j=j: e.scalar_tensor_tensor(
                out=tmp[s][:, 0:w_], in0=xb[b][:, kc, 0:w_], scalar=G[:, j, kc:kc + 1], in1=rstd[:, 0:w_],
                op0=ALU.mult, op1=ALU.mult), r=[xk, pfx + "G", pfx + "rstd"], w=[pfx + "tmp%d" % s])
            P.op("act", lambda e, kc=kc, s=s, w_=w_, j=j: e.activation(
                out=ho[s][:, 0:w_], in_=tmp[s][:, 0:w_], func=AF.Identity, bias=m_sb[:, j, 0, kc:kc + 1], scale=1.0),
                r=[pfx + "tmp%d" % s, pfx + "m"], w=[pfx + "ho%d" % s])
            P.dma("sp", hT[kc * 128:(kc + 1) * 128, t0:t1], ho[s][:, 0:w_], r=[pfx + "ho%d" % s])


def build_p1(out_dt):
    P = Prog()
    xT = P.dram("xT", [D, TOK], F32, "ExternalInput")
    gvec = P.dram("gvec", [128, KC], F32, "ExternalInput")
    modv = P.dram("modv", [128, 2, 3, KC], F32, "ExternalInput")
    hT = P.dram("hT", [D, TOK], out_dt, "ExternalOutput")
    emit_norm_mod(P, xT, hT, gvec, modv, out_dt)
    return P.finish()


def mod_layout(mod_l):
    m = np.asarray(mod_l, np.float32).reshape(2, 3, KC, 128)
    return np.ascontiguousarray(m.transpose(3, 0, 1, 2))


def shard_tokens_T(x2d, c2d):
    outs = []
    for r in range(NCORES):
        t = np.concatenate([x2d[r * TOK_X:(r + 1) * TOK_X], c2d[r * TOK_C:(r + 1) * TOK_C]], axis=0)
        outs.append(np.ascontiguousarray(t.T))
    return outs


def unshard_tokens_T(per_core):
    xs = np.concatenate([p[:, :TOK_X].T for p in per_core], axis=0)
    cs = np.concatenate([p[:, TOK_X:].T for p in per_core], axis=0)
    return xs, cs


_NC_CACHE = {}


def get_nc(key, builder):
    if key not in _NC_CACHE:
        _NC_CACHE[key] = builder()
    return _NC_CACHE[key]


def run_p1(xT_list, gvec, mod_l, final=False):
    nc = get_nc(("p1", final), lambda: build_p1(F32 if final else BF16))
    g = vec_layout(gvec)
    m = mod_layout(mod_l)
    in_maps = [{"xT": xT_list[r], "gvec": g, "modv": m} for r in range(NCORES)]
    res = run_bass_kernel_spmd(nc, in_maps, core_ids=list(range(NCORES)))
    return [r["hT"] for r in res.results]


NFM = 28
NTM = 528
FM_GROUPS = [(0, 7), (7, 14), (14, 21), (21, 28)]
A_TILES = [(0, 256)] + [(256 + 512 * i, 256 + 512 * (i + 1)) for i in range(16)]


def emit_gemm(P, hT_all, w_fm, w_tm, fm_dst, tm_vz, tm_dt):
    banks = P.get_banks()
    m0 = P.mark()
    hv = hT_all.rearrange("(k p) t -> p k t", p=128)
    wfv = w_fm.rearrange("(k p) n -> p k n", p=128)
    wtv = w_tm.rearrange("(k p) n -> p k n", p=128)
    GC = 4
    NG = NFM // GC
    wres = [P.sb("wres%d" % i, [128, KC, GC * 128], BF16) for i in range(2)]
    wtm = P.sb("wtm", [128, KC, NTM], BF16)
    stg = [P.sb("stg%d" % i, [128, 4, NTM], F32) for i in range(2)]
    hb = [P.sb("hb%d" % i, [128, KC, 512], BF16) for i in range(2)]
    ob = [P.sb("ob%d" % i, [128, 512], BF16) for i in range(4)]
    otv = [P.sb("otv%d" % i, [128, 512], BF16) for i in range(2)]
    otd = [P.sb("otd%d" % i, [128, 16], F32) for i in range(2)]
    st = {"stg": 0, "ob": 0, "hb": 0, "bank": 0, "tm": 0}

    def stage_tm(q):
        b = st["stg"] % 2
        st["stg"] += 1
        P.dma("sp", stg[b][:, :, 0:NTM], wtv[:, q * 4:(q + 1) * 4, :], w=["stg%d" % b])
        P.op("pool" if q % 2 else "dve", lambda e: e.tensor_copy(
            out=wtm[:, q * 4:(q + 1) * 4, :], in_=stg[b][:, :, 0:NTM]), r=["stg%d" % b], w=["wtm"])

    def stage_fm(gi, q):
        b = st["stg"] % 2
        st["stg"] += 1
        wi = gi % 2
        c0 = gi * GC
        P.dma("sp", stg[b][:, :, 0:GC * 128], wfv[:, q * 4:(q + 1) * 4, c0 * 128:(c0 + GC) * 128], w=["stg%d" % b])
        P.op("pool" if q % 2 else "dve", lambda e: e.tensor_copy(
            out=wres[wi][:, q * 4:(q + 1) * 4, :], in_=stg[b][:, :, 0:GC * 128]), r=["stg%d" % b], w=["wres%d" % wi])

    for q in range(8):
        stage_tm(q)
    for q in range(8):
        stage_fm(0, q)
    for gi in range(NG):
        wi = gi % 2
        c0 = gi * GC
        for ti, (t0, t1) in enumerate(A_TILES):
            tw = t1 - t0
            hbi = st["hb"] % 2
            st["hb"] += 1
            hk = "hb%d" % hbi
            P.dma("sp", hb[hbi][:, :, 0:tw], hv[:, :, t0:t1], w=[hk])
            if gi + 1 < NG and 1 <= ti <= 8:
                stage_fm(gi + 1, ti - 1)
            for cj in range(GC):
                bk = st["bank"] % 4
                st["bank"] += 1
                for kc in range(KC):
                    P.op("pe", lambda e, bk=bk, kc=kc, cj=cj, hbi=hbi, tw=tw, wi=wi: e.matmul(
                        banks[bk][:, 0:tw], lhsT=wres[wi][:, kc, cj * 128:(cj + 1) * 128], rhs=hb[hbi][:, kc, 0:tw],
                        start=(kc == 0), stop=(kc == KC - 1)), r=["wres%d" % wi, hk], w=["bank%d" % bk])
                o = st["ob"] % 4
                st["ob"] += 1
                if st["ob"] % 2:
                    P.op("act", lambda e, o=o, bk=bk, tw=tw: e.copy(out=ob[o][:, 0:tw], in_=banks[bk][:, 0:tw]),
                         r=["bank%d" % bk], w=["ob%d" % o])
                else:
                    P.op("dve", lambda e, o=o, bk=bk, tw=tw: e.tensor_copy(out=ob[o][:, 0:tw], in_=banks[bk][:, 0:tw]),
                         r=["bank%d" % bk], w=["ob%d" % o])
                P.dma("pool", fm_dst(c0 + cj)[:, t0:t1], ob[o][:, 0:tw], r=["ob%d" % o])
            if gi == 0:
                for s0 in range(0, tw, 128):
                    for kc in range(KC):
                        P.op("pe", lambda e, kc=kc, hbi=hbi, s0=s0: e.matmul(
                            banks[4][:, :], lhsT=hb[hbi][:, kc, s0:s0 + 128], rhs=wtm[:, kc, 0:512],
                            start=(kc == 0), stop=(kc == KC - 1)), r=["wtm", hk], w=["bank4"])
                        P.op("pe", lambda e, kc=kc, hbi=hbi, s0=s0: e.matmul(
                            banks[5][:, 0:16], lhsT=hb[hbi][:, kc, s0:s0 + 128], rhs=wtm[:, kc, 512:528],
                            start=(kc == 0), stop=(kc == KC - 1)), r=["wtm", hk], w=["bank5"])
                    o = st["tm"] % 2
                    st["tm"] += 1
                    P.op("act", lambda e, o=o: e.copy(out=otv[o][:, :], in_=banks[4][:, :]), r=["bank4"], w=["otv%d" % o])
                    P.op("dve", lambda e, o=o: e.tensor_copy(out=otd[o][:, :], in_=banks[5][:, 0:16]),
                         r=["bank5"], w=["otd%d" % o])
                    P.dma("pool", tm_vz[t0 + s0:t0 + s0 + 128, :], otv[o][:, :], r=["otv%d" % o])
                    P.dma("pool", tm_dt[t0 + s0:t0 + s0 + 128, :], otd[o][:, :], r=["otd%d" % o])
    P.release(m0)


def wcols_for_core(w_in_l, g):
    cols = []
    cols += list(range(512 * g, 512 * g + 512))
    cols += list(range(4096 + 128 * g, 4096 + 128 * g + 128))
    cols += list(range(5120 + 128 * g, 5120 + 128 * g + 128))
    cols += list(range(6144 + 512 * g, 6144 + 512 * g + 512))
    cols += list(range(10368 + 256 * g, 10368 + 256 * g + 256))
    cols += list(range(12416 + 256 * g, 12416 + 256 * g + 256))
    cols += list(range(14464 + 256 * g, 14464 + 256 * g + 256))
    for j in range(3):
        cols += list(range(20608 + 4096 * j + 512 * g, 20608 + 4096 * j + 512 * g + 512))
    tcols = []
    tcols += list(range(16512 + 256 * g, 16512 + 256 * g + 256))
    tcols += list(range(18560 + 256 * g, 18560 + 256 * g + 256))
    tcols += list(range(10240 + 8 * g, 10240 + 8 * g + 8))
    tcols += list(range(10304 + 8 * g, 10304 + 8 * g + 8))
    return np.ascontiguousarray(w_in_l[:, cols]), np.ascontiguousarray(w_in_l[:, tcols])


HT = 528
P3_TILES = [(0, 512, 0), (512, 528, 1)]
CFW = 2048


def emit_p3(P, yaT, vbT, zbT, ycT, gT, xT, wb, wo, sng, lng, lnb, modv, xnT):
    banks = P.get_banks()
    m0 = P.mark()
    ones_f = P.sb("ones_f", [128, 128], F32)
    ones_b = P.sb("ones_b", [128, 128], BF16)
    sng_sb = P.sb("sng", [128, 32], F32)
    lng_sb = P.sb("lng", [128, 16], F32)
    lnb_sb = P.sb("lnb", [128, 16], F32)
    m_sb = P.sb("m3", [128, 2, 3, KC], F32)
    P.op("pool", lambda e: e.memset(ones_f[:, :], 1.0), w=["ones_f"])
    P.op("pool", lambda e: e.memset(ones_b[:, :], 1.0), w=["ones_b"])
    P.dma("sp", sng_sb[:, :], sng, w=["sng"])
    P.dma("sp", lng_sb[:, :], lng, w=["lng"])
    P.dma("sp", lnb_sb[:, :], lnb, w=["lnb"])
    P.dma("sp", m_sb[:, :, :, :], modv, w=["m3"])
    ya = P.sb("ya", [128, 32, HT], BF16)
    vb = P.sb("vb", [128, 16, HT], BF16)
    zb = P.sb("zb", [128, 16, HT], BF16)
    yc = P.sb("yc", [128, 16, HT], BF16)
    mg = P.sb("mg", [128, 32, HT], BF16)
    wst = [P.sb("wst%d" % i, [128, 16, 128], F32) for i in range(2)]
    wbf = [P.sb("wbf%d" % i, [128, 64, 128], BF16) for i in range(2)]
    gsb = [P.sb("gsb%d" % i, [128, 3, HT], BF16) for i in range(2)]
    sg = [P.sb("sg%d" % i, [128, 3, HT], BF16) for i in range(2)]
    t1 = [P.sb("t1_%d" % i, [128, HT], F32) for i in range(3)]
    t2 = [P.sb("t2_%d" % i, [128, HT], F32) for i in range(3)]
    stat = P.sb("stat", [128, HT], F32)
    rstd = P.sb("rstd3", [128, HT], F32)
    mean = P.sb("mean3", [128, HT], F32)
    xin = [P.sb("xin%d" % i, [128, HT], F32) for i in range(2)]
    xo = [P.sb("xo%d" % i, [128, HT], F32) for i in range(2)]
    cnt = {"w": 0, "t": 0, "x": 0, "cast": 0}

    def stat_reduce(src_fn, nchunks, use_f32, key_r):
        for kc in range(nchunks):
            rhs_ap, rkeys = src_fn(kc)
            for (a, b_, j) in P3_TILES:
                bk = 0 if j == 0 else 1
                P.op("pe", lambda e, kc=kc, a=a, b_=b_, bk=bk, rhs_ap=rhs_ap: e.matmul(
                    banks[bk][:, 0:b_ - a], lhsT=(ones_f if use_f32 else ones_b)[:, :], rhs=rhs_ap[:, a:b_],
                    start=(kc == 0), stop=(kc == nchunks - 1)),
                    r=["ones_f", "ones_b"] + rkeys, w=["bank%d" % bk])

    def bank_to(dst, scale, bias, func):
        for (a, b_, j) in P3_TILES:
            bk = 0 if j == 0 else 1
            P.op("act", lambda e, a=a, b_=b_, bk=bk: e.activation(
                out=dst[:, a:b_], in_=banks[bk][:, 0:b_ - a], func=func, bias=bias, scale=scale),
                r=["bank%d" % bk], w=[dst.name])

    def load_weights(src4, nch, blk):
        wi = cnt["w"] % 2
        cnt["w"] += 1
        for q in range(nch // 16):
            si = cnt["cast"] % 2
            cnt["cast"] += 1
            P.dma("sp", wst[si][:, :, :], src4[blk, :, q * 16:(q + 1) * 16, :], w=["wst%d" % si])
            P.op("act", lambda e, si=si, wi=wi, q=q: e.copy(
                out=wbf[wi][:, q * 16:(q + 1) * 16, :], in_=wst[si][:, :, :]), r=["wst%d" % si], w=["wbf%d" % wi])
        return wi

    for hf in range(2):
        P.dma("sp", ya[:, :, :], yaT[hf].rearrange("(k p) t -> p k t", p=128), w=["ya"])
        P.dma("sp", vb[:, :, :], vbT[hf].rearrange("(k p) t -> p k t", p=128), w=["vb"])
        P.dma("sp", zb[:, :, :], zbT[hf].rearrange("(k p) t -> p k t", p=128), w=["zb"])
        P.dma("sp", yc[:, :, :], ycT[hf].rearrange("(k p) t -> p k t", p=128), w=["yc"])
        def sq_a(kc):
            i = cnt["t"] % 3
            cnt["t"] += 1
            P.op("act", lambda e, kc=kc, i=i: e.activation(out=t1[i][:, :], in_=ya[:, kc, :], func=AF.Square),
                 r=["ya"], w=[t1[i].name])
            return t1[i], [t1[i].name]
        stat_reduce(sq_a, 32, True, None)
        bank_to(stat, 1.0 / D, EPS, AF.Sqrt)
        P.op("dve", lambda e: e.reciprocal(out=rstd[:, :], in_=stat[:, :]), r=[stat.name], w=[rstd.name])
        for kc in range(32):
            P.op("dve", lambda e, kc=kc: e.scalar_tensor_tensor(
                out=ya[:, kc, :], in0=ya[:, kc, :], scalar=sng_sb[:, kc:kc + 1], in1=rstd[:, :],
                op0=ALU.mult, op1=ALU.mult), r=["ya", "sng", rstd.name], w=["ya"])
        stat_reduce(lambda kc: (vb[:, kc, :], ["vb"]), 16, False, None)
        bank_to(mean, 1.0 / CFW, 0.0, AF.Identity)

        def sq_b(kc):
            i = cnt["t"] % 3
            cnt["t"] += 1
            P.op("dve", lambda e, kc=kc, i=i: e.tensor_tensor(out=t2[i][:, :], in0=vb[:, kc, :], in1=mean[:, :],
                                                               op=ALU.subtract), r=["vb", mean.name], w=[t2[i].name])
            P.op("act", lambda e, i=i: e.activation(out=t1[i][:, :], in_=t2[i][:, :], func=AF.Square),
                 r=[t2[i].name], w=[t1[i].name])
            return t1[i], [t1[i].name]
        stat_reduce(sq_b, 16, True, None)
        bank_to(stat, 1.0 / CFW, EPS, AF.Sqrt)
        P.op("dve", lambda e: e.reciprocal(out=rstd[:, :], in_=stat[:, :]), r=[stat.name], w=[rstd.name])
        for kc in range(16):
            i = cnt["t"] % 3
            cnt["t"] += 1
            P.op("dve", lambda e, kc=kc, i=i: e.tensor_tensor(out=t2[i][:, :], in0=vb[:, kc, :], in1=mean[:, :],
                                                               op=ALU.subtract), r=["vb", mean.name], w=[t2[i].name])
            P.op("dve", lambda e, i=i: e.tensor_tensor(out=t2[i][:, :], in0=t2[i][:, :], in1=rstd[:, :], op=ALU.mult),
                 r=[t2[i].name, rstd.name], w=[t2[i].name])
            P.op("act", lambda e, kc=kc, i=i: e.activation(out=t2[i][:, :], in_=t2[i][:, :], func=AF.Silu,
                                                            bias=lnb_sb[:, kc:kc + 1], scale=lng_sb[:, kc:kc + 1]),
                 r=[t2[i].name, "lng", "lnb"], w=[t2[i].name])
            P.op("act", lambda e, kc=kc, i=i: e.activation(out=t1[i][:, :], in_=zb[:, kc, :], func=AF.Silu),
                 r=["zb"], w=[t1[i].name])
            P.op("dve", lambda e, kc=kc, i=i: e.tensor_tensor(out=vb[:, kc, :], in0=t2[i][:, :], in1=t1[i][:, :],
                                                               op=ALU.mult), r=[t1[i].name, t2[i].name], w=["vb"])
        gv = gT[hf].rearrange("(j n p) t -> n p j t", j=3, p=128)
        for n in range(32):
            wi = load_weights(wb, 64, n)
            gi = n % 2
            P.dma("sp", gsb[gi][:, :, :], gv[n], w=[gsb[gi].name])
            P.op("act", lambda e, gi=gi: e.activation(out=sg[gi][:, :, :], in_=gsb[gi][:, :, :], func=AF.Sigmoid),
                 r=[gsb[gi].name], w=[sg[gi].name])
            bo = 4 * (n % 2)
            srcs = [(ya, "ya", 0, 32), (vb, "vb", 32, 16), (yc, "yc", 48, 16)]
            for bi, (src, sk, c0, ncc) in enumerate(srcs):
                for kc in range(ncc):
                    for (a, b_, j) in P3_TILES:
                        if j == 0:
                            dst = banks[bo + bi][:, 0:512]
                            bk = bo + bi
                        else:
                            dst = banks[bo + 3][:, 16 * bi:16 * bi + 16]
                            bk = bo + 3
                        P.op("pe", lambda e, dst=dst, wi=wi, c0=c0, kc=kc, src=src, a=a, b_=b_, ncc=ncc: e.matmul(
                            dst, lhsT=wbf[wi][:, c0 + kc, :], rhs=src[:, kc, a:b_],
                            start=(kc == 0), stop=(kc == ncc - 1)),
                            r=["wbf%d" % wi, sk], w=["bank%d" % bk])
            i = cnt["t"] % 3
            cnt["t"] += 1
            for (a, b_, j) in P3_TILES:
                def pv(bi):
                    if j == 0:
                        return banks[bo + bi][:, 0:512], "bank%d" % (bo + bi)
                    return banks[bo + 3][:, 16 * bi:16 * bi + 16], "bank%d" % (bo + 3)
                pa, ka = pv(0)
                pb, kb = pv(1)
                pc, kc_ = pv(2)
                P.op("dve", lambda e, pa=pa, gi=gi, i=i, a=a, b_=b_: e.tensor_tensor(
                    out=t1[i][:, a:b_], in0=pa, in1=sg[gi][:, 0, a:b_], op=ALU.mult),
                    r=[ka, sg[gi].name], w=[t1[i].name])
                P.op("dve", lambda e, pb=pb, gi=gi, i=i, a=a, b_=b_: e.tensor_tensor(
                    out=t2[i][:, a:b_], in0=pb, in1=sg[gi][:, 1, a:b_], op=ALU.mult),
                    r=[kb, sg[gi].name], w=[t2[i].name])
                P.op("dve", lambda e, i=i, a=a, b_=b_: e.tensor_tensor(
                    out=t1[i][:, a:b_], in0=t1[i][:, a:b_], in1=t2[i][:, a:b_], op=ALU.add),
                    r=[t1[i].name, t2[i].name], w=[t1[i].name])
                P.op("dve", lambda e, pc=pc, gi=gi, i=i, a=a, b_=b_: e.tensor_tensor(
                    out=t2[i][:, a:b_], in0=pc, in1=sg[gi][:, 2, a:b_], op=ALU.mult),
                    r=[kc_, sg[gi].name], w=[t2[i].name])
                P.op("dve", lambda e, i=i, a=a, b_=b_, n=n: e.tensor_tensor(
                    out=mg[:, n, a:b_], in0=t1[i][:, a:b_], in1=t2[i][:, a:b_], op=ALU.add),
                    r=[t1[i].name, t2[i].name], w=["mg"])
        xv = xT[hf].rearrange("(k p) t -> k p t", p=128)
        ov = xnT[hf].rearrange("(k p) t -> k p t", p=128)
        for m in range(32):
            wi = load_weights(wo, 32, m)
            xi = m % 2
            P.dma("sp", xin[xi][:, :], xv[m], w=[xin[xi].name])
            bo = 4 * (m % 2)
            for kc in range(32):
                for (a, b_, j) in P3_TILES:
                    bk = bo + (0 if j == 0 else 1)
                    P.op("pe", lambda e, bk=bk, wi=wi, kc=kc, a=a, b_=b_: e.matmul(
                        banks[bk][:, 0:b_ - a], lhsT=wbf[wi][:, kc, :], rhs=mg[:, kc, a:b_],
                        start=(kc == 0), stop=(kc == 31)), r=["wbf%d" % wi, "mg"], w=["bank%d" % bk])
            for (a, b_, j) in P3_TILES:
                bk = bo + (0 if j == 0 else 1)
                P.op("dve", lambda e, bk=bk, xi=xi, a=a, b_=b_, j=j, m=m: e.scalar_tensor_tensor(
                    out=xo[xi][:, a:b_], in0=banks[bk][:, 0:b_ - a], scalar=m_sb[:, j, 2, m:m + 1],
                    in1=xin[xi][:, a:b_], op0=ALU.mult, op1=ALU.add),
                    r=["bank%d" % bk, xin[xi].name, "m3"], w=[xo[xi].name])
            P.dma("pool", ov[m], xo[xi][:, :], r=[xo[xi].name])
    P.release(m0)


def build_p3():
    P = Prog()
    yaT = P.dram("yaT", [2, D, HT], BF16, "ExternalInput")
    vbT = P.dram("vbT", [2, CFW, HT], BF16, "ExternalInput")
    zbT = P.dram("zbT", [2, CFW, HT], BF16, "ExternalInput")
    ycT = P.dram("ycT", [2, CFW, HT], BF16, "ExternalInput")
    gT = P.dram("gT", [2, 3 * D, HT], BF16, "ExternalInput")
    xT = P.dram("xT", [2, D, HT], F32, "ExternalInput")
    wb = P.dram("wb", [32, 128, 64, 128], F32, "ExternalInput")
    wo = P.dram("wo", [32, 128, 32, 128], F32, "ExternalInput")
    sng = P.dram("sng", [128, 32], F32, "ExternalInput")
    lng = P.dram("lng", [128, 16], F32, "ExternalInput")
    lnb = P.dram("lnb", [128, 16], F32, "ExternalInput")
    modv = P.dram("modv", [128, 2, 3, KC], F32, "ExternalInput")
    xnT = P.dram("xnT", [2, D, HT], F32, "ExternalOutput")
    emit_p3(P, yaT, vbT, zbT, ycT, gT, xT, wb, wo, sng, lng, lnb, modv, xnT)
    return P.finish()


def w_blocks(w):
    K_, N_ = w.shape
    return np.ascontiguousarray(w.reshape(K_ // 128, 128, N_ // 128, 128).transpose(2, 1, 0, 3))


def halves_T(x2d, c2d, r, dt=None):
    out = []
    for h in range(2):
        t = np.concatenate([x2d[r * TOK_X + 512 * h: r * TOK_X + 512 * (h + 1)],
                            c2d[r * TOK_C + 16 * h: r * TOK_C + 16 * (h + 1)]], axis=0)
        out.append(t.T)
    a = np.ascontiguousarray(np.stack(out))
    return a if dt is None else a.astype(dt)


NROW = SEQ // 64


def emit_conformer(P, fm_src, cfw, cfb, vb_out):
    m0 = P.mark()
    w_sb = P.sb("cfw", [128, 2, 31], F32)
    b_sb = P.sb("cfb", [128, 2], F32)
    a_sb = P.sb("cfa", [128, NTOK], BF16)
    g_sb = P.sb("cfg", [128, NTOK], BF16)
    sgm = P.sb("cfs", [128, NTOK], F32)
    upx = P.sb("upx", [128, NROW, 94], F32)
    upc = P.sb("upc", [128, CTX + 30], F32)
    acc = P.sb("cfacc", [128, NTOK], F32)
    ob = P.sb("cfob", [128, NTOK], BF16)
    P.dma("sp", w_sb[:, :, :], cfw, w=["cfw"])
    P.dma("sp", b_sb[:, :], cfb, w=["cfb"])
    P.op("pool", lambda e: e.memset(upx[:, :, :], 0.0), w=["upx"])
    P.op("pool", lambda e: e.memset(upc[:, :], 0.0), w=["upc"])
    for cc in range(2):
        P.dma("sp", a_sb[:, :], fm_src(10 + cc), w=["cfa"])
        P.dma("sp", g_sb[:, :], fm_src(12 + cc), w=["cfg"])
        P.op("act", lambda e: e.activation(out=sgm[:, :], in_=g_sb[:, :], func=AF.Sigmoid), r=["cfg"], w=["cfs"])
        P.op("dve", lambda e: e.tensor_tensor(out=upc[:, 15:15 + CTX], in0=a_sb[:, 0:CTX], in1=sgm[:, 0:CTX], op=ALU.mult),
             r=["cfa", "cfs"], w=["upc"])
        P.op("dve", lambda e: e.tensor_tensor(
            out=upx[:, :, 15:79], in0=a_sb[:, CTX:].rearrange("p (r t) -> p r t", t=64),
            in1=sgm[:, CTX:].rearrange("p (r t) -> p r t", t=64), op=ALU.mult), r=["cfa", "cfs"], w=["upx"])
        accx = acc[:, CTX:].rearrange("p (r t) -> p r t", t=64)
        for k in range(31):
            if k == 0:
                P.op("dve", lambda e, cc=cc: e.tensor_scalar(
                    out=acc[:, 0:CTX], in0=upc[:, 0:CTX], scalar1=w_sb[:, cc, 0:1], scalar2=b_sb[:, cc:cc + 1],
                    op0=ALU.mult, op1=ALU.add), r=["upc", "cfw", "cfb"], w=["cfacc"])
                P.op("dve", lambda e, cc=cc: e.tensor_scalar(
                    out=accx, in0=upx[:, :, 0:64], scalar1=w_sb[:, cc, 0:1], scalar2=b_sb[:, cc:cc + 1],
                    op0=ALU.mult, op1=ALU.add), r=["upx", "cfw", "cfb"], w=["cfacc"])
            else:
                P.op("dve", lambda e, cc=cc, k=k: e.scalar_tensor_tensor(
                    out=acc[:, 0:CTX], in0=upc[:, k:k + CTX], scalar=w_sb[:, cc, k:k + 1], in1=acc[:, 0:CTX],
                    op0=ALU.mult, op1=ALU.add), r=["upc", "cfw", "cfacc"], w=["cfacc"])
                P.op("dve", lambda e, cc=cc, k=k: e.scalar_tensor_tensor(
                    out=accx, in0=upx[:, :, k:k + 64], scalar=w_sb[:, cc, k:k + 1], in1=accx,
                    op0=ALU.mult, op1=ALU.add), r=["upx", "cfw", "cfacc"], w=["cfacc"])
        P.op("act", lambda e: e.copy(out=ob[:, :], in_=acc[:, :]), r=["cfacc"], w=["cfob"])
        P.dma("pool", vb_out[cc * 128:(cc + 1) * 128, :], ob[:, :], r=["cfob"])
    P.release(m0)


def emit_trig_table(P, dst, nrows, ncols, N, row_base, kind, name, scale=1.0):
    ip = P.sb(name + "_ip", [128, ncols], I32)
    ij = P.sb(name + "_ij", [128, ncols], I32)
    fp = P.sb(name + "_fp", [128, ncols], F32)
    fj = P.sb(name + "_fj", [128, ncols], F32)
    P.op("pool", lambda e: e.iota(ip[:, :], pattern=[[0, ncols]], base=row_base, channel_multiplier=1), w=[ip.name])
    P.op("pool", lambda e: e.iota(ij[:, :], pattern=[[1, ncols]], base=0, channel_multiplier=0), w=[ij.name])
    P.op("dve", lambda e: e.tensor_tensor(out=ip[:, :], in0=ip[:, :], in1=ij[:, :], op=ALU.mult),
         r=[ip.name, ij.name], w=[ip.name])
    off = N // 2 if kind == "sin" else 3 * N // 4
    P.op("dve", lambda e: e.tensor_scalar(out=ip[:, :], in0=ip[:, :], scalar1=float(off), scalar2=None,
                                          op0=ALU.add), r=[ip.name], w=[ip.name])
    P.op("dve", lambda e: e.tensor_scalar(out=ip[:, :], in0=ip[:, :], scalar1=int(N - 1), scalar2=None,
                                          op0=ALU.bitwise_and), r=[ip.name], w=[ip.name])
    P.op("dve", lambda e: e.tensor_scalar(out=fp[:, :], in0=ip[:, :], scalar1=float(-N / 2), scalar2=None,
                                          op0=ALU.add), r=[ip.name], w=[fp.name])
    P.op("act", lambda e: e.activation(out=fj[:, :], in_=fp[:, :], func=AF.Sin, scale=float(2 * np.pi / N)),
         r=[fp.name], w=[fj.name])
    P.op("dve", lambda e: e.tensor_scalar(out=dst, in0=fj[0:nrows, :], scalar1=float(scale), scalar2=None,
                                          op0=ALU.mult), r=[fj.name], w=[name])


def emit_fourier(P, tm_vz, t1s, yc_out):
    banks = P.get_banks()
    m0 = P.mark()
    c128 = P.sb("c128", [128, 128], BF16)
    s128 = P.sb("s128", [128, 128], BF16)
    twc = P.sb("twc", [128, 64], F32)
    tws = P.sb("tws", [128, 64], F32)
    ra = P.sb("ra", [64, 128], BF16)
    rm = P.sb("rm", [64, 128], BF16)
    c256 = P.sb("c256", [128, 2, 256], BF16)
    s256 = P.sb("s256", [128, 2, 256], BF16)
    ns256 = P.sb("ns256", [128, 2, 256], BF16)
    mt = P.mark()
    emit_trig_table(P, c128[:, :], 128, 128, 128, 0, "cos", "c128")
    emit_trig_table(P, s128[:, :], 128, 128, 128, 0, "sin", "s128")
    emit_trig_table(P, twc[:, :], 128, 64, 8192, 0, "cos", "twc")
    emit_trig_table(P, tws[:, :], 128, 64, 8192, 0, "sin", "tws")
    emit_trig_table(P, ra[:, 0:64], 64, 64, 64, 0, "cos", "ra")
    emit_trig_table(P, ra[:, 64:128], 64, 64, 64, 0, "sin", "ra")
    emit_trig_table(P, rm[:, 0:64], 64, 64, 64, 0, "sin", "rm", scale=-1.0)
    emit_trig_table(P, rm[:, 64:128], 64, 64, 64, 0, "cos", "rm")
    for cc in range(2):
        emit_trig_table(P, c256[:, cc, :], 128, 256, 256, cc * 128, "cos", "c256")
        emit_trig_table(P, s256[:, cc, :], 128, 256, 256, cc * 128, "sin", "s256")
        emit_trig_table(P, ns256[:, cc, :], 128, 256, 256, cc * 128, "sin", "ns256", scale=-1.0)
    P.release(mt)

    usb = P.sb("usb", [128, 2, 2, 64, 128], BF16)
    zt = [P.sb("zt%d" % i, [128, 256], BF16) for i in range(2)]
    zs = [P.sb("zs%d" % i, [128, 256], F32) for i in range(2)]
    yo = [P.sb("yo%d" % i, [128, 256], BF16) for i in range(2)]
    cnt = {"e": 0, "z": 0}

    def stage3(ntiles, tok_base, norm):
        for tl in range(ntiles):
            bk = 6 + (tl % 2)
            k = 0
            for cc in range(2):
                for ri in range(2):
                    rhs = (c256 if ri == 0 else ns256)[:, cc, :]
                    P.op("pe", lambda e, bk=bk, cc=cc, ri=ri, tl=tl, rhs=rhs, k=k: e.matmul(
                        banks[bk][:, 0:256], lhsT=usb[:, cc, ri, tl, :], rhs=rhs, start=(k == 0), stop=(k == 3)),
                        r=["usb", "c256", "ns256"], w=["bank%d" % bk])
                    k += 1
            zi = cnt["z"] % 2
            cnt["z"] += 1
            r0 = tok_base + tl * 128
            P.dma("sp", zt[zi][:, :], tm_vz[r0:r0 + 128, 256:512], w=[zt[zi].name])
            P.op("act", lambda e, zi=zi: e.activation(out=zs[zi][:, :], in_=zt[zi][:, :], func=AF.Silu),
                 r=[zt[zi].name], w=[zs[zi].name])
            P.op("dve", lambda e, zi=zi, bk=bk: e.scalar_tensor_tensor(
                out=yo[zi][:, :], in0=banks[bk][:, 0:256], scalar=float(norm), in1=zs[zi][:, :],
                op0=ALU.mult, op1=ALU.mult), r=["bank%d" % bk, zs[zi].name], w=[yo[zi].name])
            P.dma("pool", yc_out[r0:r0 + 128, :], yo[zi][:, :], r=[yo[zi].name])

    mc = P.mark()
    vc = P.sb("vc", [128, 2, 256], BF16)
    P.dma("sp", vc[:, :, :], tm_vz[0:CTX, 0:256].rearrange("(t p) c -> p t c", p=128), w=["vc"])
    for cc in range(2):
        for half, tab in enumerate([c256, s256]):
            bk = 2 * cc + half
            for tl in range(2):
                P.op("pe", lambda e, bk=bk, cc=cc, tl=tl, tab=tab: e.matmul(
                    banks[bk][:, 0:256], lhsT=vc[:, tl, cc * 128:(cc + 1) * 128], rhs=tab[:, tl, :],
                    start=(tl == 0), stop=(tl == 1)), r=["vc", "c256", "s256"], w=["bank%d" % bk])
            P.op("act", lambda e, bk=bk, cc=cc, half=half: e.copy(
                out=usb[:, cc, half, 0:2, :], in_=banks[bk][:, 0:256].rearrange("p (t s) -> p t s", s=128)),
                r=["bank%d" % bk], w=["usb"])
    stage3(2, 0, 1.0 / 256.0)
    P.release(mc)

    ms = P.mark()
    xr = P.sb("xr", [128, 64, 256], BF16)
    t1sb = P.sb("t1sb", [128, 64, 2, 256], BF16)
    tmpa = [P.sb("tmpa%d" % i, [128, 256], F32) for i in range(2)]
    P.dma("sp", xr[:, :, :], tm_vz[CTX:, 0:256].rearrange("(a b) c -> a b c", b=64), w=["xr"])
    for ct in range(32):
        bo = 2 * (ct % 2)
        P.op("pe", lambda e, ct=ct, bo=bo: e.matmul(
            banks[bo][:, :], lhsT=c128[:, :], rhs=xr[:, 2 * ct:2 * ct + 2, :], start=True, stop=True),
            r=["c128", "xr"], w=["bank%d" % bo])
        P.op("pe", lambda e, ct=ct, bo=bo: e.matmul(
            banks[bo + 1][:, :], lhsT=s128[:, :], rhs=xr[:, 2 * ct:2 * ct + 2, :], start=True, stop=True),
            r=["s128", "xr"], w=["bank%d" % (bo + 1)])
        for q in range(2):
            s2 = 2 * ct + q
            tr = banks[bo][:, q * 256:(q + 1) * 256]
            tn = banks[bo + 1][:, q * 256:(q + 1) * 256]
            ti = cnt["e"] % 2
            cnt["e"] += 1
            P.op("dve", lambda e, tn=tn, ti=ti, s2=s2: e.tensor_scalar(
                out=tmpa[ti][:, :], in0=tn, scalar1=tws[:, s2:s2 + 1], scalar2=-1.0, op0=ALU.mult, op1=ALU.mult),
                r=["bank%d" % (bo + 1), "tws"], w=[tmpa[ti].name])
            P.op("dve", lambda e, tr=tr, ti=ti, s2=s2: e.scalar_tensor_tensor(
                out=t1sb[:, s2, 0, :], in0=tr, scalar=twc[:, s2:s2 + 1], in1=tmpa[ti][:, :],
                op0=ALU.mult, op1=ALU.add), r=["bank%d" % bo, "twc", tmpa[ti].name], w=["t1sb"])
            ti = cnt["e"] % 2
            cnt["e"] += 1
            P.op("dve", lambda e, tr=tr, ti=ti, s2=s2: e.tensor_scalar(
                out=tmpa[ti][:, :], in0=tr, scalar1=tws[:, s2:s2 + 1], scalar2=None, op0=ALU.mult),
                r=["bank%d" % bo, "tws"], w=[tmpa[ti].name])
            P.op("dve", lambda e, tn=tn, ti=ti, s2=s2: e.scalar_tensor_tensor(
                out=t1sb[:, s2, 1, :], in0=tn, scalar=twc[:, s2:s2 + 1], in1=tmpa[ti][:, :],
                op0=ALU.mult, op1=ALU.add), r=["bank%d" % (bo + 1), "twc", tmpa[ti].name], w=["t1sb"])
    P.dma("sp", t1s, t1sb[:, :, :, :], r=["t1sb"], w=["t1s"])
    P.release(ms)
    m2 = P.mark()
    t2 = [P.sb("t2in%d" % i, [64, 32, 2, 256], BF16) for i in range(2)]
    t1v = t1s.rearrange("a b r c -> b a r c")
    ne = 0
    for gq in range(4):
        bi = gq % 2
        P.dma("sp", t2[bi][:, :, :, :], t1v[:, gq * 32:(gq + 1) * 32, :, :], r=["t1s"], w=[t2[bi].name])
        for sl in range(32):
            s1p = gq * 32 + sl
            for cc in range(2):
                bk = ne % 4
                P.op("pe", lambda e, bk=bk, bi=bi, sl=sl, cc=cc: e.matmul(
                    banks[bk][:, 0:128], lhsT=t2[bi][:, sl, 0, cc * 128:(cc + 1) * 128], rhs=ra[:, :],
                    start=True, stop=False), r=[t2[bi].name, "ra"], w=["bank%d" % bk])
                P.op("pe", lambda e, bk=bk, bi=bi, sl=sl, cc=cc: e.matmul(
                    banks[bk][:, 0:128], lhsT=t2[bi][:, sl, 1, cc * 128:(cc + 1) * 128], rhs=rm[:, :],
                    start=False, stop=True), r=[t2[bi].name, "rm"], w=["bank%d" % bk])
                eng = "act" if ne % 2 else "dve"
                ne += 1
                src = banks[bk][:, 0:128].rearrange("p (r s) -> p r s", s=64)
                if eng == "act":
                    P.op("act", lambda e, cc=cc, s1p=s1p, src=src: e.copy(out=usb[:, cc, :, :, s1p], in_=src),
                         r=["bank%d" % bk], w=["usb"])
                else:
                    P.op("dve", lambda e, cc=cc, s1p=s1p, src=src: e.tensor_copy(out=usb[:, cc, :, :, s1p], in_=src),
                         r=["bank%d" % bk], w=["usb"])
    stage3(64, CTX, 1.0 / np.sqrt(8192.0 * 256.0))
    P.release(m2)
    P.release(m0)


NCHUNK = NTOK // 128
SSD_DEBUG = None
SSD_STEP = 99


def emit_ssd(P, fm_src, za_src, tm_dt, ssdu, yfs, cw, cb, dtb, alog, dsk, ya_out):
    banks = P.get_banks()
    m0 = P.mark()
    cw_sb = P.sb("cw", [128, 6, 5], F32)
    cb_sb = P.sb("cb", [128, 6], F32)
    P.dma("sp", cw_sb[:, :, :], cw, w=["cw"])
    P.dma("sp", cb_sb[:, :], cb, w=["cb"])
    m1 = P.mark()
    pc = [P.sb("pc%d" % i, [128, CTX + 4], BF16) for i in range(2)]
    px = [P.sb("px%d" % i, [128, SEQ + 4], BF16) for i in range(2)]
    acc = [P.sb("cacc%d" % i, [128, NTOK], F32) for i in range(2)]
    ub = [P.sb("cub%d" % i, [128, NTOK], BF16) for i in range(2)]
    for i in range(2):
        P.op("pool", lambda e, i=i: e.memset(pc[i][:, :], 0.0), w=[pc[i].name])
        P.op("pool", lambda e, i=i: e.memset(px[i][:, :], 0.0), w=[px[i].name])
    for c in range(6):
        b = c % 2
        src = fm_src(c)
        P.dma("sp", pc[b][:, 2:2 + CTX], src[:, 0:CTX], w=[pc[b].name])
        P.dma("sp", px[b][:, 2:2 + SEQ], src[:, CTX:], w=[px[b].name])
        for k in range(5):
            for (pb, lo, n_) in ((pc[b], 0, CTX), (px[b], CTX, SEQ)):
                if k == 0:
                    P.op("dve", lambda e, pb=pb, lo=lo, n_=n_, c=c, b=b: e.tensor_scalar(
                        out=acc[b][:, lo:lo + n_], in0=pb[:, 0:n_], scalar1=cw_sb[:, c, 0:1], scalar2=cb_sb[:, c:c + 1],
                        op0=ALU.mult, op1=ALU.add), r=[pb.name, "cw", "cb"], w=[acc[b].name])
                else:
                    P.op("dve", lambda e, pb=pb, lo=lo, n_=n_, c=c, b=b, k=k: e.scalar_tensor_tensor(
                        out=acc[b][:, lo:lo + n_], in0=pb[:, k:k + n_], scalar=cw_sb[:, c, k:k + 1],
                        in1=acc[b][:, lo:lo + n_], op0=ALU.mult, op1=ALU.add),
                        r=[pb.name, "cw", acc[b].name], w=[acc[b].name])
        P.op("act", lambda e, b=b: e.activation(out=ub[b][:, :], in_=acc[b][:, :], func=AF.Silu),
             r=[acc[b].name], w=[ub[b].name])
        P.dma("pool", ssdu[c], ub[b][:, :], r=[ub[b].name], w=["ssdu"])
    P.release(m1)
    if SSD_DEBUG == "conv":
        P.release(m0)
        return
    dt_all = P.sb("dt_all", [128, NCHUNK, 16], F32)
    a_all = P.sb("a_all", [128, NCHUNK, 16], F32)
    dtb_sb = P.sb("dtb", [128, 16], F32)
    nex = P.sb("nex", [128, 16], F32)
    dsk_sb = P.sb("dsk", [128, 4], F32)
    P.dma("sp", dtb_sb[:, :], dtb, w=["dtb"])
    P.dma("sp", nex[:, :], alog, w=["nex"])
    P.dma("sp", dsk_sb[:, :], dsk, w=["dsk"])
    m2 = P.mark()
    xr_ = P.sb("dtx", [128, NCHUNK, 16], F32)
    ax_ = P.sb("dtax", [128, NCHUNK, 16], F32)
    P.dma("sp", xr_[:, :, :], tm_dt.rearrange("(n p) c -> p n c", p=128), w=["dtx"])
    P.op("dve", lambda e: e.tensor_tensor(out=xr_[:, :, :], in0=xr_[:, :, :],
                                          in1=dtb_sb[:, :].unsqueeze(1).broadcast_to([128, NCHUNK, 16]), op=ALU.add),
         r=["dtx", "dtb"], w=["dtx"])
    P.op("act", lambda e: e.activation(out=ax_[:, :, :], in_=xr_[:, :, :], func=AF.Abs), r=["dtx"], w=["dtax"])
    P.op("act", lambda e: e.activation(out=ax_[:, :, :], in_=ax_[:, :, :], func=AF.Exp, scale=-1.0),
         r=["dtax"], w=["dtax"])
    P.op("act", lambda e: e.activation(out=ax_[:, :, :], in_=ax_[:, :, :], func=AF.Ln, bias=1.0, scale=1.0),
         r=["dtax"], w=["dtax"])
    P.op("dve", lambda e: e.tensor_scalar_max(out=xr_[:, :, :], in0=xr_[:, :, :], scalar1=0.0), r=["dtx"], w=["dtx"])
    P.op("dve", lambda e: e.tensor_tensor(out=dt_all[:, :, :], in0=xr_[:, :, :], in1=ax_[:, :, :], op=ALU.add),
         r=["dtx", "dtax"], w=["dt_all"])
    P.op("act", lambda e: e.activation(out=nex[:, :], in_=nex[:, :], func=AF.Exp), r=["nex"], w=["nex"])
    P.op("dve", lambda e: e.scalar_tensor_tensor(
        out=a_all[:, :, :], in0=dt_all[:, :, :], scalar=-1.0,
        in1=nex[:, :].unsqueeze(1).broadcast_to([128, NCHUNK, 16]), op0=ALU.mult, op1=ALU.mult),
        r=["dt_all", "nex"], w=["a_all"])
    P.release(m2)
    onesf = P.sb("s_ones", [128, 128], F32)
    oneb = P.sb("s_oneb", [128, 128], BF16)
    idf = P.sb("s_idf", [128, 128], F32)
    idb = P.sb("s_idb", [128, 128], BF16)
    Lf = P.sb("s_Lf", [128, 128], F32)
    Uf = P.sb("s_Uf", [128, 128], F32)
    Lb = P.sb("s_Lb", [128, 128], F32)
    Ub = P.sb("s_Ub", [128, 128], F32)
    P.op("pool", lambda e: e.memset(onesf[:, :], 1.0), w=["s_ones"])
    P.op("pool", lambda e: e.memset(oneb[:, :], 1.0), w=["s_oneb"])

    def sel(dst, src, pat, cm, op):
        P.op("pool", lambda e: e.affine_select(out=dst[:, :], in_=src[:, :], pattern=[[pat, 128]], compare_op=op,
                                               fill=0.0, base=0, channel_multiplier=cm),
             r=[src.name], w=[dst.name])
    sel(idf, onesf, -1, 1, ALU.is_equal)
    sel(idb, oneb, -1, 1, ALU.is_equal)
    sel(Lf, onesf, -1, 1, ALU.is_gt)
    sel(Uf, onesf, 1, -1, ALU.is_ge)
    sel(Lb, onesf, 1, -1, ALU.is_gt)
    sel(Ub, onesf, -1, 1, ALU.is_ge)
    if SSD_DEBUG == "const":
        P.release(m0)
        return
    hst = P.sb("hst", [128, 512], F32)
    hbf = P.sb("hbf", [128, 512], BF16)
    ut = [P.sb("ut%d" % i, [128, 6, 128], BF16) for i in range(2)]
    xdt = P.sb("xdt", [128, 8, 64], BF16)
    xdtd = P.sb("xdtd", [128, 8, 64], BF16)
    xsb = P.sb("xsb", [128, 640], BF16)
    sme = P.sb("sme", [128, 24], F32)
    la = P.sb("la", [128, 8, 128], F32)
    dm = [P.sb("dm%d" % i, [128, 4, 128], BF16) for i in range(2)]
    wm = [P.sb("wm%d" % i, [128, 4, 128], BF16) for i in range(2)]
    mm = P.sb("mm", [128, 128], BF16)
    yt = [P.sb("yt%d" % i, [128, 512], F32) for i in range(2)]
    yf = [P.sb("yf%d" % i, [128, 512], F32) for i in range(2)]
    zat = [P.sb("zat%d" % i, [128, 4, 128], BF16) for i in range(2)]
    sz = P.sb("sz", [128, 4, 128], F32)
    y2 = P.sb("y2", [128, 4, 128], F32)
    yo = [P.sb("yob%d" % i, [128, 4, 128], BF16) for i in range(2)]
    htmp = P.sb("htmp", [128, 512], F32)
    b0 = banks[0][:, :].bitcast(BF16)
    uv = ssdu.rearrange("c p t -> p c t")
    zav = za_src.rearrange("c p t -> p c t")
    yov = ya_out.rearrange("(c p) t -> p c t", p=128)
    it = 0
    for d in range(2):
        order = list(range(NCHUNK)) if d == 0 else [1, 0] + list(range(NCHUNK - 1, 1, -1))
        Lm, Um = (Lf, Uf) if d == 0 else (Lb, Ub)
        P.op("pool", lambda e: e.memset(hst[:, :], 0.0), w=["hst"])
        P.op("pool", lambda e: e.memset(hbf[:, :], 0.0), w=["hbf"])
        for n in order:
            if isinstance(SSD_DEBUG, int) and it >= SSD_DEBUG:
                break
            t0 = n * 128
            ui = it % 2
            it += 1
            u = ut[ui]
            P.dma("sp", u[:, :, :], uv[:, :, t0:t0 + 128], r=["ssdu"], w=[u.name])
            a_c = a_all[:, n, d * 8:(d + 1) * 8]
            dt_c = dt_all[:, n, d * 8:(d + 1) * 8]
            for c in range(4):
                P.op("pe", lambda e, c=c, u=u: e.transpose(out=b0[:, c * 128:(c + 1) * 128], in_=u[:, c, :],
                                                           identity=idb[:, :]), r=[u.name, "s_idb"], w=["bank0"])
            P.op("pe", lambda e, u=u: e.transpose(out=b0[:, 512:640], in_=u[:, 4, :], identity=idb[:, :]),
                 r=[u.name, "s_idb"], w=["bank0"])
            if SSD_STEP < 1:
                continue
            P.op("act", lambda e: e.copy(out=xsb[:, :], in_=b0[:, 0:640]), r=["bank0"], w=["xsb"])
            P.op("dve", lambda e, dt_c=dt_c: e.tensor_tensor(
                out=xdt[:, :, :], in0=xsb[:, 0:512].rearrange("p (h q) -> p h q", q=64),
                in1=dt_c.unsqueeze(2).broadcast_to([128, 8, 64]), op=ALU.mult), r=["xsb", "dt_all"], w=["xdt"])
            if SSD_STEP < 2:
                continue
            for q, lm in enumerate((Um, Lm, onesf)):
                P.op("pe", lambda e, q=q, lm=lm, a_c=a_c: e.matmul(
                    banks[3][:, 128 + 8 * q:136 + 8 * q], lhsT=lm[:, :], rhs=a_c, start=True, stop=True),
                    r=[lm.name, "a_all"], w=["bank3s"])
            P.op("act", lambda e: e.activation(out=sme[:, :], in_=banks[3][:, 128:152], func=AF.Exp),
                 r=["bank3s"], w=["sme"])
            if SSD_STEP < 3:
                continue
            P.op("dve", lambda e, Lm=Lm, a_c=a_c: e.tensor_tensor(
                out=la[:, :, :], in0=Lm[:, :].unsqueeze(1).broadcast_to([128, 8, 128]),
                in1=a_c.unsqueeze(2).broadcast_to([128, 8, 128]), op=ALU.mult), r=[Lm.name, "a_all"], w=["la"])
            for hg in range(2):
                for hh in range(4):
                    h_ = hg * 4 + hh
                    P.op("pe", lambda e, hg=hg, hh=hh, h_=h_, Um=Um: e.matmul(
                        banks[1 + hg][:, hh * 128:(hh + 1) * 128], lhsT=la[:, h_, :], rhs=Um[:, :],
                        start=True, stop=True), r=["la", Um.name], w=["bank%d" % (1 + hg)])
                P.op("act", lambda e, hg=hg: e.activation(
                    out=dm[hg][:, :, :], in_=banks[1 + hg][:, :].rearrange("p (h i) -> p h i", i=128), func=AF.Exp),
                    r=["bank%d" % (1 + hg)], w=[dm[hg].name])
            if SSD_STEP < 4:
                continue
            P.op("pe", lambda e, u=u: e.matmul(banks[3][:, 0:128], lhsT=u[:, 4, :], rhs=u[:, 5, :], start=True, stop=True),
                 r=[u.name], w=["bank3"])
            P.op("dve", lambda e, Um=Um: e.tensor_tensor(out=mm[:, :], in0=banks[3][:, 0:128], in1=Um[:, :], op=ALU.mult),
                 r=["bank3", Um.name], w=["mm"])
            for hg in range(2):
                P.op("dve", lambda e, hg=hg: e.tensor_tensor(
                    out=wm[hg][:, :, :], in0=dm[hg][:, :, :], in1=mm[:, :].unsqueeze(1).broadcast_to([128, 4, 128]),
                    op=ALU.mult), r=[dm[hg].name, "mm"], w=[wm[hg].name])
            if SSD_STEP < 5:
                continue
            for h_ in range(8):
                P.op("pe", lambda e, h_=h_: e.matmul(
                    banks[4][:, h_ * 64:(h_ + 1) * 64], lhsT=wm[h_ // 4][:, h_ % 4, :], rhs=xdt[:, h_, :],
                    start=True, stop=True), r=[wm[h_ // 4].name, "xdt"], w=["bank4"])
            P.op("pe", lambda e, u=u: e.matmul(banks[5][:, :], lhsT=u[:, 5, :], rhs=hbf[:, :], start=True, stop=True),
                 r=[u.name, "hbf"], w=["bank5"])
            P.op("dve", lambda e: e.tensor_tensor(
                out=xdtd[:, :, :], in0=xdt[:, :, :], in1=sme[:, 8:16].unsqueeze(2).broadcast_to([128, 8, 64]),
                op=ALU.mult), r=["xdt", "sme"], w=["xdtd"])
            P.op("pe", lambda e: e.matmul(banks[6][:, :], lhsT=xsb[:, 512:640], rhs=xdtd[:, :, :], start=True, stop=True),
                 r=["xsb", "xdtd"], w=["bank6"])
            if SSD_STEP < 6:
                continue
            P.op("dve", lambda e: e.tensor_tensor(
                out=htmp[:, :].rearrange("p (h q) -> p h q", q=64), in0=hst[:, :].rearrange("p (h q) -> p h q", q=64),
                in1=sme[:, 16:24].unsqueeze(2).broadcast_to([128, 8, 64]), op=ALU.mult),
                r=["hst", "sme"], w=["htmp"])
            P.op("dve", lambda e: e.tensor_tensor(out=hst[:, :], in0=banks[6][:, :], in1=htmp[:, :], op=ALU.add),
                 r=["bank6", "htmp"], w=["hst"])
            P.op("dve", lambda e: e.tensor_copy(out=hbf[:, :], in_=hst[:, :]), r=["hst"], w=["hbf"])
            yi = it % 2
            y_ = yt[yi]
            P.op("dve", lambda e, y_=y_: e.tensor_tensor(
                out=y_[:, :].rearrange("p (h q) -> p h q", q=64),
                in0=banks[5][:, :].rearrange("p (h q) -> p h q", q=64),
                in1=sme[:, 0:8].unsqueeze(2).broadcast_to([128, 8, 64]), op=ALU.mult),
                r=["bank5", "sme"], w=[y_.name])
            P.op("dve", lambda e, y_=y_: e.tensor_tensor(out=y_[:, :], in0=banks[4][:, :], in1=y_[:, :], op=ALU.add),
                 r=["bank4", y_.name], w=[y_.name])
            if d == 0:
                P.dma("pool", yfs[t0:t0 + 128, :], y_[:, :], r=[y_.name], w=["yfs"])
            else:
                f_ = yf[yi]
                z_ = zat[yi]
                o_ = yo[yi]
                P.dma("sp", f_[:, :], yfs[t0:t0 + 128, :], r=["yfs"], w=[f_.name])
                P.dma("sp", z_[:, :, :], zav[:, :, t0:t0 + 128], w=[z_.name])
                P.op("pool", lambda e, y_=y_, f_=f_: e.tensor_tensor(out=y_[:, :], in0=y_[:, :], in1=f_[:, :], op=ALU.add),
                     r=[y_.name, f_.name], w=[y_.name])
                for c in range(4):
                    P.op("pe", lambda e, c=c, y_=y_: e.transpose(
                        out=banks[7][:, c * 128:(c + 1) * 128], in_=y_[:, c * 128:(c + 1) * 128], identity=idf[:, :]),
                        r=[y_.name, "s_idf"], w=["bank7"])
                P.op("act", lambda e, z_=z_: e.activation(out=sz[:, :, :], in_=z_[:, :, :], func=AF.Silu),
                     r=[z_.name], w=["sz"])
                for c in range(4):
                    P.op("dve", lambda e, c=c, u=u: e.scalar_tensor_tensor(
                        out=y2[:, c, :], in0=u[:, c, :], scalar=dsk_sb[:, c:c + 1],
                        in1=banks[7][:, c * 128:(c + 1) * 128], op0=ALU.mult, op1=ALU.add),
                        r=[u.name, "dsk", "bank7"], w=["y2"])
                P.op("pool", lambda e, o_=o_: e.tensor_tensor(out=o_[:, :, :], in0=y2[:, :, :], in1=sz[:, :, :], op=ALU.mult),
                     r=["y2", "sz"], w=[o_.name])
                P.dma("pool", yov[:, :, t0:t0 + 128], o_[:, :, :], r=[o_.name])
    P.release(m0)


def ssd_params_for_core(inp, l, g):
    cwf = inp["ssd_conv_w"][l]
    cbf = inp["ssd_conv_b"][l]
    chans = list(range(512 * g, 512 * g + 512)) + list(range(4096 + 128 * g, 4096 + 128 * g + 128)) + \
        list(range(5120 + 128 * g, 5120 + 128 * g + 128))
    cw = np.ascontiguousarray(cwf[:, chans].reshape(5, 6, 128).transpose(2, 1, 0))
    cb = np.ascontiguousarray(cbf[chans].reshape(6, 128).T)
    hs = slice(8 * g, 8 * g + 8)
    dtb = np.concatenate([inp["ssd_dt_bias"][l][0, hs], inp["ssd_dt_bias"][l][1, hs]])
    alog = np.concatenate([inp["ssd_a_log"][l][0, hs], inp["ssd_a_log"][l][1, hs]])
    dtb = np.ascontiguousarray(np.broadcast_to(dtb[None, :], (128, 16))).astype(np.float32)
    alog = np.ascontiguousarray(np.broadcast_to(alog[None, :], (128, 16))).astype(np.float32)
    dvec = np.repeat(inp["ssd_d"][l][hs], 64)
    dsk = np.ascontiguousarray(dvec.reshape(4, 128).T).astype(np.float32)
    return cw, cb, dtb, alog, dsk


def build_p2():
    P = Prog()
    hT = P.dram("hT", [D, NTOK], BF16, "ExternalInput")
    wf = P.dram("wf", [D, NFM * 128], F32, "ExternalInput")
    wt = P.dram("wt", [D, NTM], F32, "ExternalInput")
    cw = P.dram("cw", [128, 6, 5], F32, "ExternalInput")
    cb = P.dram("cb", [128, 6], F32, "ExternalInput")
    dtb = P.dram("dtb", [128, 16], F32, "ExternalInput")
    alog = P.dram("alog", [128, 16], F32, "ExternalInput")
    dsk = P.dram("dsk", [128, 4], F32, "ExternalInput")
    cfw = P.dram("cfw", [128, 2, 31], F32, "ExternalInput")
    cfb = P.dram("cfb", [128, 2], F32, "ExternalInput")
    ofm = P.dram("ofm", [14, 128, NTOK], BF16, "ExternalOutput")
    ya = P.dram("ya", [512, NTOK], BF16, "ExternalOutput")
    vb = P.dram("vb", [256, NTOK], BF16, "ExternalOutput")
    yc = P.dram("yc", [NTOK, 256], BF16, "ExternalOutput")
    fms = P.dram("fms", [14, 128, NTOK], BF16, "Internal")
    vz = P.dram("vzs", [NTOK, 512], BF16, "Internal")
    dts = P.dram("dts", [NTOK, 16], F32, "Internal")
    ssdu = P.dram("ssdu", [6, 128, NTOK], BF16, "Internal")
    yfs = P.dram("yfs", [NTOK, 512], F32, "Internal")
    t1s = P.dram("t1s", [128, 64, 2, 256], BF16, "Internal")

    def fm_dst(c):
        return fms[c] if c < 14 else ofm[c - 14]
    emit_gemm(P, hT, wf, wt, fm_dst, vz, dts)
    P.barrier()
    emit_ssd(P, lambda c: fms[c], fms[6:10], dts, ssdu, yfs, cw, cb, dtb, alog, dsk, ya)
    emit_conformer(P, lambda c: fms[c], cfw, cfb, vb)
    emit_fourier(P, vz, t1s, yc)
    return P.finish()


def kernel(x, c, ctx, c_ctx, w_mod, b_mod, norm_g, w_in, ssd_conv_w, ssd_conv_b, ssd_dt_bias, ssd_a_log, ssd_d,
           ssd_norm_g, cf_conv_w, cf_conv_b, cf_ln_g, cf_ln_b, w_branch, w_out, final_g):
    bf = ml_dtypes.bfloat16
    inp = dict(x=x, c=c, ctx=ctx, c_ctx=c_ctx, w_mod=w_mod, b_mod=b_mod, norm_g=norm_g, w_in=w_in,
               ssd_conv_w=ssd_conv_w, ssd_conv_b=ssd_conv_b, ssd_dt_bias=ssd_dt_bias, ssd_a_log=ssd_a_log,
               ssd_d=ssd_d, ssd_norm_g=ssd_norm_g, cf_conv_w=cf_conv_w, cf_conv_b=cf_conv_b, cf_ln_g=cf_ln_g,
               cf_ln_b=cf_ln_b, w_branch=w_branch, w_out=w_out, final_g=final_g)
    inp = {k: np.asarray(v, dtype=np.float32) for k, v in inp.items()}
    cores = list(range(NCORES))
    mod = run_p0(inp)
    xc = inp["x"][0]
    cc = inp["ctx"][0]
    depth = 2
    for l in range(depth):
        hT = run_p1(shard_tokens_T(xc, cc), inp["norm_g"][l], mod[l])
        hx, hc = unshard_tokens_T(hT)
        hT_all = np.ascontiguousarray(np.concatenate([hc, hx], axis=0).T)
        del hx, hc, hT
        nc2 = get_nc("p2", build_p2)
        in_maps = []
        for g in cores:
            w_fm, w_tm = wcols_for_core(inp["w_in"][l], g)
            cw, cb, dtb, alog, dsk = ssd_params_for_core(inp, l, g)
            cfw = np.ascontiguousarray(inp["cf_conv_w"][l][:, g * 256:(g + 1) * 256].reshape(31, 2, 128).transpose(2, 1, 0))
            cfb = np.ascontiguousarray(inp["cf_conv_b"][l][g * 256:(g + 1) * 256].reshape(2, 128).T)
            in_maps.append({"hT": hT_all, "wf": w_fm, "wt": w_tm, "cw": cw, "cb": cb, "dtb": dtb, "alog": alog,
                            "dsk": dsk, "cfw": cfw, "cfb": cfb})
        res = run_bass_kernel_spmd(nc2, in_maps, core_ids=cores).results
        del in_maps, hT_all
        ya_all = np.concatenate([np.asarray(r["ya"]) for r in res], axis=0)
        vb_all = np.concatenate([np.asarray(r["vb"]) for r in res], axis=0)
        zb_all = np.concatenate([np.asarray(r["ofm"])[0:2].reshape(256, NTOK) for r in res], axis=0)
        yc_all = np.concatenate([np.asarray(r["yc"]) for r in res], axis=1)
        g_all = np.concatenate(
            [np.concatenate([np.asarray(r["ofm"])[2 + 4 * j:6 + 4 * j].reshape(512, NTOK) for r in res], axis=0)
             for j in range(3)], axis=0)
        del res

        def fm_halves(a, r):
            out = []
            for h in range(2):
                xs_ = a[:, CTX + r * TOK_X + 512 * h: CTX + r * TOK_X + 512 * (h + 1)]
                cs_ = a[:, r * TOK_C + 16 * h: r * TOK_C + 16 * (h + 1)]
                out.append(np.concatenate([xs_, cs_], axis=1))
            return np.ascontiguousarray(np.stack(out))
        nc3 = get_nc("p3", build_p3)
        wbk = w_blocks(inp["w_branch"][l])
        wok = w_blocks(inp["w_out"][l])
        sng = vec_layout(inp["ssd_norm_g"][l])
        lng = np.ascontiguousarray(inp["cf_ln_g"][l].reshape(16, 128).T)
        lnb = np.ascontiguousarray(inp["cf_ln_b"][l].reshape(16, 128).T)
        modv = mod_layout(mod[l])
        ycT_all = np.ascontiguousarray(yc_all.T)
        in_maps = []
        for r in cores:
            in_maps.append({"yaT": fm_halves(ya_all, r), "vbT": fm_halves(vb_all, r), "zbT": fm_halves(zb_all, r),
                            "ycT": fm_halves(ycT_all, r), "gT": fm_halves(g_all, r), "xT": halves_T(xc, cc, r),
                            "wb": wbk, "wo": wok, "sng": sng, "lng": lng, "lnb": lnb, "modv": modv})
        del ya_all, vb_all, zb_all, yc_all, g_all, ycT_all
        res = run_bass_kernel_spmd(nc3, in_maps, core_ids=cores).results
        del in_maps
        xn = np.empty_like(xc)
        cn = np.empty_like(cc)
        for r in cores:
            o = np.asarray(res[r]["xnT"])
            for h in range(2):
                xn[r * TOK_X + 512 * h: r * TOK_X + 512 * (h + 1)] = o[h][:, :512].T
                cn[r * TOK_C + 16 * h: r * TOK_C + 16 * (h + 1)] = o[h][:, 512:].T
        xc, cc = xn, cn
    zmod = np.zeros((2, 3 * D), np.float32)
    oT = run_p1(shard_tokens_T(xc, cc), inp["final_g"], zmod, final=True)
    ox, _ = unshard_tokens_T([np.asarray(o) for o in oT])
    return np.ascontiguousarray(ox[None].astype(np.float32))
```

```python
import numpy as np
import ml_dtypes
import concourse.bass as bass
import concourse.mybir as mybir
from concourse.bass_utils import run_bass_kernel_spmd

F32 = mybir.dt.float32
BF16 = mybir.dt.bfloat16
I32 = mybir.dt.int32
AF = mybir.ActivationFunctionType
ALU = mybir.AluOpType
AX = mybir.AxisListType

NCORES = 8
D = 4096
SEQ = 8192
CTX = 256
TOK_X = SEQ // NCORES
TOK_C = CTX // NCORES
TOK = TOK_X + TOK_C
NTOK = SEQ + CTX
KC = D // 128
EPS = 1e-6

NDMASEM = 40
CC_INC = 16


class Prog:
    def __init__(self):
        self.nc = bass.Bass("TRN2", target_bir_lowering=False)
        nc = self.nc
        self.eng_names = ["pe", "act", "dve", "pool", "sp"]
        self.lists = {e: [] for e in self.eng_names}
        self.count = {e: 0 for e in self.eng_names}
        self.sem = {e: nc.alloc_semaphore("s_" + e) for e in ["pe", "act", "dve", "pool"]}
        self.dsem = [nc.alloc_semaphore("d%d" % i) for i in range(NDMASEM)]
        self.dval = [0] * NDMASEM
        self.dnext = 0
        self.waited = {e: {} for e in self.eng_names}
        self.last_w = {}
        self.readers = {}
        self.n_alloc = 0
        self.sb_ptr = 16512
        self.sb_end = 229344
        self.banks = None

    def dram(self, name, shape, dt, kind):
        return self.nc.dram_tensor(name, list(shape), dt, kind=kind).ap()

    def sb(self, name, shape, dt):
        esz = 4 if dt in (F32, I32) else 2
        n = 1
        for d_ in shape[1:]:
            n *= d_
        nbytes = (n * esz + 63) // 64 * 64
        off = self.sb_ptr
        assert off + nbytes <= self.sb_end, ("SBUF overflow", name, off, nbytes)
        self.sb_ptr += nbytes
        self.n_alloc += 1
        return self.nc.alloc_sbuf_tensor_at("%s_%d" % (name, self.n_alloc), list(shape), dt, offset=off)

    def mark(self):
        return self.sb_ptr

    def release(self, mark):
        self.barrier()
        self.sb_ptr = mark

    def barrier(self):
        deps = [(e, self.count[e]) for e in ["pe", "act", "dve", "pool"] if self.count[e] > 0]
        deps += [(("d", i), self.dval[i]) for i in range(NDMASEM) if self.dval[i] > 0]
        for e in self.eng_names:
            self._emit_waits(e, [d_ for d_ in deps if d_[0] != e])

    def get_banks(self):
        if self.banks is None:
            self.banks = [self.nc.alloc_psum_tensor("bank%d" % i, [128, 512], F32) for i in range(8)]
        return self.banks

    def ps(self, name, shape, dt=F32):
        return self.nc.alloc_psum_tensor(name, list(shape), dt)

    def _deps(self, eng, r, w):
        deps = []
        for k in r:
            t = self.last_w.get(k)
            if t is not None:
                deps.append(t)
        for k in w:
            t = self.last_w.get(k)
            if t is not None and t[0] != eng:
                deps.append(t)
            for t in self.readers.get(k, ()):
                if t[0] != eng:
                    deps.append(t)
        if eng == "pe":
            deps = [t for t in deps if t[0] != "pe"]
        return deps

    def _emit_waits(self, eng, deps):
        wd = self.waited[eng]
        for (s, v) in deps:
            if wd.get(s, 0) < v:
                wd[s] = v
                self.lists[eng].append(("wait", s, v))

    def _record(self, tok, r, w):
        for k in r:
            self.readers.setdefault(k, []).append(tok)
        for k in w:
            self.last_w[k] = tok
            self.readers[k] = []

    def op(self, eng, fn, r=(), w=()):
        self._emit_waits(eng, self._deps(eng, r, w))
        self.count[eng] += 1
        tok = (eng, self.count[eng])
        self.lists[eng].append(("op", fn))
        self._record(tok, r, w)
        return tok

    def dma(self, q, out, in_, r=(), w=()):
        i = self.dnext
        self.dnext = (self.dnext + 1) % NDMASEM
        deps = self._deps(("d", i), r, w)
        if self.dval[i] > 0:
            deps.append((("d", i), self.dval[i]))
        self._emit_waits(q, deps)
        self.dval[i] += 16
        tok = (("d", i), self.dval[i])
        self.lists[q].append(("dma", out, in_, i))
        self._record(tok, r, w)
        return tok

    def _semh(self, s):
        if isinstance(s, tuple):
            return self.dsem[s[1]]
        return self.sem[s]

    def finish(self):
        nc = self.nc
        for i in range(NDMASEM):
            if self.dval[i] > 0:
                self._emit_waits("sp", [(("d", i), self.dval[i])])
        for e in ["pe", "act", "dve", "pool"]:
            if self.count[e] > 0:
                self._emit_waits("sp", [(e, self.count[e])])

        def replay(ename):
            def f(eng):
                for it in self.lists[ename]:
                    if it[0] == "wait":
                        eng.wait_ge(self._semh(it[1]), it[2])
                    elif it[0] == "op":
                        it[1](eng).then_inc(self.sem[ename], 1)
                    elif it[0] == "cc":
                        eng.collective_compute(it[1], ALU.bypass, replica_groups=[list(range(NCORES))],
                                               ins=[it[3]], outs=[it[2]]).then_inc(self.dsem[it[4]], CC_INC)
                    else:
                        eng.dma_start(out=it[1], in_=it[2]).then_inc(self.dsem[it[3]], 16)
            return f

        with nc.Block() as block:
            block.tensor(replay("pe"))
            block.scalar(replay("act"))
            block.vector(replay("dve"))
            block.gpsimd(replay("pool"))
            block.sync(replay("sp"))
        return nc


MODC = 3 * D // NCORES


def build_p0():
    P = Prog()
    c_in = P.dram("c2", [2, D], F32, "ExternalInput")
    wmod = P.dram("wmod", [2, D, MODC], F32, "ExternalInput")
    bmod = P.dram("bmod", [2, MODC], F32, "ExternalInput")
    out = P.dram("mod", [2, 2, MODC], F32, "ExternalOutput")

    craw = P.sb("craw", [128, 2, KC], F32)
    cs = P.sb("cs", [128, KC, 2], F32)
    bsb = P.sb("bsb", [2, 2, MODC], F32)
    res = P.sb("res", [2, 2, MODC], F32)
    wbuf = [P.sb("wbuf%d" % i, [128, 8, MODC], F32) for i in range(2)]
    pst = [P.ps("pst%d" % i, [2, 512], F32) for i in range(3)]

    P.dma("sp", craw[:, :, :], c_in.rearrange("j (p k) -> p j k", k=KC), w=["craw"])
    for l in range(2):
        for j in range(2):
            P.dma("sp", bsb[j:j + 1, l, :], bmod[l:l + 1, :], w=["bsb"])
    for j in range(2):
        P.op("act", lambda e, j=j: e.activation(out=cs[:, :, j], in_=craw[:, j, :], func=AF.Silu),
             r=["craw"], w=["cs"])
    it = 0
    for l in range(2):
        for kg in range(4):
            b = it % 2
            it += 1
            P.dma("sp", wbuf[b][:, :, :],
                  wmod[l].rearrange("(p k) n -> p k n", k=KC)[:, kg * 8:(kg + 1) * 8, :],
                  w=["wbuf%d" % b])
            for kk in range(8):
                k = kg * 8 + kk
                for n in range(3):
                    P.op("pe", lambda e, b=b, kk=kk, n=n, k=k: e.matmul(
                        pst[n][:, :], lhsT=cs[:, k, :], rhs=wbuf[b][:, kk, n * 512:(n + 1) * 512],
                        start=(k == 0), stop=(k == KC - 1)),
                        r=["cs", "wbuf%d" % b], w=["pst%d" % n])
        for n in range(3):
            P.op("dve", lambda e, n=n, l=l: e.tensor_tensor(
                out=res[:, l, n * 512:(n + 1) * 512], in0=pst[n][:, :], in1=bsb[:, l, n * 512:(n + 1) * 512],
                op=ALU.add), r=["pst%d" % n, "bsb"], w=["res"])
    P.dma("sp", out.rearrange("l j n -> j l n"), res[:, :, :], r=["res"])
    return P.finish()


def run_p0(inp):
    nc = build_p0()
    c2 = np.stack([inp["c"][0], inp["c_ctx"]]).astype(np.float32)
    in_maps = []
    for r in range(NCORES):
        sl = slice(r * MODC, (r + 1) * MODC)
        in_maps.append({"c2": c2,
                        "wmod": np.ascontiguousarray(inp["w_mod"][:, :, sl]),
                        "bmod": np.ascontiguousarray(inp["b_mod"][:, sl])})
    res = run_bass_kernel_spmd(nc, in_maps, core_ids=list(range(NCORES)))
    return np.concatenate([r["mod"] for r in res.results], axis=2)


def _prog_collective(self, kind, out, in_, r=(), w=()):
    i = self.dnext
    self.dnext = (self.dnext + 1) % NDMASEM
    deps = self._deps(("d", i), r, w)
    if self.dval[i] > 0:
        deps.append((("d", i), self.dval[i]))
    self._emit_waits("pool", deps)
    self.dval[i] += 16
    tok = (("d", i), self.dval[i])
    self.lists["pool"].append(("cc", kind, out, in_, i))
    self._record(tok, r, w)
    return tok


Prog.collective = _prog_collective


TTILES = [(0, 512, 0), (512, 1024, 0), (1024, 1056, 1)]


def vec_layout(v):
    return np.ascontiguousarray(np.asarray(v, np.float32).reshape(KC, 128).T)


def emit_norm_mod(P, xT, hT, gvec, modv, out_dt, pfx="n"):
    ones = P.sb(pfx + "ones", [128, 128], F32)
    g_sb = P.sb(pfx + "g", [128, KC], F32)
    m_sb = P.sb(pfx + "m", [128, 2, 3, KC], F32)
    G = P.sb(pfx + "G", [128, 2, KC], F32)
    xb = [P.sb(pfx + "xb%d" % i, [128, KC, 512], F32) for i in range(2)]
    sq = [P.sb(pfx + "sq%d" % i, [128, 512], F32) for i in range(3)]
    tmp = [P.sb(pfx + "tmp%d" % i, [128, 512], F32) for i in range(3)]
    ho = [P.sb(pfx + "ho%d" % i, [128, 512], out_dt) for i in range(3)]
    rt = P.sb(pfx + "rt", [128, 512], F32)
    rstd = P.sb(pfx + "rstd", [128, 512], F32)
    pss = P.ps(pfx + "pss", [128, 512], F32)

    P.op("pool", lambda e: e.memset(ones[:, :], 1.0), w=[pfx + "ones"])
    P.dma("sp", g_sb[:, :], gvec, w=[pfx + "g"])
    P.dma("sp", m_sb[:, :, :, :], modv, w=[pfx + "m"])
    for j in range(2):
        P.op("dve", lambda e, j=j: e.scalar_tensor_tensor(
            out=G[:, j, :], in0=m_sb[:, j, 1, :], scalar=1.0, in1=g_sb[:, :], op0=ALU.add, op1=ALU.mult),
            r=[pfx + "m", pfx + "g"], w=[pfx + "G"])
    xv = xT.rearrange("(k p) t -> p k t", p=128)
    n = 0
    for ti, (t0, t1, j) in enumerate(TTILES):
        w_ = t1 - t0
        b = ti % 2
        xk = pfx + "xb%d" % b
        P.dma("sp", xb[b][:, :, 0:w_], xv[:, :, t0:t1], w=[xk])
        for kc in range(KC):
            s = kc % 3
            P.op("act", lambda e, b=b, kc=kc, s=s, w_=w_: e.activation(
                out=sq[s][:, 0:w_], in_=xb[b][:, kc, 0:w_], func=AF.Square), r=[xk], w=[pfx + "sq%d" % s])
            P.op("pe", lambda e, kc=kc, s=s, w_=w_: e.matmul(
                pss[:, 0:w_], lhsT=ones[:, :], rhs=sq[s][:, 0:w_], start=(kc == 0), stop=(kc == KC - 1)),
                r=[pfx + "ones", pfx + "sq%d" % s], w=[pfx + "pss"])
        P.op("act", lambda e, w_=w_: e.activation(out=rt[:, 0:w_], in_=pss[:, 0:w_], func=AF.Sqrt,
                                                   bias=EPS, scale=1.0 / D), r=[pfx + "pss"], w=[pfx + "rt"])
        P.op("dve", lambda e, w_=w_: e.reciprocal(out=rstd[:, 0:w_], in_=rt[:, 0:w_]), r=[pfx + "rt"], w=[pfx + "rstd"])
        for kc in range(KC):
            s = n % 3
            n += 1
            P.op("dve", lambda e, b=b, kc=kc, s=s, w_=w_, j=j: e.scalar_tensor_tensor(
                out=tmp[s][:, 0:w_], in0=xb[b][:, kc, 0:w_], scalar=G[:, j, kc:kc + 1], in1=rstd[:, 0:w_],
                op0=ALU.mult, op1=ALU.mult), r=[xk, pfx + "G", pfx + "rstd"], w=[pfx + "tmp%d" % s])
            P.op("act", lambda e, kc=kc, s=s, w_=w_, j=j: e.activation(
                out=ho[s][:, 0:w_], in_=tmp[s][:, 0:w_], func=AF.Identity, bias=m_sb[:, j, 0, kc:kc + 1], scale=1.0),
                r=[pfx + "tmp%d" % s, pfx + "m"], w=[pfx + "ho%d" % s])
            P.dma("sp", hT[kc * 128:(kc + 1) * 128, t0:t1], ho[s][:, 0:w_], r=[pfx + "ho%d" % s])


def build_p1(out_dt):
    P = Prog()
    xT = P.dram("xT", [D, TOK], F32, "ExternalInput")
    gvec = P.dram("gvec", [128, KC], F32, "ExternalInput")
    modv = P.dram("modv", [128, 2, 3, KC], F32, "ExternalInput")
    hT = P.dram("hT", [D, TOK], out_dt, "ExternalOutput")
    emit_norm_mod(P, xT, hT, gvec, modv, out_dt)
    return P.finish()


def mod_layout(mod_l):
    m = np.asarray(mod_l, np.float32).reshape(2, 3, KC, 128)
    return np.ascontiguousarray(m.transpose(3, 0, 1, 2))


def shard_tokens_T(x2d, c2d):
    outs = []
    for r in range(NCORES):
        t = np.concatenate([x2d[r * TOK_X:(r + 1) * TOK_X], c2d[r * TOK_C:(r + 1) * TOK_C]], axis=0)
        outs.append(np.ascontiguousarray(t.T))
    return outs


def unshard_tokens_T(per_core):
    xs = np.concatenate([p[:, :TOK_X].T for p in per_core], axis=0)
    cs = np.concatenate([p[:, TOK_X:].T for p in per_core], axis=0)
    return xs, cs


_NC_CACHE = {}


def get_nc(key, builder):
    if key not in _NC_CACHE:
        _NC_CACHE[key] = builder()
    return _NC_CACHE[key]


def run_p1(xT_list, gvec, mod_l, final=False):
    nc = get_nc(("p1", final), lambda: build_p1(F32 if final else BF16))
    g = vec_layout(gvec)
    m = mod_layout(mod_l)
    in_maps = [{"xT": xT_list[r], "gvec": g, "modv": m} for r in range(NCORES)]
    res = run_bass_kernel_spmd(nc, in_maps, core_ids=list(range(NCORES)))
    return [r["hT"] for r in res.results]


NFM = 28
NTM = 528
FM_GROUPS = [(0, 7), (7, 14), (14, 21), (21, 28)]
A_TILES = [(0, 256)] + [(256 + 512 * i, 256 + 512 * (i + 1)) for i in range(16)]


def emit_gemm(P, hT_all, w_fm, w_tm, fm_dst, tm_vz, tm_dt):
    banks = P.get_banks()
    m0 = P.mark()
    hv = hT_all.rearrange("(k p) t -> p k t", p=128)
    wfv = w_fm.rearrange("(k p) n -> p k n", p=128)
    wtv = w_tm.rearrange("(k p) n -> p k n", p=128)
    GC = 4
    NG = NFM // GC
    wres = [P.sb("wres%d" % i, [128, KC, GC * 128], BF16) for i in range(2)]
    wtm = P.sb("wtm", [128, KC, NTM], BF16)
    stg = [P.sb("stg%d" % i, [128, 4, NTM], F32) for i in range(2)]
    hb = [P.sb("hb%d" % i, [128, KC, 512], BF16) for i in range(2)]
    ob = [P.sb("ob%d" % i, [128, 512], BF16) for i in range(4)]
    otv = [P.sb("otv%d" % i, [128, 512], BF16) for i in range(2)]
    otd = [P.sb("otd%d" % i, [128, 16], F32) for i in range(2)]
    st = {"stg": 0, "ob": 0, "hb": 0, "bank": 0, "tm": 0}

    def stage_tm(q):
        b = st["stg"] % 2
        st["stg"] += 1
        P.dma("sp", stg[b][:, :, 0:NTM], wtv[:, q * 4:(q + 1) * 4, :], w=["stg%d" % b])
        P.op("pool" if q % 2 else "dve", lambda e: e.tensor_copy(
            out=wtm[:, q * 4:(q + 1) * 4, :], in_=stg[b][:, :, 0:NTM]), r=["stg%d" % b], w=["wtm"])

    def stage_fm(gi, q):
        b = st["stg"] % 2
        st["stg"] += 1
        wi = gi % 2
        c0 = gi * GC
        P.dma("sp", stg[b][:, :, 0:GC * 128], wfv[:, q * 4:(q + 1) * 4, c0 * 128:(c0 + GC) * 128], w=["stg%d" % b])
        P.op("pool" if q % 2 else "dve", lambda e: e.tensor_copy(
            out=wres[wi][:, q * 4:(q + 1) * 4, :], in_=stg[b][:, :, 0:GC * 128]), r=["stg%d" % b], w=["wres%d" % wi])

    for q in range(8):
        stage_tm(q)
    for q in range(8):
        stage_fm(0, q)
    for gi in range(NG):
        wi = gi % 2
        c0 = gi * GC
        for ti, (t0, t1) in enumerate(A_TILES):
            tw = t1 - t0
            hbi = st["hb"] % 2
            st["hb"] += 1
            hk = "hb%d" % hbi
            P.dma("sp", hb[hbi][:, :, 0:tw], hv[:, :, t0:t1], w=[hk])
            if gi + 1 < NG and 1 <= ti <= 8:
                stage_fm(gi + 1, ti - 1)
            for cj in range(GC):
                bk = st["bank"] % 4
                st["bank"] += 1
                for kc in range(KC):
                    P.op("pe", lambda e, bk=bk, kc=kc, cj=cj, hbi=hbi, tw=tw, wi=wi: e.matmul(
                        banks[bk][:, 0:tw], lhsT=wres[wi][:, kc, cj * 128:(cj + 1) * 128], rhs=hb[hbi][:, kc, 0:tw],
                        start=(kc == 0), stop=(kc == KC - 1)), r=["wres%d" % wi, hk], w=["bank%d" % bk])
                o = st["ob"] % 4
                st["ob"] += 1
                if st["ob"] % 2:
                    P.op("act", lambda e, o=o, bk=bk, tw=tw: e.copy(out=ob[o][:, 0:tw], in_=banks[bk][:, 0:tw]),
                         r=["bank%d" % bk], w=["ob%d" % o])
                else:
                    P.op("dve", lambda e, o=o, bk=bk, tw=tw: e.tensor_copy(out=ob[o][:, 0:tw], in_=banks[bk][:, 0:tw]),
                         r=["bank%d" % bk], w=["ob%d" % o])
                P.dma("pool", fm_dst(c0 + cj)[:, t0:t1], ob[o][:, 0:tw], r=["ob%d" % o])
            if gi == 0:
                for s0 in range(0, tw, 128):
                    for kc in range(KC):
                        P.op("pe", lambda e, kc=kc, hbi=hbi, s0=s0: e.matmul(
                            banks[4][:, :], lhsT=hb[hbi][:, kc, s0:s0 + 128], rhs=wtm[:, kc, 0:512],
                            start=(kc == 0), stop=(kc == KC - 1)), r=["wtm", hk], w=["bank4"])
                        P.op("pe", lambda e, kc=kc, hbi=hbi, s0=s0: e.matmul(
                            banks[5][:, 0:16], lhsT=hb[hbi][:, kc, s0:s0 + 128], rhs=wtm[:, kc, 512:528],
                            start=(kc == 0), stop=(kc == KC - 1)), r=["wtm", hk], w=["bank5"])
                    o = st["tm"] % 2
                    st["tm"] += 1
                    P.op("act", lambda e, o=o: e.copy(out=otv[o][:, :], in_=banks[4][:, :]), r=["bank4"], w=["otv%d" % o])
                    P.op("dve", lambda e, o=o: e.tensor_copy(out=otd[o][:, :], in_=banks[5][:, 0:16]),
                         r=["bank5"], w=["otd%d" % o])
                    P.dma("pool", tm_vz[t0 + s0:t0 + s0 + 128, :], otv[o][:, :], r=["otv%d" % o])
                    P.dma("pool", tm_dt[t0 + s0:t0 + s0 + 128, :], otd[o][:, :], r=["otd%d" % o])
    P.release(m0)


def wcols_for_core(w_in_l, g):
    cols = []
    cols += list(range(512 * g, 512 * g + 512))
    cols += list(range(4096 + 128 * g, 4096 + 128 * g + 128))
    cols += list(range(5120 + 128 * g, 5120 + 128 * g + 128))
    cols += list(range(6144 + 512 * g, 6144 + 512 * g + 512))
    cols += list(range(10368 + 256 * g, 10368 + 256 * g + 256))
    cols += list(range(12416 + 256 * g, 12416 + 256 * g + 256))
    cols += list(range(14464 + 256 * g, 14464 + 256 * g + 256))
    for j in range(3):
        cols += list(range(20608 + 4096 * j + 512 * g, 20608 + 4096 * j + 512 * g + 512))
    tcols = []
    tcols += list(range(16512 + 256 * g, 16512 + 256 * g + 256))
    tcols += list(range(18560 + 256 * g, 18560 + 256 * g + 256))
    tcols += list(range(10240 + 8 * g, 10240 + 8 * g + 8))
    tcols += list(range(10304 + 8 * g, 10304 + 8 * g + 8))
    return np.ascontiguousarray(w_in_l[:, cols]), np.ascontiguousarray(w_in_l[:, tcols])


HT = 528
P3_TILES = [(0, 512, 0), (512, 528, 1)]
CFW = 2048


def emit_p3(P, yaT, vbT, zbT, ycT, gT, xT, wb, wo, sng, lng, lnb, modv, xnT):
    banks = P.get_banks()
    m0 = P.mark()
    ones_f = P.sb("ones_f", [128, 128], F32)
    ones_b = P.sb("ones_b", [128, 128], BF16)
    sng_sb = P.sb("sng", [128, 32], F32)
    lng_sb = P.sb("lng", [128, 16], F32)
    lnb_sb = P.sb("lnb", [128, 16], F32)
    m_sb = P.sb("m3", [128, 2, 3, KC], F32)
    P.op("pool", lambda e: e.memset(ones_f[:, :], 1.0), w=["ones_f"])
    P.op("pool", lambda e: e.memset(ones_b[:, :], 1.0), w=["ones_b"])
    P.dma("sp", sng_sb[:, :], sng, w=["sng"])
    P.dma("sp", lng_sb[:, :], lng, w=["lng"])
    P.dma("sp", lnb_sb[:, :], lnb, w=["lnb"])
    P.dma("sp", m_sb[:, :, :, :], modv, w=["m3"])
    ya = P.sb("ya", [128, 32, HT], BF16)
    vb = P.sb("vb", [128, 16, HT], BF16)
    zb = P.sb("zb", [128, 16, HT], BF16)
    yc = P.sb("yc", [128, 16, HT], BF16)
    mg = P.sb("mg", [128, 32, HT], BF16)
    wst = [P.sb("wst%d" % i, [128, 16, 128], F32) for i in range(2)]
    wbf = [P.sb("wbf%d" % i, [128, 64, 128], BF16) for i in range(2)]
    gsb = [P.sb("gsb%d" % i, [128, 3, HT], BF16) for i in range(2)]
    sg = [P.sb("sg%d" % i, [128, 3, HT], BF16) for i in range(2)]
    t1 = [P.sb("t1_%d" % i, [128, HT], F32) for i in range(3)]
    t2 = [P.sb("t2_%d" % i, [128, HT], F32) for i in range(3)]
    stat = P.sb("stat", [128, HT], F32)
    rstd = P.sb("rstd3", [128, HT], F32)
    mean = P.sb("mean3", [128, HT], F32)
    xin = [P.sb("xin%d" % i, [128, HT], F32) for i in range(2)]
    xo = [P.sb("xo%d" % i, [128, HT], F32) for i in range(2)]
    cnt = {"w": 0, "t": 0, "x": 0, "cast": 0}

    def stat_reduce(src_fn, nchunks, use_f32, key_r):
        for kc in range(nchunks):
            rhs_ap, rkeys = src_fn(kc)
            for (a, b_, j) in P3_TILES:
                bk = 0 if j == 0 else 1
                P.op("pe", lambda e, kc=kc, a=a, b_=b_, bk=bk, rhs_ap=rhs_ap: e.matmul(
                    banks[bk][:, 0:b_ - a], lhsT=(ones_f if use_f32 else ones_b)[:, :], rhs=rhs_ap[:, a:b_],
                    start=(kc == 0), stop=(kc == nchunks - 1)),
                    r=["ones_f", "ones_b"] + rkeys, w=["bank%d" % bk])

    def bank_to(dst, scale, bias, func):
        for (a, b_, j) in P3_TILES:
            bk = 0 if j == 0 else 1
            P.op("act", lambda e, a=a, b_=b_, bk=bk: e.activation(
                out=dst[:, a:b_], in_=banks[bk][:, 0:b_ - a], func=func, bias=bias, scale=scale),
                r=["bank%d" % bk], w=[dst.name])

    def load_weights(src4, nch, blk):
        wi = cnt["w"] % 2
        cnt["w"] += 1
        for q in range(nch // 16):
            si = cnt["cast"] % 2
            cnt["cast"] += 1
            P.dma("sp", wst[si][:, :, :], src4[blk, :, q * 16:(q + 1) * 16, :], w=["wst%d" % si])
            P.op("act", lambda e, si=si, wi=wi, q=q: e.copy(
                out=wbf[wi][:, q * 16:(q + 1) * 16, :], in_=wst[si][:, :, :]), r=["wst%d" % si], w=["wbf%d" % wi])
        return wi

    for hf in range(2):
        P.dma("sp", ya[:, :, :], yaT[hf].rearrange("(k p) t -> p k t", p=128), w=["ya"])
        P.dma("sp", vb[:, :, :], vbT[hf].rearrange("(k p) t -> p k t", p=128), w=["vb"])
        P.dma("sp", zb[:, :, :], zbT[hf].rearrange("(k p) t -> p k t", p=128), w=["zb"])
        P.dma("sp", yc[:, :, :], ycT[hf].rearrange("(k p) t -> p k t", p=128), w=["yc"])
        def sq_a(kc):
            i = cnt["t"] % 3
            cnt["t"] += 1
            P.op("act", lambda e, kc=kc, i=i: e.activation(out=t1[i][:, :], in_=ya[:, kc, :], func=AF.Square),
                 r=["ya"], w=[t1[i].name])
            return t1[i], [t1[i].name]
        stat_reduce(sq_a, 32, True, None)
        bank_to(stat, 1.0 / D, EPS, AF.Sqrt)
        P.op("dve", lambda e: e.reciprocal(out=rstd[:, :], in_=stat[:, :]), r=[stat.name], w=[rstd.name])
        for kc in range(32):
            P.op("dve", lambda e, kc=kc: e.scalar_tensor_tensor(
                out=ya[:, kc, :], in0=ya[:, kc, :], scalar=sng_sb[:, kc:kc + 1], in1=rstd[:, :],
                op0=ALU.mult, op1=ALU.mult), r=["ya", "sng", rstd.name], w=["ya"])
        stat_reduce(lambda kc: (vb[:, kc, :], ["vb"]), 16, False, None)
        bank_to(mean, 1.0 / CFW, 0.0, AF.Identity)

        def sq_b(kc):
            i = cnt["t"] % 3
            cnt["t"] += 1
            P.op("dve", lambda e, kc=kc, i=i: e.tensor_tensor(out=t2[i][:, :], in0=vb[:, kc, :], in1=mean[:, :],
                                                               op=ALU.subtract), r=["vb", mean.name], w=[t2[i].name])
            P.op("act", lambda e, i=i: e.activation(out=t1[i][:, :], in_=t2[i][:, :], func=AF.Square),
                 r=[t2[i].name], w=[t1[i].name])
            return t1[i], [t1[i].name]
        stat_reduce(sq_b, 16, True, None)
        bank_to(stat, 1.0 / CFW, EPS, AF.Sqrt)
        P.op("dve", lambda e: e.reciprocal(out=rstd[:, :], in_=stat[:, :]), r=[stat.name], w=[rstd.name])
        for kc in range(16):
            i = cnt["t"] % 3
            cnt["t"] += 1
            P.op("dve", lambda e, kc=kc, i=i: e.tensor_tensor(out=t2[i][:, :], in0=vb[:, kc, :], in1=mean[:, :],
                                                               op=ALU.subtract), r=["vb", mean.name], w=[t2[i].name])
            P.op("dve", lambda e, i=i: e.tensor_tensor(out=t2[i][:, :], in0=t2[i][:, :], in1=rstd[:, :], op=ALU.mult),
                 r=[t2[i].name, rstd.name], w=[t2[i].name])
            P.op("act", lambda e, kc=kc, i=i: e.activation(out=t2[i][:, :], in_=t2[i][:, :], func=AF.Silu,
                                                            bias=lnb_sb[:, kc:kc + 1], scale=lng_sb[:, kc:kc + 1]),
                 r=[t2[i].name, "lng", "lnb"], w=[t2[i].name])
            P.op("act", lambda e, kc=kc, i=i: e.activation(out=t1[i][:, :], in_=zb[:, kc, :], func=AF.Silu),
                 r=["zb"], w=[t1[i].name])
            P.op("dve", lambda e, kc=kc, i=i: e.tensor_tensor(out=vb[:, kc, :], in0=t2[i][:, :], in1=t1[i][:, :],
                                                               op=ALU.mult), r=[t1[i].name, t2[i].name], w=["vb"])
        gv = gT[hf].rearrange("(j n p) t -> n p j t", j=3, p=128)
        for n in range(32):
            wi = load_weights(wb, 64, n)
            gi = n % 2
            P.dma("sp", gsb[gi][:, :, :], gv[n], w=[gsb[gi].name])
            P.op("act", lambda e, gi=gi: e.activation(out=sg[gi][:, :, :], in_=gsb[gi][:, :, :], func=AF.Sigmoid),
                 r=[gsb[gi].name], w=[sg[gi].name])
            bo = 4 * (n % 2)
            srcs = [(ya, "ya", 0, 32), (vb, "vb", 32, 16), (yc, "yc", 48, 16)]
            for bi, (src, sk, c0, ncc) in enumerate(srcs):
                for kc in range(ncc):
                    for (a, b_, j) in P3_TILES:
                        if j == 0:
                            dst = banks[bo + bi][:, 0:512]
                            bk = bo + bi
                        else:
                            dst = banks[bo + 3][:, 16 * bi:16 * bi + 16]
                            bk = bo + 3
                        P.op("pe", lambda e, dst=dst, wi=wi, c0=c0, kc=kc, src=src, a=a, b_=b_, ncc=ncc: e.matmul(
                            dst, lhsT=wbf[wi][:, c0 + kc, :], rhs=src[:, kc, a:b_],
                            start=(kc == 0), stop=(kc == ncc - 1)),
                            r=["wbf%d" % wi, sk], w=["bank%d" % bk])
            i = cnt["t"] % 3
            cnt["t"] += 1
            for (a, b_, j) in P3_TILES:
                def pv(bi):
                    if j == 0:
                        return banks[bo + bi][:, 0:512], "bank%d" % (bo + bi)
                    return banks[bo + 3][:, 16 * bi:16 * bi + 16], "bank%d" % (bo + 3)
                pa, ka = pv(0)
                pb, kb = pv(1)
                pc, kc_ = pv(2)
                P.op("dve", lambda e, pa=pa, gi=gi, i=i, a=a, b_=b_: e.tensor_tensor(
                    out=t1[i][:, a:b_], in0=pa, in1=sg[gi][:, 0, a:b_], op=ALU.mult),
                    r=[ka, sg[gi].name], w=[t1[i].name])
                P.op("dve", lambda e, pb=pb, gi=gi, i=i, a=a, b_=b_: e.tensor_tensor(
                    out=t2[i][:, a:b_], in0=pb, in1=sg[gi][:, 1, a:b_], op=ALU.mult),
                    r=[kb, sg[gi].name], w=[t2[i].name])
                P.op("dve", lambda e, i=i, a=a, b_=b_: e.tensor_tensor(
                    out=t1[i][:, a:b_], in0=t1[i][:, a:b_], in1=t2[i][:, a:b_], op=ALU.add),
                    r=[t1[i].name, t2[i].name], w=[t1[i].name])
                P.op("dve", lambda e, pc=pc, gi=gi, i=i, a=a, b_=b_: e.tensor_tensor(
                    out=t2[i][:, a:b_], in0=pc, in1=sg[gi][:, 2, a:b_], op=ALU.mult),
                    r=[kc_, sg[gi].name], w=[t2[i].name])
                P.op("dve", lambda e, i=i, a=a, b_=b_, n=n: e.tensor_tensor(
                    out=mg[:, n, a:b_], in0=t1[i][:, a:b_], in1=t2[i][:, a:b_], op=ALU.add),
                    r=[t1[i].name, t2[i].name], w=["mg"])
        xv = xT[hf].rearrange("(k p) t -> k p t", p=128)
        ov = xnT[hf].rearrange("(k p) t -> k p t", p=128)
        for m in range(32):
            wi = load_weights(wo, 32, m)
            xi = m % 2
            P.dma("sp", xin[xi][:, :], xv[m], w=[xin[xi].name])
            bo = 4 * (m % 2)
            for kc in range(32):
                for (a, b_, j) in P3_TILES:
                    bk = bo + (0 if j == 0 else 1)
                    P.op("pe", lambda e, bk=bk, wi=wi, kc=kc, a=a, b_=b_: e.matmul(
                        banks[bk][:, 0:b_ - a], lhsT=wbf[wi][:, kc, :], rhs=mg[:, kc, a:b_],
                        start=(kc == 0), stop=(kc == 31)), r=["wbf%d" % wi, "mg"], w=["bank%d" % bk])
            for (a, b_, j) in P3_TILES:
                bk = bo + (0 if j == 0 else 1)
                P.op("dve", lambda e, bk=bk, xi=xi, a=a, b_=b_, j=j, m=m: e.scalar_tensor_tensor(
                    out=xo[xi][:, a:b_], in0=banks[bk][:, 0:b_ - a], scalar=m_sb[:, j, 2, m:m + 1],
                    in1=xin[xi][:, a:b_], op0=ALU.mult, op1=ALU.add),
                    r=["bank%d" % bk, xin[xi].name, "m3"], w=[xo[xi].name])
            P.dma("pool", ov[m], xo[xi][:, :], r=[xo[xi].name])
    P.release(m0)


def build_p3():
    P = Prog()
    yaT = P.dram("yaT", [2, D, HT], BF16, "ExternalInput")
    vbT = P.dram("vbT", [2, CFW, HT], BF16, "ExternalInput")
    zbT = P.dram("zbT", [2, CFW, HT], BF16, "ExternalInput")
    ycT = P.dram("ycT", [2, CFW, HT], BF16, "ExternalInput")
    gT = P.dram("gT", [2, 3 * D, HT], BF16, "ExternalInput")
    xT = P.dram("xT", [2, D, HT], F32, "ExternalInput")
    wb = P.dram("wb", [32, 128, 64, 128], F32, "ExternalInput")
    wo = P.dram("wo", [32, 128, 32, 128], F32, "ExternalInput")
    sng = P.dram("sng", [128, 32], F32, "ExternalInput")
    lng = P.dram("lng", [128, 16], F32, "ExternalInput")
    lnb = P.dram("lnb", [128, 16], F32, "ExternalInput")
    modv = P.dram("modv", [128, 2, 3, KC], F32, "ExternalInput")
    xnT = P.dram("xnT", [2, D, HT], F32, "ExternalOutput")
    emit_p3(P, yaT, vbT, zbT, ycT, gT, xT, wb, wo, sng, lng, lnb, modv, xnT)
    return P.finish()


def w_blocks(w):
    K_, N_ = w.shape
    return np.ascontiguousarray(w.reshape(K_ // 128, 128, N_ // 128, 128).transpose(2, 1, 0, 3))


def halves_T(x2d, c2d, r, dt=None):
    out = []
    for h in range(2):
        t = np.concatenate([x2d[r * TOK_X + 512 * h: r * TOK_X + 512 * (h + 1)],
                            c2d[r * TOK_C + 16 * h: r * TOK_C + 16 * (h + 1)]], axis=0)
        out.append(t.T)
    a = np.ascontiguousarray(np.stack(out))
    return a if dt is None else a.astype(dt)


NROW = SEQ // 64


def emit_conformer(P, fm_src, cfw, cfb, vb_out):
    banks = P.get_banks()
    m0 = P.mark()
    w_sb = P.sb("cfw", [128, 2, 31], F32)
    b_sb = P.sb("cfb", [128, 2], F32)
    onesf = P.sb("cf1", [128, 128], F32)
    idf = P.sb("cfid", [128, 128], F32)
    dg = P.sb("cfdg", [128, 31, 128], BF16)
    a_sb = P.sb("cfa", [128, NTOK], BF16)
    g_sb = P.sb("cfg", [128, NTOK], BF16)
    sgm = P.sb("cfs", [128, NTOK], BF16)
    upx = P.sb("upx", [128, NROW, 94], BF16)
    upc = P.sb("upc", [128, CTX + 30], BF16)
    ob = P.sb("cfob", [128, NTOK], BF16)
    P.dma("sp", w_sb[:, :, :], cfw, w=["cfw"])
    P.dma("sp", b_sb[:, :], cfb, w=["cfb"])
    P.op("pool", lambda e: e.memset(onesf[:, :], 1.0), w=["cf1"])
    P.op("pool", lambda e: e.affine_select(out=idf[:, :], in_=onesf[:, :], pattern=[[-1, 128]], compare_op=ALU.is_equal,
                                           fill=0.0, base=0, channel_multiplier=1), r=["cf1"], w=["cfid"])
    P.op("pool", lambda e: e.memset(upx[:, :, :], 0.0), w=["upx"])
    P.op("pool", lambda e: e.memset(upc[:, :], 0.0), w=["upc"])
    nb = 0
    for cc in range(2):
        P.dma("sp", a_sb[:, :], fm_src(10 + cc), w=["cfa"])
        P.dma("sp", g_sb[:, :], fm_src(12 + cc), w=["cfg"])
        P.op("dve", lambda e, cc=cc: e.tensor_tensor(
            out=dg[:, :, :], in0=idf[:, :].unsqueeze(1).broadcast_to([128, 31, 128]),
            in1=w_sb[:, cc, :].unsqueeze(2).broadcast_to([128, 31, 128]), op=ALU.mult),
            r=["cfid", "cfw"], w=["cfdg"])
        P.op("act", lambda e: e.activation(out=sgm[:, :], in_=g_sb[:, :], func=AF.Sigmoid), r=["cfg"], w=["cfs"])
        P.op("dve", lambda e: e.tensor_tensor(out=upc[:, 15:15 + CTX], in0=a_sb[:, 0:CTX], in1=sgm[:, 0:CTX], op=ALU.mult),
             r=["cfa", "cfs"], w=["upc"])
        P.op("dve", lambda e: e.tensor_tensor(
            out=upx[:, :, 15:79], in0=a_sb[:, CTX:].rearrange("p (r t) -> p r t", t=64),
            in1=sgm[:, CTX:].rearrange("p (r t) -> p r t", t=64), op=ALU.mult), r=["cfa", "cfs"], w=["upx"])
        tiles = [("c", 0)] + [("x", r0) for r0 in range(0, NROW, 8)]
        for (kind, r0) in tiles:
            bk = nb % 4
            nb += 1
            for k in range(31):
                if kind == "c":
                    P.op("pe", lambda e, bk=bk, k=k: e.matmul(
                        banks[bk][:, 0:CTX], lhsT=dg[:, k, :], rhs=upc[:, k:k + CTX], start=(k == 0), stop=(k == 30)),
                        r=["cfdg", "upc"], w=["bank%d" % bk])
                else:
                    P.op("pe", lambda e, bk=bk, k=k, r0=r0: e.matmul(
                        banks[bk][:, :].rearrange("p (r t) -> p r t", t=64), lhsT=dg[:, k, :],
                        rhs=upx[:, r0:r0 + 8, k:k + 64], start=(k == 0), stop=(k == 30)),
                        r=["cfdg", "upx"], w=["bank%d" % bk])
            if kind == "c":
                lo, n_ = 0, CTX
            else:
                lo, n_ = CTX + r0 * 64, 512
            if nb % 2:
                P.op("act", lambda e, bk=bk, lo=lo, n_=n_, cc=cc: e.activation(
                    out=ob[:, lo:lo + n_], in_=banks[bk][:, 0:n_], func=AF.Identity, bias=b_sb[:, cc:cc + 1], scale=1.0),
                    r=["bank%d" % bk, "cfb"], w=["cfob"])
            else:
                P.op("dve", lambda e, bk=bk, lo=lo, n_=n_, cc=cc: e.tensor_scalar(
                    out=ob[:, lo:lo + n_], in0=banks[bk][:, 0:n_], scalar1=b_sb[:, cc:cc + 1], scalar2=None, op0=ALU.add),
                    r=["bank%d" % bk, "cfb"], w=["cfob"])
        P.dma("pool", vb_out[cc * 128:(cc + 1) * 128, :], ob[:, :], r=["cfob"])
    P.release(m0)


def emit_trig_table(P, dst, nrows, ncols, N, row_base, kind, name, scale=1.0):
    ip = P.sb(name + "_ip", [128, ncols], I32)
    ij = P.sb(name + "_ij", [128, ncols], I32)
    fp = P.sb(name + "_fp", [128, ncols], F32)
    fj = P.sb(name + "_fj", [128, ncols], F32)
    P.op("pool", lambda e: e.iota(ip[:, :], pattern=[[0, ncols]], base=row_base, channel_multiplier=1), w=[ip.name])
    P.op("pool", lambda e: e.iota(ij[:, :], pattern=[[1, ncols]], base=0, channel_multiplier=0), w=[ij.name])
    P.op("dve", lambda e: e.tensor_tensor(out=ip[:, :], in0=ip[:, :], in1=ij[:, :], op=ALU.mult),
         r=[ip.name, ij.name], w=[ip.name])
    off = N // 2 if kind == "sin" else 3 * N // 4
    P.op("dve", lambda e: e.tensor_scalar(out=ip[:, :], in0=ip[:, :], scalar1=float(off), scalar2=None,
                                          op0=ALU.add), r=[ip.name], w=[ip.name])
    P.op("dve", lambda e: e.tensor_scalar(out=ip[:, :], in0=ip[:, :], scalar1=int(N - 1), scalar2=None,
                                          op0=ALU.bitwise_and), r=[ip.name], w=[ip.name])
    P.op("dve", lambda e: e.tensor_scalar(out=fp[:, :], in0=ip[:, :], scalar1=float(-N / 2), scalar2=None,
                                          op0=ALU.add), r=[ip.name], w=[fp.name])
    P.op("act", lambda e: e.activation(out=fj[:, :], in_=fp[:, :], func=AF.Sin, scale=float(2 * np.pi / N)),
         r=[fp.name], w=[fj.name])
    P.op("dve", lambda e: e.tensor_scalar(out=dst, in0=fj[0:nrows, :], scalar1=float(scale), scalar2=None,
                                          op0=ALU.mult), r=[fj.name], w=[name])


def emit_fourier(P, tm_vz, t1s, yc_out):
    banks = P.get_banks()
    m0 = P.mark()
    c128 = P.sb("c128", [128, 128], BF16)
    s128 = P.sb("s128", [128, 128], BF16)
    twc = P.sb("twc", [128, 64], F32)
    tws = P.sb("tws", [128, 64], F32)
    ra = P.sb("ra", [64, 128], BF16)
    rm = P.sb("rm", [64, 128], BF16)
    c256 = P.sb("c256", [128, 2, 256], BF16)
    s256 = P.sb("s256", [128, 2, 256], BF16)
    ns256 = P.sb("ns256", [128, 2, 256], BF16)
    mt = P.mark()
    emit_trig_table(P, c128[:, :], 128, 128, 128, 0, "cos", "c128")
    emit_trig_table(P, s128[:, :], 128, 128, 128, 0, "sin", "s128")
    emit_trig_table(P, twc[:, :], 128, 64, 8192, 0, "cos", "twc")
    emit_trig_table(P, tws[:, :], 128, 64, 8192, 0, "sin", "tws")
    emit_trig_table(P, ra[:, 0:64], 64, 64, 64, 0, "cos", "ra")
    emit_trig_table(P, ra[:, 64:128], 64, 64, 64, 0, "sin", "ra")
    emit_trig_table(P, rm[:, 0:64], 64, 64, 64, 0, "sin", "rm", scale=-1.0)
    emit_trig_table(P, rm[:, 64:128], 64, 64, 64, 0, "cos", "rm")
    for cc in range(2):
        emit_trig_table(P, c256[:, cc, :], 128, 256, 256, cc * 128, "cos", "c256")
        emit_trig_table(P, s256[:, cc, :], 128, 256, 256, cc * 128, "sin", "s256")
        emit_trig_table(P, ns256[:, cc, :], 128, 256, 256, cc * 128, "sin", "ns256", scale=-1.0)
    P.release(mt)

    usb = P.sb("usb", [128, 2, 2, 64, 128], BF16)
    zt = [P.sb("zt%d" % i, [128, 256], BF16) for i in range(2)]
    zs = [P.sb("zs%d" % i, [128, 256], F32) for i in range(2)]
    yo = [P.sb("yo%d" % i, [128, 256], BF16) for i in range(2)]
    cnt = {"e": 0, "z": 0}

    def stage3(ntiles, tok_base, norm):
        for tl in range(ntiles):
            bk = 6 + (tl % 2)
            k = 0
            for cc in range(2):
                for ri in range(2):
                    rhs = (c256 if ri == 0 else ns256)[:, cc, :]
                    P.op("pe", lambda e, bk=bk, cc=cc, ri=ri, tl=tl, rhs=rhs, k=k: e.matmul(
                        banks[bk][:, 0:256], lhsT=usb[:, cc, ri, tl, :], rhs=rhs, start=(k == 0), stop=(k == 3)),
                        r=["usb", "c256", "ns256"], w=["bank%d" % bk])
                    k += 1
            zi = cnt["z"] % 2
            cnt["z"] += 1
            r0 = tok_base + tl * 128
            P.dma("sp", zt[zi][:, :], tm_vz[r0:r0 + 128, 256:512], w=[zt[zi].name])
            P.op("act", lambda e, zi=zi: e.activation(out=zs[zi][:, :], in_=zt[zi][:, :], func=AF.Silu),
                 r=[zt[zi].name], w=[zs[zi].name])
            P.op("dve", lambda e, zi=zi, bk=bk: e.scalar_tensor_tensor(
                out=yo[zi][:, :], in0=banks[bk][:, 0:256], scalar=float(norm), in1=zs[zi][:, :],
                op0=ALU.mult, op1=ALU.mult), r=["bank%d" % bk, zs[zi].name], w=[yo[zi].name])
            P.dma("pool", yc_out[r0:r0 + 128, :], yo[zi][:, :], r=[yo[zi].name])

    mc = P.mark()
    vc = P.sb("vc", [128, 2, 256], BF16)
    P.dma("sp", vc[:, :, :], tm_vz[0:CTX, 0:256].rearrange("(t p) c -> p t c", p=128), w=["vc"])
    for cc in range(2):
        for half, tab in enumerate([c256, s256]):
            bk = 2 * cc + half
            for tl in range(2):
                P.op("pe", lambda e, bk=bk, cc=cc, tl=tl, tab=tab: e.matmul(
                    banks[bk][:, 0:256], lhsT=vc[:, tl, cc * 128:(cc + 1) * 128], rhs=tab[:, tl, :],
                    start=(tl == 0), stop=(tl == 1)), r=["vc", "c256", "s256"], w=["bank%d" % bk])
            P.op("act", lambda e, bk=bk, cc=cc, half=half: e.copy(
                out=usb[:, cc, half, 0:2, :], in_=banks[bk][:, 0:256].rearrange("p (t s) -> p t s", s=128)),
                r=["bank%d" % bk], w=["usb"])
    stage3(2, 0, 1.0 / 256.0)
    P.release(mc)

    ms = P.mark()
    xr = P.sb("xr", [128, 64, 256], BF16)
    t1sb = P.sb("t1sb", [128, 64, 2, 256], BF16)
    tmpa = [P.sb("tmpa%d" % i, [128, 256], F32) for i in range(2)]
    P.dma("sp", xr[:, :, :], tm_vz[CTX:, 0:256].rearrange("(a b) c -> a b c", b=64), w=["xr"])
    for ct in range(32):
        bo = 2 * (ct % 2)
        P.op("pe", lambda e, ct=ct, bo=bo: e.matmul(
            banks[bo][:, :], lhsT=c128[:, :], rhs=xr[:, 2 * ct:2 * ct + 2, :], start=True, stop=True),
            r=["c128", "xr"], w=["bank%d" % bo])
        P.op("pe", lambda e, ct=ct, bo=bo: e.matmul(
            banks[bo + 1][:, :], lhsT=s128[:, :], rhs=xr[:, 2 * ct:2 * ct + 2, :], start=True, stop=True),
            r=["s128", "xr"], w=["bank%d" % (bo + 1)])
        for q in range(2):
            s2 = 2 * ct + q
            tr = banks[bo][:, q * 256:(q + 1) * 256]
            tn = banks[bo + 1][:, q * 256:(q + 1) * 256]
            ti = cnt["e"] % 2
            cnt["e"] += 1
            P.op("dve", lambda e, tn=tn, ti=ti, s2=s2: e.tensor_scalar(
                out=tmpa[ti][:, :], in0=tn, scalar1=tws[:, s2:s2 + 1], scalar2=-1.0, op0=ALU.mult, op1=ALU.mult),
                r=["bank%d" % (bo + 1), "tws"], w=[tmpa[ti].name])
            P.op("dve", lambda e, tr=tr, ti=ti, s2=s2: e.scalar_tensor_tensor(
                out=t1sb[:, s2, 0, :], in0=tr, scalar=twc[:, s2:s2 + 1], in1=tmpa[ti][:, :],
                op0=ALU.mult, op1=ALU.add), r=["bank%d" % bo, "twc", tmpa[ti].name], w=["t1sb"])
            ti = cnt["e"] % 2
            cnt["e"] += 1
            P.op("dve", lambda e, tr=tr, ti=ti, s2=s2: e.tensor_scalar(
                out=tmpa[ti][:, :], in0=tr, scalar1=tws[:, s2:s2 + 1], scalar2=None, op0=ALU.mult),
                r=["bank%d" % bo, "tws"], w=[tmpa[ti].name])
            P.op("dve", lambda e, tn=tn, ti=ti, s2=s2: e.scalar_tensor_tensor(
                out=t1sb[:, s2, 1, :], in0=tn, scalar=twc[:, s2:s2 + 1], in1=tmpa[ti][:, :],
                op0=ALU.mult, op1=ALU.add), r=["bank%d" % (bo + 1), "twc", tmpa[ti].name], w=["t1sb"])
    P.dma("sp", t1s, t1sb[:, :, :, :], r=["t1sb"], w=["t1s"])
    P.release(ms)
    m2 = P.mark()
    t2 = [P.sb("t2in%d" % i, [64, 32, 2, 256], BF16) for i in range(2)]
    t1v = t1s.rearrange("a b r c -> b a r c")
    ne = 0
    for gq in range(4):
        bi = gq % 2
        P.dma("sp", t2[bi][:, :, :, :], t1v[:, gq * 32:(gq + 1) * 32, :, :], r=["t1s"], w=[t2[bi].name])
        for sl in range(32):
            s1p = gq * 32 + sl
            for cc in range(2):
                bk = ne % 4
                P.op("pe", lambda e, bk=bk, bi=bi, sl=sl, cc=cc: e.matmul(
                    banks[bk][:, 0:128], lhsT=t2[bi][:, sl, 0, cc * 128:(cc + 1) * 128], rhs=ra[:, :],
                    start=True, stop=False), r=[t2[bi].name, "ra"], w=["bank%d" % bk])
                P.op("pe", lambda e, bk=bk, bi=bi, sl=sl, cc=cc: e.matmul(
                    banks[bk][:, 0:128], lhsT=t2[bi][:, sl, 1, cc * 128:(cc + 1) * 128], rhs=rm[:, :],
                    start=False, stop=True), r=[t2[bi].name, "rm"], w=["bank%d" % bk])
                eng = "act" if ne % 2 else "dve"
                ne += 1
                src = banks[bk][:, 0:128].rearrange("p (r s) -> p r s", s=64)
                if eng == "act":
                    P.op("act", lambda e, cc=cc, s1p=s1p, src=src: e.copy(out=usb[:, cc, :, :, s1p], in_=src),
                         r=["bank%d" % bk], w=["usb"])
                else:
                    P.op("dve", lambda e, cc=cc, s1p=s1p, src=src: e.tensor_copy(out=usb[:, cc, :, :, s1p], in_=src),
                         r=["bank%d" % bk], w=["usb"])
    stage3(64, CTX, 1.0 / np.sqrt(8192.0 * 256.0))
    P.release(m2)
    P.release(m0)


NCHUNK = NTOK // 128
SSD_DEBUG = None
SSD_STEP = 99


def emit_ssd(P, fm_src, za_src, tm_dt, ssdu, yfs, cw, cb, dtb, alog, dsk, ya_out):
    banks = P.get_banks()
    m0 = P.mark()
    cw_sb = P.sb("cw", [128, 6, 5], F32)
    cb_sb = P.sb("cb", [128, 6], F32)
    P.dma("sp", cw_sb[:, :, :], cw, w=["cw"])
    P.dma("sp", cb_sb[:, :], cb, w=["cb"])
    m1 = P.mark()
    pc = [P.sb("pc%d" % i, [128, CTX + 4], BF16) for i in range(2)]
    px = [P.sb("px%d" % i, [128, SEQ + 4], BF16) for i in range(2)]
    ub = [P.sb("cub%d" % i, [128, NTOK], BF16) for i in range(2)]
    c1 = P.sb("cv1", [128, 128], F32)
    cid = P.sb("cvid", [128, 128], F32)
    dg5 = [P.sb("dg5_%d" % i, [128, 5, 128], BF16) for i in range(2)]
    P.op("pool", lambda e: e.memset(c1[:, :], 1.0), w=["cv1"])
    P.op("pool", lambda e: e.affine_select(out=cid[:, :], in_=c1[:, :], pattern=[[-1, 128]], compare_op=ALU.is_equal,
                                           fill=0.0, base=0, channel_multiplier=1), r=["cv1"], w=["cvid"])
    for i in range(2):
        P.op("pool", lambda e, i=i: e.memset(pc[i][:, :], 0.0), w=[pc[i].name])
        P.op("pool", lambda e, i=i: e.memset(px[i][:, :], 0.0), w=[px[i].name])
    nbk = 0
    for c in range(6):
        b = c % 2
        src = fm_src(c)
        P.dma("sp", pc[b][:, 2:2 + CTX], src[:, 0:CTX], w=[pc[b].name])
        P.dma("sp", px[b][:, 2:2 + SEQ], src[:, CTX:], w=[px[b].name])
        P.op("dve", lambda e, c=c, b=b: e.tensor_tensor(
            out=dg5[b][:, :, :], in0=cid[:, :].unsqueeze(1).broadcast_to([128, 5, 128]),
            in1=cw_sb[:, c, :].unsqueeze(2).broadcast_to([128, 5, 128]), op=ALU.mult),
            r=["cvid", "cw"], w=[dg5[b].name])
        tiles = [(pc[b], 0, 0, CTX)] + [(px[b], t_, CTX + t_, 512) for t_ in range(0, SEQ, 512)]
        for (pb, so, lo, n_) in tiles:
            bk = nbk % 4
            nbk += 1
            for k in range(5):
                P.op("pe", lambda e, bk=bk, k=k, pb=pb, so=so, n_=n_, b=b: e.matmul(
                    banks[bk][:, 0:n_], lhsT=dg5[b][:, k, :], rhs=pb[:, so + k:so + k + n_],
                    start=(k == 0), stop=(k == 4)), r=[dg5[b].name, pb.name], w=["bank%d" % bk])
            P.op("act", lambda e, bk=bk, lo=lo, n_=n_, c=c, b=b: e.activation(
                out=ub[b][:, lo:lo + n_], in_=banks[bk][:, 0:n_], func=AF.Silu, bias=cb_sb[:, c:c + 1], scale=1.0),
                r=["bank%d" % bk, "cb"], w=[ub[b].name])
        P.dma("pool", ssdu[c], ub[b][:, :], r=[ub[b].name], w=["ssdu"])
    P.release(m1)
    if SSD_DEBUG == "conv":
        P.release(m0)
        return
    dt_all = P.sb("dt_all", [128, NCHUNK, 16], F32)
    a_all = P.sb("a_all", [128, NCHUNK, 16], F32)
    dtb_sb = P.sb("dtb", [128, 16], F32)
    nex = P.sb("nex", [128, 16], F32)
    dsk_sb = P.sb("dsk", [128, 4], F32)
    P.dma("sp", dtb_sb[:, :], dtb, w=["dtb"])
    P.dma("sp", nex[:, :], alog, w=["nex"])
    P.dma("sp", dsk_sb[:, :], dsk, w=["dsk"])
    m2 = P.mark()
    xr_ = P.sb("dtx", [128, NCHUNK, 16], F32)
    ax_ = P.sb("dtax", [128, NCHUNK, 16], F32)
    P.dma("sp", xr_[:, :, :], tm_dt.rearrange("(n p) c -> p n c", p=128), w=["dtx"])
    P.op("dve", lambda e: e.tensor_tensor(out=xr_[:, :, :], in0=xr_[:, :, :],
                                          in1=dtb_sb[:, :].unsqueeze(1).broadcast_to([128, NCHUNK, 16]), op=ALU.add),
         r=["dtx", "dtb"], w=["dtx"])
    P.op("act", lambda e: e.activation(out=ax_[:, :, :], in_=xr_[:, :, :], func=AF.Abs), r=["dtx"], w=["dtax"])
    P.op("act", lambda e: e.activation(out=ax_[:, :, :], in_=ax_[:, :, :], func=AF.Exp, scale=-1.0),
         r=["dtax"], w=["dtax"])
    P.op("act", lambda e: e.activation(out=ax_[:, :, :], in_=ax_[:, :, :], func=AF.Ln, bias=1.0, scale=1.0),
         r=["dtax"], w=["dtax"])
    P.op("dve", lambda e: e.tensor_scalar_max(out=xr_[:, :, :], in0=xr_[:, :, :], scalar1=0.0), r=["dtx"], w=["dtx"])
    P.op("dve", lambda e: e.tensor_tensor(out=dt_all[:, :, :], in0=xr_[:, :, :], in1=ax_[:, :, :], op=ALU.add),
         r=["dtx", "dtax"], w=["dt_all"])
    P.op("act", lambda e: e.activation(out=nex[:, :], in_=nex[:, :], func=AF.Exp), r=["nex"], w=["nex"])
    P.op("dve", lambda e: e.scalar_tensor_tensor(
        out=a_all[:, :, :], in0=dt_all[:, :, :], scalar=-1.0,
        in1=nex[:, :].unsqueeze(1).broadcast_to([128, NCHUNK, 16]), op0=ALU.mult, op1=ALU.mult),
        r=["dt_all", "nex"], w=["a_all"])
    P.release(m2)
    onesf = P.sb("s_ones", [128, 128], F32)
    oneb = P.sb("s_oneb", [128, 128], BF16)
    idf = P.sb("s_idf", [128, 128], F32)
    idb = P.sb("s_idb", [128, 128], BF16)
    Lf = P.sb("s_Lf", [128, 128], F32)
    Uf = P.sb("s_Uf", [128, 128], F32)
    Lb = P.sb("s_Lb", [128, 128], F32)
    Ub = P.sb("s_Ub", [128, 128], F32)
    P.op("pool", lambda e: e.memset(onesf[:, :], 1.0), w=["s_ones"])
    P.op("pool", lambda e: e.memset(oneb[:, :], 1.0), w=["s_oneb"])

    def sel(dst, src, pat, cm, op):
        P.op("pool", lambda e: e.affine_select(out=dst[:, :], in_=src[:, :], pattern=[[pat, 128]], compare_op=op,
                                               fill=0.0, base=0, channel_multiplier=cm),
             r=[src.name], w=[dst.name])
    sel(idf, onesf, -1, 1, ALU.is_equal)
    sel(idb, oneb, -1, 1, ALU.is_equal)
    sel(Lf, onesf, -1, 1, ALU.is_gt)
    sel(Uf, onesf, 1, -1, ALU.is_ge)
    sel(Lb, onesf, 1, -1, ALU.is_gt)
    sel(Ub, onesf, -1, 1, ALU.is_ge)
    if SSD_DEBUG == "const":
        P.release(m0)
        return
    hst = P.sb("hst", [128, 512], F32)
    hbf = P.sb("hbf", [128, 512], BF16)
    ut = [P.sb("ut%d" % i, [128, 6, 128], BF16) for i in range(2)]
    xdt = P.sb("xdt", [128, 8, 64], BF16)
    xdtd = P.sb("xdtd", [128, 8, 64], BF16)
    xsb = P.sb("xsb", [128, 640], BF16)
    sme = P.sb("sme", [128, 24], F32)
    la = P.sb("la", [128, 8, 128], F32)
    dm = [P.sb("dm%d" % i, [128, 4, 128], BF16) for i in range(2)]
    wm = [P.sb("wm%d" % i, [128, 4, 128], BF16) for i in range(2)]
    mm = P.sb("mm", [128, 128], BF16)
    yt = [P.sb("yt%d" % i, [128, 512], F32) for i in range(2)]
    yf = [P.sb("yf%d" % i, [128, 512], F32) for i in range(2)]
    zat = [P.sb("zat%d" % i, [128, 4, 128], BF16) for i in range(2)]
    sz = P.sb("sz", [128, 4, 128], F32)
    y2 = P.sb("y2", [128, 4, 128], F32)
    yo = [P.sb("yob%d" % i, [128, 4, 128], BF16) for i in range(2)]
    htmp = P.sb("htmp", [128, 512], F32)
    b0 = banks[0][:, :].bitcast(BF16)
    uv = ssdu.rearrange("c p t -> p c t")
    zav = za_src.rearrange("c p t -> p c t")
    yov = ya_out.rearrange("(c p) t -> p c t", p=128)
    it = 0
    for d in range(2):
        order = list(range(NCHUNK)) if d == 0 else [1, 0] + list(range(NCHUNK - 1, 1, -1))
        Lm, Um = (Lf, Uf) if d == 0 else (Lb, Ub)
        P.op("pool", lambda e: e.memset(hst[:, :], 0.0), w=["hst"])
        P.op("pool", lambda e: e.memset(hbf[:, :], 0.0), w=["hbf"])
        for n in order:
            if isinstance(SSD_DEBUG, int) and it >= SSD_DEBUG:
                break
            t0 = n * 128
            ui = it % 2
            it += 1
            u = ut[ui]
            P.dma("sp", u[:, :, :], uv[:, :, t0:t0 + 128], r=["ssdu"], w=[u.name])
            a_c = a_all[:, n, d * 8:(d + 1) * 8]
            dt_c = dt_all[:, n, d * 8:(d + 1) * 8]
            for c in range(4):
                P.op("pe", lambda e, c=c, u=u: e.transpose(out=b0[:, c * 128:(c + 1) * 128], in_=u[:, c, :],
                                                           identity=idb[:, :]), r=[u.name, "s_idb"], w=["bank0"])
            P.op("pe", lambda e, u=u: e.transpose(out=b0[:, 512:640], in_=u[:, 4, :], identity=idb[:, :]),
                 r=[u.name, "s_idb"], w=["bank0"])
            if SSD_STEP < 1:
                continue
            P.op("act", lambda e: e.copy(out=xsb[:, :], in_=b0[:, 0:640]), r=["bank0"], w=["xsb"])
            P.op("dve", lambda e, dt_c=dt_c: e.tensor_tensor(
                out=xdt[:, :, :], in0=xsb[:, 0:512].rearrange("p (h q) -> p h q", q=64),
                in1=dt_c.unsqueeze(2).broadcast_to([128, 8, 64]), op=ALU.mult), r=["xsb", "dt_all"], w=["xdt"])
            if SSD_STEP < 2:
                continue
            for q, lm in enumerate((Um, Lm, onesf)):
                P.op("pe", lambda e, q=q, lm=lm, a_c=a_c: e.matmul(
                    banks[3][:, 128 + 8 * q:136 + 8 * q], lhsT=lm[:, :], rhs=a_c, start=True, stop=True),
                    r=[lm.name, "a_all"], w=["bank3s"])
            P.op("act", lambda e: e.activation(out=sme[:, :], in_=banks[3][:, 128:152], func=AF.Exp),
                 r=["bank3s"], w=["sme"])
            if SSD_STEP < 3:
                continue
            P.op("dve", lambda e, Lm=Lm, a_c=a_c: e.tensor_tensor(
                out=la[:, :, :], in0=Lm[:, :].unsqueeze(1).broadcast_to([128, 8, 128]),
                in1=a_c.unsqueeze(2).broadcast_to([128, 8, 128]), op=ALU.mult), r=[Lm.name, "a_all"], w=["la"])
            for hg in range(2):
                for hh in range(4):
                    h_ = hg * 4 + hh
                    P.op("pe", lambda e, hg=hg, hh=hh, h_=h_, Um=Um: e.matmul(
                        banks[1 + hg][:, hh * 128:(hh + 1) * 128], lhsT=la[:, h_, :], rhs=Um[:, :],
                        start=True, stop=True), r=["la", Um.name], w=["bank%d" % (1 + hg)])
                P.op("act", lambda e, hg=hg: e.activation(
                    out=dm[hg][:, :, :], in_=banks[1 + hg][:, :].rearrange("p (h i) -> p h i", i=128), func=AF.Exp),
                    r=["bank%d" % (1 + hg)], w=[dm[hg].name])
            if SSD_STEP < 4:
                continue
            P.op("pe", lambda e, u=u: e.matmul(banks[3][:, 0:128], lhsT=u[:, 4, :], rhs=u[:, 5, :], start=True, stop=True),
                 r=[u.name], w=["bank3"])
            P.op("dve", lambda e, Um=Um: e.tensor_tensor(out=mm[:, :], in0=banks[3][:, 0:128], in1=Um[:, :], op=ALU.mult),
                 r=["bank3", Um.name], w=["mm"])
            for hg in range(2):
                P.op("dve", lambda e, hg=hg: e.tensor_tensor(
                    out=wm[hg][:, :, :], in0=dm[hg][:, :, :], in1=mm[:, :].unsqueeze(1).broadcast_to([128, 4, 128]),
                    op=ALU.mult), r=[dm[hg].name, "mm"], w=[wm[hg].name])
            if SSD_STEP < 5:
                continue
            for h_ in range(8):
                P.op("pe", lambda e, h_=h_: e.matmul(
                    banks[4][:, h_ * 64:(h_ + 1) * 64], lhsT=wm[h_ // 4][:, h_ % 4, :], rhs=xdt[:, h_, :],
                    start=True, stop=True), r=[wm[h_ // 4].name, "xdt"], w=["bank4"])
            P.op("pe", lambda e, u=u: e.matmul(banks[5][:, :], lhsT=u[:, 5, :], rhs=hbf[:, :], start=True, stop=True),
                 r=[u.name, "hbf"], w=["bank5"])
            P.op("dve", lambda e: e.tensor_tensor(
                out=xdtd[:, :, :], in0=xdt[:, :, :], in1=sme[:, 8:16].unsqueeze(2).broadcast_to([128, 8, 64]),
                op=ALU.mult), r=["xdt", "sme"], w=["xdtd"])
            P.op("pe", lambda e: e.matmul(banks[6][:, :], lhsT=xsb[:, 512:640], rhs=xdtd[:, :, :], start=True, stop=True),
                 r=["xsb", "xdtd"], w=["bank6"])
            if SSD_STEP < 6:
                continue
            P.op("dve", lambda e: e.tensor_tensor(
                out=htmp[:, :].rearrange("p (h q) -> p h q", q=64), in0=hst[:, :].rearrange("p (h q) -> p h q", q=64),
                in1=sme[:, 16:24].unsqueeze(2).broadcast_to([128, 8, 64]), op=ALU.mult),
                r=["hst", "sme"], w=["htmp"])
            P.op("dve", lambda e: e.tensor_tensor(out=hst[:, :], in0=banks[6][:, :], in1=htmp[:, :], op=ALU.add),
                 r=["bank6", "htmp"], w=["hst"])
            P.op("dve", lambda e: e.tensor_copy(out=hbf[:, :], in_=hst[:, :]), r=["hst"], w=["hbf"])
            yi = it % 2
            y_ = yt[yi]
            P.op("dve", lambda e, y_=y_: e.tensor_tensor(
                out=y_[:, :].rearrange("p (h q) -> p h q", q=64),
                in0=banks[5][:, :].rearrange("p (h q) -> p h q", q=64),
                in1=sme[:, 0:8].unsqueeze(2).broadcast_to([128, 8, 64]), op=ALU.mult),
                r=["bank5", "sme"], w=[y_.name])
            P.op("dve", lambda e, y_=y_: e.tensor_tensor(out=y_[:, :], in0=banks[4][:, :], in1=y_[:, :], op=ALU.add),
                 r=["bank4", y_.name], w=[y_.name])
            if d == 0:
                P.dma("pool", yfs[t0:t0 + 128, :], y_[:, :], r=[y_.name], w=["yfs"])
            else:
                f_ = yf[yi]
                z_ = zat[yi]
                o_ = yo[yi]
                P.dma("sp", f_[:, :], yfs[t0:t0 + 128, :], r=["yfs"], w=[f_.name])
                P.dma("sp", z_[:, :, :], zav[:, :, t0:t0 + 128], w=[z_.name])
                P.op("pool", lambda e, y_=y_, f_=f_: e.tensor_tensor(out=y_[:, :], in0=y_[:, :], in1=f_[:, :], op=ALU.add),
                     r=[y_.name, f_.name], w=[y_.name])
                for c in range(4):
                    P.op("pe", lambda e, c=c, y_=y_: e.transpose(
                        out=banks[7][:, c * 128:(c + 1) * 128], in_=y_[:, c * 128:(c + 1) * 128], identity=idf[:, :]),
                        r=[y_.name, "s_idf"], w=["bank7"])
                P.op("act", lambda e, z_=z_: e.activation(out=sz[:, :, :], in_=z_[:, :, :], func=AF.Silu),
                     r=[z_.name], w=["sz"])
                for c in range(4):
                    P.op("dve", lambda e, c=c, u=u: e.scalar_tensor_tensor(
                        out=y2[:, c, :], in0=u[:, c, :], scalar=dsk_sb[:, c:c + 1],
                        in1=banks[7][:, c * 128:(c + 1) * 128], op0=ALU.mult, op1=ALU.add),
                        r=[u.name, "dsk", "bank7"], w=["y2"])
                P.op("pool", lambda e, o_=o_: e.tensor_tensor(out=o_[:, :, :], in0=y2[:, :, :], in1=sz[:, :, :], op=ALU.mult),
                     r=["y2", "sz"], w=[o_.name])
                P.dma("pool", yov[:, :, t0:t0 + 128], o_[:, :, :], r=[o_.name])
    P.release(m0)


def ssd_params_for_core(inp, l, g):
    cwf = inp["ssd_conv_w"][l]
    cbf = inp["ssd_conv_b"][l]
    chans = list(range(512 * g, 512 * g + 512)) + list(range(4096 + 128 * g, 4096 + 128 * g + 128)) + \
        list(range(5120 + 128 * g, 5120 + 128 * g + 128))
    cw = np.ascontiguousarray(cwf[:, chans].reshape(5, 6, 128).transpose(2, 1, 0))
    cb = np.ascontiguousarray(cbf[chans].reshape(6, 128).T)
    hs = slice(8 * g, 8 * g + 8)
    dtb = np.concatenate([inp["ssd_dt_bias"][l][0, hs], inp["ssd_dt_bias"][l][1, hs]])
    alog = np.concatenate([inp["ssd_a_log"][l][0, hs], inp["ssd_a_log"][l][1, hs]])
    dtb = np.ascontiguousarray(np.broadcast_to(dtb[None, :], (128, 16))).astype(np.float32)
    alog = np.ascontiguousarray(np.broadcast_to(alog[None, :], (128, 16))).astype(np.float32)
    dvec = np.repeat(inp["ssd_d"][l][hs], 64)
    dsk = np.ascontiguousarray(dvec.reshape(4, 128).T).astype(np.float32)
    return cw, cb, dtb, alog, dsk


def build_p2():
    P = Prog()
    hT = P.dram("hT", [D, NTOK], BF16, "ExternalInput")
    wf = P.dram("wf", [D, NFM * 128], F32, "ExternalInput")
    wt = P.dram("wt", [D, NTM], F32, "ExternalInput")
    cw = P.dram("cw", [128, 6, 5], F32, "ExternalInput")
    cb = P.dram("cb", [128, 6], F32, "ExternalInput")
    dtb = P.dram("dtb", [128, 16], F32, "ExternalInput")
    alog = P.dram("alog", [128, 16], F32, "ExternalInput")
    dsk = P.dram("dsk", [128, 4], F32, "ExternalInput")
    cfw = P.dram("cfw", [128, 2, 31], F32, "ExternalInput")
    cfb = P.dram("cfb", [128, 2], F32, "ExternalInput")
    ofm = P.dram("ofm", [14, 128, NTOK], BF16, "ExternalOutput")
    ya = P.dram("ya", [512, NTOK], BF16, "ExternalOutput")
    vb = P.dram("vb", [256, NTOK], BF16, "ExternalOutput")
    yc = P.dram("yc", [NTOK, 256], BF16, "ExternalOutput")
    fms = P.dram("fms", [14, 128, NTOK], BF16, "Internal")
    vz = P.dram("vzs", [NTOK, 512], BF16, "Internal")
    dts = P.dram("dts", [NTOK, 16], F32, "Internal")
    ssdu = P.dram("ssdu", [6, 128, NTOK], BF16, "Internal")
    yfs = P.dram("yfs", [NTOK, 512], F32, "Internal")
    t1s = P.dram("t1s", [128, 64, 2, 256], BF16, "Internal")

    def fm_dst(c):
        return fms[c] if c < 14 else ofm[c - 14]
    emit_gemm(P, hT, wf, wt, fm_dst, vz, dts)
    P.barrier()
    emit_ssd(P, lambda c: fms[c], fms[6:10], dts, ssdu, yfs, cw, cb, dtb, alog, dsk, ya)
    emit_conformer(P, lambda c: fms[c], cfw, cfb, vb)
    emit_fourier(P, vz, t1s, yc)
    return P.finish()


def kernel(x, c, ctx, c_ctx, w_mod, b_mod, norm_g, w_in, ssd_conv_w, ssd_conv_b, ssd_dt_bias, ssd_a_log, ssd_d,
           ssd_norm_g, cf_conv_w, cf_conv_b, cf_ln_g, cf_ln_b, w_branch, w_out, final_g):
    bf = ml_dtypes.bfloat16
    inp = dict(x=x, c=c, ctx=ctx, c_ctx=c_ctx, w_mod=w_mod, b_mod=b_mod, norm_g=norm_g, w_in=w_in,
               ssd_conv_w=ssd_conv_w, ssd_conv_b=ssd_conv_b, ssd_dt_bias=ssd_dt_bias, ssd_a_log=ssd_a_log,
               ssd_d=ssd_d, ssd_norm_g=ssd_norm_g, cf_conv_w=cf_conv_w, cf_conv_b=cf_conv_b, cf_ln_g=cf_ln_g,
               cf_ln_b=cf_ln_b, w_branch=w_branch, w_out=w_out, final_g=final_g)
    inp = {k: np.asarray(v, dtype=np.float32) for k, v in inp.items()}
    cores = list(range(NCORES))
    mod = run_p0(inp)
    xc = inp["x"][0]
    cc = inp["ctx"][0]
    depth = 2
    for l in range(depth):
        hT = run_p1(shard_tokens_T(xc, cc), inp["norm_g"][l], mod[l])
        hx, hc = unshard_tokens_T(hT)
        hT_all = np.ascontiguousarray(np.concatenate([hc, hx], axis=0).T)
        del hx, hc, hT
        nc2 = get_nc("p2", build_p2)
        in_maps = []
        for g in cores:
            w_fm, w_tm = wcols_for_core(inp["w_in"][l], g)
            cw, cb, dtb, alog, dsk = ssd_params_for_core(inp, l, g)
            cfw = np.ascontiguousarray(inp["cf_conv_w"][l][:, g * 256:(g + 1) * 256].reshape(31, 2, 128).transpose(2, 1, 0))
            cfb = np.ascontiguousarray(inp["cf_conv_b"][l][g * 256:(g + 1) * 256].reshape(2, 128).T)
            in_maps.append({"hT": hT_all, "wf": w_fm, "wt": w_tm, "cw": cw, "cb": cb, "dtb": dtb, "alog": alog,
                            "dsk": dsk, "cfw": cfw, "cfb": cfb})
        res = run_bass_kernel_spmd(nc2, in_maps, core_ids=cores).results
        del in_maps, hT_all
        ya_all = np.concatenate([np.asarray(r["ya"]) for r in res], axis=0)
        vb_all = np.concatenate([np.asarray(r["vb"]) for r in res], axis=0)
        zb_all = np.concatenate([np.asarray(r["ofm"])[0:2].reshape(256, NTOK) for r in res], axis=0)
        yc_all = np.concatenate([np.asarray(r["yc"]) for r in res], axis=1)
        g_all = np.concatenate(
            [np.concatenate([np.asarray(r["ofm"])[2 + 4 * j:6 + 4 * j].reshape(512, NTOK) for r in res], axis=0)
             for j in range(3)], axis=0)
        del res

        def fm_halves(a, r):
            out = []
            for h in range(2):
                xs_ = a[:, CTX + r * TOK_X + 512 * h: CTX + r * TOK_X + 512 * (h + 1)]
                cs_ = a[:, r * TOK_C + 16 * h: r * TOK_C + 16 * (h + 1)]
                out.append(np.concatenate([xs_, cs_], axis=1))
            return np.ascontiguousarray(np.stack(out))
        nc3 = get_nc("p3", build_p3)
        wbk = w_blocks(inp["w_branch"][l])
        wok = w_blocks(inp["w_out"][l])
        sng = vec_layout(inp["ssd_norm_g"][l])
        lng = np.ascontiguousarray(inp["cf_ln_g"][l].reshape(16, 128).T)
        lnb = np.ascontiguousarray(inp["cf_ln_b"][l].reshape(16, 128).T)
        modv = mod_layout(mod[l])
        ycT_all = np.ascontiguousarray(yc_all.T)
        in_maps = []
        for r in cores:
            in_maps.append({"yaT": fm_halves(ya_all, r), "vbT": fm_halves(vb_all, r), "zbT": fm_halves(zb_all, r),
                            "ycT": fm_halves(ycT_all, r), "gT": fm_halves(g_all, r), "xT": halves_T(xc, cc, r),
                            "wb": wbk, "wo": wok, "sng": sng, "lng": lng, "lnb": lnb, "modv": modv})
        del ya_all, vb_all, zb_all, yc_all, g_all, ycT_all
        res = run_bass_kernel_spmd(nc3, in_maps, core_ids=cores).results
        del in_maps
        xn = np.empty_like(xc)
        cn = np.empty_like(cc)
        for r in cores:
            o = np.asarray(res[r]["xnT"])
            for h in range(2):
                xn[r * TOK_X + 512 * h: r * TOK_X + 512 * (h + 1)] = o[h][:, :512].T
                cn[r * TOK_C + 16 * h: r * TOK_C + 16 * (h + 1)] = o[h][:, 512:].T
        xc, cc = xn, cn
    zmod = np.zeros((2, 3 * D), np.float32)
    oT = run_p1(shard_tokens_T(xc, cc), inp["final_g"], zmod, final=True)
    ox, _ = unshard_tokens_T([np.asarray(o) for o in oT])
    return np.ascontiguousarray(ox[None].astype(np.float32))
```

```python
import numpy as np
import ml_dtypes
import concourse.bass as bass
import concourse.mybir as mybir
from concourse.bass_utils import run_bass_kernel_spmd

F32 = mybir.dt.float32
BF16 = mybir.dt.bfloat16
I32 = mybir.dt.int32
AF = mybir.ActivationFunctionType
ALU = mybir.AluOpType
AX = mybir.AxisListType

NCORES = 8
D = 4096
SEQ = 8192
CTX = 256
TOK_X = SEQ // NCORES
TOK_C = CTX // NCORES
TOK = TOK_X + TOK_C
NTOK = SEQ + CTX
KC = D // 128
EPS = 1e-6

NDMASEM = 40
CC_INC = 16


class Prog:
    def __init__(self):
        self.nc = bass.Bass("TRN2", target_bir_lowering=False)
        nc = self.nc
        self.eng_names = ["pe", "act", "dve", "pool", "sp"]
        self.lists = {e: [] for e in self.eng_names}
        self.count = {e: 0 for e in self.eng_names}
        self.sem = {e: nc.alloc_semaphore("s_" + e) for e in ["pe", "act", "dve", "pool"]}
        self.dsem = [nc.alloc_semaphore("d%d" % i) for i in range(NDMASEM)]
        self.dval = [0] * NDMASEM
        self.dnext = 0
        self.waited = {e: {} for e in self.eng_names}
        self.last_w = {}
        self.readers = {}
        self.n_alloc = 0
        self.sb_ptr = 16512
        self.sb_end = 229344
        self.banks = None

    def dram(self, name, shape, dt, kind):
        return self.nc.dram_tensor(name, list(shape), dt, kind=kind).ap()

    def sb(self, name, shape, dt):
        esz = 4 if dt in (F32, I32) else 2
        n = 1
        for d_ in shape[1:]:
            n *= d_
        nbytes = (n * esz + 63) // 64 * 64
        off = self.sb_ptr
        assert off + nbytes <= self.sb_end, ("SBUF overflow", name, off, nbytes)
        self.sb_ptr += nbytes
        self.n_alloc += 1
        return self.nc.alloc_sbuf_tensor_at("%s_%d" % (name, self.n_alloc), list(shape), dt, offset=off)

    def mark(self):
        return self.sb_ptr

    def release(self, mark):
        self.barrier()
        self.sb_ptr = mark

    def barrier(self):
        deps = [(e, self.count[e]) for e in ["pe", "act", "dve", "pool"] if self.count[e] > 0]
        deps += [(("d", i), self.dval[i]) for i in range(NDMASEM) if self.dval[i] > 0]
        for e in self.eng_names:
            self._emit_waits(e, [d_ for d_ in deps if d_[0] != e])

    def get_banks(self):
        if self.banks is None:
            self.banks = [self.nc.alloc_psum_tensor("bank%d" % i, [128, 512], F32) for i in range(8)]
        return self.banks

    def ps(self, name, shape, dt=F32):
        return self.nc.alloc_psum_tensor(name, list(shape), dt)

    def _deps(self, eng, r, w):
        deps = []
        for k in r:
            t = self.last_w.get(k)
            if t is not None:
                deps.append(t)
        for k in w:
            t = self.last_w.get(k)
            if t is not None and t[0] != eng:
                deps.append(t)
            for t in self.readers.get(k, ()):
                if t[0] != eng:
                    deps.append(t)
        if eng == "pe":
            deps = [t for t in deps if t[0] != "pe"]
        return deps

    def _emit_waits(self, eng, deps):
        wd = self.waited[eng]
        for (s, v) in deps:
            if wd.get(s, 0) < v:
                wd[s] = v
                self.lists[eng].append(("wait", s, v))

    def _record(self, tok, r, w):
        for k in r:
            self.readers.setdefault(k, []).append(tok)
        for k in w:
            self.last_w[k] = tok
            self.readers[k] = []

    def op(self, eng, fn, r=(), w=()):
        self._emit_waits(eng, self._deps(eng, r, w))
        self.count[eng] += 1
        tok = (eng, self.count[eng])
        self.lists[eng].append(("op", fn))
        self._record(tok, r, w)
        return tok

    def dma(self, q, out, in_, r=(), w=()):
        i = self.dnext
        self.dnext = (self.dnext + 1) % NDMASEM
        deps = self._deps(("d", i), r, w)
        if self.dval[i] > 0:
            deps.append((("d", i), self.dval[i]))
        self._emit_waits(q, deps)
        self.dval[i] += 16
        tok = (("d", i), self.dval[i])
        self.lists[q].append(("dma", out, in_, i))
        self._record(tok, r, w)
        return tok

    def _semh(self, s):
        if isinstance(s, tuple):
            return self.dsem[s[1]]
        return self.sem[s]

    def finish(self):
        nc = self.nc
        for i in range(NDMASEM):
            if self.dval[i] > 0:
                self._emit_waits("sp", [(("d", i), self.dval[i])])
        for e in ["pe", "act", "dve", "pool"]:
            if self.count[e] > 0:
                self._emit_waits("sp", [(e, self.count[e])])

        def replay(ename):
            def f(eng):
                for it in self.lists[ename]:
                    if it[0] == "wait":
                        eng.wait_ge(self._semh(it[1]), it[2])
                    elif it[0] == "op":
                        it[1](eng).then_inc(self.sem[ename], 1)
                    elif it[0] == "cc":
                        eng.collective_compute(it[1], ALU.bypass, replica_groups=[list(range(NCORES))],
                                               ins=[it[3]], outs=[it[2]]).then_inc(self.dsem[it[4]], CC_INC)
                    else:
                        eng.dma_start(out=it[1], in_=it[2]).then_inc(self.dsem[it[3]], 16)
            return f

        with nc.Block() as block:
            block.tensor(replay("pe"))
            block.scalar(replay("act"))
            block.vector(replay("dve"))
            block.gpsimd(replay("pool"))
            block.sync(replay("sp"))
        return nc


MODC = 3 * D // NCORES


def build_p0():
    P = Prog()
    c_in = P.dram("c2", [2, D], F32, "ExternalInput")
    wmod = P.dram("wmod", [2, D, MODC], F32, "ExternalInput")
    bmod = P.dram("bmod", [2, MODC], F32, "ExternalInput")
    out = P.dram("mod", [2, 2, MODC], F32, "ExternalOutput")

    craw = P.sb("craw", [128, 2, KC], F32)
    cs = P.sb("cs", [128, KC, 2], F32)
    bsb = P.sb("bsb", [2, 2, MODC], F32)
    res = P.sb("res", [2, 2, MODC], F32)
    wbuf = [P.sb("wbuf%d" % i, [128, 8, MODC], F32) for i in range(2)]
    pst = [P.ps("pst%d" % i, [2, 512], F32) for i in range(3)]

    P.dma("sp", craw[:, :, :], c_in.rearrange("j (p k) -> p j k", k=KC), w=["craw"])
    for l in range(2):
        for j in range(2):
            P.dma("sp", bsb[j:j + 1, l, :], bmod[l:l + 1, :], w=["bsb"])
    for j in range(2):
        P.op("act", lambda e, j=j: e.activation(out=cs[:, :, j], in_=craw[:, j, :], func=AF.Silu),
             r=["craw"], w=["cs"])
    it = 0
    for l in range(2):
        for kg in range(4):
            b = it % 2
            it += 1
            P.dma("sp", wbuf[b][:, :, :],
                  wmod[l].rearrange("(p k) n -> p k n", k=KC)[:, kg * 8:(kg + 1) * 8, :],
                  w=["wbuf%d" % b])
            for kk in range(8):
                k = kg * 8 + kk
                for n in range(3):
                    P.op("pe", lambda e, b=b, kk=kk, n=n, k=k: e.matmul(
                        pst[n][:, :], lhsT=cs[:, k, :], rhs=wbuf[b][:, kk, n * 512:(n + 1) * 512],
                        start=(k == 0), stop=(k == KC - 1)),
                        r=["cs", "wbuf%d" % b], w=["pst%d" % n])
        for n in range(3):
            P.op("dve", lambda e, n=n, l=l: e.tensor_tensor(
                out=res[:, l, n * 512:(n + 1) * 512], in0=pst[n][:, :], in1=bsb[:, l, n * 512:(n + 1) * 512],
                op=ALU.add), r=["pst%d" % n, "bsb"], w=["res"])
    P.dma("sp", out.rearrange("l j n -> j l n"), res[:, :, :], r=["res"])
    return P.finish()


def run_p0(inp):
    nc = build_p0()
    c2 = np.stack([inp["c"][0], inp["c_ctx"]]).astype(np.float32)
    in_maps = []
    for r in range(NCORES):
        sl = slice(r * MODC, (r + 1) * MODC)
        in_maps.append({"c2": c2,
                        "wmod": np.ascontiguousarray(inp["w_mod"][:, :, sl]),
                        "bmod": np.ascontiguousarray(inp["b_mod"][:, sl])})
    res = run_bass_kernel_spmd(nc, in_maps, core_ids=list(range(NCORES)))
    return np.concatenate([r["mod"] for r in res.results], axis=2)


def _prog_collective(self, kind, out, in_, r=(), w=()):
    i = self.dnext
    self.dnext = (self.dnext + 1) % NDMASEM
    deps = self._deps(("d", i), r, w)
    if self.dval[i] > 0:
        deps.append((("d", i), self.dval[i]))
    self._emit_waits("pool", deps)
    self.dval[i] += 16
    tok = (("d", i), self.dval[i])
    self.lists["pool"].append(("cc", kind, out, in_, i))
    self._record(tok, r, w)
    return tok


Prog.collective = _prog_collective


TTILES = [(0, 512, 0), (512, 1024, 0), (1024, 1056, 1)]


def vec_layout(v):
    return np.ascontiguousarray(np.asarray(v, np.float32).reshape(KC, 128).T)


def emit_norm_mod(P, xT, hT, gvec, modv, out_dt, pfx="n"):
    ones = P.sb(pfx + "ones", [128, 128], F32)
    g_sb = P.sb(pfx + "g", [128, KC], F32)
    m_sb = P.sb(pfx + "m", [128, 2, 3, KC], F32)
    G = P.sb(pfx + "G", [128, 2, KC], F32)
    xb = [P.sb(pfx + "xb%d" % i, [128, KC, 512], F32) for i in range(2)]
    sq = [P.sb(pfx + "sq%d" % i, [128, 512], F32) for i in range(3)]
    tmp = [P.sb(pfx + "tmp%d" % i, [128, 512], F32) for i in range(3)]
    ho = [P.sb(pfx + "ho%d" % i, [128, 512], out_dt) for i in range(3)]
    rt = P.sb(pfx + "rt", [128, 512], F32)
    rstd = P.sb(pfx + "rstd", [128, 512], F32)
    pss = P.ps(pfx + "pss", [128, 512], F32)

    P.op("pool", lambda e: e.memset(ones[:, :], 1.0), w=[pfx + "ones"])
    P.dma("sp", g_sb[:, :], gvec, w=[pfx + "g"])
    P.dma("sp", m_sb[:, :, :, :], modv, w=[pfx + "m"])
    for j in range(2):
        P.op("dve", lambda e, j=j: e.scalar_tensor_tensor(
            out=G[:, j, :], in0=m_sb[:, j, 1, :], scalar=1.0, in1=g_sb[:, :], op0=ALU.add, op1=ALU.mult),
            r=[pfx + "m", pfx + "g"], w=[pfx + "G"])
    xv = xT.rearrange("(k p) t -> p k t", p=128)
    n = 0
    for ti, (t0, t1, j) in enumerate(TTILES):
        w_ = t1 - t0
        b = ti % 2
        xk = pfx + "xb%d" % b
        P.dma("sp", xb[b][:, :, 0:w_], xv[:, :, t0:t1], w=[xk])
        for kc in range(KC):
            s = kc % 3
            P.op("act", lambda e, b=b, kc=kc, s=s, w_=w_: e.activation(
                out=sq[s][:, 0:w_], in_=xb[b][:, kc, 0:w_], func=AF.Square), r=[xk], w=[pfx + "sq%d" % s])
            P.op("pe", lambda e, kc=kc, s=s, w_=w_: e.matmul(
                pss[:, 0:w_], lhsT=ones[:, :], rhs=sq[s][:, 0:w_], start=(kc == 0), stop=(kc == KC - 1)),
                r=[pfx + "ones", pfx + "sq%d" % s], w=[pfx + "pss"])
        P.op("act", lambda e, w_=w_: e.activation(out=rt[:, 0:w_], in_=pss[:, 0:w_], func=AF.Sqrt,
                                                   bias=EPS, scale=1.0 / D), r=[pfx + "pss"], w=[pfx + "rt"])
        P.op("dve", lambda e, w_=w_: e.reciprocal(out=rstd[:, 0:w_], in_=rt[:, 0:w_]), r=[pfx + "rt"], w=[pfx + "rstd"])
        for kc in range(KC):
            s = n % 3
            n += 1
            P.op("dve", lambda e, b=b, kc=kc, s=s, w_=w_, j=j: e.scalar_tensor_tensor(
                out=tmp[s][:, 0:w_], in0=xb[b][:, kc, 0:w_], scalar=G[:, j, kc:kc + 1], in1=rstd[:, 0:w_],
                op0=ALU.mult, op1=ALU.mult), r=[xk, pfx + "G", pfx + "rstd"], w=[pfx + "tmp%d" % s])
            P.op("act", lambda e, kc=kc, s=s, w_=w_, j=j: e.activation(
                out=ho[s][:, 0:w_], in_=tmp[s][:, 0:w_], func=AF.Identity, bias=m_sb[:, j, 0, kc:kc + 1], scale=1.0),
                r=[pfx + "tmp%d" % s, pfx + "m"], w=[pfx + "ho%d" % s])
            P.dma("sp", hT[kc * 128:(kc + 1) * 128, t0:t1], ho[s][:, 0:w_], r=[pfx + "ho%d" % s])


def build_p1(out_dt):
    P = Prog()
    xT = P.dram("xT", [D, TOK], F32, "ExternalInput")
    gvec = P.dram("gvec", [128, KC], F32, "ExternalInput")
    modv = P.dram("modv", [128, 2, 3, KC], F32, "ExternalInput")
    hT = P.dram("hT", [D, TOK], out_dt, "ExternalOutput")
    emit_norm_mod(P, xT, hT, gvec, modv, out_dt)
    return P.finish()


def mod_layout(mod_l):
    m = np.asarray(mod_l, np.float32).reshape(2, 3, KC, 128)
    return np.ascontiguousarray(m.transpose(3, 0, 1, 2))


def shard_tokens_T(x2d, c2d):
    outs = []
    for r in range(NCORES):
        t = np.concatenate([x2d[r * TOK_X:(r + 1) * TOK_X], c2d[r * TOK_C:(r + 1) * TOK_C]], axis=0)
        outs.append(np.ascontiguousarray(t.T))
    return outs


def unshard_tokens_T(per_core):
    xs = np.concatenate([p[:, :TOK_X].T for p in per_core], axis=0)
    cs = np.concatenate([p[:, TOK_X:].T for p in per_core], axis=0)
    return xs, cs


_NC_CACHE = {}


def get_nc(key, builder):
    if key not in _NC_CACHE:
        _NC_CACHE[key] = builder()
    return _NC_CACHE[key]


def run_p1(xT_list, gvec, mod_l, final=False):
    nc = get_nc(("p1", final), lambda: build_p1(F32 if final else BF16))
    g = vec_layout(gvec)
    m = mod_layout(mod_l)
    in_maps = [{"xT": xT_list[r], "gvec": g, "modv": m} for r in range(NCORES)]
    res = run_bass_kernel_spmd(nc, in_maps, core_ids=list(range(NCORES)))
    return [r["hT"] for r in res.results]


NFM = 28
NTM = 528
FM_GROUPS = [(0, 7), (7, 14), (14, 21), (21, 28)]
A_TILES = [(0, 256)] + [(256 + 512 * i, 256 + 512 * (i + 1)) for i in range(16)]


def emit_gemm(P, hT_all, w_fm, w_tm, fm_dst, tm_vz, tm_dt):
    banks = P.get_banks()
    m0 = P.mark()
    hv = hT_all.rearrange("(k p) t -> p k t", p=128)
    wfv = w_fm.rearrange("(k p) n -> p k n", p=128)
    wtv = w_tm.rearrange("(k p) n -> p k n", p=128)
    GC = 4
    NG = NFM // GC
    wres = [P.sb("wres%d" % i, [128, KC, GC * 128], BF16) for i in range(2)]
    wtm = P.sb("wtm", [128, KC, NTM], BF16)
    stg = [P.sb("stg%d" % i, [128, 4, NTM], F32) for i in range(2)]
    hb = [P.sb("hb%d" % i, [128, KC, 512], BF16) for i in range(2)]
    ob = [P.sb("ob%d" % i, [128, 512], BF16) for i in range(4)]
    otv = [P.sb("otv%d" % i, [128, 512], BF16) for i in range(2)]
    otd = [P.sb("otd%d" % i, [128, 16], F32) for i in range(2)]
    st = {"stg": 0, "ob": 0, "hb": 0, "bank": 0, "tm": 0}

    def stage_tm(q):
        b = st["stg"] % 2
        st["stg"] += 1
        P.dma("sp", stg[b][:, :, 0:NTM], wtv[:, q * 4:(q + 1) * 4, :], w=["stg%d" % b])
        P.op("pool" if q % 2 else "dve", lambda e: e.tensor_copy(
            out=wtm[:, q * 4:(q + 1) * 4, :], in_=stg[b][:, :, 0:NTM]), r=["stg%d" % b], w=["wtm"])

    def stage_fm(gi, q):
        b = st["stg"] % 2
        st["stg"] += 1
        wi = gi % 2
        c0 = gi * GC
        P.dma("sp", stg[b][:, :, 0:GC * 128], wfv[:, q * 4:(q + 1) * 4, c0 * 128:(c0 + GC) * 128], w=["stg%d" % b])
        P.op("pool" if q % 2 else "dve", lambda e: e.tensor_copy(
            out=wres[wi][:, q * 4:(q + 1) * 4, :], in_=stg[b][:, :, 0:GC * 128]), r=["stg%d" % b], w=["wres%d" % wi])

    for q in range(8):
        stage_tm(q)
    for q in range(8):
        stage_fm(0, q)
    for gi in range(NG):
        wi = gi % 2
        c0 = gi * GC
        for ti, (t0, t1) in enumerate(A_TILES):
            tw = t1 - t0
            hbi = st["hb"] % 2
            st["hb"] += 1
            hk = "hb%d" % hbi
            P.dma("sp", hb[hbi][:, :, 0:tw], hv[:, :, t0:t1], w=[hk])
            if gi + 1 < NG and 1 <= ti <= 8:
                stage_fm(gi + 1, ti - 1)
            for cj in range(GC):
                bk = st["bank"] % 4
                st["bank"] += 1
                for kc in range(KC):
                    P.op("pe", lambda e, bk=bk, kc=kc, cj=cj, hbi=hbi, tw=tw, wi=wi: e.matmul(
                        banks[bk][:, 0:tw], lhsT=wres[wi][:, kc, cj * 128:(cj + 1) * 128], rhs=hb[hbi][:, kc, 0:tw],
                        start=(kc == 0), stop=(kc == KC - 1)), r=["wres%d" % wi, hk], w=["bank%d" % bk])
                o = st["ob"] % 4
                st["ob"] += 1
                if st["ob"] % 2:
                    P.op("act", lambda e, o=o, bk=bk, tw=tw: e.copy(out=ob[o][:, 0:tw], in_=banks[bk][:, 0:tw]),
                         r=["bank%d" % bk], w=["ob%d" % o])
                else:
                    P.op("dve", lambda e, o=o, bk=bk, tw=tw: e.tensor_copy(out=ob[o][:, 0:tw], in_=banks[bk][:, 0:tw]),
                         r=["bank%d" % bk], w=["ob%d" % o])
                P.dma("pool", fm_dst(c0 + cj)[:, t0:t1], ob[o][:, 0:tw], r=["ob%d" % o])
            if gi == 0:
                for s0 in range(0, tw, 128):
                    for kc in range(KC):
                        P.op("pe", lambda e, kc=kc, hbi=hbi, s0=s0: e.matmul(
                            banks[4][:, :], lhsT=hb[hbi][:, kc, s0:s0 + 128], rhs=wtm[:, kc, 0:512],
                            start=(kc == 0), stop=(kc == KC - 1)), r=["wtm", hk], w=["bank4"])
                        P.op("pe", lambda e, kc=kc, hbi=hbi, s0=s0: e.matmul(
                            banks[5][:, 0:16], lhsT=hb[hbi][:, kc, s0:s0 + 128], rhs=wtm[:, kc, 512:528],
                            start=(kc == 0), stop=(kc == KC - 1)), r=["wtm", hk], w=["bank5"])
                    o = st["tm"] % 2
                    st["tm"] += 1
                    P.op("act", lambda e, o=o: e.copy(out=otv[o][:, :], in_=banks[4][:, :]), r=["bank4"], w=["otv%d" % o])
                    P.op("dve", lambda e, o=o: e.tensor_copy(out=otd[o][:, :], in_=banks[5][:, 0:16]),
                         r=["bank5"], w=["otd%d" % o])
                    P.dma("pool", tm_vz[t0 + s0:t0 + s0 + 128, :], otv[o][:, :], r=["otv%d" % o])
                    P.dma("pool", tm_dt[t0 + s0:t0 + s0 + 128, :], otd[o][:, :], r=["otd%d" % o])
    P.release(m0)


def wcols_for_core(w_in_l, g):
    cols = []
    cols += list(range(512 * g, 512 * g + 512))
    cols += list(range(4096 + 128 * g, 4096 + 128 * g + 128))
    cols += list(range(5120 + 128 * g, 5120 + 128 * g + 128))
    cols += list(range(6144 + 512 * g, 6144 + 512 * g + 512))
    cols += list(range(10368 + 256 * g, 10368 + 256 * g + 256))
    cols += list(range(12416 + 256 * g, 12416 + 256 * g + 256))
    cols += list(range(14464 + 256 * g, 14464 + 256 * g + 256))
    for j in range(3):
        cols += list(range(20608 + 4096 * j + 512 * g, 20608 + 4096 * j + 512 * g + 512))
    tcols = []
    tcols += list(range(16512 + 256 * g, 16512 + 256 * g + 256))
    tcols += list(range(18560 + 256 * g, 18560 + 256 * g + 256))
    tcols += list(range(10240 + 8 * g, 10240 + 8 * g + 8))
    tcols += list(range(10304 + 8 * g, 10304 + 8 * g + 8))
    return np.ascontiguousarray(w_in_l[:, cols]), np.ascontiguousarray(w_in_l[:, tcols])


HT = 528
P3_TILES = [(0, 512, 0), (512, 528, 1)]
CFW = 2048


def emit_p3(P, yaT, vbT, zbT, ycT, gT, xT, wb, wo, sng, lng, lnb, modv, xnT):
    banks = P.get_banks()
    m0 = P.mark()
    ones_f = P.sb("ones_f", [128, 128], F32)
    ones_b = P.sb("ones_b", [128, 128], BF16)
    sng_sb = P.sb("sng", [128, 32], F32)
    lng_sb = P.sb("lng", [128, 16], F32)
    lnb_sb = P.sb("lnb", [128, 16], F32)
    m_sb = P.sb("m3", [128, 2, 3, KC], F32)
    P.op("pool", lambda e: e.memset(ones_f[:, :], 1.0), w=["ones_f"])
    P.op("pool", lambda e: e.memset(ones_b[:, :], 1.0), w=["ones_b"])
    P.dma("sp", sng_sb[:, :], sng, w=["sng"])
    P.dma("sp", lng_sb[:, :], lng, w=["lng"])
    P.dma("sp", lnb_sb[:, :], lnb, w=["lnb"])
    P.dma("sp", m_sb[:, :, :, :], modv, w=["m3"])
    ya = P.sb("ya", [128, 32, HT], BF16)
    vb = P.sb("vb", [128, 16, HT], BF16)
    zb = P.sb("zb", [128, 16, HT], BF16)
    yc = P.sb("yc", [128, 16, HT], BF16)
    mg = P.sb("mg", [128, 32, HT], BF16)
    wst = [P.sb("wst%d" % i, [128, 16, 128], F32) for i in range(2)]
    wbf = [P.sb("wbf%d" % i, [128, 64, 128], BF16) for i in range(2)]
    gsb = [P.sb("gsb%d" % i, [128, 3, HT], BF16) for i in range(2)]
    sg = [P.sb("sg%d" % i, [128, 3, HT], BF16) for i in range(2)]
    t1 = [P.sb("t1_%d" % i, [128, HT], F32) for i in range(3)]
    t2 = [P.sb("t2_%d" % i, [128, HT], F32) for i in range(3)]
    stat = P.sb("stat", [128, HT], F32)
    rstd = P.sb("rstd3", [128, HT], F32)
    mean = P.sb("mean3", [128, HT], F32)
    xin = [P.sb("xin%d" % i, [128, HT], F32) for i in range(2)]
    xo = [P.sb("xo%d" % i, [128, HT], F32) for i in range(2)]
    cnt = {"w": 0, "t": 0, "x": 0, "cast": 0}

    def stat_reduce(src_fn, nchunks, use_f32, key_r):
        for kc in range(nchunks):
            rhs_ap, rkeys = src_fn(kc)
            for (a, b_, j) in P3_TILES:
                bk = 0 if j == 0 else 1
                P.op("pe", lambda e, kc=kc, a=a, b_=b_, bk=bk, rhs_ap=rhs_ap: e.matmul(
                    banks[bk][:, 0:b_ - a], lhsT=(ones_f if use_f32 else ones_b)[:, :], rhs=rhs_ap[:, a:b_],
                    start=(kc == 0), stop=(kc == nchunks - 1)),
                    r=["ones_f", "ones_b"] + rkeys, w=["bank%d" % bk])

    def bank_to(dst, scale, bias, func):
        for (a, b_, j) in P3_TILES:
            bk = 0 if j == 0 else 1
            P.op("act", lambda e, a=a, b_=b_, bk=bk: e.activation(
                out=dst[:, a:b_], in_=banks[bk][:, 0:b_ - a], func=func, bias=bias, scale=scale),
                r=["bank%d" % bk], w=[dst.name])

    def load_weights(src4, nch, blk):
        wi = cnt["w"] % 2
        cnt["w"] += 1
        for q in range(nch // 16):
            si = cnt["cast"] % 2
            cnt["cast"] += 1
            P.dma("sp", wst[si][:, :, :], src4[blk, :, q * 16:(q + 1) * 16, :], w=["wst%d" % si])
            P.op("act", lambda e, si=si, wi=wi, q=q: e.copy(
                out=wbf[wi][:, q * 16:(q + 1) * 16, :], in_=wst[si][:, :, :]), r=["wst%d" % si], w=["wbf%d" % wi])
        return wi

    for hf in range(2):
        P.dma("sp", ya[:, :, :], yaT[hf].rearrange("(k p) t -> p k t", p=128), w=["ya"])
        P.dma("sp", vb[:, :, :], vbT[hf].rearrange("(k p) t -> p k t", p=128), w=["vb"])
        P.dma("sp", zb[:, :, :], zbT[hf].rearrange("(k p) t -> p k t", p=128), w=["zb"])
        P.dma("sp", yc[:, :, :], ycT[hf].rearrange("(k p) t -> p k t", p=128), w=["yc"])
        def sq_a(kc):
            i = cnt["t"] % 3
            cnt["t"] += 1
            P.op("act", lambda e, kc=kc, i=i: e.activation(out=t1[i][:, :], in_=ya[:, kc, :], func=AF.Square),
                 r=["ya"], w=[t1[i].name])
            return t1[i], [t1[i].name]
        stat_reduce(sq_a, 32, True, None)
        bank_to(stat, 1.0 / D, EPS, AF.Sqrt)
        P.op("dve", lambda e: e.reciprocal(out=rstd[:, :], in_=stat[:, :]), r=[stat.name], w=[rstd.name])
        for kc in range(32):
            P.op("dve", lambda e, kc=kc: e.scalar_tensor_tensor(
                out=ya[:, kc, :], in0=ya[:, kc, :], scalar=sng_sb[:, kc:kc + 1], in1=rstd[:, :],
                op0=ALU.mult, op1=ALU.mult), r=["ya", "sng", rstd.name], w=["ya"])
        stat_reduce(lambda kc: (vb[:, kc, :], ["vb"]), 16, False, None)
        bank_to(mean, 1.0 / CFW, 0.0, AF.Identity)

        def sq_b(kc):
            i = cnt["t"] % 3
            cnt["t"] += 1
            P.op("dve", lambda e, kc=kc, i=i: e.tensor_tensor(out=t2[i][:, :], in0=vb[:, kc, :], in1=mean[:, :],
                                                               op=ALU.subtract), r=["vb", mean.name], w=[t2[i].name])
            P.op("act", lambda e, i=i: e.activation(out=t1[i][:, :], in_=t2[i][:, :], func=AF.Square),
                 r=[t2[i].name], w=[t1[i].name])
            return t1[i], [t1[i].name]
        stat_reduce(sq_b, 16, True, None)
        bank_to(stat, 1.0 / CFW, EPS, AF.Sqrt)
        P.op("dve", lambda e: e.reciprocal(out=rstd[:, :], in_=stat[:, :]), r=[stat.name], w=[rstd.name])
        for kc in range(16):
            i = cnt["t"] % 3
            cnt["t"] += 1
            P.op("dve", lambda e, kc=kc, i=i: e.tensor_tensor(out=t2[i][:, :], in0=vb[:, kc, :], in1=mean[:, :],
                                                               op=ALU.subtract), r=["vb", mean.name], w=[t2[i].name])
            P.op("dve", lambda e, i=i: e.tensor_tensor(out=t2[i][:, :], in0=t2[i][:, :], in1=rstd[:, :], op=ALU.mult),
                 r=[t2[i].name, rstd.name], w=[t2[i].name])
            P.op("act", lambda e, kc=kc, i=i: e.activation(out=t2[i][:, :], in_=t2[i][:, :], func=AF.Silu,
                                                            bias=lnb_sb[:, kc:kc + 1], scale=lng_sb[:, kc:kc + 1]),
                 r=[t2[i].name, "lng", "lnb"], w=[t2[i].name])
            P.op("act", lambda e, kc=kc, i=i: e.activation(out=t1[i][:, :], in_=zb[:, kc, :], func=AF.Silu),
                 r=["zb"], w=[t1[i].name])
            P.op("dve", lambda e, kc=kc, i=i: e.tensor_tensor(out=vb[:, kc, :], in0=t2[i][:, :], in1=t1[i][:, :],
                                                               op=ALU.mult), r=[t1[i].name, t2[i].name], w=["vb"])
        gv = gT[hf].rearrange("(j n p) t -> n p j t", j=3, p=128)
        for n in range(32):
            wi = load_weights(wb, 64, n)
            gi = n % 2
            P.dma("sp", gsb[gi][:, :, :], gv[n], w=[gsb[gi].name])
            P.op("act", lambda e, gi=gi: e.activation(out=sg[gi][:, :, :], in_=gsb[gi][:, :, :], func=AF.Sigmoid),
                 r=[gsb[gi].name], w=[sg[gi].name])
            bo = 4 * (n % 2)
            srcs = [(ya, "ya", 0, 32), (vb, "vb", 32, 16), (yc, "yc", 48, 16)]
            for bi, (src, sk, c0, ncc) in enumerate(srcs):
                for kc in range(ncc):
                    for (a, b_, j) in P3_TILES:
                        if j == 0:
                            dst = banks[bo + bi][:, 0:512]
                            bk = bo + bi
                        else:
                            dst = banks[bo + 3][:, 16 * bi:16 * bi + 16]
                            bk = bo + 3
                        P.op("pe", lambda e, dst=dst, wi=wi, c0=c0, kc=kc, src=src, a=a, b_=b_, ncc=ncc: e.matmul(
                            dst, lhsT=wbf[wi][:, c0 + kc, :], rhs=src[:, kc, a:b_],
                            start=(kc == 0), stop=(kc == ncc - 1)),
                            r=["wbf%d" % wi, sk], w=["bank%d" % bk])
            i = cnt["t"] % 3
            cnt["t"] += 1
            for (a, b_, j) in P3_TILES:
                def pv(bi):
                    if j == 0:
                        return banks[bo + bi][:, 0:512], "bank%d" % (bo + bi)
                    return banks[bo + 3][:, 16 * bi:16 * bi + 16], "bank%d" % (bo + 3)
                pa, ka = pv(0)
                pb, kb = pv(1)
                pc, kc_ = pv(2)
                P.op("dve", lambda e, pa=pa, gi=gi, i=i, a=a, b_=b_: e.tensor_tensor(
                    out=t1[i][:, a:b_], in0=pa, in1=sg[gi][:, 0, a:b_], op=ALU.mult),
                    r=[ka, sg[gi].name], w=[t1[i].name])
                P.op("dve", lambda e, pb=pb, gi=gi, i=i, a=a, b_=b_: e.tensor_tensor(
                    out=t2[i][:, a:b_], in0=pb, in1=sg[gi][:, 1, a:b_], op=ALU.mult),
                    r=[kb, sg[gi].name], w=[t2[i].name])
                P.op("dve", lambda e, i=i, a=a, b_=b_: e.tensor_tensor(
                    out=t1[i][:, a:b_], in0=t1[i][:, a:b_], in1=t2[i][:, a:b_], op=ALU.add),
                    r=[t1[i].name, t2[i].name], w=[t1[i].name])
                P.op("dve", lambda e, pc=pc, gi=gi, i=i, a=a, b_=b_: e.tensor_tensor(
                    out=t2[i][:, a:b_], in0=pc, in1=sg[gi][:, 2, a:b_], op=ALU.mult),
                    r=[kc_, sg[gi].name], w=[t2[i].name])
                P.op("dve", lambda e, i=i, a=a, b_=b_, n=n: e.tensor_tensor(
                    out=mg[:, n, a:b_], in0=t1[i][:, a:b_], in1=t2[i][:, a:b_], op=ALU.add),
                    r=[t1[i].name, t2[i].name], w=["mg"])
        xv = xT[hf].rearrange("(k p) t -> k p t", p=128)
        ov = xnT[hf].rearrange("(k p) t -> k p t", p=128)
        for m in range(32):
            wi = load_weights(wo, 32, m)
            xi = m % 2
            P.dma("sp", xin[xi][:, :], xv[m], w=[xin[xi].name])
            bo = 4 * (m % 2)
            for kc in range(32):
                for (a, b_, j) in P3_TILES:
                    bk = bo + (0 if j == 0 else 1)
                    P.op("pe", lambda e, bk=bk, wi=wi, kc=kc, a=a, b_=b_: e.matmul(
                        banks[bk][:, 0:b_ - a], lhsT=wbf[wi][:, kc, :], rhs=mg[:, kc, a:b_],
                        start=(kc == 0), stop=(kc == 31)), r=["wbf%d" % wi, "mg"], w=["bank%d" % bk])
            for (a, b_, j) in P3_TILES:
                bk = bo + (0 if j == 0 else 1)
                P.op("dve", lambda e, bk=bk, xi=xi, a=a, b_=b_, j=j, m=m: e.scalar_tensor_tensor(
                    out=xo[xi][:, a:b_], in0=banks[bk][:, 0:b_ - a], scalar=m_sb[:, j, 2, m:m + 1],
                    in1=xin[xi][:, a:b_], op0=ALU.mult, op1=ALU.add),
                    r=["bank%d" % bk, xin[xi].name, "m3"], w=[xo[xi].name])
            P.dma("pool", ov[m], xo[xi][:, :], r=[xo[xi].name])
    P.release(m0)


def build_p3():
    P = Prog()
    yaT = P.dram("yaT", [2, D, HT], BF16, "ExternalInput")
    vbT = P.dram("vbT", [2, CFW, HT], BF16, "ExternalInput")
    zbT = P.dram("zbT", [2, CFW, HT], BF16, "ExternalInput")
    ycT = P.dram("ycT", [2, CFW, HT], BF16, "ExternalInput")
    gT = P.dram("gT", [2, 3 * D, HT], BF16, "ExternalInput")
    xT = P.dram("xT", [2, D, HT], F32, "ExternalInput")
    wb = P.dram("wb", [32, 128, 64, 128], F32, "ExternalInput")
    wo = P.dram("wo", [32, 128, 32, 128], F32, "ExternalInput")
    sng = P.dram("sng", [128, 32], F32, "ExternalInput")
    lng = P.dram("lng", [128, 16], F32, "ExternalInput")
    lnb = P.dram("lnb", [128, 16], F32, "ExternalInput")
    modv = P.dram("modv", [128, 2, 3, KC], F32, "ExternalInput")
    xnT = P.dram("xnT", [2, D, HT], F32, "ExternalOutput")
    emit_p3(P, yaT, vbT, zbT, ycT, gT, xT, wb, wo, sng, lng, lnb, modv, xnT)
    return P.finish()


def w_blocks(w):
    K_, N_ = w.shape
    return np.ascontiguousarray(w.reshape(K_ // 128, 128, N_ // 128, 128).transpose(2, 1, 0, 3))


def halves_T(x2d, c2d, r, dt=None):
    out = []
    for h in range(2):
        t = np.concatenate([x2d[r * TOK_X + 512 * h: r * TOK_X + 512 * (h + 1)],
                            c2d[r * TOK_C + 16 * h: r * TOK_C + 16 * (h + 1)]], axis=0)
        out.append(t.T)
    a = np.ascontiguousarray(np.stack(out))
    return a if dt is None else a.astype(dt)


NROW = SEQ // 64


def emit_conformer(P, fm_src, cfw, cfb, vb_out):
    banks = P.get_banks()
    m0 = P.mark()
    w_sb = P.sb("cfw", [128, 2, 31], F32)
    b_sb = P.sb("cfb", [128, 2], F32)
    onesf = P.sb("cf1", [128, 128], F32)
    idf = P.sb("cfid", [128, 128], F32)
    dg = P.sb("cfdg", [128, 31, 128], BF16)
    a_sb = P.sb("cfa", [128, NTOK], BF16)
    g_sb = P.sb("cfg", [128, NTOK], BF16)
    sgm = P.sb("cfs", [128, NTOK], BF16)
    upx = P.sb("upx", [128, NROW, 94], BF16)
    upc = P.sb("upc", [128, CTX + 30], BF16)
    ob = P.sb("cfob", [128, NTOK], BF16)
    P.dma("sp", w_sb[:, :, :], cfw, w=["cfw"])
    P.dma("sp", b_sb[:, :], cfb, w=["cfb"])
    P.op("pool", lambda e: e.memset(onesf[:, :], 1.0), w=["cf1"])
    P.op("pool", lambda e: e.affine_select(out=idf[:, :], in_=onesf[:, :], pattern=[[-1, 128]], compare_op=ALU.is_equal,
                                           fill=0.0, base=0, channel_multiplier=1), r=["cf1"], w=["cfid"])
    P.op("pool", lambda e: e.memset(upx[:, :, :], 0.0), w=["upx"])
    P.op("pool", lambda e: e.memset(upc[:, :], 0.0), w=["upc"])
    nb = 0
    for cc in range(2):
        P.dma("sp", a_sb[:, :], fm_src(10 + cc), w=["cfa"])
        P.dma("sp", g_sb[:, :], fm_src(12 + cc), w=["cfg"])
        P.op("dve", lambda e, cc=cc: e.tensor_tensor(
            out=dg[:, :, :], in0=idf[:, :].unsqueeze(1).broadcast_to([128, 31, 128]),
            in1=w_sb[:, cc, :].unsqueeze(2).broadcast_to([128, 31, 128]), op=ALU.mult),
            r=["cfid", "cfw"], w=["cfdg"])
        P.op("act", lambda e: e.activation(out=sgm[:, :], in_=g_sb[:, :], func=AF.Sigmoid), r=["cfg"], w=["cfs"])
        P.op("dve", lambda e: e.tensor_tensor(out=upc[:, 15:15 + CTX], in0=a_sb[:, 0:CTX], in1=sgm[:, 0:CTX], op=ALU.mult),
             r=["cfa", "cfs"], w=["upc"])
        P.op("dve", lambda e: e.tensor_tensor(
            out=upx[:, :, 15:79], in0=a_sb[:, CTX:].rearrange("p (r t) -> p r t", t=64),
            in1=sgm[:, CTX:].rearrange("p (r t) -> p r t", t=64), op=ALU.mult), r=["cfa", "cfs"], w=["upx"])
        tiles = [("c", 0)] + [("x", r0) for r0 in range(0, NROW, 8)]
        for (kind, r0) in tiles:
            bk = nb % 4
            nb += 1
            for k in range(31):
                if kind == "c":
                    P.op("pe", lambda e, bk=bk, k=k: e.matmul(
                        banks[bk][:, 0:CTX], lhsT=dg[:, k, :], rhs=upc[:, k:k + CTX], start=(k == 0), stop=(k == 30)),
                        r=["cfdg", "upc"], w=["bank%d" % bk])
                else:
                    P.op("pe", lambda e, bk=bk, k=k, r0=r0: e.matmul(
                        banks[bk][:, :].rearrange("p (r t) -> p r t", t=64), lhsT=dg[:, k, :],
                        rhs=upx[:, r0:r0 + 8, k:k + 64], start=(k == 0), stop=(k == 30)),
                        r=["cfdg", "upx"], w=["bank%d" % bk])
            if kind == "c":
                lo, n_ = 0, CTX
            else:
                lo, n_ = CTX + r0 * 64, 512
            if nb % 2:
                P.op("act", lambda e, bk=bk, lo=lo, n_=n_, cc=cc: e.activation(
                    out=ob[:, lo:lo + n_], in_=banks[bk][:, 0:n_], func=AF.Identity, bias=b_sb[:, cc:cc + 1], scale=1.0),
                    r=["bank%d" % bk, "cfb"], w=["cfob"])
            else:
                P.op("dve", lambda e, bk=bk, lo=lo, n_=n_, cc=cc: e.tensor_scalar(
                    out=ob[:, lo:lo + n_], in0=banks[bk][:, 0:n_], scalar1=b_sb[:, cc:cc + 1], scalar2=None, op0=ALU.add),
                    r=["bank%d" % bk, "cfb"], w=["cfob"])
        P.dma("pool", vb_out[cc * 128:(cc + 1) * 128, :], ob[:, :], r=["cfob"])
    P.release(m0)


def emit_trig_table(P, dst, nrows, ncols, N, row_base, kind, name, scale=1.0):
    ip = P.sb(name + "_ip", [128, ncols], I32)
    ij = P.sb(name + "_ij", [128, ncols], I32)
    fp = P.sb(name + "_fp", [128, ncols], F32)
    fj = P.sb(name + "_fj", [128, ncols], F32)
    P.op("pool", lambda e: e.iota(ip[:, :], pattern=[[0, ncols]], base=row_base, channel_multiplier=1), w=[ip.name])
    P.op("pool", lambda e: e.iota(ij[:, :], pattern=[[1, ncols]], base=0, channel_multiplier=0), w=[ij.name])
    P.op("dve", lambda e: e.tensor_tensor(out=ip[:, :], in0=ip[:, :], in1=ij[:, :], op=ALU.mult),
         r=[ip.name, ij.name], w=[ip.name])
    off = N // 2 if kind == "sin" else 3 * N // 4
    P.op("dve", lambda e: e.tensor_scalar(out=ip[:, :], in0=ip[:, :], scalar1=float(off), scalar2=None,
                                          op0=ALU.add), r=[ip.name], w=[ip.name])
    P.op("dve", lambda e: e.tensor_scalar(out=ip[:, :], in0=ip[:, :], scalar1=int(N - 1), scalar2=None,
                                          op0=ALU.bitwise_and), r=[ip.name], w=[ip.name])
    P.op("dve", lambda e: e.tensor_scalar(out=fp[:, :], in0=ip[:, :], scalar1=float(-N / 2), scalar2=None,
                                          op0=ALU.add), r=[ip.name], w=[fp.name])
    P.op("act", lambda e: e.activation(out=fj[:, :], in_=fp[:, :], func=AF.Sin, scale=float(2 * np.pi / N)),
         r=[fp.name], w=[fj.name])
    P.op("dve", lambda e: e.tensor_scalar(out=dst, in0=fj[0:nrows, :], scalar1=float(scale), scalar2=None,
                                          op0=ALU.mult), r=[fj.name], w=[name])


def emit_fourier(P, tm_vz, t1s, yc_out):
    banks = P.get_banks()
    m0 = P.mark()
    c128 = P.sb("c128", [128, 128], BF16)
    s128 = P.sb("s128", [128, 128], BF16)
    twc = P.sb("twc", [128, 64], F32)
    tws = P.sb("tws", [128, 64], F32)
    ra = P.sb("ra", [64, 128], BF16)
    rm = P.sb("rm", [64, 128], BF16)
    c256 = P.sb("c256", [128, 2, 256], BF16)
    s256 = P.sb("s256", [128, 2, 256], BF16)
    ns256 = P.sb("ns256", [128, 2, 256], BF16)
    mt = P.mark()
    emit_trig_table(P, c128[:, :], 128, 128, 128, 0, "cos", "c128")
    emit_trig_table(P, s128[:, :], 128, 128, 128, 0, "sin", "s128")
    emit_trig_table(P, twc[:, :], 128, 64, 8192, 0, "cos", "twc")
    emit_trig_table(P, tws[:, :], 128, 64, 8192, 0, "sin", "tws")
    emit_trig_table(P, ra[:, 0:64], 64, 64, 64, 0, "cos", "ra")
    emit_trig_table(P, ra[:, 64:128], 64, 64, 64, 0, "sin", "ra")
    emit_trig_table(P, rm[:, 0:64], 64, 64, 64, 0, "sin", "rm", scale=-1.0)
    emit_trig_table(P, rm[:, 64:128], 64, 64, 64, 0, "cos", "rm")
    for cc in range(2):
        emit_trig_table(P, c256[:, cc, :], 128, 256, 256, cc * 128, "cos", "c256")
        emit_trig_table(P, s256[:, cc, :], 128, 256, 256, cc * 128, "sin", "s256")
        emit_trig_table(P, ns256[:, cc, :], 128, 256, 256, cc * 128, "sin", "ns256", scale=-1.0)
    P.release(mt)

    usb = P.sb("usb", [128, 2, 2, 64, 128], BF16)
    zt = [P.sb("zt%d" % i, [128, 256], BF16) for i in range(2)]
    zs = [P.sb("zs%d" % i, [128, 256], F32) for i in range(2)]
    yo = [P.sb("yo%d" % i, [128, 256], BF16) for i in range(2)]
    cnt = {"e": 0, "z": 0}

    def stage3(ntiles, tok_base, norm):
        for tl in range(ntiles):
            bk = 6 + (tl % 2)
            k = 0
            for cc in range(2):
                for ri in range(2):
                    rhs = (c256 if ri == 0 else ns256)[:, cc, :]
                    P.op("pe", lambda e, bk=bk, cc=cc, ri=ri, tl=tl, rhs=rhs, k=k: e.matmul(
                        banks[bk][:, 0:256], lhsT=usb[:, cc, ri, tl, :], rhs=rhs, start=(k == 0), stop=(k == 3)),
                        r=["usb", "c256", "ns256"], w=["bank%d" % bk])
                    k += 1
            zi = cnt["z"] % 2
            cnt["z"] += 1
            r0 = tok_base + tl * 128
            P.dma("sp", zt[zi][:, :], tm_vz[r0:r0 + 128, 256:512], w=[zt[zi].name])
            P.op("act", lambda e, zi=zi: e.activation(out=zs[zi][:, :], in_=zt[zi][:, :], func=AF.Silu),
                 r=[zt[zi].name], w=[zs[zi].name])
            P.op("dve", lambda e, zi=zi, bk=bk: e.scalar_tensor_tensor(
                out=yo[zi][:, :], in0=banks[bk][:, 0:256], scalar=float(norm), in1=zs[zi][:, :],
                op0=ALU.mult, op1=ALU.mult), r=["bank%d" % bk, zs[zi].name], w=[yo[zi].name])
            P.dma("pool", yc_out[r0:r0 + 128, :], yo[zi][:, :], r=[yo[zi].name])

    mc = P.mark()
    vc = P.sb("vc", [128, 2, 256], BF16)
    P.dma("sp", vc[:, :, :], tm_vz[0:CTX, 0:256].rearrange("(t p) c -> p t c", p=128), w=["vc"])
    for cc in range(2):
        for half, tab in enumerate([c256, s256]):
            bk = 2 * cc + half
            for tl in range(2):
                P.op("pe", lambda e, bk=bk, cc=cc, tl=tl, tab=tab: e.matmul(
                    banks[bk][:, 0:256], lhsT=vc[:, tl, cc * 128:(cc + 1) * 128], rhs=tab[:, tl, :],
                    start=(tl == 0), stop=(tl == 1)), r=["vc", "c256", "s256"], w=["bank%d" % bk])
            P.op("act", lambda e, bk=bk, cc=cc, half=half: e.copy(
                out=usb[:, cc, half, 0:2, :], in_=banks[bk][:, 0:256].rearrange("p (t s) -> p t s", s=128)),
                r=["bank%d" % bk], w=["usb"])
    stage3(2, 0, 1.0 / 256.0)
    P.release(mc)

    ms = P.mark()
    xr = P.sb("xr", [128, 64, 256], BF16)
    t1sb = P.sb("t1sb", [128, 64, 2, 256], BF16)
    tmpa = [P.sb("tmpa%d" % i, [128, 256], F32) for i in range(2)]
    P.dma("sp", xr[:, :, :], tm_vz[CTX:, 0:256].rearrange("(a b) c -> a b c", b=64), w=["xr"])
    for ct in range(32):
        bo = 2 * (ct % 2)
        P.op("pe", lambda e, ct=ct, bo=bo: e.matmul(
            banks[bo][:, :], lhsT=c128[:, :], rhs=xr[:, 2 * ct:2 * ct + 2, :], start=True, stop=True),
            r=["c128", "xr"], w=["bank%d" % bo])
        P.op("pe", lambda e, ct=ct, bo=bo: e.matmul(
            banks[bo + 1][:, :], lhsT=s128[:, :], rhs=xr[:, 2 * ct:2 * ct + 2, :], start=True, stop=True),
            r=["s128", "xr"], w=["bank%d" % (bo + 1)])
        for q in range(2):
            s2 = 2 * ct + q
            tr = banks[bo][:, q * 256:(q + 1) * 256]
            tn = banks[bo + 1][:, q * 256:(q + 1) * 256]
            ti = cnt["e"] % 2
            cnt["e"] += 1
            P.op("dve", lambda e, tn=tn, ti=ti, s2=s2: e.tensor_scalar(
                out=tmpa[ti][:, :], in0=tn, scalar1=tws[:, s2:s2 + 1], scalar2=-1.0, op0=ALU.mult, op1=ALU.mult),
                r=["bank%d" % (bo + 1), "tws"], w=[tmpa[ti].name])
            P.op("dve", lambda e, tr=tr, ti=ti, s2=s2: e.scalar_tensor_tensor(
                out=t1sb[:, s2, 0, :], in0=tr, scalar=twc[:, s2:s2 + 1], in1=tmpa[ti][:, :],
                op0=ALU.mult, op1=ALU.add), r=["bank%d" % bo, "twc", tmpa[ti].name], w=["t1sb"])
            ti = cnt["e"] % 2
            cnt["e"] += 1
            P.op("dve", lambda e, tr=tr, ti=ti, s2=s2: e.tensor_scalar(
                out=tmpa[ti][:, :], in0=tr, scalar1=tws[:, s2:s2 + 1], scalar2=None, op0=ALU.mult),
                r=["bank%d" % bo, "tws"], w=[tmpa[ti].name])
            P.op("dve", lambda e, tn=tn, ti=ti, s2=s2: e.scalar_tensor_tensor(
                out=t1sb[:, s2, 1, :], in0=tn, scalar=twc[:, s2:s2 + 1], in1=tmpa[ti][:, :],
                op0=ALU.mult, op1=ALU.add), r=["bank%d" % (bo + 1), "twc", tmpa[ti].name], w=["t1sb"])
    P.dma("sp", t1s, t1sb[:, :, :, :], r=["t1sb"], w=["t1s"])
    P.release(ms)
    m2 = P.mark()
    t2 = [P.sb("t2in%d" % i, [64, 32, 2, 256], BF16) for i in range(2)]
    t1v = t1s.rearrange("a b r c -> b a r c")
    ne = 0
    for gq in range(4):
        bi = gq % 2
        P.dma("sp", t2[bi][:, :, :, :], t1v[:, gq * 32:(gq + 1) * 32, :, :], r=["t1s"], w=[t2[bi].name])
        for sl in range(32):
            s1p = gq * 32 + sl
            for cc in range(2):
                bk = ne % 4
                P.op("pe", lambda e, bk=bk, bi=bi, sl=sl, cc=cc: e.matmul(
                    banks[bk][:, 0:128], lhsT=t2[bi][:, sl, 0, cc * 128:(cc + 1) * 128], rhs=ra[:, :],
                    start=True, stop=False), r=[t2[bi].name, "ra"], w=["bank%d" % bk])
                P.op("pe", lambda e, bk=bk, bi=bi, sl=sl, cc=cc: e.matmul(
                    banks[bk][:, 0:128], lhsT=t2[bi][:, sl, 1, cc * 128:(cc + 1) * 128], rhs=rm[:, :],
                    start=False, stop=True), r=[t2[bi].name, "rm"], w=["bank%d" % bk])
                eng = "act" if ne % 2 else "dve"
                ne += 1
                src = banks[bk][:, 0:128].rearrange("p (r s) -> p r s", s=64)
                if eng == "act":
                    P.op("act", lambda e, cc=cc, s1p=s1p, src=src: e.copy(out=usb[:, cc, :, :, s1p], in_=src),
                         r=["bank%d" % bk], w=["usb"])
                else:
                    P.op("dve", lambda e, cc=cc, s1p=s1p, src=src: e.tensor_copy(out=usb[:, cc, :, :, s1p], in_=src),
                         r=["bank%d" % bk], w=["usb"])
    stage3(64, CTX, 1.0 / np.sqrt(8192.0 * 256.0))
    P.release(m2)
    P.release(m0)


NCHUNK = NTOK // 128
SSD_DEBUG = None
SSD_STEP = 99


def emit_ssd(P, fm_src, za_src, tm_dt, ssdu, yfs, cw, cb, dtb, alog, dsk, ya_out):
    banks = P.get_banks()
    m0 = P.mark()
    cw_sb = P.sb("cw", [128, 6, 5], F32)
    cb_sb = P.sb("cb", [128, 6], F32)
    P.dma("sp", cw_sb[:, :, :], cw, w=["cw"])
    P.dma("sp", cb_sb[:, :], cb, w=["cb"])
    m1 = P.mark()
    pc = [P.sb("pc%d" % i, [128, CTX + 4], BF16) for i in range(2)]
    px = [P.sb("px%d" % i, [128, SEQ + 4], BF16) for i in range(2)]
    ub = [P.sb("cub%d" % i, [128, NTOK], BF16) for i in range(2)]
    c1 = P.sb("cv1", [128, 128], F32)
    cid = P.sb("cvid", [128, 128], F32)
    dg5 = [P.sb("dg5_%d" % i, [128, 5, 128], BF16) for i in range(2)]
    P.op("pool", lambda e: e.memset(c1[:, :], 1.0), w=["cv1"])
    P.op("pool", lambda e: e.affine_select(out=cid[:, :], in_=c1[:, :], pattern=[[-1, 128]], compare_op=ALU.is_equal,
                                           fill=0.0, base=0, channel_multiplier=1), r=["cv1"], w=["cvid"])
    for i in range(2):
        P.op("pool", lambda e, i=i: e.memset(pc[i][:, :], 0.0), w=[pc[i].name])
        P.op("pool", lambda e, i=i: e.memset(px[i][:, :], 0.0), w=[px[i].name])
    nbk = 0
    for c in range(6):
        b = c % 2
        src = fm_src(c)
        P.dma("sp", pc[b][:, 2:2 + CTX], src[:, 0:CTX], w=[pc[b].name])
        P.dma("sp", px[b][:, 2:2 + SEQ], src[:, CTX:], w=[px[b].name])
        P.op("dve", lambda e, c=c, b=b: e.tensor_tensor(
            out=dg5[b][:, :, :], in0=cid[:, :].unsqueeze(1).broadcast_to([128, 5, 128]),
            in1=cw_sb[:, c, :].unsqueeze(2).broadcast_to([128, 5, 128]), op=ALU.mult),
            r=["cvid", "cw"], w=[dg5[b].name])
        tiles = [(pc[b], 0, 0, CTX)] + [(px[b], t_, CTX + t_, 512) for t_ in range(0, SEQ, 512)]
        for (pb, so, lo, n_) in tiles:
            bk = nbk % 4
            nbk += 1
            for k in range(5):
                P.op("pe", lambda e, bk=bk, k=k, pb=pb, so=so, n_=n_, b=b: e.matmul(
                    banks[bk][:, 0:n_], lhsT=dg5[b][:, k, :], rhs=pb[:, so + k:so + k + n_],
                    start=(k == 0), stop=(k == 4)), r=[dg5[b].name, pb.name], w=["bank%d" % bk])
            P.op("act", lambda e, bk=bk, lo=lo, n_=n_, c=c, b=b: e.activation(
                out=ub[b][:, lo:lo + n_], in_=banks[bk][:, 0:n_], func=AF.Silu, bias=cb_sb[:, c:c + 1], scale=1.0),
                r=["bank%d" % bk, "cb"], w=[ub[b].name])
        P.dma("pool", ssdu[c], ub[b][:, :], r=[ub[b].name], w=["ssdu"])
    P.release(m1)
    if SSD_DEBUG == "conv":
        P.release(m0)
        return
    dt_all = P.sb("dt_all", [128, NCHUNK, 16], F32)
    a_all = P.sb("a_all", [128, NCHUNK, 16], F32)
    dtb_sb = P.sb("dtb", [128, 16], F32)
    nex = P.sb("nex", [128, 16], F32)
    dsk_sb = P.sb("dsk", [128, 4], F32)
    P.dma("sp", dtb_sb[:, :], dtb, w=["dtb"])
    P.dma("sp", nex[:, :], alog, w=["nex"])
    P.dma("sp", dsk_sb[:, :], dsk, w=["dsk"])
    m2 = P.mark()
    xr_ = P.sb("dtx", [128, NCHUNK, 16], F32)
    ax_ = P.sb("dtax", [128, NCHUNK, 16], F32)
    P.dma("sp", xr_[:, :, :], tm_dt.rearrange("(n p) c -> p n c", p=128), w=["dtx"])
    P.op("dve", lambda e: e.tensor_tensor(out=xr_[:, :, :], in0=xr_[:, :, :],
                                          in1=dtb_sb[:, :].unsqueeze(1).broadcast_to([128, NCHUNK, 16]), op=ALU.add),
         r=["dtx", "dtb"], w=["dtx"])
    P.op("act", lambda e: e.activation(out=ax_[:, :, :], in_=xr_[:, :, :], func=AF.Abs), r=["dtx"], w=["dtax"])
    P.op("act", lambda e: e.activation(out=ax_[:, :, :], in_=ax_[:, :, :], func=AF.Exp, scale=-1.0),
         r=["dtax"], w=["dtax"])
    P.op("act", lambda e: e.activation(out=ax_[:, :, :], in_=ax_[:, :, :], func=AF.Ln, bias=1.0, scale=1.0),
         r=["dtax"], w=["dtax"])
    P.op("dve", lambda e: e.tensor_scalar_max(out=xr_[:, :, :], in0=xr_[:, :, :], scalar1=0.0), r=["dtx"], w=["dtx"])
    P.op("dve", lambda e: e.tensor_tensor(out=dt_all[:, :, :], in0=xr_[:, :, :], in1=ax_[:, :, :], op=ALU.add),
         r=["dtx", "dtax"], w=["dt_all"])
    P.op("act", lambda e: e.activation(out=nex[:, :], in_=nex[:, :], func=AF.Exp), r=["nex"], w=["nex"])
    P.op("dve", lambda e: e.scalar_tensor_tensor(
        out=a_all[:, :, :], in0=dt_all[:, :, :], scalar=-1.0,
        in1=nex[:, :].unsqueeze(1).broadcast_to([128, NCHUNK, 16]), op0=ALU.mult, op1=ALU.mult),
        r=["dt_all", "nex"], w=["a_all"])
    P.release(m2)
    onesf = P.sb("s_ones", [128, 128], F32)
    oneb = P.sb("s_oneb", [128, 128], BF16)
    idf = P.sb("s_idf", [128, 128], F32)
    idb = P.sb("s_idb", [128, 128], BF16)
    Lf = P.sb("s_Lf", [128, 128], F32)
    Uf = P.sb("s_Uf", [128, 128], F32)
    Lb = P.sb("s_Lb", [128, 128], F32)
    Ub = P.sb("s_Ub", [128, 128], F32)
    P.op("pool", lambda e: e.memset(onesf[:, :], 1.0), w=["s_ones"])
    P.op("pool", lambda e: e.memset(oneb[:, :], 1.0), w=["s_oneb"])

    def sel(dst, src, pat, cm, op):
        P.op("pool", lambda e: e.affine_select(out=dst[:, :], in_=src[:, :], pattern=[[pat, 128]], compare_op=op,
                                               fill=0.0, base=0, channel_multiplier=cm),
             r=[src.name], w=[dst.name])
    sel(idf, onesf, -1, 1, ALU.is_equal)
    sel(idb, oneb, -1, 1, ALU.is_equal)
    sel(Lf, onesf, -1, 1, ALU.is_gt)
    sel(Uf, onesf, 1, -1, ALU.is_ge)
    sel(Lb, onesf, 1, -1, ALU.is_gt)
    sel(Ub, onesf, -1, 1, ALU.is_ge)
    if SSD_DEBUG == "const":
        P.release(m0)
        return
    hst = P.sb("hst", [128, 512], F32)
    hbf = P.sb("hbf", [128, 512], BF16)
    ut = [P.sb("ut%d" % i, [128, 6, 128], BF16) for i in range(3)]
    xdt = P.sb("xdt", [128, 8, 64], BF16)
    xdtd = P.sb("xdtd", [128, 8, 64], BF16)
    xsb = P.sb("xsb", [128, 640], BF16)
    smes = [P.sb("sme%d" % i, [128, 24], F32) for i in range(2)]
    pend = [None]
    la = P.sb("la", [128, 8, 128], F32)
    dm = [P.sb("dm%d" % i, [128, 4, 128], BF16) for i in range(2)]
    wm = [P.sb("wm%d" % i, [128, 4, 128], BF16) for i in range(2)]
    mm = P.sb("mm", [128, 128], BF16)
    yt = [P.sb("yt%d" % i, [128, 512], F32) for i in range(2)]
    yf = [P.sb("yf%d" % i, [128, 512], F32) for i in range(2)]
    zat = [P.sb("zat%d" % i, [128, 4, 128], BF16) for i in range(2)]
    sz = P.sb("sz", [128, 4, 128], F32)
    y2 = P.sb("y2", [128, 4, 128], F32)
    yo = [P.sb("yob%d" % i, [128, 4, 128], BF16) for i in range(2)]
    htmp = P.sb("htmp", [128, 512], F32)
    b0 = banks[0][:, :].bitcast(BF16)
    uv = ssdu.rearrange("c p t -> p c t")
    zav = za_src.rearrange("c p t -> p c t")
    yov = ya_out.rearrange("(c p) t -> p c t", p=128)
    seq = []
    for d in range(2):
        order = list(range(NCHUNK)) if d == 0 else [1, 0] + list(range(NCHUNK - 1, 1, -1))
        for k_, n in enumerate(order):
            seq.append((d, n, k_ == 0))
    if isinstance(SSD_DEBUG, int):
        seq = seq[:SSD_DEBUG]

    def chunk(idx):
        d, n, first = seq[idx]
        Lm, Um = (Lf, Uf) if d == 0 else (Lb, Ub)
        ui = idx % 2
        if first:
            P.op("pool", lambda e: e.memset(hst[:, :], 0.0), w=["hst"])
            P.op("pool", lambda e: e.memset(hbf[:, :], 0.0), w=["hbf"])
        t0 = n * 128
        sme = smes[idx % 2]
        u = ut[idx % 3]
        P.dma("sp", u[:, :, :], uv[:, :, t0:t0 + 128], r=["ssdu"], w=[u.name])
        a_c = a_all[:, n, d * 8:(d + 1) * 8]
        dt_c = dt_all[:, n, d * 8:(d + 1) * 8]
        for c in range(4):
            P.op("pe", lambda e, c=c, u=u: e.transpose(out=b0[:, c * 128:(c + 1) * 128], in_=u[:, c, :],
                                                       identity=idb[:, :]), r=[u.name, "s_idb"], w=["bank0"])
        P.op("pe", lambda e, u=u: e.transpose(out=b0[:, 512:640], in_=u[:, 4, :], identity=idb[:, :]),
             r=[u.name, "s_idb"], w=["bank0"])
        P.op("act", lambda e: e.copy(out=xsb[:, :], in_=b0[:, 0:640]), r=["bank0"], w=["xsb"])
        P.op("dve", lambda e, dt_c=dt_c: e.tensor_tensor(
            out=xdt[:, :, :], in0=xsb[:, 0:512].rearrange("p (h q) -> p h q", q=64),
            in1=dt_c.unsqueeze(2).broadcast_to([128, 8, 64]), op=ALU.mult), r=["xsb", "dt_all"], w=["xdt"])
        for q, lm in enumerate((Um, Lm, onesf)):
            P.op("pe", lambda e, q=q, lm=lm, a_c=a_c: e.matmul(
                banks[3][:, 128 + 8 * q:136 + 8 * q], lhsT=lm[:, :], rhs=a_c, start=True, stop=True),
                r=[lm.name, "a_all"], w=["bank3s"])
        P.op("act", lambda e: e.activation(out=sme[:, :], in_=banks[3][:, 128:152], func=AF.Exp),
             r=["bank3s"], w=[sme.name])
        P.op("dve", lambda e, Lm=Lm, a_c=a_c: e.tensor_tensor(
            out=la[:, :, :], in0=Lm[:, :].unsqueeze(1).broadcast_to([128, 8, 128]),
            in1=a_c.unsqueeze(2).broadcast_to([128, 8, 128]), op=ALU.mult), r=[Lm.name, "a_all"], w=["la"])
        if pend[0] is not None:
            pend[0]()
            pend[0] = None
        for hg in range(2):
            for hh in range(4):
                h_ = hg * 4 + hh
                P.op("pe", lambda e, hg=hg, hh=hh, h_=h_, Um=Um: e.matmul(
                    banks[1 + hg][:, hh * 128:(hh + 1) * 128], lhsT=la[:, h_, :], rhs=Um[:, :],
                    start=True, stop=True), r=["la", Um.name], w=["bank%d" % (1 + hg)])
            P.op("act", lambda e, hg=hg: e.activation(
                out=dm[hg][:, :, :], in_=banks[1 + hg][:, :].rearrange("p (h i) -> p h i", i=128), func=AF.Exp),
                r=["bank%d" % (1 + hg)], w=[dm[hg].name])
        P.op("pe", lambda e, u=u: e.matmul(banks[3][:, 0:128], lhsT=u[:, 4, :], rhs=u[:, 5, :], start=True, stop=True),
             r=[u.name], w=["bank3"])
        P.op("dve", lambda e, Um=Um: e.tensor_tensor(out=mm[:, :], in0=banks[3][:, 0:128], in1=Um[:, :], op=ALU.mult),
             r=["bank3", Um.name], w=["mm"])
        for hg in range(2):
            P.op("dve", lambda e, hg=hg: e.tensor_tensor(
                out=wm[hg][:, :, :], in0=dm[hg][:, :, :], in1=mm[:, :].unsqueeze(1).broadcast_to([128, 4, 128]),
                op=ALU.mult), r=[dm[hg].name, "mm"], w=[wm[hg].name])
        for h_ in range(8):
            P.op("pe", lambda e, h_=h_: e.matmul(
                banks[4][:, h_ * 64:(h_ + 1) * 64], lhsT=wm[h_ // 4][:, h_ % 4, :], rhs=xdt[:, h_, :],
                start=True, stop=True), r=[wm[h_ // 4].name, "xdt"], w=["bank4"])
        P.op("pe", lambda e, u=u: e.matmul(banks[5][:, :], lhsT=u[:, 5, :], rhs=hbf[:, :], start=True, stop=True),
             r=[u.name, "hbf"], w=["bank5"])
        P.op("dve", lambda e: e.tensor_tensor(
            out=xdtd[:, :, :], in0=xdt[:, :, :], in1=sme[:, 8:16].unsqueeze(2).broadcast_to([128, 8, 64]),
            op=ALU.mult), r=["xdt", sme.name], w=["xdtd"])
        P.op("pe", lambda e: e.matmul(banks[6][:, :], lhsT=xsb[:, 512:640], rhs=xdtd[:, :, :], start=True, stop=True),
             r=["xsb", "xdtd"], w=["bank6"])
        P.op("dve", lambda e: e.tensor_tensor(
            out=htmp[:, :].rearrange("p (h q) -> p h q", q=64), in0=hst[:, :].rearrange("p (h q) -> p h q", q=64),
            in1=sme[:, 16:24].unsqueeze(2).broadcast_to([128, 8, 64]), op=ALU.mult),
            r=["hst", sme.name], w=["htmp"])
        P.op("dve", lambda e: e.tensor_tensor(out=hst[:, :], in0=banks[6][:, :], in1=htmp[:, :], op=ALU.add),
             r=["bank6", "htmp"], w=["hst"])
        P.op("dve", lambda e: e.tensor_copy(out=hbf[:, :], in_=hst[:, :]), r=["hst"], w=["hbf"])
        yi = (idx + 1) % 2
        y_ = yt[yi]

        def comb(y_=y_, sme=sme, d=d, t0=t0):
            P.op("dve", lambda e: e.tensor_tensor(
                out=y_[:, :].rearrange("p (h q) -> p h q", q=64),
                in0=banks[5][:, :].rearrange("p (h q) -> p h q", q=64),
                in1=sme[:, 0:8].unsqueeze(2).broadcast_to([128, 8, 64]), op=ALU.mult),
                r=["bank5", sme.name], w=[y_.name])
            P.op("dve", lambda e: e.tensor_tensor(out=y_[:, :], in0=banks[4][:, :], in1=y_[:, :], op=ALU.add),
                 r=["bank4", y_.name], w=[y_.name])
            if d == 0:
                P.dma("pool", yfs[t0:t0 + 128, :], y_[:, :], r=[y_.name], w=["yfs"])
        pend[0] = comb
        if d == 0:
            return
        else:
            yield
            f_ = yf[yi]
            z_ = zat[yi]
            o_ = yo[yi]
            P.dma("sp", f_[:, :], yfs[t0:t0 + 128, :], r=["yfs"], w=[f_.name])
            P.dma("sp", z_[:, :, :], zav[:, :, t0:t0 + 128], w=[z_.name])
            P.op("pool", lambda e, y_=y_, f_=f_: e.tensor_tensor(out=y_[:, :], in0=y_[:, :], in1=f_[:, :], op=ALU.add),
                 r=[y_.name, f_.name], w=[y_.name])
            for c in range(4):
                P.op("pe", lambda e, c=c, y_=y_: e.transpose(
                    out=banks[7][:, c * 128:(c + 1) * 128], in_=y_[:, c * 128:(c + 1) * 128], identity=idf[:, :]),
                    r=[y_.name, "s_idf"], w=["bank7"])
            P.op("act", lambda e, z_=z_: e.activation(out=sz[:, :, :], in_=z_[:, :, :], func=AF.Silu),
                 r=[z_.name], w=["sz"])
            for c in range(4):
                P.op("dve", lambda e, c=c, u=u: e.scalar_tensor_tensor(
                    out=y2[:, c, :], in0=u[:, c, :], scalar=dsk_sb[:, c:c + 1],
                    in1=banks[7][:, c * 128:(c + 1) * 128], op0=ALU.mult, op1=ALU.add),
                    r=[u.name, "dsk", "bank7"], w=["y2"])
            P.op("pool", lambda e, o_=o_: e.tensor_tensor(out=o_[:, :, :], in0=y2[:, :, :], in1=sz[:, :, :], op=ALU.mult),
                 r=["y2", "sz"], w=[o_.name])
            P.dma("pool", yov[:, :, t0:t0 + 128], o_[:, :, :], r=[o_.name])


    gens = [chunk(i) for i in range(len(seq))]
    pending = None
    for g_ in gens:
        try:
            next(g_)
            deferred = g_
        except StopIteration:
            deferred = None
        if pending is not None:
            for _ in pending:
                pass
        pending = deferred
    if pend[0] is not None:
        pend[0]()
        pend[0] = None
    if pending is not None:
        for _ in pending:
            pass
    P.release(m0)


def ssd_params_for_core(inp, l, g):
    cwf = inp["ssd_conv_w"][l]
    cbf = inp["ssd_conv_b"][l]
    chans = list(range(512 * g, 512 * g + 512)) + list(range(4096 + 128 * g, 4096 + 128 * g + 128)) + \
        list(range(5120 + 128 * g, 5120 + 128 * g + 128))
    cw = np.ascontiguousarray(cwf[:, chans].reshape(5, 6, 128).transpose(2, 1, 0))
    cb = np.ascontiguousarray(cbf[chans].reshape(6, 128).T)
    hs = slice(8 * g, 8 * g + 8)
    dtb = np.concatenate([inp["ssd_dt_bias"][l][0, hs], inp["ssd_dt_bias"][l][1, hs]])
    alog = np.concatenate([inp["ssd_a_log"][l][0, hs], inp["ssd_a_log"][l][1, hs]])
    dtb = np.ascontiguousarray(np.broadcast_to(dtb[None, :], (128, 16))).astype(np.float32)
    alog = np.ascontiguousarray(np.broadcast_to(alog[None, :], (128, 16))).astype(np.float32)
    dvec = np.repeat(inp["ssd_d"][l][hs], 64)
    dsk = np.ascontiguousarray(dvec.reshape(4, 128).T).astype(np.float32)
    return cw, cb, dtb, alog, dsk


def build_p2():
    P = Prog()
    hT = P.dram("hT", [D, NTOK], BF16, "ExternalInput")
    wf = P.dram("wf", [D, NFM * 128], F32, "ExternalInput")
    wt = P.dram("wt", [D, NTM], F32, "ExternalInput")
    cw = P.dram("cw", [128, 6, 5], F32, "ExternalInput")
    cb = P.dram("cb", [128, 6], F32, "ExternalInput")
    dtb = P.dram("dtb", [128, 16], F32, "ExternalInput")
    alog = P.dram("alog", [128, 16], F32, "ExternalInput")
    dsk = P.dram("dsk", [128, 4], F32, "ExternalInput")
    cfw = P.dram("cfw", [128, 2, 31], F32, "ExternalInput")
    cfb = P.dram("cfb", [128, 2], F32, "ExternalInput")
    ofm = P.dram("ofm", [14, 128, NTOK], BF16, "ExternalOutput")
    ya = P.dram("ya", [512, NTOK], BF16, "ExternalOutput")
    vb = P.dram("vb", [256, NTOK], BF16, "ExternalOutput")
    yc = P.dram("yc", [NTOK, 256], BF16, "ExternalOutput")
    fms = P.dram("fms", [14, 128, NTOK], BF16, "Internal")
    vz = P.dram("vzs", [NTOK, 512], BF16, "Internal")
    dts = P.dram("dts", [NTOK, 16], F32, "Internal")
    ssdu = P.dram("ssdu", [6, 128, NTOK], BF16, "Internal")
    yfs = P.dram("yfs", [NTOK, 512], F32, "Internal")
    t1s = P.dram("t1s", [128, 64, 2, 256], BF16, "Internal")

    def fm_dst(c):
        return fms[c] if c < 14 else ofm[c - 14]
    emit_gemm(P, hT, wf, wt, fm_dst, vz, dts)
    P.barrier()
    emit_ssd(P, lambda c: fms[c], fms[6:10], dts, ssdu, yfs, cw, cb, dtb, alog, dsk, ya)
    emit_conformer(P, lambda c: fms[c], cfw, cfb, vb)
    emit_fourier(P, vz, t1s, yc)
    return P.finish()


def kernel(x, c, ctx, c_ctx, w_mod, b_mod, norm_g, w_in, ssd_conv_w, ssd_conv_b, ssd_dt_bias, ssd_a_log, ssd_d,
           ssd_norm_g, cf_conv_w, cf_conv_b, cf_ln_g, cf_ln_b, w_branch, w_out, final_g):
    bf = ml_dtypes.bfloat16
    inp = dict(x=x, c=c, ctx=ctx, c_ctx=c_ctx, w_mod=w_mod, b_mod=b_mod, norm_g=norm_g, w_in=w_in,
               ssd_conv_w=ssd_conv_w, ssd_conv_b=ssd_conv_b, ssd_dt_bias=ssd_dt_bias, ssd_a_log=ssd_a_log,
               ssd_d=ssd_d, ssd_norm_g=ssd_norm_g, cf_conv_w=cf_conv_w, cf_conv_b=cf_conv_b, cf_ln_g=cf_ln_g,
               cf_ln_b=cf_ln_b, w_branch=w_branch, w_out=w_out, final_g=final_g)
    inp = {k: np.asarray(v, dtype=np.float32) for k, v in inp.items()}
    cores = list(range(NCORES))
    mod = run_p0(inp)
    xc = inp["x"][0]
    cc = inp["ctx"][0]
    depth = 2
    for l in range(depth):
        hT = run_p1(shard_tokens_T(xc, cc), inp["norm_g"][l], mod[l])
        hx, hc = unshard_tokens_T(hT)
        hT_all = np.ascontiguousarray(np.concatenate([hc, hx], axis=0).T)
        del hx, hc, hT
        nc2 = get_nc("p2", build_p2)
        in_maps = []
        for g in cores:
            w_fm, w_tm = wcols_for_core(inp["w_in"][l], g)
            cw, cb, dtb, alog, dsk = ssd_params_for_core(inp, l, g)
            cfw = np.ascontiguousarray(inp["cf_conv_w"][l][:, g * 256:(g + 1) * 256].reshape(31, 2, 128).transpose(2, 1, 0))
            cfb = np.ascontiguousarray(inp["cf_conv_b"][l][g * 256:(g + 1) * 256].reshape(2, 128).T)
            in_maps.append({"hT": hT_all, "wf": w_fm, "wt": w_tm, "cw": cw, "cb": cb, "dtb": dtb, "alog": alog,
                            "dsk": dsk, "cfw": cfw, "cfb": cfb})
        res = run_bass_kernel_spmd(nc2, in_maps, core_ids=cores).results
        del in_maps, hT_all
        ya_all = np.concatenate([np.asarray(r["ya"]) for r in res], axis=0)
        vb_all = np.concatenate([np.asarray(r["vb"]) for r in res], axis=0)
        zb_all = np.concatenate([np.asarray(r["ofm"])[0:2].reshape(256, NTOK) for r in res], axis=0)
        yc_all = np.concatenate([np.asarray(r["yc"]) for r in res], axis=1)
        g_all = np.concatenate(
            [np.concatenate([np.asarray(r["ofm"])[2 + 4 * j:6 + 4 * j].reshape(512, NTOK) for r in res], axis=0)
             for j in range(3)], axis=0)
        del res

        def fm_halves(a, r):
            out = []
            for h in range(2):
                xs_ = a[:, CTX + r * TOK_X + 512 * h: CTX + r * TOK_X + 512 * (h + 1)]
                cs_ = a[:, r * TOK_C + 16 * h: r * TOK_C + 16 * (h + 1)]
                out.append(np.concatenate([xs_, cs_], axis=1))
            return np.ascontiguousarray(np.stack(out))
        nc3 = get_nc("p3", build_p3)
        wbk = w_blocks(inp["w_branch"][l])
        wok = w_blocks(inp["w_out"][l])
        sng = vec_layout(inp["ssd_norm_g"][l])
        lng = np.ascontiguousarray(inp["cf_ln_g"][l].reshape(16, 128).T)
        lnb = np.ascontiguousarray(inp["cf_ln_b"][l].reshape(16, 128).T)
        modv = mod_layout(mod[l])
        ycT_all = np.ascontiguousarray(yc_all.T)
        in_maps = []
        for r in cores:
            in_maps.append({"yaT": fm_halves(ya_all, r), "vbT": fm_halves(vb_all, r), "zbT": fm_halves(zb_all, r),
                            "ycT": fm_halves(ycT_all, r), "gT": fm_halves(g_all, r), "xT": halves_T(xc, cc, r),
                            "wb": wbk, "wo": wok, "sng": sng, "lng": lng, "lnb": lnb, "modv": modv})
        del ya_all, vb_all, zb_all, yc_all, g_all, ycT_all
        res = run_bass_kernel_spmd(nc3, in_maps, core_ids=cores).results
        del in_maps
        xn = np.empty_like(xc)
        cn = np.empty_like(cc)
        for r in cores:
            o = np.asarray(res[r]["xnT"])
            for h in range(2):
                xn[r * TOK_X + 512 * h: r * TOK_X + 512 * (h + 1)] = o[h][:, :512].T
                cn[r * TOK_C + 16 * h: r * TOK_C + 16 * (h + 1)] = o[h][:, 512:].T
        xc, cc = xn, cn
    zmod = np.zeros((2, 3 * D), np.float32)
    oT = run_p1(shard_tokens_T(xc, cc), inp["final_g"], zmod, final=True)
    ox, _ = unshard_tokens_T([np.asarray(o) for o in oT])
    return np.ascontiguousarray(ox[None].astype(np.float32))
```
